# Optimizing a Trainium2 kernel written in Bass

```python
import jax, jax.numpy as jnp
from jax import lax
import numpy as np

D_MODEL = 1024
BATCH = 8
SEQ = 2048
DEPTH = 1

ATTN_HEADS = 8
ATTN_KV_HEADS = 2
HEAD_DIM = 64
ATTN_GROUP = ATTN_HEADS // ATTN_KV_HEADS
ATTN_WIDTH = ATTN_HEADS * HEAD_DIM
KV_WIDTH = ATTN_KV_HEADS * HEAD_DIM
ROT_DIM = HEAD_DIM // 4
ROPE_THETA = 500000.0
IDX_HEADS = 4
IDX_DIM = 64
TOPK_MAX = 256
Q_BLOCK = 128
SSM_HEADS = 16
SSM_HEAD_DIM = 64
SSM_WIDTH = SSM_HEADS * SSM_HEAD_DIM
SSM_GROUPS = 4
SSM_HEADS_PER_GROUP = SSM_HEADS // SSM_GROUPS
SSM_STATE = 64
CONV_K = 4
CHUNK = 128
CONV_WIDTH = SSM_WIDTH + 2 * SSM_GROUPS * SSM_STATE
N_BRANCH = 2
EPS = 1e-6
SPLIT_SIZES = (ATTN_WIDTH, KV_WIDTH, KV_WIDTH, ATTN_WIDTH,
               IDX_HEADS * IDX_DIM, IDX_DIM, IDX_HEADS,
               SSM_WIDTH, SSM_WIDTH, SSM_GROUPS * SSM_STATE, SSM_GROUPS * SSM_STATE, SSM_HEADS,
               N_BRANCH * D_MODEL)
IN_WIDTH = sum(SPLIT_SIZES)
SPLIT_OFFSETS = tuple(int(o) for o in np.cumsum(SPLIT_SIZES)[:-1])

kernel_name = "hybrid_dsa_ssd_gated_merge"


def _rmsnorm(x, w):
    xf = x.astype(jnp.float32)
    y = xf * lax.rsqrt(jnp.mean(xf * xf, axis=-1, keepdims=True) + EPS)
    return (y * w.astype(jnp.float32)).astype(x.dtype)


def _partial_rope(x, cos, sin):
    half = ROT_DIM // 2
    x1 = x[..., :half].astype(jnp.float32)
    x2 = x[..., half:ROT_DIM].astype(jnp.float32)
    rot = jnp.concatenate([x1 * cos - x2 * sin, x2 * cos + x1 * sin], axis=-1).astype(x.dtype)
    return jnp.concatenate([rot, x[..., ROT_DIM:]], axis=-1)


def _sparse_attention(q, k, v, q_idx, k_idx, w_idx):
    bsz, seq = q.shape[0], q.shape[1]
    n_blk = seq // Q_BLOCK
    top_k = min(TOPK_MAX, seq // 4)
    key_pos = jnp.arange(seq)
    idx_scale = IDX_DIM ** -0.5
    head_w_scale = IDX_HEADS ** -0.5
    attn_scale = HEAD_DIM ** -0.5

    def to_blocks(a):
        return jnp.moveaxis(a.reshape(bsz, n_blk, Q_BLOCK, *a.shape[2:]), 1, 0)

    def one_block(args):
        qb, qib, wb, t0 = args
        q_pos = t0 + jnp.arange(Q_BLOCK)
        causal = key_pos[None, :] <= q_pos[:, None]
        logits = jnp.einsum('bthd,bsd->bths', qib, k_idx).astype(jnp.float32) * idx_scale
        score = jnp.einsum('bth,bths->bts', wb.astype(jnp.float32) * head_w_scale, jax.nn.relu(logits))
        score = jnp.where(causal[None], score, -jnp.inf)
        _, sel = lax.top_k(score, top_k)
        valid = sel <= q_pos[None, :, None]
        k_sel = jax.vmap(lambda kb, ib: kb[ib])(k, sel)
        v_sel = jax.vmap(lambda vb, ib: vb[ib])(v, sel)
        qg = qb.reshape(bsz, Q_BLOCK, ATTN_KV_HEADS, ATTN_GROUP, HEAD_DIM)
        s = jnp.einsum('btkgd,btskd->btkgs', qg, k_sel).astype(jnp.float32) * attn_scale
        s = jnp.where(valid[:, :, None, None, :], s, -jnp.inf)
        p = jax.nn.softmax(s, axis=-1).astype(v.dtype)
        o = jnp.einsum('btkgs,btskd->btkgd', p, v_sel)
        return o.reshape(bsz, Q_BLOCK, ATTN_WIDTH)

    out = lax.map(one_block, (to_blocks(q), to_blocks(q_idx), to_blocks(w_idx),
                              jnp.arange(n_blk) * Q_BLOCK))
    return jnp.moveaxis(out, 0, 1).reshape(bsz, seq, ATTN_WIDTH)


def _causal_depthwise_conv(x, w, b):
    y = lax.conv_general_dilated(x, w[:, None, :].astype(x.dtype), window_strides=(1,),
                                 padding=[(CONV_K - 1, 0)],
                                 dimension_numbers=('NWC', 'WIO', 'NWC'),
                                 feature_group_count=x.shape[-1])
    return y + b.astype(x.dtype)


def _segsum(a):
    t = a.shape[-1]
    a_rep = jnp.broadcast_to(a[..., None], a.shape + (t,))
    a_rep = jnp.where(jnp.tril(jnp.ones((t, t), bool), -1), a_rep, 0.0)
    seg = jnp.cumsum(a_rep, axis=-2)
    return jnp.where(jnp.tril(jnp.ones((t, t), bool)), seg, -jnp.inf)


def _ssd(x, dt, a, bm, cm):
    bsz, seq = x.shape[0], x.shape[1]
    nc = seq // CHUNK
    g, r = SSM_GROUPS, SSM_HEADS_PER_GROUP
    xd = (x * dt[..., None]).reshape(bsz, nc, CHUNK, g, r, SSM_HEAD_DIM)
    da = (dt * a).reshape(bsz, nc, CHUNK, g, r).transpose(0, 3, 4, 1, 2)
    bc = bm.reshape(bsz, nc, CHUNK, g, SSM_STATE)
    cc = cm.reshape(bsz, nc, CHUNK, g, SSM_STATE)
    a_cum = jnp.cumsum(da, axis=-1)
    l_mat = jnp.exp(_segsum(da))
    cb = jnp.einsum('bclgn,bcsgn->bcgls', cc, bc)
    y_diag = jnp.einsum('bcgls,bgrcls,bcsgrp->bclgrp', cb, l_mat, xd)
    decay_states = jnp.exp(a_cum[..., -1:] - a_cum)
    states = jnp.einsum('bclgn,bgrcl,bclgrp->bcgrpn', bc, decay_states, xd)
    states = jnp.concatenate([jnp.zeros_like(states[:, :1]), states], axis=1)
    chunk_a = jnp.pad(a_cum[..., -1], ((0, 0), (0, 0), (0, 0), (1, 0)))
    decay_chunk = jnp.exp(_segsum(chunk_a))
    new_states = jnp.einsum('bgrzc,bcgrpn->bzgrpn', decay_chunk, states)
    states = new_states[:, :-1]
    y_off = jnp.einsum('bclgn,bcgrpn,bgrcl->bclgrp', cc, states, jnp.exp(a_cum))
    return (y_diag + y_off).reshape(bsz, seq, SSM_HEADS, SSM_HEAD_DIM)


def _gated_group_rmsnorm(y, z, w):
    yz = (y * jax.nn.silu(z)).astype(jnp.float32)
    yg = yz.reshape(*y.shape[:-1], SSM_GROUPS, SSM_WIDTH // SSM_GROUPS)
    yg = yg * lax.rsqrt(jnp.mean(yg * yg, axis=-1, keepdims=True) + EPS)
    return (yg.reshape(y.shape) * w.astype(jnp.float32)).astype(y.dtype)


def setup_inputs(seed: int = 0) -> dict:
    key = jax.random.key(seed)
    ks = jax.random.split(key, 16)
    f32 = jnp.float32
    x = jax.random.normal(ks[0], (BATCH, SEQ, D_MODEL), f32)
    offset = jax.random.randint(ks[1], (BATCH, 1), 0, 4096, dtype=jnp.int32)
    positions = jnp.arange(SEQ, dtype=jnp.int32)[None, :] + offset
    norm_w = 1.0 + 0.02 * jax.random.normal(ks[2], (DEPTH, D_MODEL), f32)
    w_in = jax.random.normal(ks[3], (DEPTH, D_MODEL, IN_WIDTH), f32) * D_MODEL ** -0.5
    gate_bias = 0.01 * jax.random.normal(ks[4], (DEPTH, N_BRANCH * D_MODEL), f32)
    conv_w = jax.random.normal(ks[5], (DEPTH, CONV_K, CONV_WIDTH), f32) * CONV_K ** -0.5
    conv_b = 0.01 * jax.random.normal(ks[6], (DEPTH, CONV_WIDTH), f32)
    u = jax.random.uniform(ks[7], (DEPTH, SSM_HEADS), f32)
    dt0 = jnp.exp(u * (jnp.log(0.1) - jnp.log(0.001)) + jnp.log(0.001))
    dt_bias = dt0 + jnp.log(-jnp.expm1(-dt0))
    a_log = jnp.log(jax.random.uniform(ks[8], (DEPTH, SSM_HEADS), f32, 1.0, 16.0))
    d_skip = 1.0 + 0.1 * jax.random.normal(ks[9], (DEPTH, SSM_HEADS), f32)
    ssm_norm_w = 1.0 + 0.02 * jax.random.normal(ks[10], (DEPTH, SSM_WIDTH), f32)
    w_branch_a = jax.random.normal(ks[11], (DEPTH, ATTN_WIDTH, D_MODEL), f32) * ATTN_WIDTH ** -0.5
    w_branch_b = jax.random.normal(ks[12], (DEPTH, SSM_WIDTH, D_MODEL), f32) * SSM_WIDTH ** -0.5
    w_out = jax.random.normal(ks[13], (DEPTH, D_MODEL, D_MODEL), f32) * D_MODEL ** -0.5
    final_norm_w = 1.0 + 0.02 * jax.random.normal(ks[14], (D_MODEL,), f32)
    return {"x": x, "positions": positions, "norm_w": norm_w, "w_in": w_in, "gate_bias": gate_bias,
            "conv_w": conv_w, "conv_b": conv_b, "dt_bias": dt_bias, "a_log": a_log, "d_skip": d_skip,
            "ssm_norm_w": ssm_norm_w, "w_branch_a": w_branch_a, "w_branch_b": w_branch_b,
            "w_out": w_out, "final_norm_w": final_norm_w}


def reference(x, positions, norm_w, w_in, gate_bias, conv_w, conv_b, dt_bias, a_log, d_skip,
              ssm_norm_w, w_branch_a, w_branch_b, w_out, final_norm_w):
    bsz, seq, _ = x.shape
    inv_freq = ROPE_THETA ** (-jnp.arange(0, ROT_DIM, 2, dtype=jnp.float32) / ROT_DIM)
    ang = positions.astype(jnp.float32)[..., None] * inv_freq
    cos = jnp.cos(ang)[:, :, None, :]
    sin = jnp.sin(ang)[:, :, None, :]

    for i in range(DEPTH):
        h = _rmsnorm(x, norm_w[i])
        proj = h @ w_in[i].astype(h.dtype)
        (q, k, v, z_a, q_idx, k_idx, w_idx,
         z_b, x_b, b_b, c_b, dt_raw, gates) = jnp.split(proj, SPLIT_OFFSETS, axis=-1)

        q = _partial_rope(q.reshape(bsz, seq, ATTN_HEADS, HEAD_DIM), cos, sin)
        k = _partial_rope(k.reshape(bsz, seq, ATTN_KV_HEADS, HEAD_DIM), cos, sin)
        v = v.reshape(bsz, seq, ATTN_KV_HEADS, HEAD_DIM)
        q_idx = _partial_rope(q_idx.reshape(bsz, seq, IDX_HEADS, IDX_DIM), cos, sin)
        k_idx = _partial_rope(k_idx[:, :, None, :], cos, sin)[:, :, 0]
        o_a = _sparse_attention(q, k, v, q_idx, k_idx, w_idx) * jax.nn.silu(z_a)

        xbc = jnp.concatenate([x_b, b_b, c_b], axis=-1)
        xbc = jax.nn.silu(_causal_depthwise_conv(xbc, conv_w[i], conv_b[i]))
        x_s, b_s, c_s = jnp.split(xbc, [SSM_WIDTH, SSM_WIDTH + SSM_GROUPS * SSM_STATE], axis=-1)
        dt = jax.nn.softplus(dt_raw.astype(jnp.float32) + dt_bias[i].astype(jnp.float32))
        a = -jnp.exp(a_log[i].astype(jnp.float32))
        x_h = x_s.reshape(bsz, seq, SSM_HEADS, SSM_HEAD_DIM).astype(jnp.float32)
        y = _ssd(x_h, dt, a,
                 b_s.reshape(bsz, seq, SSM_GROUPS, SSM_STATE).astype(jnp.float32),
                 c_s.reshape(bsz, seq, SSM_GROUPS, SSM_STATE).astype(jnp.float32))
        y = y + d_skip[i].astype(jnp.float32)[:, None] * x_h
        y = y.reshape(bsz, seq, SSM_WIDTH).astype(x.dtype)
        o_b = _gated_group_rmsnorm(y, z_b, ssm_norm_w[i])

        g = jax.nn.sigmoid((gates + gate_bias[i].astype(gates.dtype)).astype(jnp.float32)).astype(x.dtype)
        g_a, g_b = jnp.split(g, [D_MODEL], axis=-1)
        merged = g_a * (o_a @ w_branch_a[i].astype(o_a.dtype)) + g_b * (o_b @ w_branch_b[i].astype(o_b.dtype))
        x = x + merged @ w_out[i].astype(merged.dtype)

    return _rmsnorm(x, final_norm_w)
```

```python
import numpy as np
import math
import concourse.bass as bass
import concourse.mybir as mybir
from concourse.bass_utils import run_bass_kernel_spmd
from contextlib import ExitStack

F32 = mybir.dt.float32
BF16 = mybir.dt.bfloat16
I32 = mybir.dt.int32
ALU = mybir.AluOpType
AF = mybir.ActivationFunctionType
AX = mybir.AxisListType

S = 2048
D = 1024
NT = 16
INW = 6228
EPS = 1e-6
NITER = 12
A_SRC = [(0, 512, 0), (512, 640, 512), (1280, 1536, 640), (1536, 1600, 896), (1600, 1604, 960),
         (640, 768, 964), (768, 1280, 1092)]
NA = 1604


class Buf:
    __slots__ = ("w", "r", "name", "excl")

    def __init__(self, name="", excl=False):
        self.w = {}
        self.r = {}
        self.name = name
        self.excl = excl


def _merge(deps, d):
    for k, (s, v) in d.items():
        if k not in deps or deps[k][1] < v:
            deps[k] = (s, v)


class Eng:
    def __init__(self, K, name, eng, selfdep=True):
        self.K = K
        self.name = name
        self.eng = eng
        self.selfdep = selfdep
        self.sem = K.es.enter_context(K.nc.semaphore("s_" + name))
        self.cnt = 0
        self.waited = {}
        self.pending = False

    def wait_deps(self, deps):
        for k, (s, v) in deps.items():
            if k == self.name and not self.selfdep:
                continue
            if self.waited.get(k, 0) < v:
                self.eng.wait_ge(s, v)
                self.waited[k] = v

    def __call__(self, fn, r=(), w=(), inc=True, extra=(), selfwait=None):
        if selfwait is not None:
            assert selfwait[0] == self.name
            if self.waited.get("self", 0) < selfwait[2]:
                self.eng.wait_ge(selfwait[1], selfwait[2])
                self.waited["self"] = selfwait[2]
        w = list(w) + [b for b in r if b.excl]
        r = [b for b in r if not b.excl]
        deps = {}
        for b in r:
            _merge(deps, b.w)
        for b in w:
            _merge(deps, b.w)
            _merge(deps, b.r)
        for t in extra:
            _merge(deps, {t[0]: (t[1], t[2])})
        self.wait_deps(deps)
        ins = fn()
        if inc:
            self.cnt += 1
            ins.then_inc(self.sem, 1)
            tok = (self.sem, self.cnt)
            self.pending = False
        else:
            tok = (self.sem, self.cnt + 1)
            self.pending = True
        for b in r:
            _merge(b.r, {self.name: tok})
        for b in w:
            b.w = {self.name: tok}
            b.r = {}
        return (self.name,) + tok


class DmaQ:
    def __init__(self, K, name, waiter, nsem=8):
        self.K = K
        self.name = name
        self.waiter = waiter
        self.sems = [K.es.enter_context(K.nc.semaphore(f"d_{name}{j}")) for j in range(nsem)]
        self.vals = [0] * nsem
        self.idx = 0

    def __call__(self, out, in_, r=(), w=(), extra=(), **kw):
        deps = {}
        for b in r:
            _merge(deps, b.w)
        for b in w:
            _merge(deps, b.w)
            _merge(deps, b.r)
        for t in extra:
            _merge(deps, {t[0]: (t[1], t[2])})
        k = self.idx
        self.idx = (k + 1) % len(self.sems)
        key = f"{self.name}{k}"
        if self.vals[k] > 0:
            _merge(deps, {key: (self.sems[k], self.vals[k])})
        self.waiter.wait_deps(deps)
        ins = self.waiter.eng.dma_start(out=out, in_=in_, **kw)
        self.vals[k] += 16
        ins.then_inc(self.sems[k], 16)
        tok = (self.sems[k], self.vals[k])
        for b in r:
            _merge(b.r, {key: tok})
        for b in w:
            b.w = {key: tok}
            b.r = {}
        return (key,) + tok


class K:
    def __init__(self, nc, es):
        self.nc = nc
        self.es = es
        self.pe = Eng(self, "pe", nc.tensor, selfdep=False)
        self.act = Eng(self, "act", nc.scalar)
        self.dve = Eng(self, "dve", nc.vector)
        self.pool = Eng(self, "pool", nc.gpsimd)
        self.sp = Eng(self, "sp", nc.sync)
        self.engs = [self.pe, self.act, self.dve, self.pool, self.sp]
        self.dsync = DmaQ(self, "qs", self.sp, nsem=8)
        self.dpool = DmaQ(self, "qp", self.pool, nsem=8)
        self.dqs = [self.dsync, self.dpool]
        self.dumps = []

    def sb(self, name, shape, dt, es=None):
        t = (es or self.es).enter_context(self.nc.sbuf_tensor(name, list(shape), dt))
        return t

    def ps(self, name, shape, dt, es=None):
        return (es or self.es).enter_context(self.nc.psum_tensor(name, list(shape), dt))

    def barrier(self):
        deps = {}
        for e in self.engs:
            assert not e.pending, e.name
            if e.cnt > 0:
                deps[e.name] = (e.sem, e.cnt)
        for q in self.dqs:
            for j, s in enumerate(q.sems):
                if q.vals[j] > 0:
                    deps[f"{q.name}{j}"] = (s, q.vals[j])
        for e in self.engs:
            e.wait_deps(deps)

    def dump(self, name, ap, buf):
        d = self.nc.dram_tensor(name, list(ap.shape), ap.dtype, kind="ExternalOutput").ap()
        self.dsync(out=d, in_=ap, r=(buf if isinstance(buf, (list, tuple)) else [buf]))
        self.dumps.append(name)


class Stop(Exception):
    pass


def build(stop_after="all", dbg=()):
    try:
        return _build(stop_after, dbg)
    except Stop as s:
        return s.args


def _build(stop_after="all", dbg=()):
    nc = bass.Bass("TRN2", target_bir_lowering=False)
    dbg = set(dbg)
    x_d = nc.dram_tensor("x", [S, D], F32, kind="ExternalInput").ap()
    posT_d = nc.dram_tensor("posT", [128, NT], I32, kind="ExternalInput").ap()
    invf_d = nc.dram_tensor("invf", [128, 8], F32, kind="ExternalInput").ap()
    normw_d = nc.dram_tensor("norm_w", [1, D], F32, kind="ExternalInput").ap()
    win_d = nc.dram_tensor("w_in", [D, INW], F32, kind="ExternalInput").ap()
    out_d = nc.dram_tensor("out", [S, D], F32, kind="ExternalOutput").ap()
    cwT_d = nc.dram_tensor("cwT", [128, 12, 4], F32, kind="ExternalInput").ap()
    cbT_d = nc.dram_tensor("cbT", [128, 12], F32, kind="ExternalInput").ap()
    dtb_d = nc.dram_tensor("dt_bias", [1, 16], F32, kind="ExternalInput").ap()
    alog_d = nc.dram_tensor("a_log", [1, 16], F32, kind="ExternalInput").ap()
    dsk_d = nc.dram_tensor("d_skip", [1, 16], F32, kind="ExternalInput").ap()
    snw_d = nc.dram_tensor("ssm_norm_w", [1, D], F32, kind="ExternalInput").ap()
    gbT_d = nc.dram_tensor("gbT", [128, 16], F32, kind="ExternalInput").ap()
    wpa_d = nc.dram_tensor("w_branch_a", [512, D], F32, kind="ExternalInput").ap()
    wpb_d = nc.dram_tensor("w_branch_b", [D, D], F32, kind="ExternalInput").ap()
    wout_d = nc.dram_tensor("w_out", [D, D], F32, kind="ExternalInput").ap()
    fnw_d = nc.dram_tensor("final_norm_w", [1, D], F32, kind="ExternalInput").ap()

    with ExitStack() as es:
        k = K(nc, es)
        pe, act, dve, pool, sp = k.pe, k.act, k.dve, k.pool, k.sp
        dsync, dpool = k.dsync, k.dpool

        def ck(name):
            if stop_after == name:
                k.barrier()
                raise Stop(nc, k)

        ident = k.sb("ident", [128, 128], BF16)
        Uf = k.sb("Uf", [128, 128], F32)
        Ub = k.sb("Ub", [128, 128], BF16)
        cbias = k.sb("cbias", [128, 128], F32)
        pow2 = k.sb("pow2", [128, NITER + 2], F32)
        negU = k.sb("negU", [128, 128], BF16)
        negUb = k.sb("negUb", [128, 128], BF16)
        mhalf = k.sb("mhalf", [128, 16], F32)
        b_const = Buf("const")
        pool(lambda: nc.gpsimd.memset(ident[:], 1.0), w=[b_const])
        pool(lambda: nc.gpsimd.affine_select(out=ident[:], in_=ident[:], pattern=[[-1, 128]],
                                             compare_op=ALU.is_equal, fill=0.0, base=0, channel_multiplier=1),
             w=[b_const])
        pool(lambda: nc.gpsimd.memset(Uf[:], 1.0), w=[b_const])
        pool(lambda: nc.gpsimd.affine_select(out=Uf[:], in_=Uf[:], pattern=[[1, 128]], compare_op=ALU.is_ge,
                                             fill=0.0, base=0, channel_multiplier=-1), w=[b_const])
        pool(lambda: nc.gpsimd.tensor_copy(out=Ub[:], in_=Uf[:]), w=[b_const])
        pool(lambda: nc.gpsimd.tensor_scalar(out=negUb[:], in0=Ub[:], scalar1=-1.0, scalar2=None, op0=ALU.mult),
             w=[b_const])
        pool(lambda: nc.gpsimd.memset(mhalf[:], -0.5), w=[b_const])
        pool(lambda: nc.gpsimd.memset(negU[:], 0.0), w=[b_const])
        pool(lambda: nc.gpsimd.affine_select(out=negU[:], in_=negU[:], pattern=[[1, 128]], compare_op=ALU.is_ge,
                                             fill=-30000.0, base=0, channel_multiplier=-1), w=[b_const])
        pool(lambda: nc.gpsimd.memset(cbias[:], 0.0), w=[b_const])
        pool(lambda: nc.gpsimd.affine_select(out=cbias[:], in_=cbias[:], pattern=[[-1, 128]], compare_op=ALU.is_ge,
                                             fill=-1e30, base=0, channel_multiplier=1), w=[b_const])
        for j in range(NITER + 2):
            pool(lambda: nc.gpsimd.memset(pow2[:, j:j + 1], 2.0 ** (-j)), w=[b_const])
        pool(lambda: nc.gpsimd.memset(pow2[:, 0:1], 1.0), w=[b_const])

        hT = k.sb("hT", [128, 8, S], BF16)
        b_hT = [Buf(f"hT{i}") for i in range(NT)]
        wBC = k.sb("wBC", [128, 8, 1024], BF16)
        b_wslot = [Buf(), Buf()]
        wG0 = wBC[:, :, 0:256].rearrange("p k (h n) -> p k h n", h=2)
        Wpb0 = wBC[:, :, 256:384]
        Wpa0 = wBC[:, 0:4, 384:512]
        CONV0 = 2628
        oaT = k.sb("oaT", [128, 4, S], BF16)
        b_oaT = [Buf() for _ in range(NT)]
        esWA = ExitStack()
        wA = k.sb("wA", [128, 8, NA], BF16, esWA)
        b_wA = Buf()
        for (c0, c1, dst) in A_SRC:
            dpool(out=wA[:, :, dst:dst + (c1 - c0)],
                  in_=win_d[:, c0:c1].rearrange("(k p) n -> p k n", p=128), w=[b_wA])

        with ExitStack() as es0:
            normw_b = k.sb("normw_b", [128, D], F32, es0)
            b_normw = Buf()
            dsync(out=normw_b[:], in_=normw_d[0, :].partition_broadcast(128), w=[b_normw])
            xt = k.sb("xt_all", [128, NT, D], F32, es0)
            b_xt = [Buf() for _ in range(NT)]
            xn = [k.sb(f"xn{j}", [128, D], BF16, es0) for j in range(3)]
            b_xn = [Buf(), Buf(), Buf()]
            junk = k.sb("junk0", [128, D], BF16, es0)
            b_junk = Buf()
            ss = k.sb("ss", [128, NT], F32, es0)
            sd = k.sb("sd", [128, NT], F32, es0)
            rstd = k.sb("rstd", [128, NT], F32, es0)
            b_ssg = [Buf() for _ in range(4)]
            pt = [k.ps(f"pt{j}", [128, 8, 128], BF16, es0) for j in range(2)]
            b_pt = [Buf(excl=True), Buf(excl=True)]
            for i in range(NT):
                (dsync if i % 2 == 0 else dsync)(out=xt[:, i, :], in_=x_d[i * 128:(i + 1) * 128, :], w=[b_xt[i]])

            def p0_stats(gq):
                for i in range(4 * gq, 4 * gq + 4):
                    act(lambda: nc.scalar.activation(out=junk[:], in_=xt[:, i, :], func=AF.Square,
                                                     accum_out=ss[:, i:i + 1]),
                        r=[b_xt[i]], w=[b_junk, b_ssg[gq]])
                act(lambda: nc.scalar.activation(out=sd[:, 4 * gq:4 * gq + 4], in_=ss[:, 4 * gq:4 * gq + 4],
                                                 func=AF.Sqrt, scale=1.0 / D, bias=EPS), w=[b_ssg[gq]])
                dve(lambda: nc.vector.reciprocal(out=rstd[:, 4 * gq:4 * gq + 4], in_=sd[:, 4 * gq:4 * gq + 4]),
                    w=[b_ssg[gq]])

            def p0_apply(gq):
                for i in range(4 * gq, 4 * gq + 4):
                    j = i % 3
                    jp = i % 2
                    dve(lambda: nc.vector.scalar_tensor_tensor(out=xn[j][:], in0=xt[:, i, :], scalar=rstd[:, i:i + 1],
                                                               in1=normw_b[:], op0=ALU.mult, op1=ALU.mult),
                        r=[b_xt[i], b_ssg[gq], b_normw], w=[b_xn[j]])
                    for c in range(8):
                        pe(lambda: nc.tensor.transpose(out=pt[jp][:, c, :], in_=xn[j][:, c * 128:(c + 1) * 128],
                                                       identity=ident[:]),
                           r=[b_xn[j], b_const], w=[b_pt[jp]], inc=(c == 7))
                    act(lambda: nc.scalar.copy(out=hT[:, :, i * 128:(i + 1) * 128], in_=pt[jp][:]),
                        r=[b_pt[jp]], w=[b_hT[i]])

            p0_stats(0)
            for gq in range(4):
                if gq + 1 < 4:
                    p0_stats(gq + 1)
                p0_apply(gq)
            k.barrier()
        if "hT" in dbg:
            k.dump("d_hT", hT[:], b_hT[NT - 1])
        if stop_after == "p0":
            k.barrier()
            return nc, k


        PB = [k.ps(f"pb{j}", [128, 512], F32) for j in range(8)]
        b_PB = [Buf(f"pb{j}", excl=True) for j in range(8)]

        def bfv(j):
            return PB[j][:].bitcast(BF16).rearrange("p (s t) -> p s t", t=128)

        with ExitStack() as esA:
            dpool(out=wBC[:, :, 0:512], in_=win_d[:, CONV0:CONV0 + 512].rearrange("(k p) n -> p k n", p=128),
                  w=[b_wslot[0]])
            dpool(out=wBC[:, :, 512:1024], in_=win_d[:, CONV0 + 512:CONV0 + 1024].rearrange("(k p) n -> p k n", p=128),
                  w=[b_wslot[1]])
            posi = k.sb("posi", [128, NT], I32, esA)
            posf = k.sb("posf", [128, NT], F32, esA)
            invf = k.sb("invf_sb", [128, 8], F32, esA)
            ang = k.sb("ang", [128, NT, 8], F32, esA)
            cos_t = k.sb("cos_t", [128, NT, 8], F32, esA)
            sin_t = k.sb("sin_t", [128, NT, 8], F32, esA)
            ry = k.sb("ry", [128, NT, 8], F32, esA)
            rki = k.sb("rki", [128, NT, 8], I32, esA)
            rkf = k.sb("rkf", [128, NT, 8], F32, esA)
            rg = k.sb("rg", [128, NT, 8], F32, esA)
            b_tab = Buf()
            dsync(out=posi[:], in_=posT_d[:, :], w=[b_tab])
            dsync(out=invf[:], in_=invf_d[:, :], w=[b_tab])
            dve(lambda: nc.vector.tensor_copy(out=posf[:], in_=posi[:]), r=[b_tab], w=[b_tab])
            dve(lambda: nc.vector.tensor_tensor(out=ang[:], in0=posf[:].unsqueeze(2).to_broadcast([128, NT, 8]),
                                                in1=invf[:].unsqueeze(1).to_broadcast([128, NT, 8]), op=ALU.mult),
                r=[b_tab], w=[b_tab])
            TWO_PI = 2.0 * math.pi
            for (dst_t, off) in ((sin_t, 0.0), (cos_t, 0.25)):
                dve(lambda: nc.vector.tensor_scalar(out=ry[:], in0=ang[:], scalar1=1.0 / TWO_PI, scalar2=off,
                                                    op0=ALU.mult, op1=ALU.add), r=[b_tab], w=[b_tab])
                dve(lambda: nc.vector.tensor_copy(out=rki[:], in_=ry[:]), r=[b_tab], w=[b_tab])
                dve(lambda: nc.vector.tensor_copy(out=rkf[:], in_=rki[:]), r=[b_tab], w=[b_tab])
                dve(lambda: nc.vector.tensor_tensor(out=ry[:], in0=ry[:], in1=rkf[:], op=ALU.subtract),
                    r=[b_tab], w=[b_tab])
                dve(lambda: nc.vector.tensor_scalar(out=rg[:], in0=ry[:], scalar1=0.5, scalar2=None, op0=ALU.is_ge),
                    r=[b_tab], w=[b_tab])
                dve(lambda: nc.vector.tensor_tensor(out=ry[:], in0=ry[:], in1=rg[:], op=ALU.subtract),
                    r=[b_tab], w=[b_tab])
                dve(lambda: nc.vector.tensor_scalar(out=rg[:], in0=ry[:], scalar1=-0.5, scalar2=None, op0=ALU.is_lt),
                    r=[b_tab], w=[b_tab])
                dve(lambda: nc.vector.tensor_tensor(out=ry[:], in0=ry[:], in1=rg[:], op=ALU.add),
                    r=[b_tab], w=[b_tab])
                act(lambda: nc.scalar.activation(out=dst_t[:], in_=ry[:], func=AF.Sin, scale=TWO_PI * (1.0 - 1e-6)),
                    r=[b_tab], w=[b_tab])
            ck("Atab")
            if "rope" in dbg:
                k.dump("d_cos", cos_t[:], b_tab)
                k.dump("d_sin", sin_t[:], b_tab)

            qk_sb = [k.sb(f"qk_sb{j}", [128, 15, 64], BF16, esA) for j in range(2)]
            b_qk = [Buf(), Buf()]
            rt = [k.sb(f"rt{j}", [128, 15, 8], F32, esA) for j in range(4)]
            b_rt = Buf()
            rsrc = k.sb("rsrc", [128, 15, 16], F32, esA)
            b_rsrc = Buf()
            QT = [k.sb(f"QT{j}", [128, 12, 128], BF16, esA) for j in range(2)]
            b_QT = [Buf(), Buf()]
            KT = k.sb("KT", [128, 2, S], BF16, esA)
            KIT = k.sb("KIT", [128, S], BF16, esA)
            b_KT = [Buf() for _ in range(NT)]
            for j_ in range(2):
                pool(lambda: nc.gpsimd.memset(QT[j_][64:128], 0.0), w=[b_QT[j_]])
            pool(lambda: nc.gpsimd.memset(KT[64:128], 0.0), w=b_KT)
            pool(lambda: nc.gpsimd.memset(KIT[64:128], 0.0), w=b_KT)
            Vaug = k.sb("Vaug", [128, NT, 2, 65], BF16, esA)
            b_V = [Buf() for _ in range(NT)]
            wv = k.sb("wv", [128, NT, 4], F32, esA)
            b_wv = [Buf() for _ in range(NT)]
            sza = [k.sb(f"sza{j}", [128, 512], F32, esA) for j in range(2)]
            b_sza = [Buf(), Buf()]
            sc = [k.sb(f"sc{j}", [128, S], F32, esA) for j in range(2)]
            b_sc = [Buf(), Buf()]
            rl = [k.sb(f"rl{j}", [128, S], F32, esA) for j in range(2)]
            b_rl = [Buf(), Buf()]
            junkb = k.sb("junkb", [128, S], BF16, esA)
            b_junkb = Buf()
            m01 = k.sb("m01", [128, S], BF16, esA)
            b_m01 = Buf()
            maskT = [k.sb(f"maskT{j}", [128, NT, 128], BF16, esA) for j in range(2)]
            b_maskT = [Buf(), Buf()]
            Bv = k.sb("Bv", [128, 1], F32, esA)
            Bk = k.sb("Bk", [128, NITER + 2], F32, esA)
            mid = [k.sb(f"mid{j}", [128, 1], F32, esA) for j in range(2)]
            cnt = k.sb("cnt", [128, 1], F32, esA)
            dd = k.sb("dd", [128, 1], F32, esA)
            thr = k.sb("thr", [128, NT], F32, esA)
            b_bis = Buf()
            Eb = [k.sb(f"Eb{j}", [128, 512], BF16, esA) for j in range(3)]
            b_Eb = [Buf(), Buf(), Buf()]
            rinv = k.sb("rinv", [128, 4], F32, esA)
            otmp = k.sb("otmp", [128, 4, 64], F32, esA)
            rinv8 = k.sb("rinv8", [128, 8], F32, esA)
            otmp8 = k.sb("otmp8", [128, 8, 64], F32, esA)
            b_otmp = Buf()
            oa_sb = k.sb("oa_sb", [128, 512], BF16, esA)
            b_oa = Buf()
            pool(lambda: nc.gpsimd.memset(Vaug[:], 1.0), w=b_V)

            A_BANK = [(0, 0, 512), (1, 512, 452), (2, 964, 512), (3, 1476, 128)]
            T0v = bfv(4)
            T1v = bfv(5)
            ctr = {"st": 0, "ix": 0, "e": 0}

            def hdr_(i):
                return i % 2, slice(i * 128, (i + 1) * 128), (i + 1) * 128, i >= 2

            def stage1(i):
                j, tsl, L, masked = hdr_(i)

                for (bk, c0, n) in A_BANK:
                    for kc in range(8):
                        pe(lambda: nc.tensor.matmul(PB[bk][:, 0:n], lhsT=hT[:, kc, tsl], rhs=wA[:, kc, c0:c0 + n],
                                                    start=(kc == 0), stop=(kc == 7)),
                           r=[b_hT[i], b_wA], w=[b_PB[bk]], inc=(kc == 7))
                ck(f"Aproj{i}")
                p0v = PB[0][:, 0:512].rearrange("p (h d) -> p h d", d=64)
                p1v = PB[1][:, 0:448].rearrange("p (h d) -> p h d", d=64)
                act(lambda: nc.scalar.copy(out=qk_sb[j][:, 0:8, 16:64], in_=p0v[:, :, 16:64]),
                    r=[b_PB[0]], w=[b_qk[j]])
                act(lambda: nc.scalar.copy(out=qk_sb[j][:, 8:15, 16:64], in_=p1v[:, :, 16:64]),
                    r=[b_PB[1]], w=[b_qk[j]])
                ck(f"Ae1_{i}")
                act(lambda: nc.scalar.copy(out=rsrc[:, 0:8, :], in_=p0v[:, :, 0:16]), r=[b_PB[0]], w=[b_rsrc])
                act(lambda: nc.scalar.copy(out=rsrc[:, 8:15, :], in_=p1v[:, :, 0:16]), r=[b_PB[1]], w=[b_rsrc])
                nh = 15
                cb = cos_t[:, i:i + 1, :].to_broadcast([128, nh, 8])
                sb_ = sin_t[:, i:i + 1, :].to_broadcast([128, nh, 8])
                x1 = rsrc[:, :, 0:8]
                x2 = rsrc[:, :, 8:16]
                dve(lambda: nc.vector.tensor_tensor(out=rt[0][:], in0=x1, in1=cb, op=ALU.mult),
                    r=[b_rsrc, b_tab], w=[b_rt])
                dve(lambda: nc.vector.tensor_tensor(out=rt[1][:], in0=x2, in1=sb_, op=ALU.mult),
                    r=[b_rsrc, b_tab], w=[b_rt])
                dve(lambda: nc.vector.tensor_tensor(out=qk_sb[j][:, :, 0:8], in0=rt[0][:], in1=rt[1][:], op=ALU.subtract),
                    r=[b_rt], w=[b_qk[j]])
                dve(lambda: nc.vector.tensor_tensor(out=rt[2][:], in0=x2, in1=cb, op=ALU.mult),
                    r=[b_rsrc, b_tab], w=[b_rt])
                dve(lambda: nc.vector.tensor_tensor(out=rt[3][:], in0=x1, in1=sb_, op=ALU.mult),
                    r=[b_rsrc, b_tab], w=[b_rt])
                dve(lambda: nc.vector.tensor_tensor(out=qk_sb[j][:, :, 8:16], in0=rt[2][:], in1=rt[3][:], op=ALU.add),
                    r=[b_rt], w=[b_qk[j]])
                ck(f"Ae2_{i}")
                act(lambda: nc.scalar.copy(out=wv[:, i, :], in_=PB[1][:, 448:452]), r=[b_PB[1]], w=[b_wv[i]])
                act(lambda: nc.scalar.copy(out=Vaug[:, i, :, 0:64],
                                           in_=PB[2][:, 0:128].rearrange("p (g d) -> p g d", d=64)),
                    r=[b_PB[2]], w=[b_V[i]])
                ck(f"Ae3_{i}")
                act(lambda: nc.scalar.activation(out=sza[j][:, 0:384], in_=PB[2][:, 128:512], func=AF.Silu),
                    r=[b_PB[2]], w=[b_sza[j]])
                act(lambda: nc.scalar.activation(out=sza[j][:, 384:512], in_=PB[3][:, 0:128], func=AF.Silu),
                    r=[b_PB[3]], w=[b_sza[j]])
                ck(f"Aevac{i}")
                for h in range(15):
                    tv, sl, bk = (T0v, h, 4) if h < 8 else (T1v, h - 8, 5)
                    pe(lambda: nc.tensor.transpose(out=tv[0:64, sl, :], in_=qk_sb[j][:, h, :], identity=ident[:]),
                       r=[b_qk[j], b_const], w=[b_PB[bk]], inc=(h == 7 or h == 14))
                act(lambda: nc.scalar.copy(out=QT[j][0:64, 0:8, :], in_=T0v[0:64, :, :]), r=[b_PB[4]], w=[b_QT[j]])
                act(lambda: nc.scalar.copy(out=KT[0:64, :, tsl], in_=T1v[0:64, 0:2, :]), r=[b_PB[5]], w=[b_KT[i]])
                act(lambda: nc.scalar.copy(out=QT[j][0:64, 8:12, :], in_=T1v[0:64, 2:6, :]), r=[b_PB[5]], w=[b_QT[j]])
                act(lambda: nc.scalar.copy(out=KIT[0:64, tsl], in_=T1v[0:64, 6, :]), r=[b_PB[5]], w=[b_KT[i]])


            def stage2(i):
                j, tsl, L, masked = hdr_(i)
                if not masked:
                    return

                nch = (L + 511) // 512
                for h in range(4):
                    q = h % 2
                    for c in range(nch):
                        c0 = c * 512
                        n = min(512, L - c0)
                        bk = ctr["ix"] % 2
                        ctr["ix"] += 1
                        pe(lambda: nc.tensor.matmul(PB[bk][:, 0:n], lhsT=QT[j][:, 8 + h, :], rhs=KIT[:, c0:c0 + n],
                                                    start=True, stop=True),
                           r=[b_QT[j]] + b_KT[0:i + 1], w=[b_PB[bk]])
                        act(lambda: nc.scalar.activation(out=rl[q][:, c0:c0 + n], in_=PB[bk][:, 0:n], func=AF.Relu),
                            r=[b_PB[bk]], w=[b_rl[q]])
                    if h == 0:
                        dve(lambda: nc.vector.tensor_scalar(out=sc[j][:, 0:L], in0=rl[q][:, 0:L],
                                                            scalar1=wv[:, i, 0:1], scalar2=None, op0=ALU.mult),
                            r=[b_rl[q], b_wv[i]], w=[b_sc[j]])
                    else:
                        dve(lambda: nc.vector.scalar_tensor_tensor(out=sc[j][:, 0:L], in0=rl[q][:, 0:L],
                                                                   scalar=wv[:, i, h:h + 1], in1=sc[j][:, 0:L],
                                                                   op0=ALU.mult, op1=ALU.add),
                            r=[b_rl[q], b_wv[i]], w=[b_sc[j]])


            def bisect(i):
                j, tsl, L, masked = hdr_(i)
                if not masked:
                    return
                yield

                dve(lambda: nc.vector.tensor_reduce(out=Bv[:], in_=sc[j][:, 0:L], axis=AX.X, op=ALU.max,
                                                    apply_absolute_value=True),
                    r=[b_sc[j]], w=[b_bis])
                dve(lambda: nc.vector.tensor_tensor(out=sc[j][:, L - 128:L], in0=sc[j][:, L - 128:L],
                                                    in1=cbias[:], op=ALU.add),
                    r=[b_const], w=[b_sc[j]])
                dve(lambda: nc.vector.tensor_scalar(out=Bk[:], in0=pow2[:], scalar1=Bv[:, 0:1], scalar2=None,
                                                    op0=ALU.mult), r=[b_const], w=[b_bis])
                dve(lambda: nc.vector.memset(mid[0][:], 0.0), w=[b_bis])
                for it in range(NITER):
                    ma, mb = mid[it % 2], mid[(it + 1) % 2]
                    dve(lambda: nc.vector.tensor_scalar(out=junkb[:, 0:L], in0=sc[j][:, 0:L], scalar1=ma[:, 0:1],
                                                        scalar2=None, op0=ALU.is_ge, op1=ALU.add,
                                                        accum_out=cnt[:, 0:1]),
                        r=[b_sc[j]], w=[b_bis, b_junkb])
                    dve(lambda: nc.vector.tensor_scalar(out=dd[:], in0=cnt[:], scalar1=255.5,
                                                        scalar2=Bk[:, it:it + 1], op0=ALU.is_ge, op1=ALU.mult),
                        w=[b_bis])
                    dve(lambda: nc.vector.tensor_scalar(out=mb[:], in0=dd[:], scalar1=Bk[:, it + 1:it + 2],
                                                        scalar2=ma[:, 0:1], op0=ALU.subtract, op1=ALU.add),
                        w=[b_bis])
                    yield
                mfin = mid[NITER % 2]
                dve(lambda: nc.vector.tensor_tensor(out=thr[:, i:i + 1], in0=mfin[:], in1=Bk[:, NITER:NITER + 1],
                                                    op=ALU.subtract), w=[b_bis])
                dve(lambda: nc.vector.tensor_scalar(out=m01[:, 0:L], in0=sc[j][:, 0:L], scalar1=thr[:, i:i + 1],
                                                    scalar2=None, op0=ALU.is_ge),
                    r=[b_sc[j], b_bis], w=[b_m01])
                if ("sc%d" % i) in dbg:
                    k.dump("d_sc", sc[j][:, 0:L], b_sc[j])
                    k.dump("d_thr", thr[:, i:i + 1], b_bis)


            def masktr(i):
                j, tsl, L, masked = hdr_(i)
                if not masked:
                    return

                for jb in range(i + 1):
                    tv, sl, bk = (T0v, jb, 4) if jb < 8 else (T1v, jb - 8, 5)
                    last = (jb == i) or (jb == 7)
                    pe(lambda: nc.tensor.transpose(out=tv[:, sl, :], in_=m01[:, jb * 128:(jb + 1) * 128],
                                                   identity=ident[:]),
                       r=[b_m01, b_const], w=[b_PB[bk]], inc=last)
                n0 = min(i + 1, 8)
                act(lambda: nc.scalar.activation(out=maskT[j][:, 0:n0, :], in_=T0v[:, 0:n0, :], func=AF.Identity,
                                                 scale=30000.0, bias=-30000.0),
                    r=[b_PB[4]], w=[b_maskT[j]])
                if i + 1 > 8:
                    act(lambda: nc.scalar.activation(out=maskT[j][:, 8:i + 1, :], in_=T1v[:, 0:i + 1 - 8, :],
                                                     func=AF.Identity, scale=30000.0, bias=-30000.0),
                        r=[b_PB[5]], w=[b_maskT[j]])


            def attn_main(i):
                j, tsl, L, masked = hdr_(i)
                nkt = i + 1
                for g in range(2):
                    Ov = PB[6 + g][:, 0:260].rearrange("p (h d) -> p h d", d=65)

                    def st_mm(jb):
                        bk = 2 + (ctr["st"] % 2)
                        ctr["st"] += 1
                        has_mask = masked or jb == i
                        pe(lambda: nc.tensor.matmul(PB[bk][:].rearrange("p (h t) -> p h t", t=128),
                                                    lhsT=KT[:, g, jb * 128:(jb + 1) * 128],
                                                    rhs=QT[j][:, 4 * g:4 * g + 4, :], start=True, stop=(not has_mask)),
                           r=[b_QT[j], b_KT[jb]], w=[b_PB[bk]], inc=(not has_mask))
                        if has_mask:
                            if masked:
                                mb_ap = maskT[j][:, jb:jb + 1, :].to_broadcast([128, 4, 128])
                                rd = [b_maskT[j], b_const]
                            else:
                                mb_ap = negU[:].unsqueeze(1).to_broadcast([128, 4, 128])
                                rd = [b_const]
                            pe(lambda: nc.tensor.matmul(PB[bk][:].rearrange("p (h t) -> p h t", t=128),
                                                        lhsT=ident[:], rhs=mb_ap, start=False, stop=True),
                               r=rd, w=[b_PB[bk]])
                        return bk

                    def exp_pv(jb, bk):
                        e = ctr["e"] % 3
                        ctr["e"] += 1
                        act(lambda: nc.scalar.activation(out=Eb[e][:], in_=PB[bk][:], func=AF.Exp, scale=0.125),
                            r=[b_PB[bk]], w=[b_Eb[e]])
                        for hh in range(4):
                            pe(lambda: nc.tensor.matmul(Ov[:, hh, :], lhsT=Eb[e][:, hh * 128:(hh + 1) * 128],
                                                        rhs=Vaug[:, jb, g, :], start=(jb == 0 and hh == 0),
                                                        stop=(jb == i and hh == 3)),
                               r=[b_Eb[e], b_V[jb]], w=[b_PB[6 + g]], inc=(hh == 3))

                    bks = {0: st_mm(0)}
                    for jb in range(nkt):
                        if jb + 1 < nkt:
                            bks[jb + 1] = st_mm(jb + 1)
                        exp_pv(jb, bks[jb])

            def attn_fin(i):
                j, tsl, L, masked = hdr_(i)
                for g in range(2):
                    Ov = PB[6 + g][:, 0:260].rearrange("p (h d) -> p h d", d=65)
                    dve(lambda: nc.vector.reciprocal(out=rinv8[:, 4 * g:4 * g + 4].unsqueeze(2), in_=Ov[:, :, 64:65]),
                        r=[b_PB[6 + g]], w=[b_otmp])
                    dve(lambda: nc.vector.tensor_tensor(out=otmp8[:, 4 * g:4 * g + 4, :], in0=Ov[:, :, 0:64],
                                                        in1=rinv8[:, 4 * g:4 * g + 4].unsqueeze(2)
                                                        .to_broadcast([128, 4, 64]), op=ALU.mult),
                        r=[b_PB[6 + g]], w=[b_otmp])
                dve(lambda: nc.vector.tensor_tensor(out=oa_sb[:], in0=otmp8[:].rearrange("p h d -> p (h d)"),
                                                    in1=sza[j][:], op=ALU.mult),
                    r=[b_otmp, b_sza[j]], w=[b_oa])


            def fin_tr(i):
                j, tsl, L, masked = hdr_(i)
                for c in range(4):
                    pe(lambda: nc.tensor.transpose(out=T0v[:, c, :], in_=oa_sb[:, c * 128:(c + 1) * 128],
                                                   identity=ident[:]),
                       r=[b_oa, b_const], w=[b_PB[4]], inc=(c == 3))
                act(lambda: nc.scalar.copy(out=oaT[:, :, tsl], in_=T0v[:, 0:4, :]), r=[b_PB[4]], w=[b_oaT[i]])


            nA = NT if "skipA" not in dbg else 0
            pend = None
            for i in range(nA + 2):
                if i < nA:
                    stage1(i)
                if pend is not None:
                    for _ in pend:
                        pass
                    pend = None
                if i < nA:
                    stage2(i)
                if 1 <= i <= nA:
                    masktr(i - 1)
                if 2 <= i <= nA + 1:
                    fin_tr(i - 2)
                if 1 <= i <= nA:
                    attn_main(i - 1)
                if i < nA:
                    g_ = bisect(i)
                    nsteps = (NITER - 3) if (i + 1 < nA) else 10 ** 9
                    done_ = False
                    for _s in range(nsteps):
                        try:
                            next(g_)
                        except StopIteration:
                            done_ = True
                            break
                    if not done_:
                        pend = g_
                if 1 <= i <= nA:
                    attn_fin(i - 1)
            i = nA - 1
            j = i % 2
            L = (i + 1) * 128

            if "qk" in dbg:
                k.dump("d_KT", KT[:, :, 0:L], b_KT)
                k.dump("d_KIT", KIT[:, 0:L], b_KT)
                k.dump("d_QT", QT[j][:], b_QT[j])
                k.dump("d_V", Vaug[:, 0:i + 1], b_V)
            if "oaT" in dbg:
                k.dump("d_oaT", oaT[:, :, 0:L], b_oaT)
            k.barrier()
        esWA.close()
        if stop_after.startswith("A"):
            return nc, k


        obT = k.sb("obT", [128, 8, S], BF16)
        b_obT = [Buf() for _ in range(NT)]
        with ExitStack() as esB:
            wdt = k.sb("wdt", [128, 8, 16], BF16, esB)
            b_wdt = Buf()
            dpool(out=wdt[:], in_=win_d[:, 4164:4180].rearrange("(k p) n -> p k n", p=128), w=[b_wdt])
            X_tm = k.sb("X_tm", [128, NT, 1024], BF16, esB)
            B_tm = k.sb("B_tm", [128, NT, 256], BF16, esB)
            BT = k.sb("BT", [128, 2, S], BF16, esB)
            CT = k.sb("CT", [128, 2, S], BF16, esB)
            b_X = Buf()
            dtb_b = k.sb("dtb_b", [128, 16], F32, esB)
            a_b = k.sb("a_b", [128, 16], F32, esB)
            dsk_b = k.sb("dsk_b", [128, 16], F32, esB)
            snw_b = k.sb("snw_b", [128, D], F32, esB)
            dt_all = k.sb("dt_all", [128, NT, 16], F32, esB)
            dA_all = k.sb("dA_all", [128, NT, 16], F32, esB)
            spt = [k.sb(f"spt{j}", [128, NT, 16], F32, esB) for j in range(3)]
            ones_f = k.sb("ones_f", [128, 128], F32, esB)
            NEGU4 = k.sb("NEGU4", [128, 4, 128], BF16, esB)
            Dg = k.sb("Dg", [128, 16, 128], BF16, esB)
            b_ptab = Buf()
            dsync(out=dtb_b[:], in_=dtb_d[0, :].partition_broadcast(128), w=[b_ptab])
            dsync(out=a_b[:], in_=alog_d[0, :].partition_broadcast(128), w=[b_ptab])
            dsync(out=dsk_b[:], in_=dsk_d[0, :].partition_broadcast(128), w=[b_ptab])
            dsync(out=snw_b[:], in_=snw_d[0, :].partition_broadcast(128), w=[b_ptab])
            pool(lambda: nc.gpsimd.memset(ones_f[:], 1.0), w=[b_ptab])
            pool(lambda: nc.gpsimd.memset(NEGU4[:], 0.0), w=[b_ptab])
            pool(lambda: nc.gpsimd.affine_select(out=NEGU4[:], in_=NEGU4[:], pattern=[[0, 4], [1, 128]],
                                                 compare_op=ALU.is_ge, fill=-1.0e4, base=0, channel_multiplier=-1),
                 w=[b_ptab])
            act(lambda: nc.scalar.activation(out=a_b[:], in_=a_b[:], func=AF.Exp), w=[b_ptab])
            dve(lambda: nc.vector.tensor_scalar(out=a_b[:], in0=a_b[:], scalar1=-1.0, scalar2=None, op0=ALU.mult),
                w=[b_ptab])
            dve(lambda: nc.vector.tensor_tensor(out=Dg[:], in0=ident[:].unsqueeze(1).to_broadcast([128, 16, 128]),
                                                in1=dsk_b[:].unsqueeze(2).to_broadcast([128, 16, 128]), op=ALU.mult),
                r=[b_const], w=[b_ptab])
            ck("Btab")

            for i in range(NT):
                for kc in range(8):
                    pe(lambda: nc.tensor.matmul(PB[0][:, i * 16:(i + 1) * 16], lhsT=hT[:, kc, i * 128:(i + 1) * 128],
                                                rhs=wdt[:, kc, :], start=(kc == 0), stop=(kc == 7)),
                       r=[b_hT[i], b_wdt], w=[b_PB[0]], inc=(kc == 7))
            dve(lambda: nc.vector.tensor_tensor(out=spt[0][:], in0=PB[0][:, 0:256].rearrange("p (i h) -> p i h", h=16),
                                                in1=dtb_b[:].unsqueeze(1).to_broadcast([128, NT, 16]), op=ALU.add),
                r=[b_PB[0]], w=[b_ptab])
            dve(lambda: nc.vector.tensor_scalar(out=spt[2][:], in0=spt[0][:], scalar1=-1.0, scalar2=None, op0=ALU.mult),
                w=[b_ptab])
            dve(lambda: nc.vector.tensor_tensor(out=spt[1][:], in0=spt[0][:], in1=spt[2][:], op=ALU.max), w=[b_ptab])
            act(lambda: nc.scalar.activation(out=spt[1][:], in_=spt[1][:], func=AF.Exp, scale=-1.0), w=[b_ptab])
            act(lambda: nc.scalar.activation(out=spt[1][:], in_=spt[1][:], func=AF.Ln, bias=1.0), w=[b_ptab])
            dve(lambda: nc.vector.tensor_scalar(out=spt[2][:], in0=spt[0][:], scalar1=0.0, scalar2=None, op0=ALU.max),
                w=[b_ptab])
            dve(lambda: nc.vector.tensor_tensor(out=dt_all[:], in0=spt[2][:], in1=spt[1][:], op=ALU.add), w=[b_ptab])
            dve(lambda: nc.vector.tensor_tensor(out=dA_all[:], in0=dt_all[:],
                                                in1=a_b[:].unsqueeze(1).to_broadcast([128, NT, 16]), op=ALU.mult),
                w=[b_ptab])
            if "dt" in dbg:
                k.dump("d_dt", dt_all[:], b_ptab)
            ck("Bdt")

            with ExitStack() as esC:
                cwT = k.sb("cwT_sb", [128, 12, 4], F32, esC)
                cbT = k.sb("cbT_sb", [128, 12], F32, esC)
                b_cw = Buf()
                dsync(out=cwT[:], in_=cwT_d[:, :, :], w=[b_cw])
                dsync(out=cbT[:], in_=cbT_d[:, :], w=[b_cw])
                pre = [k.sb(f"pre{j}", [128, S + 3], F32, esC) for j in range(2)]
                b_pre = [Buf(), Buf()]
                b_preh = [[Buf(), Buf()], [Buf(), Buf()]]
                accs = [k.sb(f"acc{j}", [128, S], F32, esC) for j in range(2)]
                b_accs = [Buf(), Buf()]
                xs_fm = k.sb("xs_fm", [128, S], BF16, esC)
                b_xs = Buf()
                for q in range(2):
                    pool(lambda: nc.gpsimd.memset(pre[q][:, 0:3], 0.0), w=[b_preh[q][0]])
                b_xs2 = [Buf(), Buf()]
                b_cv = [Buf() for _ in range(12)]

                def cproj(m):
                    q = m % 2
                    slot = (m // 4) % 2
                    wc0 = slot * 512 + (m % 4) * 128
                    for tc in range(4):
                        for kc in range(8):
                            pe(lambda: nc.tensor.matmul(PB[tc][:], lhsT=wBC[:, kc, wc0:wc0 + 128],
                                                        rhs=hT[:, kc, tc * 512:(tc + 1) * 512],
                                                        start=(kc == 0), stop=(kc == 7)),
                               r=b_hT[tc * 4:(tc + 1) * 4] + [b_wslot[slot]], w=[b_PB[tc]], inc=(kc == 7))
                        act(lambda: nc.scalar.copy(out=pre[q][:, 3 + tc * 512:3 + (tc + 1) * 512], in_=PB[tc][:]),
                            r=[b_PB[tc]], w=[b_preh[q][tc // 2]])

                def cpost(m):
                    q = m % 2
                    acc = accs[q]
                    b_acc = b_accs[q]
                    for hv in range(2):
                        o0 = hv * 1024
                        rd = [b_preh[q][0], b_cw] if hv == 0 else [b_preh[q][0], b_preh[q][1], b_cw]
                        dve(lambda: nc.vector.tensor_scalar(out=acc[:, o0:o0 + 1024], in0=pre[q][:, o0:o0 + 1024],
                                                            scalar1=cwT[:, m, 0:1], scalar2=None, op0=ALU.mult),
                            r=rd, w=[b_acc])
                        for kk in range(1, 4):
                            dve(lambda: nc.vector.scalar_tensor_tensor(out=acc[:, o0:o0 + 1024],
                                                                       in0=pre[q][:, o0 + kk:o0 + kk + 1024],
                                                                       scalar=cwT[:, m, kk:kk + 1],
                                                                       in1=acc[:, o0:o0 + 1024],
                                                                       op0=ALU.mult, op1=ALU.add),
                                r=rd, w=[b_acc])
                    if m < 10:
                        dst = xs_fm[:] if m < 8 else BT[:, m - 8, :]
                        bdst = b_xs if m < 8 else b_cv[m]
                        act(lambda: nc.scalar.activation(out=dst, in_=acc[:], func=AF.Silu, bias=cbT[:, m:m + 1]),
                            r=[b_acc, b_cw], w=[bdst])
                        for half in range(2):
                            bk = 4 + half
                            tv = bfv(bk)
                            for s8 in range(8):
                                ti_ = half * 8 + s8
                                in_ap = (xs_fm[:, ti_ * 128:(ti_ + 1) * 128] if m < 8
                                         else BT[:, m - 8, ti_ * 128:(ti_ + 1) * 128])
                                pe(lambda: nc.tensor.transpose(out=tv[:, s8, :], in_=in_ap, identity=ident[:]),
                                   r=[bdst, b_const], w=[b_PB[bk]], inc=(s8 == 7))
                            if m < 8:
                                act(lambda: nc.scalar.copy(out=X_tm[:, half * 8:(half + 1) * 8, m * 128:(m + 1) * 128],
                                                           in_=tv[:, :, :]), r=[b_PB[bk]], w=[b_cv[m]])
                            else:
                                act(lambda: nc.scalar.copy(out=B_tm[:, half * 8:(half + 1) * 8,
                                                                    (m - 8) * 128:(m - 7) * 128],
                                                           in_=tv[:, :, :]), r=[b_PB[bk]], w=[b_cv[m]])
                    else:
                        act(lambda: nc.scalar.activation(out=CT[:, m - 10, :], in_=acc[:], func=AF.Silu,
                                                         bias=cbT[:, m:m + 1]),
                            r=[b_acc, b_cw], w=[b_cv[m]])

                def wreload(m_done):
                    if m_done == 3:
                        dpool(out=wBC[:, :, 0:512], in_=win_d[:, CONV0 + 1024:CONV0 + 1536]
                              .rearrange("(k p) n -> p k n", p=128), w=[b_wslot[0]])
                    elif m_done == 7:
                        dpool(out=wBC[:, :, 512:1024], in_=win_d[:, 1604:2116]
                              .rearrange("(k p) n -> p k n", p=128), w=[b_wslot[1]])
                    elif m_done == 11:
                        dpool(out=wBC[:, :, 0:512], in_=win_d[:, 2116:2628]
                              .rearrange("(k p) n -> p k n", p=128), w=[b_wslot[0]])

                cproj(0)
                wreload(0)
                for m in range(12):
                    if m + 1 < 12:
                        cproj(m + 1)
                        wreload(m + 1)
                    cpost(m)
                    ck(f"Bconv{m}")
                b_X.w = {}
                for bb in b_cv:
                    _merge(b_X.w, bb.w)
                if "conv" in dbg:
                    k.dump("d_Xtm", X_tm[:], b_X)
                    k.dump("d_Btm", B_tm[:], b_X)
                    k.dump("d_BT", BT[:], b_X)
                    k.dump("d_CT", CT[:], b_X)
                ck("Bconv")
                k.barrier()

            with ExitStack() as esS:
                ones_b = k.sb("ones_b", [128, 128], BF16, esS)
                dAhl = k.sb("dAhl", [128, NT, 2, 16], BF16, esS)
                dAres = spt[0]
                b_hl = Buf()
                pool(lambda: nc.gpsimd.memset(ones_b[:], 1.0), w=[b_hl])
                dve(lambda: nc.vector.tensor_copy(out=dAhl[:, :, 0, :], in_=dA_all[:]), r=[b_ptab], w=[b_hl])
                dve(lambda: nc.vector.tensor_tensor(out=dAres[:], in0=dA_all[:], in1=dAhl[:, :, 0, :], op=ALU.subtract),
                    r=[b_ptab], w=[b_hl])
                dve(lambda: nc.vector.tensor_copy(out=dAhl[:, :, 1, :], in_=dAres[:]), w=[b_hl])
                szb = [k.sb(f"szb{j}", [128, 1024], BF16, esS) for j in range(2)]
                b_szb = [Buf(), Buf()]
                smalls = [k.sb(f"small{j}", [128, 32], F32, esS) for j in range(2)]
                nacums = [k.sb(f"nacum{j}", [128, 16], F32, esS) for j in range(2)]
                eas = [k.sb(f"ea{j}", [128, 16], F32, esS) for j in range(2)]
                decs = [k.sb(f"dec{j}", [128, 16], F32, esS) for j in range(2)]
                dtds = [k.sb(f"dtd{j}", [128, 16], F32, esS) for j in range(2)]
                eASs = [k.sb(f"eAS{j}", [128, 2, 4], F32, esS) for j in range(2)]
                b_sms = [Buf(), Buf()]
                LTg = [k.sb(f"LTg{j}", [128, 4, 128], F32, esS) for j in range(2)]
                b_LT = [Buf(), Buf()]
                MTg = [k.sb(f"MTg{j}", [128, 4, 128], BF16, esS) for j in range(2)]
                b_MT = [Buf(), Buf()]
                CBs = [k.sb("CBs0", [128, 4, 128], F32, esS)] * 2
                b_CBs = [Buf()] * 2
                xds = [k.sb(f"xd{j}", [128, 16, 64], BF16, esS) for j in range(2)]
                xdds = [k.sb(f"xdd{j}", [128, 16, 64], BF16, esS) for j in range(2)]
                b_xds = [Buf(), Buf()]
                b_xdds = [Buf(), Buf()]
                ysb = k.sb("ysb", [128, 16, 64], F32, esS)
                b_y = Buf()
                ssq = k.sb("ssq", [128, 4], F32, esS)
                rs4 = k.sb("rs4", [128, 4], F32, esS)
                junkf = k.sb("junkf", [128, 256], F32, esS)
                ob_sb = k.sb("ob_sb", [128, 1024], BF16, esS)
                b_ob = Buf()
                S_sb = k.sb("S_sb", [128, 2, 256], F32, esS)
                S_bf = k.sb("S_bf", [128, 2, 256], BF16, esS)
                b_S = Buf()
                b_Sbf = Buf()
                gctr = {"g": 0, "d": 0}
                Yv = [PB[4][:].rearrange("p (h d) -> p h d", d=64), PB[5][:].rearrange("p (h d) -> p h d", d=64)]

                def head(c):
                    q = c % 2
                    csl = slice(c * 128, (c + 1) * 128)
                    small, nacum, ea, dec, dtd, eAS, b_sm = smalls[q], nacums[q], eas[q], decs[q], dtds[q], eASs[q], b_sms[q]
                    pe(lambda: nc.tensor.matmul(PB[2][:, 0:16], lhsT=Uf[:], rhs=dA_all[:, c, :], start=True, stop=False),
                       r=[b_const, b_ptab], w=[b_PB[2]], inc=False)
                    pe(lambda: nc.tensor.matmul(PB[2][:, 16:32], lhsT=ones_f[:], rhs=dA_all[:, c, :], start=False,
                                                stop=True), r=[b_ptab], w=[b_PB[2]])
                    CBv = PB[3][:].rearrange("p (g l) -> p g l", l=128)
                    tk = None
                    for gi, g in enumerate((0, 2, 1, 3)):
                        p0 = (g % 2) * 64
                        tk2 = pe(lambda: nc.tensor.matmul(CBv[:, g, :], lhsT=BT[p0:p0 + 64, g // 2, csl],
                                                          rhs=CT[p0:p0 + 64, g // 2, csl], start=(gi == 0),
                                                          stop=(gi == 3)),
                                 r=[b_X], w=[b_PB[3]], inc=(gi == 1 or gi == 3), selfwait=(tk if gi == 2 else None))
                        if gi == 1:
                            tk = tk2
                    for hb in range(2):
                        for kc in range(8):
                            pe(lambda: nc.tensor.matmul(PB[hb][:], lhsT=hT[:, kc, csl],
                                                        rhs=wBC[:, kc, (1 - hb) * 512:(2 - hb) * 512],
                                                        start=(kc == 0), stop=(kc == 7)),
                               r=[b_hT[c], b_wslot[1 - hb]], w=[b_PB[hb]], inc=(kc == 7))
                    yield
                    dve(lambda: nc.vector.tensor_copy(out=small[:], in_=PB[2][:, 0:32]), r=[b_PB[2]], w=[b_sm])
                    acum = small[:, 0:16]
                    atot = small[:, 16:32]
                    dve(lambda: nc.vector.tensor_tensor(out=dec[:], in0=atot, in1=acum, op=ALU.subtract), w=[b_sm])
                    act(lambda: nc.scalar.activation(out=ea[:], in_=acum, func=AF.Exp), w=[b_sm])
                    act(lambda: nc.scalar.activation(out=dec[:], in_=dec[:], func=AF.Exp), w=[b_sm])
                    atv = small[:, 16:32].rearrange("p (s f h) -> p s f h", s=2, f=2)
                    act(lambda: nc.scalar.activation(out=eAS[0:64], in_=atv[0:64, :, 0, :], func=AF.Exp), w=[b_sm])
                    act(lambda: nc.scalar.activation(out=eAS[64:128], in_=atv[64:128, :, 1, :], func=AF.Exp), w=[b_sm])
                    act(lambda: nc.scalar.copy(out=CBs[q][:], in_=CBv), r=[b_PB[3]], w=[b_CBs[q]])
                    for hb in range(2):
                        act(lambda: nc.scalar.activation(out=szb[q][:, hb * 512:(hb + 1) * 512], in_=PB[hb][:],
                                                         func=AF.Silu), r=[b_PB[hb]], w=[b_szb[q]])
                    dve(lambda: nc.vector.tensor_tensor(out=dtd[:], in0=dt_all[:, c, :], in1=dec[:], op=ALU.mult),
                        r=[b_ptab], w=[b_sm])
                    Xc = X_tm[:, c, :].rearrange("p (h d) -> p h d", d=64)
                    dve(lambda: nc.vector.tensor_tensor(out=xds[q][:], in0=Xc,
                                                        in1=dt_all[:, c, :].unsqueeze(2).to_broadcast([128, 16, 64]),
                                                        op=ALU.mult), r=[b_X, b_ptab], w=[b_xds[q]])
                    dve(lambda: nc.vector.tensor_tensor(out=xdds[q][:], in0=Xc,
                                                        in1=dtd[:].unsqueeze(2).to_broadcast([128, 16, 64]),
                                                        op=ALU.mult), r=[b_X, b_sm], w=[b_xdds[q]])

                    yield

                def groups(c):
                    q = c % 2
                    csl = slice(c * 128, (c + 1) * 128)
                    nacum, b_sm = nacums[q], b_sms[q]
                    xd = xds[q]
                    st = {}

                    def acumb(g):
                        gq = gctr["g"] % 2
                        gctr["g"] += 1
                        abk = 2 if gq == 0 else 6
                        first = True
                        for hh in range(4):
                            hd = 4 * g + hh
                            for part in range(2):
                                pe(lambda: nc.tensor.matmul(PB[abk][:, hh * 128:(hh + 1) * 128],
                                                            lhsT=dAhl[:, c, part, hd:hd + 1].to_broadcast([128, 128]),
                                                            rhs=Ub[:], start=first, stop=False),
                                   r=[b_hl, b_const], w=[b_PB[abk]], inc=False)
                                first = False
                        for part in range(2):
                            pe(lambda: nc.tensor.matmul(
                                PB[abk][:].rearrange("p (h l) -> p h l", l=128), lhsT=negUb[:],
                                rhs=dAhl[:, c, part, 4 * g:4 * g + 4].unsqueeze(2).to_broadcast([128, 4, 128]),
                                start=False, stop=False), r=[b_hl, b_const], w=[b_PB[abk]], inc=False)
                        pe(lambda: nc.tensor.matmul(PB[abk][:], lhsT=ident[:],
                                                    rhs=NEGU4[:].rearrange("p h l -> p (h l)"), start=False, stop=True),
                           r=[b_const, b_ptab], w=[b_PB[abk]])
                        st[g] = (gq, abk)

                    def ymm(g):
                        gq, abk = st[g]
                        act(lambda: nc.scalar.activation(out=LTg[gq][:].rearrange("p h l -> p (h l)"), in_=PB[abk][:],
                                                         func=AF.Exp), r=[b_PB[abk]], w=[b_LT[gq]])
                        dve(lambda: nc.vector.tensor_tensor(out=MTg[gq][:], in0=LTg[gq][:],
                                                            in1=CBs[q][:, g:g + 1, :].to_broadcast([128, 4, 128]),
                                                            op=ALU.mult),
                            r=[b_LT[gq], b_CBs[q]], w=[b_MT[gq]])
                        for hh in range(4):
                            hd = 4 * g + hh
                            yb = 4 + hd // 8
                            pe(lambda: nc.tensor.matmul(Yv[hd // 8][:, hd % 8, :], lhsT=MTg[gq][:, hh, :],
                                                        rhs=xd[:, hd, :], start=(hd % 8 == 0), stop=False),
                               r=[b_MT[gq], b_xds[q]], w=[b_PB[yb]], inc=False)
                            pe(lambda: nc.tensor.matmul(Yv[hd // 8][:, hd % 8, :], lhsT=Dg[:, hd, :],
                                                        rhs=X_tm[:, c, hd * 64:(hd + 1) * 64], start=False,
                                                        stop=(hd % 8 == 7)),
                               r=[b_ptab, b_X], w=[b_PB[yb]], inc=(hh == 3))

                    acumb(0)
                    acumb(1)
                    yield
                    ymm(0)
                    yield
                    acumb(2)
                    ymm(1)
                    yield
                    acumb(3)
                    ymm(2)
                    yield
                    ymm(3)
                    yield

                def tail(c):
                    q = c % 2
                    csl = slice(c * 128, (c + 1) * 128)
                    ea, eAS, b_sm = eas[q], eASs[q], b_sms[q]
                    xdd = xdds[q]
                    if c > 0:
                        tk = None
                        for gi, g in enumerate((0, 2, 1, 3)):
                            p0 = (g % 2) * 64
                            ob_ = 6 + g // 2
                            tk2 = pe(lambda: nc.tensor.matmul(PB[ob_][:, (g % 2) * 256:(g % 2 + 1) * 256],
                                                              lhsT=CT[p0:p0 + 64, g // 2, csl],
                                                              rhs=S_bf[p0:p0 + 64, g // 2, :], start=(g % 2 == 0),
                                                              stop=(g % 2 == 1)),
                                     r=[b_X, b_Sbf], w=[b_PB[ob_]], inc=(gi >= 1),
                                     selfwait=(tk if gi == 2 else None))
                            if gi == 1:
                                tk = tk2
                        for hb in range(2):
                            dve(lambda: nc.vector.tensor_tensor(
                                out=ysb[:, hb * 8:(hb + 1) * 8, :],
                                in0=PB[6 + hb][:].rearrange("p (h d) -> p h d", d=64),
                                in1=ea[:, hb * 8:(hb + 1) * 8].unsqueeze(2).to_broadcast([128, 8, 64]), op=ALU.mult),
                                r=[b_PB[6 + hb], b_sm], w=[b_y])
                            dve(lambda: nc.vector.tensor_tensor(out=ysb[:, hb * 8:(hb + 1) * 8, :], in0=Yv[hb],
                                                                in1=ysb[:, hb * 8:(hb + 1) * 8, :], op=ALU.add),
                                r=[b_PB[4 + hb]], w=[b_y])
                    else:
                        for hb in range(2):
                            dve(lambda: nc.vector.tensor_copy(out=ysb[:, hb * 8:(hb + 1) * 8, :], in_=Yv[hb]),
                                r=[b_PB[4 + hb]], w=[b_y])
                    yield
                    dve(lambda: nc.vector.tensor_tensor(out=ysb[:].rearrange("p h d -> p (h d)"),
                                                        in0=ysb[:].rearrange("p h d -> p (h d)"), in1=szb[q][:],
                                                        op=ALU.mult), r=[b_szb[q]], w=[b_y])
                    yf = ysb[:].rearrange("p h d -> p (h d)")
                    for g in range(4):
                        act(lambda: nc.scalar.activation(out=junkf[:], in_=yf[:, g * 256:(g + 1) * 256], func=AF.Square,
                                                         accum_out=ssq[:, g:g + 1]), r=[b_y], w=[b_ob])
                    pool(lambda: nc.gpsimd.tensor_scalar(out=rs4[:], in0=ssq[:], scalar1=1.0 / 256, scalar2=EPS,
                                                         op0=ALU.mult, op1=ALU.add), w=[b_ob])
                    pool(lambda: nc.gpsimd.tensor_tensor(out=rs4[:], in0=rs4[:], in1=mhalf[:, 0:4], op=ALU.pow),
                         r=[b_const], w=[b_ob])
                    yield
                    if c < NT - 1:
                        for g in range(4):
                            p0 = (g % 2) * 64
                            pe(lambda: nc.tensor.matmul(PB[7][p0:p0 + 64, (g // 2) * 256:(g // 2 + 1) * 256],
                                                        lhsT=B_tm[:, c, g * 64:(g + 1) * 64],
                                                        rhs=xdd[:, 4 * g:4 * g + 4, :].rearrange("p h d -> p (h d)"),
                                                        start=(g < 2), stop=(g >= 2)),
                               r=[b_X, b_xdds[q]], w=[b_PB[7]], inc=(g == 3))
                        Sv = S_sb[:].rearrange("p s (h d) -> p (s h) d", d=64)
                        if c == 0:
                            dve(lambda: nc.vector.tensor_copy(out=S_sb[:].rearrange("p s f -> p (s f)"), in_=PB[7][:]),
                                r=[b_PB[7]], w=[b_S])
                        else:
                            dve(lambda: nc.vector.tensor_tensor(
                                out=Sv, in0=Sv,
                                in1=eAS[:].rearrange("p s h -> p (s h)").unsqueeze(2).to_broadcast([128, 8, 64]),
                                op=ALU.mult), r=[b_sm, b_Sbf], w=[b_S])
                            dve(lambda: nc.vector.tensor_tensor(out=S_sb[:].rearrange("p s f -> p (s f)"),
                                                                in0=S_sb[:].rearrange("p s f -> p (s f)"),
                                                                in1=PB[7][:], op=ALU.add),
                                r=[b_PB[7]], w=[b_S])
                        act(lambda: nc.scalar.copy(out=S_bf[:], in_=S_sb[:]), r=[b_S], w=[b_Sbf])
                    yield
                    for g in range(4):
                        dve(lambda: nc.vector.scalar_tensor_tensor(out=ob_sb[:, g * 256:(g + 1) * 256],
                                                                   in0=yf[:, g * 256:(g + 1) * 256],
                                                                   scalar=rs4[:, g:g + 1],
                                                                   in1=snw_b[:, g * 256:(g + 1) * 256],
                                                                   op0=ALU.mult, op1=ALU.mult),
                            r=[b_y, b_ptab], w=[b_ob])
                    yield
                    tv = bfv(7)
                    for cc in range(8):
                        pe(lambda: nc.tensor.transpose(out=tv[:, cc, :], in_=ob_sb[:, cc * 128:(cc + 1) * 128],
                                                       identity=ident[:]),
                           r=[b_ob, b_const], w=[b_PB[7]], inc=(cc == 7))
                    act(lambda: nc.scalar.copy(out=obT[:, :, csl], in_=tv[:, :, :]), r=[b_PB[7]], w=[b_obT[c]])

                def front(c):
                    yield from head(c)
                    yield from groups(c)

                def interleave(gens):
                    gens = list(gens)
                    while gens:
                        for g_ in list(gens):
                            try:
                                next(g_)
                            except StopIteration:
                                gens.remove(g_)

                interleave([front(0)])
                for c in range(NT):
                    gl = [tail(c)]
                    if c + 1 < NT:
                        gl.append(front(c + 1))
                    interleave(gl)
                    if c == NT - 2:
                        dpool(out=wG0[:, :, 0, :], in_=win_d[:, 4180:4180 + 128].rearrange("(k p) n -> p k n", p=128),
                              w=[b_wslot[0]])
                        dpool(out=wG0[:, :, 1, :], in_=win_d[:, 4180 + 1024:4180 + 1152]
                              .rearrange("(k p) n -> p k n", p=128), w=[b_wslot[0]])
                        dpool(out=Wpa0, in_=wpa_d[:, 0:128].rearrange("(k p) n -> p k n", p=128), w=[b_wslot[0]])
                        dpool(out=Wpb0, in_=wpb_d[:, 0:128].rearrange("(k p) n -> p k n", p=128), w=[b_wslot[0]])
                if "obT" in dbg:
                    k.dump("d_obT", obT[:], b_obT)
                k.barrier()
        if stop_after.startswith("B"):
            return nc, k


        with ExitStack() as esM:
            Wout = k.sb("Wout", [128, 8, D], BF16, esM)
            b_W = Buf()
            gbT = k.sb("gbT_sb", [128, 16], F32, esM)
            fnw_b = k.sb("fnw_b", [128, D], F32, esM)
            b_ct = Buf()
            dsync(out=gbT[:], in_=gbT_d[:, :], w=[b_ct])
            dsync(out=fnw_b[:], in_=fnw_d[0, :].partition_broadcast(128), w=[b_ct])
            mT = k.sb("mT", [128, 8, S], BF16, esM)
            b_mT = [Buf() for _ in range(4)]
            with ExitStack() as esM1:
                wG = [k.sb(f"wG{j}", [128, 8, 2, 128], BF16, esM1) for j in range(2)]
                Wpa = [k.sb(f"Wpa{j}", [128, 4, 128], BF16, esM1) for j in range(2)]
                Wpb = [k.sb(f"Wpb{j}", [128, 8, 128], BF16, esM1) for j in range(2)]
                b_wG = [Buf(), Buf()]
                gA = [k.sb(f"gA{j}", [128, 512], F32, esM1) for j in range(2)]
                gB = [k.sb(f"gB{j}", [128, 512], F32, esM1) for j in range(2)]
                b_g = [Buf(), Buf()]
                t1 = [k.sb(f"t1{j}", [128, 512], F32, esM1) for j in range(2)]
                t2 = [k.sb(f"t2{j}", [128, 512], F32, esM1) for j in range(2)]
                b_t = [Buf(), Buf()]
                it = 0
                for m in range(8):
                    wq = m % 2
                    if m == 0:
                        gw_, pa_, pb_, bw_ = wG0, Wpa0, Wpb0, b_wslot[0]
                    else:
                        gw_, pa_, pb_, bw_ = wG[wq][:], Wpa[wq][:], Wpb[wq][:], b_wG[wq]
                        dpool(out=gw_[:, :, 0, :], in_=win_d[:, 4180 + m * 128:4180 + (m + 1) * 128]
                              .rearrange("(k p) n -> p k n", p=128), w=[bw_])
                        dpool(out=gw_[:, :, 1, :], in_=win_d[:, 4180 + (8 + m) * 128:4180 + (9 + m) * 128]
                              .rearrange("(k p) n -> p k n", p=128), w=[bw_])
                        dpool(out=pa_, in_=wpa_d[:, m * 128:(m + 1) * 128].rearrange("(k p) n -> p k n", p=128),
                              w=[bw_])
                        dpool(out=pb_, in_=wpb_d[:, m * 128:(m + 1) * 128].rearrange("(k p) n -> p k n", p=128),
                              w=[bw_])
                    if m == 0:
                        for hf in range(2):
                            cs_ = slice(hf * 512, (hf + 1) * 512)
                            dpool(out=Wout[:, :, cs_], in_=wout_d[:, cs_].rearrange("(k p) n -> p k n", p=128),
                                  w=[b_W])
                    for tc in range(4):
                        q = it % 2
                        it += 1
                        b0 = 4 * q
                        ts_ = slice(tc * 512, (tc + 1) * 512)
                        for kc in range(8):
                            pe(lambda: nc.tensor.matmul(PB[b0][:], lhsT=gw_[:, kc, 0, :], rhs=hT[:, kc, ts_],
                                                        start=(kc == 0), stop=(kc == 7)),
                               r=b_hT[tc * 4:(tc + 1) * 4] + [bw_], w=[b_PB[b0]], inc=(kc == 7))
                        for kc in range(8):
                            pe(lambda: nc.tensor.matmul(PB[b0 + 1][:], lhsT=gw_[:, kc, 1, :], rhs=hT[:, kc, ts_],
                                                        start=(kc == 0), stop=(kc == 7)),
                               r=b_hT[tc * 4:(tc + 1) * 4] + [bw_], w=[b_PB[b0 + 1]], inc=(kc == 7))
                        for kc in range(4):
                            pe(lambda: nc.tensor.matmul(PB[b0 + 2][:], lhsT=pa_[:, kc, :],
                                                        rhs=oaT[:, kc, ts_], start=(kc == 0), stop=(kc == 3)),
                               r=b_oaT[tc * 4:(tc + 1) * 4] + [bw_], w=[b_PB[b0 + 2]], inc=(kc == 3))
                        for kc in range(8):
                            pe(lambda: nc.tensor.matmul(PB[b0 + 3][:], lhsT=pb_[:, kc, :],
                                                        rhs=obT[:, kc, ts_], start=(kc == 0), stop=(kc == 7)),
                               r=b_obT[tc * 4:(tc + 1) * 4] + [bw_], w=[b_PB[b0 + 3]], inc=(kc == 7))
                        act(lambda: nc.scalar.activation(out=gA[q][:], in_=PB[b0][:], func=AF.Sigmoid,
                                                         bias=gbT[:, m:m + 1]), r=[b_PB[b0], b_ct], w=[b_g[q]])
                        act(lambda: nc.scalar.activation(out=gB[q][:], in_=PB[b0 + 1][:], func=AF.Sigmoid,
                                                         bias=gbT[:, 8 + m:9 + m]), r=[b_PB[b0 + 1], b_ct], w=[b_g[q]])
                        dve(lambda: nc.vector.tensor_tensor(out=t1[q][:], in0=PB[b0 + 2][:], in1=gA[q][:], op=ALU.mult),
                            r=[b_PB[b0 + 2], b_g[q]], w=[b_t[q]])
                        dve(lambda: nc.vector.tensor_tensor(out=t2[q][:], in0=PB[b0 + 3][:], in1=gB[q][:], op=ALU.mult),
                            r=[b_PB[b0 + 3], b_g[q]], w=[b_t[q]])
                        dve(lambda: nc.vector.tensor_tensor(out=mT[:, m, ts_], in0=t1[q][:], in1=t2[q][:], op=ALU.add),
                            r=[b_t[q]], w=[b_mT[tc]])
                k.barrier()
            if "mT" in dbg:
                k.dump("d_mT", mT[:], b_mT)
            ck("Cm")
            xr = [k.sb(f"xr{j}", [128, D], F32, esM) for j in range(2)]
            b_xr = [Buf(), Buf()]
            xo = [k.sb(f"xo{j}", [128, D], F32, esM) for j in range(2)]
            b_xo = [Buf(), Buf()]
            fo = [k.sb(f"fo{j}", [128, D], F32, esM) for j in range(2)]
            b_fo = [Buf(), Buf()]
            junkc = k.sb("junkc", [128, D], BF16, esM)
            fss = k.sb("fss", [128, NT], F32, esM)
            b_fs = Buf()
            dpool(out=xr[0][:], in_=x_d[0:128, :], w=[b_xr[0]])
            b_fsi = [Buf() for _ in range(NT)]

            def out1(i):
                q = i % 2
                tsl = slice(i * 128, (i + 1) * 128)
                if i + 1 < NT:
                    dpool(out=xr[1 - q][:], in_=x_d[(i + 1) * 128:(i + 2) * 128, :], w=[b_xr[1 - q]])
                for hf in range(2):
                    bk = 2 * q + hf
                    for kc in range(8):
                        pe(lambda: nc.tensor.matmul(PB[bk][:], lhsT=mT[:, kc, tsl], rhs=Wout[:, kc, hf * 512:(hf + 1) * 512],
                                                    start=(kc == 0), stop=(kc == 7)),
                           r=[b_mT[i // 4], b_W], w=[b_PB[bk]], inc=(kc == 7))
                    dve(lambda: nc.vector.tensor_tensor(out=xo[q][:, hf * 512:(hf + 1) * 512], in0=PB[bk][:],
                                                        in1=xr[q][:, hf * 512:(hf + 1) * 512], op=ALU.add),
                        r=[b_PB[bk], b_xr[q]], w=[b_xo[q]])
                act(lambda: nc.scalar.activation(out=junkc[:], in_=xo[q][:], func=AF.Square, accum_out=fss[:, i:i + 1]),
                    r=[b_xo[q]], w=[b_fs, b_fsi[i]])
                act(lambda: nc.scalar.activation(out=fss[:, i:i + 1], in_=fss[:, i:i + 1], func=AF.Sqrt, scale=1.0 / D,
                                                 bias=EPS), w=[b_fsi[i]])

            def out2(i):
                q = i % 2
                tsl = slice(i * 128, (i + 1) * 128)
                dve(lambda: nc.vector.reciprocal(out=fss[:, i:i + 1], in_=fss[:, i:i + 1]), w=[b_fsi[i]])
                dve(lambda: nc.vector.scalar_tensor_tensor(out=fo[q][:], in0=xo[q][:], scalar=fss[:, i:i + 1],
                                                           in1=fnw_b[:], op0=ALU.mult, op1=ALU.mult),
                    r=[b_xo[q], b_ct, b_fsi[i]], w=[b_fo[q]])
                dsync(out=out_d[tsl, :], in_=fo[q][:], r=[b_fo[q]])

            out1(0)
            for i in range(NT):
                if i + 1 < NT:
                    out1(i + 1)
                out2(i)
            k.barrier()

        k.barrier()
    return nc, k


_NC_CACHE = {}


def kernel(x, positions, norm_w, w_in, gate_bias, conv_w, conv_b, dt_bias, a_log, d_skip,
           ssm_norm_w, w_branch_a, w_branch_b, w_out, final_norm_w):
    f32 = np.float32
    x = np.asarray(x, dtype=f32)
    positions = np.asarray(positions).astype(np.int32)
    nb = x.shape[0]
    assert nb == 8 and x.shape[1] == S and x.shape[2] == D
    if "nc" not in _NC_CACHE:
        _NC_CACHE["nc"] = build()[0]
    nc = _NC_CACHE["nc"]
    invf = (500000.0 ** (-np.arange(0, 16, 2, dtype=f32) / 16)).astype(f32)
    shared = {
        "invf": np.ascontiguousarray(np.broadcast_to(invf, (128, 8))),
        "norm_w": np.ascontiguousarray(np.asarray(norm_w, f32).reshape(1, D)),
        "w_in": np.ascontiguousarray(np.asarray(w_in, f32)[0]),
        "cwT": np.ascontiguousarray(np.asarray(conv_w, f32)[0].reshape(4, 12, 128).transpose(2, 1, 0)),
        "cbT": np.ascontiguousarray(np.asarray(conv_b, f32)[0].reshape(12, 128).T),
        "dt_bias": np.ascontiguousarray(np.asarray(dt_bias, f32).reshape(1, 16)),
        "a_log": np.ascontiguousarray(np.asarray(a_log, f32).reshape(1, 16)),
        "d_skip": np.ascontiguousarray(np.asarray(d_skip, f32).reshape(1, 16)),
        "ssm_norm_w": np.ascontiguousarray(np.asarray(ssm_norm_w, f32).reshape(1, D)),
        "gbT": np.ascontiguousarray(np.asarray(gate_bias, f32)[0].reshape(16, 128).T),
        "w_branch_a": np.ascontiguousarray(np.asarray(w_branch_a, f32)[0]),
        "w_branch_b": np.ascontiguousarray(np.asarray(w_branch_b, f32)[0]),
        "w_out": np.ascontiguousarray(np.asarray(w_out, f32)[0]),
        "final_norm_w": np.ascontiguousarray(np.asarray(final_norm_w, f32).reshape(1, D)),
    }
    in_maps = []
    for b in range(nb):
        m = dict(shared)
        m["x"] = np.ascontiguousarray(x[b])
        m["posT"] = np.ascontiguousarray(positions[b].reshape(NT, 128).T)
        in_maps.append(m)
    res = run_bass_kernel_spmd(nc, in_maps, core_ids=list(range(nb)))
    out = np.stack([np.asarray(res.results[b]["out"], dtype=f32) for b in range(nb)], axis=0)
    return out
```

```python
import numpy as np
import math
import concourse.bass as bass
import concourse.mybir as mybir
from concourse.bass_utils import run_bass_kernel_spmd
from contextlib import ExitStack

F32 = mybir.dt.float32
BF16 = mybir.dt.bfloat16
I32 = mybir.dt.int32
ALU = mybir.AluOpType
AF = mybir.ActivationFunctionType
AX = mybir.AxisListType

S = 2048
D = 1024
NT = 16
INW = 6228
EPS = 1e-6
NITER = 11
A_SRC = [(0, 512, 0), (512, 640, 512), (1280, 1536, 640), (1536, 1600, 896), (1600, 1604, 960),
         (640, 768, 964), (768, 1280, 1092)]
NA = 1604


class Buf:
    __slots__ = ("w", "r", "name", "excl")

    def __init__(self, name="", excl=False):
        self.w = {}
        self.r = {}
        self.name = name
        self.excl = excl


def _merge(deps, d):
    for k, (s, v) in d.items():
        if k not in deps or deps[k][1] < v:
            deps[k] = (s, v)


class Eng:
    def __init__(self, K, name, eng, selfdep=True):
        self.K = K
        self.name = name
        self.eng = eng
        self.selfdep = selfdep
        self.sem = K.es.enter_context(K.nc.semaphore("s_" + name))
        self.cnt = 0
        self.waited = {}
        self.pending = False

    def wait_deps(self, deps):
        for k, (s, v) in deps.items():
            if k == self.name and not self.selfdep:
                continue
            if self.waited.get(k, 0) < v:
                self.eng.wait_ge(s, v)
                self.waited[k] = v

    def __call__(self, fn, r=(), w=(), inc=True, extra=(), selfwait=None):
        if selfwait is not None:
            assert selfwait[0] == self.name
            if self.waited.get("self", 0) < selfwait[2]:
                self.eng.wait_ge(selfwait[1], selfwait[2])
                self.waited["self"] = selfwait[2]
        w = list(w) + [b for b in r if b.excl]
        r = [b for b in r if not b.excl]
        deps = {}
        for b in r:
            _merge(deps, b.w)
        for b in w:
            _merge(deps, b.w)
            _merge(deps, b.r)
        for t in extra:
            _merge(deps, {t[0]: (t[1], t[2])})
        self.wait_deps(deps)
        ins = fn()
        if inc:
            self.cnt += 1
            ins.then_inc(self.sem, 1)
            tok = (self.sem, self.cnt)
            self.pending = False
        else:
            tok = (self.sem, self.cnt + 1)
            self.pending = True
        for b in r:
            _merge(b.r, {self.name: tok})
        for b in w:
            b.w = {self.name: tok}
            b.r = {}
        return (self.name,) + tok


class DmaQ:
    def __init__(self, K, name, waiter, nsem=8):
        self.K = K
        self.name = name
        self.waiter = waiter
        self.sems = [K.es.enter_context(K.nc.semaphore(f"d_{name}{j}")) for j in range(nsem)]
        self.vals = [0] * nsem
        self.idx = 0

    def __call__(self, out, in_, r=(), w=(), extra=(), **kw):
        deps = {}
        for b in r:
            _merge(deps, b.w)
        for b in w:
            _merge(deps, b.w)
            _merge(deps, b.r)
        for t in extra:
            _merge(deps, {t[0]: (t[1], t[2])})
        k = self.idx
        self.idx = (k + 1) % len(self.sems)
        key = f"{self.name}{k}"
        if self.vals[k] > 0:
            _merge(deps, {key: (self.sems[k], self.vals[k])})
        self.waiter.wait_deps(deps)
        ins = self.waiter.eng.dma_start(out=out, in_=in_, **kw)
        self.vals[k] += 16
        ins.then_inc(self.sems[k], 16)
        tok = (self.sems[k], self.vals[k])
        for b in r:
            _merge(b.r, {key: tok})
        for b in w:
            b.w = {key: tok}
            b.r = {}
        return (key,) + tok


class K:
    def __init__(self, nc, es):
        self.nc = nc
        self.es = es
        self.pe = Eng(self, "pe", nc.tensor, selfdep=False)
        self.act = Eng(self, "act", nc.scalar)
        self.dve = Eng(self, "dve", nc.vector)
        self.pool = Eng(self, "pool", nc.gpsimd)
        self.sp = Eng(self, "sp", nc.sync)
        self.engs = [self.pe, self.act, self.dve, self.pool, self.sp]
        self.dsync = DmaQ(self, "qs", self.sp, nsem=8)
        self.dpool = DmaQ(self, "qp", self.pool, nsem=8)
        self.dqs = [self.dsync, self.dpool]
        self.dumps = []

    def sb(self, name, shape, dt, es=None):
        t = (es or self.es).enter_context(self.nc.sbuf_tensor(name, list(shape), dt))
        return t

    def ps(self, name, shape, dt, es=None):
        return (es or self.es).enter_context(self.nc.psum_tensor(name, list(shape), dt))

    def barrier(self):
        deps = {}
        for e in self.engs:
            assert not e.pending, e.name
            if e.cnt > 0:
                deps[e.name] = (e.sem, e.cnt)
        for q in self.dqs:
            for j, s in enumerate(q.sems):
                if q.vals[j] > 0:
                    deps[f"{q.name}{j}"] = (s, q.vals[j])
        for e in self.engs:
            e.wait_deps(deps)

    def dump(self, name, ap, buf):
        d = self.nc.dram_tensor(name, list(ap.shape), ap.dtype, kind="ExternalOutput").ap()
        self.dsync(out=d, in_=ap, r=(buf if isinstance(buf, (list, tuple)) else [buf]))
        self.dumps.append(name)


class Stop(Exception):
    pass


def build(stop_after="all", dbg=()):
    try:
        return _build(stop_after, dbg)
    except Stop as s:
        return s.args


def _build(stop_after="all", dbg=()):
    nc = bass.Bass("TRN2", target_bir_lowering=False)
    dbg = set(dbg)
    x_d = nc.dram_tensor("x", [S, D], F32, kind="ExternalInput").ap()
    posT_d = nc.dram_tensor("posT", [128, NT], I32, kind="ExternalInput").ap()
    invf_d = nc.dram_tensor("invf", [128, 8], F32, kind="ExternalInput").ap()
    normw_d = nc.dram_tensor("norm_w", [1, D], F32, kind="ExternalInput").ap()
    win_d = nc.dram_tensor("w_in", [D, INW], F32, kind="ExternalInput").ap()
    out_d = nc.dram_tensor("out", [S, D], F32, kind="ExternalOutput").ap()
    cwT_d = nc.dram_tensor("cwT", [128, 12, 4], F32, kind="ExternalInput").ap()
    cbT_d = nc.dram_tensor("cbT", [128, 12], F32, kind="ExternalInput").ap()
    dtb_d = nc.dram_tensor("dt_bias", [1, 16], F32, kind="ExternalInput").ap()
    alog_d = nc.dram_tensor("a_log", [1, 16], F32, kind="ExternalInput").ap()
    dsk_d = nc.dram_tensor("d_skip", [1, 16], F32, kind="ExternalInput").ap()
    snw_d = nc.dram_tensor("ssm_norm_w", [1, D], F32, kind="ExternalInput").ap()
    gbT_d = nc.dram_tensor("gbT", [128, 16], F32, kind="ExternalInput").ap()
    wpa_d = nc.dram_tensor("w_branch_a", [512, D], F32, kind="ExternalInput").ap()
    wpb_d = nc.dram_tensor("w_branch_b", [D, D], F32, kind="ExternalInput").ap()
    wout_d = nc.dram_tensor("w_out", [D, D], F32, kind="ExternalInput").ap()
    fnw_d = nc.dram_tensor("final_norm_w", [1, D], F32, kind="ExternalInput").ap()

    with ExitStack() as es:
        k = K(nc, es)
        pe, act, dve, pool, sp = k.pe, k.act, k.dve, k.pool, k.sp
        dsync, dpool = k.dsync, k.dpool

        def ck(name):
            if stop_after == name:
                k.barrier()
                raise Stop(nc, k)

        ident = k.sb("ident", [128, 128], BF16)
        Uf = k.sb("Uf", [128, 128], F32)
        Ub = k.sb("Ub", [128, 128], BF16)
        cbias = k.sb("cbias", [128, 128], F32)
        pow2 = k.sb("pow2", [128, NITER + 2], F32)
        negU = k.sb("negU", [128, 128], BF16)
        negUb = k.sb("negUb", [128, 128], BF16)
        mhalf = k.sb("mhalf", [128, 16], F32)
        b_const = Buf("const")
        pool(lambda: nc.gpsimd.memset(ident[:], 1.0), w=[b_const])
        pool(lambda: nc.gpsimd.affine_select(out=ident[:], in_=ident[:], pattern=[[-1, 128]],
                                             compare_op=ALU.is_equal, fill=0.0, base=0, channel_multiplier=1),
             w=[b_const])
        pool(lambda: nc.gpsimd.memset(Uf[:], 1.0), w=[b_const])
        pool(lambda: nc.gpsimd.affine_select(out=Uf[:], in_=Uf[:], pattern=[[1, 128]], compare_op=ALU.is_ge,
                                             fill=0.0, base=0, channel_multiplier=-1), w=[b_const])
        pool(lambda: nc.gpsimd.tensor_copy(out=Ub[:], in_=Uf[:]), w=[b_const])
        pool(lambda: nc.gpsimd.tensor_scalar(out=negUb[:], in0=Ub[:], scalar1=-1.0, scalar2=None, op0=ALU.mult),
             w=[b_const])
        pool(lambda: nc.gpsimd.memset(mhalf[:], -0.5), w=[b_const])
        pool(lambda: nc.gpsimd.memset(negU[:], 0.0), w=[b_const])
        pool(lambda: nc.gpsimd.affine_select(out=negU[:], in_=negU[:], pattern=[[1, 128]], compare_op=ALU.is_ge,
                                             fill=-30000.0, base=0, channel_multiplier=-1), w=[b_const])
        pool(lambda: nc.gpsimd.memset(cbias[:], 0.0), w=[b_const])
        pool(lambda: nc.gpsimd.affine_select(out=cbias[:], in_=cbias[:], pattern=[[-1, 128]], compare_op=ALU.is_ge,
                                             fill=-1e30, base=0, channel_multiplier=1), w=[b_const])
        for j in range(NITER + 2):
            pool(lambda: nc.gpsimd.memset(pow2[:, j:j + 1], 2.0 ** (-j)), w=[b_const])
        pool(lambda: nc.gpsimd.memset(pow2[:, 0:1], 1.0), w=[b_const])

        hT = k.sb("hT", [128, 8, S], BF16)
        b_hT = [Buf(f"hT{i}") for i in range(NT)]
        wBC = k.sb("wBC", [128, 8, 1024], BF16)
        b_wslot = [Buf(), Buf()]
        wG0 = wBC[:, :, 0:256].rearrange("p k (h n) -> p k h n", h=2)
        Wpb0 = wBC[:, :, 256:384]
        Wpa0 = wBC[:, 0:4, 384:512]
        CONV0 = 2628
        oaT = k.sb("oaT", [128, 4, S], BF16)
        b_oaT = [Buf() for _ in range(NT)]
        esWA = ExitStack()
        wA = k.sb("wA", [128, 8, NA], BF16, esWA)
        b_wA = Buf()
        for (c0, c1, dst) in A_SRC:
            dpool(out=wA[:, :, dst:dst + (c1 - c0)],
                  in_=win_d[:, c0:c1].rearrange("(k p) n -> p k n", p=128), w=[b_wA])

        with ExitStack() as es0:
            normw_b = k.sb("normw_b", [128, D], F32, es0)
            b_normw = Buf()
            dsync(out=normw_b[:], in_=normw_d[0, :].partition_broadcast(128), w=[b_normw])
            xt = k.sb("xt_all", [128, NT, D], F32, es0)
            b_xt = [Buf() for _ in range(NT)]
            xn = [k.sb(f"xn{j}", [128, D], BF16, es0) for j in range(3)]
            b_xn = [Buf(), Buf(), Buf()]
            junk = k.sb("junk0", [128, D], BF16, es0)
            b_junk = Buf()
            ss = k.sb("ss", [128, NT], F32, es0)
            sd = k.sb("sd", [128, NT], F32, es0)
            rstd = k.sb("rstd", [128, NT], F32, es0)
            b_ssg = [Buf() for _ in range(4)]
            pt = [k.ps(f"pt{j}", [128, 8, 128], BF16, es0) for j in range(2)]
            b_pt = [Buf(excl=True), Buf(excl=True)]
            for i in range(NT):
                (dsync if i % 2 == 0 else dsync)(out=xt[:, i, :], in_=x_d[i * 128:(i + 1) * 128, :], w=[b_xt[i]])

            def p0_stats(gq):
                for i in range(4 * gq, 4 * gq + 4):
                    act(lambda: nc.scalar.activation(out=junk[:], in_=xt[:, i, :], func=AF.Square,
                                                     accum_out=ss[:, i:i + 1]),
                        r=[b_xt[i]], w=[b_junk, b_ssg[gq]])
                act(lambda: nc.scalar.activation(out=sd[:, 4 * gq:4 * gq + 4], in_=ss[:, 4 * gq:4 * gq + 4],
                                                 func=AF.Sqrt, scale=1.0 / D, bias=EPS), w=[b_ssg[gq]])
                dve(lambda: nc.vector.reciprocal(out=rstd[:, 4 * gq:4 * gq + 4], in_=sd[:, 4 * gq:4 * gq + 4]),
                    w=[b_ssg[gq]])

            def p0_apply(gq):
                for i in range(4 * gq, 4 * gq + 4):
                    j = i % 3
                    jp = i % 2
                    dve(lambda: nc.vector.scalar_tensor_tensor(out=xn[j][:], in0=xt[:, i, :], scalar=rstd[:, i:i + 1],
                                                               in1=normw_b[:], op0=ALU.mult, op1=ALU.mult),
                        r=[b_xt[i], b_ssg[gq], b_normw], w=[b_xn[j]])
                    for c in range(8):
                        pe(lambda: nc.tensor.transpose(out=pt[jp][:, c, :], in_=xn[j][:, c * 128:(c + 1) * 128],
                                                       identity=ident[:]),
                           r=[b_xn[j], b_const], w=[b_pt[jp]], inc=(c == 7))
                    act(lambda: nc.scalar.copy(out=hT[:, :, i * 128:(i + 1) * 128], in_=pt[jp][:]),
                        r=[b_pt[jp]], w=[b_hT[i]])

            p0_stats(0)
            for gq in range(4):
                if gq + 1 < 4:
                    p0_stats(gq + 1)
                p0_apply(gq)
            k.barrier()
        if "hT" in dbg:
            k.dump("d_hT", hT[:], b_hT[NT - 1])
        if stop_after == "p0":
            k.barrier()
            return nc, k


        PB = [k.ps(f"pb{j}", [128, 512], F32) for j in range(8)]
        b_PB = [Buf(f"pb{j}", excl=True) for j in range(8)]

        def bfv(j):
            return PB[j][:].bitcast(BF16).rearrange("p (s t) -> p s t", t=128)

        with ExitStack() as esA:
            dpool(out=wBC[:, :, 0:512], in_=win_d[:, CONV0:CONV0 + 512].rearrange("(k p) n -> p k n", p=128),
                  w=[b_wslot[0]])
            dpool(out=wBC[:, :, 512:1024], in_=win_d[:, CONV0 + 512:CONV0 + 1024].rearrange("(k p) n -> p k n", p=128),
                  w=[b_wslot[1]])
            posi = k.sb("posi", [128, NT], I32, esA)
            posf = k.sb("posf", [128, NT], F32, esA)
            invf = k.sb("invf_sb", [128, 8], F32, esA)
            ang = k.sb("ang", [128, NT, 8], F32, esA)
            cos_t = k.sb("cos_t", [128, NT, 8], F32, esA)
            sin_t = k.sb("sin_t", [128, NT, 8], F32, esA)
            ry = k.sb("ry", [128, NT, 8], F32, esA)
            rki = k.sb("rki", [128, NT, 8], I32, esA)
            rkf = k.sb("rkf", [128, NT, 8], F32, esA)
            rg = k.sb("rg", [128, NT, 8], F32, esA)
            b_tab = Buf()
            dsync(out=posi[:], in_=posT_d[:, :], w=[b_tab])
            dsync(out=invf[:], in_=invf_d[:, :], w=[b_tab])
            dve(lambda: nc.vector.tensor_copy(out=posf[:], in_=posi[:]), r=[b_tab], w=[b_tab])
            dve(lambda: nc.vector.tensor_tensor(out=ang[:], in0=posf[:].unsqueeze(2).to_broadcast([128, NT, 8]),
                                                in1=invf[:].unsqueeze(1).to_broadcast([128, NT, 8]), op=ALU.mult),
                r=[b_tab], w=[b_tab])
            TWO_PI = 2.0 * math.pi
            for (dst_t, off) in ((sin_t, 0.0), (cos_t, 0.25)):
                dve(lambda: nc.vector.tensor_scalar(out=ry[:], in0=ang[:], scalar1=1.0 / TWO_PI, scalar2=off,
                                                    op0=ALU.mult, op1=ALU.add), r=[b_tab], w=[b_tab])
                dve(lambda: nc.vector.tensor_copy(out=rki[:], in_=ry[:]), r=[b_tab], w=[b_tab])
                dve(lambda: nc.vector.tensor_copy(out=rkf[:], in_=rki[:]), r=[b_tab], w=[b_tab])
                dve(lambda: nc.vector.tensor_tensor(out=ry[:], in0=ry[:], in1=rkf[:], op=ALU.subtract),
                    r=[b_tab], w=[b_tab])
                dve(lambda: nc.vector.tensor_scalar(out=rg[:], in0=ry[:], scalar1=0.5, scalar2=None, op0=ALU.is_ge),
                    r=[b_tab], w=[b_tab])
                dve(lambda: nc.vector.tensor_tensor(out=ry[:], in0=ry[:], in1=rg[:], op=ALU.subtract),
                    r=[b_tab], w=[b_tab])
                dve(lambda: nc.vector.tensor_scalar(out=rg[:], in0=ry[:], scalar1=-0.5, scalar2=None, op0=ALU.is_lt),
                    r=[b_tab], w=[b_tab])
                dve(lambda: nc.vector.tensor_tensor(out=ry[:], in0=ry[:], in1=rg[:], op=ALU.add),
                    r=[b_tab], w=[b_tab])
                act(lambda: nc.scalar.activation(out=dst_t[:], in_=ry[:], func=AF.Sin, scale=TWO_PI * (1.0 - 1e-6)),
                    r=[b_tab], w=[b_tab])
            ck("Atab")
            if "rope" in dbg:
                k.dump("d_cos", cos_t[:], b_tab)
                k.dump("d_sin", sin_t[:], b_tab)

            qk_sb = [k.sb(f"qk_sb{j}", [128, 15, 64], BF16, esA) for j in range(2)]
            b_qk = [Buf(), Buf()]
            rt = [k.sb(f"rt{j}", [128, 15, 8], F32, esA) for j in range(4)]
            b_rt = Buf()
            rsrc = k.sb("rsrc", [128, 15, 16], F32, esA)
            b_rsrc = Buf()
            QT = [k.sb(f"QT{j}", [128, 12, 128], BF16, esA) for j in range(2)]
            b_QT = [Buf(), Buf()]
            KT = k.sb("KT", [128, 2, S], BF16, esA)
            KIT = k.sb("KIT", [128, S], BF16, esA)
            b_KT = [Buf() for _ in range(NT)]
            for j_ in range(2):
                pool(lambda: nc.gpsimd.memset(QT[j_][64:128], 0.0), w=[b_QT[j_]])
            pool(lambda: nc.gpsimd.memset(KT[64:128], 0.0), w=b_KT)
            pool(lambda: nc.gpsimd.memset(KIT[64:128], 0.0), w=b_KT)
            Vaug = k.sb("Vaug", [128, NT, 2, 65], BF16, esA)
            b_V = [Buf() for _ in range(NT)]
            wv = k.sb("wv", [128, NT, 4], F32, esA)
            b_wv = [Buf() for _ in range(NT)]
            sza = [k.sb(f"sza{j}", [128, 512], F32, esA) for j in range(2)]
            b_sza = [Buf(), Buf()]
            sc = [k.sb(f"sc{j}", [128, S], F32, esA) for j in range(2)]
            b_sc = [Buf(), Buf()]
            rl = [k.sb(f"rl{j}", [128, S], F32, esA) for j in range(2)]
            b_rl = [Buf(), Buf()]
            junkb = k.sb("junkb", [128, S], BF16, esA)
            b_junkb = Buf()
            m01 = k.sb("m01", [128, S], BF16, esA)
            b_m01 = Buf()
            maskT = [k.sb(f"maskT{j}", [128, NT, 128], BF16, esA) for j in range(2)]
            b_maskT = [Buf(), Buf()]
            Bv = k.sb("Bv", [128, 1], F32, esA)
            Bk = k.sb("Bk", [128, NITER + 2], F32, esA)
            mid = [k.sb(f"mid{j}", [128, 1], F32, esA) for j in range(2)]
            cnt = k.sb("cnt", [128, 1], F32, esA)
            dd = k.sb("dd", [128, 1], F32, esA)
            thr = k.sb("thr", [128, NT], F32, esA)
            b_bis = Buf()
            Eb = [k.sb(f"Eb{j}", [128, 512], BF16, esA) for j in range(3)]
            b_Eb = [Buf(), Buf(), Buf()]
            rinv = k.sb("rinv", [128, 4], F32, esA)
            otmp = k.sb("otmp", [128, 4, 64], F32, esA)
            rinv8 = k.sb("rinv8", [128, 8], F32, esA)
            otmp8 = k.sb("otmp8", [128, 8, 64], F32, esA)
            b_otmp = Buf()
            oa_sb = k.sb("oa_sb", [128, 512], BF16, esA)
            b_oa = Buf()
            pool(lambda: nc.gpsimd.memset(Vaug[:], 1.0), w=b_V)

            A_BANK = [(0, 0, 512), (1, 512, 452), (2, 964, 512), (3, 1476, 128)]
            T0v = bfv(4)
            T1v = bfv(5)
            ctr = {"st": 0, "ix": 0, "e": 0}

            def hdr_(i):
                return i % 2, slice(i * 128, (i + 1) * 128), (i + 1) * 128, i >= 2

            def stage1(i):
                j, tsl, L, masked = hdr_(i)

                for (bk, c0, n) in A_BANK:
                    for kc in range(8):
                        pe(lambda: nc.tensor.matmul(PB[bk][:, 0:n], lhsT=hT[:, kc, tsl], rhs=wA[:, kc, c0:c0 + n],
                                                    start=(kc == 0), stop=(kc == 7)),
                           r=[b_hT[i], b_wA], w=[b_PB[bk]], inc=(kc == 7))
                ck(f"Aproj{i}")
                p0v = PB[0][:, 0:512].rearrange("p (h d) -> p h d", d=64)
                p1v = PB[1][:, 0:448].rearrange("p (h d) -> p h d", d=64)
                act(lambda: nc.scalar.copy(out=qk_sb[j][:, 0:8, 16:64], in_=p0v[:, :, 16:64]),
                    r=[b_PB[0]], w=[b_qk[j]])
                act(lambda: nc.scalar.copy(out=qk_sb[j][:, 8:15, 16:64], in_=p1v[:, :, 16:64]),
                    r=[b_PB[1]], w=[b_qk[j]])
                ck(f"Ae1_{i}")
                act(lambda: nc.scalar.copy(out=rsrc[:, 0:8, :], in_=p0v[:, :, 0:16]), r=[b_PB[0]], w=[b_rsrc])
                act(lambda: nc.scalar.copy(out=rsrc[:, 8:15, :], in_=p1v[:, :, 0:16]), r=[b_PB[1]], w=[b_rsrc])
                nh = 15
                cb = cos_t[:, i:i + 1, :].to_broadcast([128, nh, 8])
                sb_ = sin_t[:, i:i + 1, :].to_broadcast([128, nh, 8])
                x1 = rsrc[:, :, 0:8]
                x2 = rsrc[:, :, 8:16]
                dve(lambda: nc.vector.tensor_tensor(out=rt[0][:], in0=x1, in1=cb, op=ALU.mult),
                    r=[b_rsrc, b_tab], w=[b_rt])
                dve(lambda: nc.vector.tensor_tensor(out=rt[1][:], in0=x2, in1=sb_, op=ALU.mult),
                    r=[b_rsrc, b_tab], w=[b_rt])
                dve(lambda: nc.vector.tensor_tensor(out=qk_sb[j][:, :, 0:8], in0=rt[0][:], in1=rt[1][:], op=ALU.subtract),
                    r=[b_rt], w=[b_qk[j]])
                dve(lambda: nc.vector.tensor_tensor(out=rt[2][:], in0=x2, in1=cb, op=ALU.mult),
                    r=[b_rsrc, b_tab], w=[b_rt])
                dve(lambda: nc.vector.tensor_tensor(out=rt[3][:], in0=x1, in1=sb_, op=ALU.mult),
                    r=[b_rsrc, b_tab], w=[b_rt])
                dve(lambda: nc.vector.tensor_tensor(out=qk_sb[j][:, :, 8:16], in0=rt[2][:], in1=rt[3][:], op=ALU.add),
                    r=[b_rt], w=[b_qk[j]])
                ck(f"Ae2_{i}")
                act(lambda: nc.scalar.copy(out=wv[:, i, :], in_=PB[1][:, 448:452]), r=[b_PB[1]], w=[b_wv[i]])
                act(lambda: nc.scalar.copy(out=Vaug[:, i, :, 0:64],
                                           in_=PB[2][:, 0:128].rearrange("p (g d) -> p g d", d=64)),
                    r=[b_PB[2]], w=[b_V[i]])
                ck(f"Ae3_{i}")
                act(lambda: nc.scalar.activation(out=sza[j][:, 0:384], in_=PB[2][:, 128:512], func=AF.Silu),
                    r=[b_PB[2]], w=[b_sza[j]])
                act(lambda: nc.scalar.activation(out=sza[j][:, 384:512], in_=PB[3][:, 0:128], func=AF.Silu),
                    r=[b_PB[3]], w=[b_sza[j]])
                ck(f"Aevac{i}")
                for h in range(15):
                    tv, sl, bk = (T0v, h, 4) if h < 8 else (T1v, h - 8, 5)
                    pe(lambda: nc.tensor.transpose(out=tv[0:64, sl, :], in_=qk_sb[j][:, h, :], identity=ident[:]),
                       r=[b_qk[j], b_const], w=[b_PB[bk]], inc=(h == 7 or h == 14))
                act(lambda: nc.scalar.copy(out=QT[j][0:64, 0:8, :], in_=T0v[0:64, :, :]), r=[b_PB[4]], w=[b_QT[j]])
                act(lambda: nc.scalar.copy(out=KT[0:64, :, tsl], in_=T1v[0:64, 0:2, :]), r=[b_PB[5]], w=[b_KT[i]])
                act(lambda: nc.scalar.copy(out=QT[j][0:64, 8:12, :], in_=T1v[0:64, 2:6, :]), r=[b_PB[5]], w=[b_QT[j]])
                act(lambda: nc.scalar.copy(out=KIT[0:64, tsl], in_=T1v[0:64, 6, :]), r=[b_PB[5]], w=[b_KT[i]])


            def stage2(i):
                j, tsl, L, masked = hdr_(i)
                if not masked:
                    return

                nch = (L + 511) // 512
                for h in range(4):
                    q = h % 2
                    for c in range(nch):
                        c0 = c * 512
                        n = min(512, L - c0)
                        bk = ctr["ix"] % 2
                        ctr["ix"] += 1
                        pe(lambda: nc.tensor.matmul(PB[bk][:, 0:n], lhsT=QT[j][:, 8 + h, :], rhs=KIT[:, c0:c0 + n],
                                                    start=True, stop=True),
                           r=[b_QT[j]] + b_KT[0:i + 1], w=[b_PB[bk]])
                        act(lambda: nc.scalar.activation(out=rl[q][:, c0:c0 + n], in_=PB[bk][:, 0:n], func=AF.Relu),
                            r=[b_PB[bk]], w=[b_rl[q]])
                    if h == 0:
                        dve(lambda: nc.vector.tensor_scalar(out=sc[j][:, 0:L], in0=rl[q][:, 0:L],
                                                            scalar1=wv[:, i, 0:1], scalar2=None, op0=ALU.mult),
                            r=[b_rl[q], b_wv[i]], w=[b_sc[j]])
                    else:
                        dve(lambda: nc.vector.scalar_tensor_tensor(out=sc[j][:, 0:L], in0=rl[q][:, 0:L],
                                                                   scalar=wv[:, i, h:h + 1], in1=sc[j][:, 0:L],
                                                                   op0=ALU.mult, op1=ALU.add),
                            r=[b_rl[q], b_wv[i]], w=[b_sc[j]])


            def bisect(i):
                j, tsl, L, masked = hdr_(i)
                if not masked:
                    return
                yield

                dve(lambda: nc.vector.tensor_reduce(out=Bv[:], in_=sc[j][:, 0:L], axis=AX.X, op=ALU.max,
                                                    apply_absolute_value=True),
                    r=[b_sc[j]], w=[b_bis])
                dve(lambda: nc.vector.tensor_tensor(out=sc[j][:, L - 128:L], in0=sc[j][:, L - 128:L],
                                                    in1=cbias[:], op=ALU.add),
                    r=[b_const], w=[b_sc[j]])
                dve(lambda: nc.vector.tensor_scalar(out=Bk[:], in0=pow2[:], scalar1=Bv[:, 0:1], scalar2=None,
                                                    op0=ALU.mult), r=[b_const], w=[b_bis])
                dve(lambda: nc.vector.memset(mid[0][:], 0.0), w=[b_bis])
                for it in range(NITER):
                    ma, mb = mid[it % 2], mid[(it + 1) % 2]
                    dve(lambda: nc.vector.tensor_scalar(out=junkb[:, 0:L], in0=sc[j][:, 0:L], scalar1=ma[:, 0:1],
                                                        scalar2=None, op0=ALU.is_ge, op1=ALU.add,
                                                        accum_out=cnt[:, 0:1]),
                        r=[b_sc[j]], w=[b_bis, b_junkb])
                    dve(lambda: nc.vector.tensor_scalar(out=dd[:], in0=cnt[:], scalar1=255.5,
                                                        scalar2=Bk[:, it:it + 1], op0=ALU.is_ge, op1=ALU.mult),
                        w=[b_bis])
                    dve(lambda: nc.vector.tensor_scalar(out=mb[:], in0=dd[:], scalar1=Bk[:, it + 1:it + 2],
                                                        scalar2=ma[:, 0:1], op0=ALU.subtract, op1=ALU.add),
                        w=[b_bis])
                    yield
                mfin = mid[NITER % 2]
                dve(lambda: nc.vector.tensor_tensor(out=thr[:, i:i + 1], in0=mfin[:], in1=Bk[:, NITER:NITER + 1],
                                                    op=ALU.subtract), w=[b_bis])
                dve(lambda: nc.vector.tensor_scalar(out=m01[:, 0:L], in0=sc[j][:, 0:L], scalar1=thr[:, i:i + 1],
                                                    scalar2=None, op0=ALU.is_ge),
                    r=[b_sc[j], b_bis], w=[b_m01])
                if ("sc%d" % i) in dbg:
                    k.dump("d_sc", sc[j][:, 0:L], b_sc[j])
                    k.dump("d_thr", thr[:, i:i + 1], b_bis)


            def masktr(i):
                j, tsl, L, masked = hdr_(i)
                if not masked:
                    return

                for jb in range(i + 1):
                    tv, sl, bk = (T0v, jb, 4) if jb < 8 else (T1v, jb - 8, 5)
                    last = (jb == i) or (jb == 7)
                    pe(lambda: nc.tensor.transpose(out=tv[:, sl, :], in_=m01[:, jb * 128:(jb + 1) * 128],
                                                   identity=ident[:]),
                       r=[b_m01, b_const], w=[b_PB[bk]], inc=last)
                n0 = min(i + 1, 8)
                act(lambda: nc.scalar.activation(out=maskT[j][:, 0:n0, :], in_=T0v[:, 0:n0, :], func=AF.Identity,
                                                 scale=30000.0, bias=-30000.0),
                    r=[b_PB[4]], w=[b_maskT[j]])
                if i + 1 > 8:
                    act(lambda: nc.scalar.activation(out=maskT[j][:, 8:i + 1, :], in_=T1v[:, 0:i + 1 - 8, :],
                                                     func=AF.Identity, scale=30000.0, bias=-30000.0),
                        r=[b_PB[5]], w=[b_maskT[j]])


            def attn_main(i):
                j, tsl, L, masked = hdr_(i)
                nkt = i + 1
                for g in range(2):
                    Ov = PB[6 + g][:, 0:260].rearrange("p (h d) -> p h d", d=65)

                    def st_mm(jb):
                        bk = 2 + (ctr["st"] % 2)
                        ctr["st"] += 1
                        has_mask = masked or jb == i
                        pe(lambda: nc.tensor.matmul(PB[bk][:].rearrange("p (h t) -> p h t", t=128),
                                                    lhsT=KT[:, g, jb * 128:(jb + 1) * 128],
                                                    rhs=QT[j][:, 4 * g:4 * g + 4, :], start=True, stop=(not has_mask)),
                           r=[b_QT[j], b_KT[jb]], w=[b_PB[bk]], inc=(not has_mask))
                        if has_mask:
                            if masked:
                                mb_ap = maskT[j][:, jb:jb + 1, :].to_broadcast([128, 4, 128])
                                rd = [b_maskT[j], b_const]
                            else:
                                mb_ap = negU[:].unsqueeze(1).to_broadcast([128, 4, 128])
                                rd = [b_const]
                            pe(lambda: nc.tensor.matmul(PB[bk][:].rearrange("p (h t) -> p h t", t=128),
                                                        lhsT=ident[:], rhs=mb_ap, start=False, stop=True),
                               r=rd, w=[b_PB[bk]])
                        return bk

                    def exp_pv(jb, bk):
                        e = ctr["e"] % 3
                        ctr["e"] += 1
                        act(lambda: nc.scalar.activation(out=Eb[e][:], in_=PB[bk][:], func=AF.Exp, scale=0.125),
                            r=[b_PB[bk]], w=[b_Eb[e]])
                        for hh in range(4):
                            pe(lambda: nc.tensor.matmul(Ov[:, hh, :], lhsT=Eb[e][:, hh * 128:(hh + 1) * 128],
                                                        rhs=Vaug[:, jb, g, :], start=(jb == 0 and hh == 0),
                                                        stop=(jb == i and hh == 3)),
                               r=[b_Eb[e], b_V[jb]], w=[b_PB[6 + g]], inc=(hh == 3))

                    bks = {0: st_mm(0)}
                    for jb in range(nkt):
                        if jb + 1 < nkt:
                            bks[jb + 1] = st_mm(jb + 1)
                        exp_pv(jb, bks[jb])

            def attn_fin(i):
                j, tsl, L, masked = hdr_(i)
                for g in range(2):
                    Ov = PB[6 + g][:, 0:260].rearrange("p (h d) -> p h d", d=65)
                    dve(lambda: nc.vector.reciprocal(out=rinv8[:, 4 * g:4 * g + 4].unsqueeze(2), in_=Ov[:, :, 64:65]),
                        r=[b_PB[6 + g]], w=[b_otmp])
                    dve(lambda: nc.vector.tensor_tensor(out=otmp8[:, 4 * g:4 * g + 4, :], in0=Ov[:, :, 0:64],
                                                        in1=rinv8[:, 4 * g:4 * g + 4].unsqueeze(2)
                                                        .to_broadcast([128, 4, 64]), op=ALU.mult),
                        r=[b_PB[6 + g]], w=[b_otmp])
                dve(lambda: nc.vector.tensor_tensor(out=oa_sb[:], in0=otmp8[:].rearrange("p h d -> p (h d)"),
                                                    in1=sza[j][:], op=ALU.mult),
                    r=[b_otmp, b_sza[j]], w=[b_oa])


            def fin_tr(i):
                j, tsl, L, masked = hdr_(i)
                for c in range(4):
                    pe(lambda: nc.tensor.transpose(out=T0v[:, c, :], in_=oa_sb[:, c * 128:(c + 1) * 128],
                                                   identity=ident[:]),
                       r=[b_oa, b_const], w=[b_PB[4]], inc=(c == 3))
                act(lambda: nc.scalar.copy(out=oaT[:, :, tsl], in_=T0v[:, 0:4, :]), r=[b_PB[4]], w=[b_oaT[i]])


            nA = NT if "skipA" not in dbg else 0
            pend = None
            for i in range(nA + 2):
                if i < nA:
                    stage1(i)
                if pend is not None:
                    for _ in pend:
                        pass
                    pend = None
                if i < nA:
                    stage2(i)
                if 1 <= i <= nA:
                    masktr(i - 1)
                if 2 <= i <= nA + 1:
                    fin_tr(i - 2)
                if 1 <= i <= nA:
                    attn_main(i - 1)
                if i < nA:
                    g_ = bisect(i)
                    nsteps = (NITER - 3) if (i + 1 < nA) else 10 ** 9
                    done_ = False
                    for _s in range(nsteps):
                        try:
                            next(g_)
                        except StopIteration:
                            done_ = True
                            break
                    if not done_:
                        pend = g_
                if 1 <= i <= nA:
                    attn_fin(i - 1)
            i = nA - 1
            j = i % 2
            L = (i + 1) * 128

            if "qk" in dbg:
                k.dump("d_KT", KT[:, :, 0:L], b_KT)
                k.dump("d_KIT", KIT[:, 0:L], b_KT)
                k.dump("d_QT", QT[j][:], b_QT[j])
                k.dump("d_V", Vaug[:, 0:i + 1], b_V)
            if "oaT" in dbg:
                k.dump("d_oaT", oaT[:, :, 0:L], b_oaT)
            k.barrier()
        esWA.close()
        if stop_after.startswith("A"):
            return nc, k


        obT = k.sb("obT", [128, 8, S], BF16)
        b_obT = [Buf() for _ in range(NT)]
        with ExitStack() as esB:
            wdt = k.sb("wdt", [128, 8, 16], BF16, esB)
            b_wdt = Buf()
            dpool(out=wdt[:], in_=win_d[:, 4164:4180].rearrange("(k p) n -> p k n", p=128), w=[b_wdt])
            X_tm = k.sb("X_tm", [128, NT, 1024], BF16, esB)
            B_tm = k.sb("B_tm", [128, NT, 256], BF16, esB)
            BT = k.sb("BT", [128, 2, S], BF16, esB)
            CT = k.sb("CT", [128, 2, S], BF16, esB)
            b_X = Buf()
            dtb_b = k.sb("dtb_b", [128, 16], F32, esB)
            a_b = k.sb("a_b", [128, 16], F32, esB)
            dsk_b = k.sb("dsk_b", [128, 16], F32, esB)
            snw_b = k.sb("snw_b", [128, D], F32, esB)
            dt_all = k.sb("dt_all", [128, NT, 16], F32, esB)
            dA_all = k.sb("dA_all", [128, NT, 16], F32, esB)
            spt = [k.sb(f"spt{j}", [128, NT, 16], F32, esB) for j in range(3)]
            ones_f = k.sb("ones_f", [128, 128], F32, esB)
            NEGU4 = k.sb("NEGU4", [128, 4, 128], BF16, esB)
            Dg = k.sb("Dg", [128, 16, 128], BF16, esB)
            b_ptab = Buf()
            dsync(out=dtb_b[:], in_=dtb_d[0, :].partition_broadcast(128), w=[b_ptab])
            dsync(out=a_b[:], in_=alog_d[0, :].partition_broadcast(128), w=[b_ptab])
            dsync(out=dsk_b[:], in_=dsk_d[0, :].partition_broadcast(128), w=[b_ptab])
            dsync(out=snw_b[:], in_=snw_d[0, :].partition_broadcast(128), w=[b_ptab])
            pool(lambda: nc.gpsimd.memset(ones_f[:], 1.0), w=[b_ptab])
            pool(lambda: nc.gpsimd.memset(NEGU4[:], 0.0), w=[b_ptab])
            pool(lambda: nc.gpsimd.affine_select(out=NEGU4[:], in_=NEGU4[:], pattern=[[0, 4], [1, 128]],
                                                 compare_op=ALU.is_ge, fill=-1.0e4, base=0, channel_multiplier=-1),
                 w=[b_ptab])
            act(lambda: nc.scalar.activation(out=a_b[:], in_=a_b[:], func=AF.Exp), w=[b_ptab])
            dve(lambda: nc.vector.tensor_scalar(out=a_b[:], in0=a_b[:], scalar1=-1.0, scalar2=None, op0=ALU.mult),
                w=[b_ptab])
            dve(lambda: nc.vector.tensor_tensor(out=Dg[:], in0=ident[:].unsqueeze(1).to_broadcast([128, 16, 128]),
                                                in1=dsk_b[:].unsqueeze(2).to_broadcast([128, 16, 128]), op=ALU.mult),
                r=[b_const], w=[b_ptab])
            ck("Btab")

            for i in range(NT):
                for kc in range(8):
                    pe(lambda: nc.tensor.matmul(PB[0][:, i * 16:(i + 1) * 16], lhsT=hT[:, kc, i * 128:(i + 1) * 128],
                                                rhs=wdt[:, kc, :], start=(kc == 0), stop=(kc == 7)),
                       r=[b_hT[i], b_wdt], w=[b_PB[0]], inc=(kc == 7))
            dve(lambda: nc.vector.tensor_tensor(out=spt[0][:], in0=PB[0][:, 0:256].rearrange("p (i h) -> p i h", h=16),
                                                in1=dtb_b[:].unsqueeze(1).to_broadcast([128, NT, 16]), op=ALU.add),
                r=[b_PB[0]], w=[b_ptab])
            dve(lambda: nc.vector.tensor_scalar(out=spt[2][:], in0=spt[0][:], scalar1=-1.0, scalar2=None, op0=ALU.mult),
                w=[b_ptab])
            dve(lambda: nc.vector.tensor_tensor(out=spt[1][:], in0=spt[0][:], in1=spt[2][:], op=ALU.max), w=[b_ptab])
            act(lambda: nc.scalar.activation(out=spt[1][:], in_=spt[1][:], func=AF.Exp, scale=-1.0), w=[b_ptab])
            act(lambda: nc.scalar.activation(out=spt[1][:], in_=spt[1][:], func=AF.Ln, bias=1.0), w=[b_ptab])
            dve(lambda: nc.vector.tensor_scalar(out=spt[2][:], in0=spt[0][:], scalar1=0.0, scalar2=None, op0=ALU.max),
                w=[b_ptab])
            dve(lambda: nc.vector.tensor_tensor(out=dt_all[:], in0=spt[2][:], in1=spt[1][:], op=ALU.add), w=[b_ptab])
            dve(lambda: nc.vector.tensor_tensor(out=dA_all[:], in0=dt_all[:],
                                                in1=a_b[:].unsqueeze(1).to_broadcast([128, NT, 16]), op=ALU.mult),
                w=[b_ptab])
            if "dt" in dbg:
                k.dump("d_dt", dt_all[:], b_ptab)
            ck("Bdt")

            with ExitStack() as esC:
                cwT = k.sb("cwT_sb", [128, 12, 4], F32, esC)
                cbT = k.sb("cbT_sb", [128, 12], F32, esC)
                b_cw = Buf()
                dsync(out=cwT[:], in_=cwT_d[:, :, :], w=[b_cw])
                dsync(out=cbT[:], in_=cbT_d[:, :], w=[b_cw])
                pre = [k.sb(f"pre{j}", [128, S + 3], F32, esC) for j in range(2)]
                b_pre = [Buf(), Buf()]
                accs = [k.sb(f"acc{j}", [128, S], F32, esC) for j in range(2)]
                b_accs = [Buf(), Buf()]
                xs_fm = k.sb("xs_fm", [128, S], BF16, esC)
                b_xs = Buf()
                for q in range(2):
                    pool(lambda: nc.gpsimd.memset(pre[q][:, 0:3], 0.0), w=[b_pre[q]])
                b_xs2 = [Buf(), Buf()]
                b_cv = [Buf() for _ in range(12)]

                def cproj(m):
                    q = m % 2
                    slot = (m // 4) % 2
                    wc0 = slot * 512 + (m % 4) * 128
                    for tc in range(4):
                        for kc in range(8):
                            pe(lambda: nc.tensor.matmul(PB[tc][:], lhsT=wBC[:, kc, wc0:wc0 + 128],
                                                        rhs=hT[:, kc, tc * 512:(tc + 1) * 512],
                                                        start=(kc == 0), stop=(kc == 7)),
                               r=b_hT[tc * 4:(tc + 1) * 4] + [b_wslot[slot]], w=[b_PB[tc]], inc=(kc == 7))
                        act(lambda: nc.scalar.copy(out=pre[q][:, 3 + tc * 512:3 + (tc + 1) * 512], in_=PB[tc][:]),
                            r=[b_PB[tc]], w=[b_pre[q]])

                def cpost(m):
                    q = m % 2
                    acc = accs[q]
                    b_acc = b_accs[q]
                    dve(lambda: nc.vector.tensor_scalar(out=acc[:], in0=pre[q][:, 0:S], scalar1=cwT[:, m, 0:1],
                                                        scalar2=None, op0=ALU.mult),
                        r=[b_pre[q], b_cw], w=[b_acc])
                    for kk in range(1, 4):
                        dve(lambda: nc.vector.scalar_tensor_tensor(out=acc[:], in0=pre[q][:, kk:kk + S],
                                                                   scalar=cwT[:, m, kk:kk + 1], in1=acc[:],
                                                                   op0=ALU.mult, op1=ALU.add),
                            r=[b_pre[q], b_cw], w=[b_acc])
                    if m < 10:
                        dst = xs_fm[:] if m < 8 else BT[:, m - 8, :]
                        bdst = b_xs if m < 8 else b_cv[m]
                        act(lambda: nc.scalar.activation(out=dst, in_=acc[:], func=AF.Silu, bias=cbT[:, m:m + 1]),
                            r=[b_acc, b_cw], w=[bdst])
                        for half in range(2):
                            bk = 4 + half
                            tv = bfv(bk)
                            for s8 in range(8):
                                ti_ = half * 8 + s8
                                in_ap = (xs_fm[:, ti_ * 128:(ti_ + 1) * 128] if m < 8
                                         else BT[:, m - 8, ti_ * 128:(ti_ + 1) * 128])
                                pe(lambda: nc.tensor.transpose(out=tv[:, s8, :], in_=in_ap, identity=ident[:]),
                                   r=[bdst, b_const], w=[b_PB[bk]], inc=(s8 == 7))
                            if m < 8:
                                act(lambda: nc.scalar.copy(out=X_tm[:, half * 8:(half + 1) * 8, m * 128:(m + 1) * 128],
                                                           in_=tv[:, :, :]), r=[b_PB[bk]], w=[b_cv[m]])
                            else:
                                act(lambda: nc.scalar.copy(out=B_tm[:, half * 8:(half + 1) * 8,
                                                                    (m - 8) * 128:(m - 7) * 128],
                                                           in_=tv[:, :, :]), r=[b_PB[bk]], w=[b_cv[m]])
                    else:
                        act(lambda: nc.scalar.activation(out=CT[:, m - 10, :], in_=acc[:], func=AF.Silu,
                                                         bias=cbT[:, m:m + 1]),
                            r=[b_acc, b_cw], w=[b_cv[m]])

                def wreload(m_done):
                    if m_done == 3:
                        dpool(out=wBC[:, :, 0:512], in_=win_d[:, CONV0 + 1024:CONV0 + 1536]
                              .rearrange("(k p) n -> p k n", p=128), w=[b_wslot[0]])
                    elif m_done == 7:
                        dpool(out=wBC[:, :, 512:1024], in_=win_d[:, 1604:2116]
                              .rearrange("(k p) n -> p k n", p=128), w=[b_wslot[1]])
                    elif m_done == 11:
                        dpool(out=wBC[:, :, 0:512], in_=win_d[:, 2116:2628]
                              .rearrange("(k p) n -> p k n", p=128), w=[b_wslot[0]])

                cproj(0)
                wreload(0)
                for m in range(12):
                    if m + 1 < 12:
                        cproj(m + 1)
                        wreload(m + 1)
                    cpost(m)
                    ck(f"Bconv{m}")
                b_X.w = {}
                for bb in b_cv:
                    _merge(b_X.w, bb.w)
                if "conv" in dbg:
                    k.dump("d_Xtm", X_tm[:], b_X)
                    k.dump("d_Btm", B_tm[:], b_X)
                    k.dump("d_BT", BT[:], b_X)
                    k.dump("d_CT", CT[:], b_X)
                ck("Bconv")
                k.barrier()

            with ExitStack() as esS:
                ones_b = k.sb("ones_b", [128, 128], BF16, esS)
                dAhl = k.sb("dAhl", [128, NT, 2, 16], BF16, esS)
                dAres = spt[0]
                b_hl = Buf()
                pool(lambda: nc.gpsimd.memset(ones_b[:], 1.0), w=[b_hl])
                dve(lambda: nc.vector.tensor_copy(out=dAhl[:, :, 0, :], in_=dA_all[:]), r=[b_ptab], w=[b_hl])
                dve(lambda: nc.vector.tensor_tensor(out=dAres[:], in0=dA_all[:], in1=dAhl[:, :, 0, :], op=ALU.subtract),
                    r=[b_ptab], w=[b_hl])
                dve(lambda: nc.vector.tensor_copy(out=dAhl[:, :, 1, :], in_=dAres[:]), w=[b_hl])
                szb = [k.sb(f"szb{j}", [128, 1024], BF16, esS) for j in range(2)]
                b_szb = [Buf(), Buf()]
                smalls = [k.sb(f"small{j}", [128, 32], F32, esS) for j in range(2)]
                nacums = [k.sb(f"nacum{j}", [128, 16], F32, esS) for j in range(2)]
                eas = [k.sb(f"ea{j}", [128, 16], F32, esS) for j in range(2)]
                decs = [k.sb(f"dec{j}", [128, 16], F32, esS) for j in range(2)]
                dtds = [k.sb(f"dtd{j}", [128, 16], F32, esS) for j in range(2)]
                eASs = [k.sb(f"eAS{j}", [128, 2, 4], F32, esS) for j in range(2)]
                b_sms = [Buf(), Buf()]
                LTg = [k.sb(f"LTg{j}", [128, 4, 128], F32, esS) for j in range(2)]
                b_LT = [Buf(), Buf()]
                MTg = [k.sb(f"MTg{j}", [128, 4, 128], BF16, esS) for j in range(2)]
                b_MT = [Buf(), Buf()]
                CBs = [k.sb("CBs0", [128, 4, 128], F32, esS)] * 2
                b_CBs = [Buf()] * 2
                xds = [k.sb(f"xd{j}", [128, 16, 64], BF16, esS) for j in range(2)]
                xdds = [k.sb(f"xdd{j}", [128, 16, 64], BF16, esS) for j in range(2)]
                b_xds = [Buf(), Buf()]
                b_xdds = [Buf(), Buf()]
                ysb = k.sb("ysb", [128, 16, 64], F32, esS)
                b_y = Buf()
                ssq = k.sb("ssq", [128, 4], F32, esS)
                rs4 = k.sb("rs4", [128, 4], F32, esS)
                junkf = k.sb("junkf", [128, 256], F32, esS)
                ob_sb = k.sb("ob_sb", [128, 1024], BF16, esS)
                b_ob = Buf()
                S_sb = k.sb("S_sb", [128, 2, 256], F32, esS)
                S_bf = k.sb("S_bf", [128, 2, 256], BF16, esS)
                b_S = Buf()
                b_Sbf = Buf()
                gctr = {"g": 0, "d": 0}
                Yv = [PB[4][:].rearrange("p (h d) -> p h d", d=64), PB[5][:].rearrange("p (h d) -> p h d", d=64)]

                def head(c):
                    q = c % 2
                    csl = slice(c * 128, (c + 1) * 128)
                    small, nacum, ea, dec, dtd, eAS, b_sm = smalls[q], nacums[q], eas[q], decs[q], dtds[q], eASs[q], b_sms[q]
                    pe(lambda: nc.tensor.matmul(PB[2][:, 0:16], lhsT=Uf[:], rhs=dA_all[:, c, :], start=True, stop=False),
                       r=[b_const, b_ptab], w=[b_PB[2]], inc=False)
                    pe(lambda: nc.tensor.matmul(PB[2][:, 16:32], lhsT=ones_f[:], rhs=dA_all[:, c, :], start=False,
                                                stop=True), r=[b_ptab], w=[b_PB[2]])
                    CBv = PB[3][:].rearrange("p (g l) -> p g l", l=128)
                    tk = None
                    for gi, g in enumerate((0, 2, 1, 3)):
                        p0 = (g % 2) * 64
                        tk2 = pe(lambda: nc.tensor.matmul(CBv[:, g, :], lhsT=BT[p0:p0 + 64, g // 2, csl],
                                                          rhs=CT[p0:p0 + 64, g // 2, csl], start=(gi == 0),
                                                          stop=(gi == 3)),
                                 r=[b_X], w=[b_PB[3]], inc=(gi == 1 or gi == 3), selfwait=(tk if gi == 2 else None))
                        if gi == 1:
                            tk = tk2
                    for hb in range(2):
                        for kc in range(8):
                            pe(lambda: nc.tensor.matmul(PB[hb][:], lhsT=hT[:, kc, csl],
                                                        rhs=wBC[:, kc, (1 - hb) * 512:(2 - hb) * 512],
                                                        start=(kc == 0), stop=(kc == 7)),
                               r=[b_hT[c], b_wslot[1 - hb]], w=[b_PB[hb]], inc=(kc == 7))
                    yield
                    dve(lambda: nc.vector.tensor_copy(out=small[:], in_=PB[2][:, 0:32]), r=[b_PB[2]], w=[b_sm])
                    acum = small[:, 0:16]
                    atot = small[:, 16:32]
                    dve(lambda: nc.vector.tensor_tensor(out=dec[:], in0=atot, in1=acum, op=ALU.subtract), w=[b_sm])
                    act(lambda: nc.scalar.activation(out=ea[:], in_=acum, func=AF.Exp), w=[b_sm])
                    act(lambda: nc.scalar.activation(out=dec[:], in_=dec[:], func=AF.Exp), w=[b_sm])
                    atv = small[:, 16:32].rearrange("p (s f h) -> p s f h", s=2, f=2)
                    act(lambda: nc.scalar.activation(out=eAS[0:64], in_=atv[0:64, :, 0, :], func=AF.Exp), w=[b_sm])
                    act(lambda: nc.scalar.activation(out=eAS[64:128], in_=atv[64:128, :, 1, :], func=AF.Exp), w=[b_sm])
                    act(lambda: nc.scalar.copy(out=CBs[q][:], in_=CBv), r=[b_PB[3]], w=[b_CBs[q]])
                    for hb in range(2):
                        act(lambda: nc.scalar.activation(out=szb[q][:, hb * 512:(hb + 1) * 512], in_=PB[hb][:],
                                                         func=AF.Silu), r=[b_PB[hb]], w=[b_szb[q]])
                    dve(lambda: nc.vector.tensor_tensor(out=dtd[:], in0=dt_all[:, c, :], in1=dec[:], op=ALU.mult),
                        r=[b_ptab], w=[b_sm])
                    Xc = X_tm[:, c, :].rearrange("p (h d) -> p h d", d=64)
                    dve(lambda: nc.vector.tensor_tensor(out=xds[q][:], in0=Xc,
                                                        in1=dt_all[:, c, :].unsqueeze(2).to_broadcast([128, 16, 64]),
                                                        op=ALU.mult), r=[b_X, b_ptab], w=[b_xds[q]])
                    dve(lambda: nc.vector.tensor_tensor(out=xdds[q][:], in0=Xc,
                                                        in1=dtd[:].unsqueeze(2).to_broadcast([128, 16, 64]),
                                                        op=ALU.mult), r=[b_X, b_sm], w=[b_xdds[q]])

                    yield

                def groups(c):
                    q = c % 2
                    csl = slice(c * 128, (c + 1) * 128)
                    nacum, b_sm = nacums[q], b_sms[q]
                    xd = xds[q]
                    st = {}

                    def acumb(g):
                        gq = gctr["g"] % 2
                        gctr["g"] += 1
                        abk = 2 if gq == 0 else 6
                        first = True
                        for hh in range(4):
                            hd = 4 * g + hh
                            for part in range(2):
                                pe(lambda: nc.tensor.matmul(PB[abk][:, hh * 128:(hh + 1) * 128],
                                                            lhsT=dAhl[:, c, part, hd:hd + 1].to_broadcast([128, 128]),
                                                            rhs=Ub[:], start=first, stop=False),
                                   r=[b_hl, b_const], w=[b_PB[abk]], inc=False)
                                first = False
                        for part in range(2):
                            pe(lambda: nc.tensor.matmul(
                                PB[abk][:].rearrange("p (h l) -> p h l", l=128), lhsT=negUb[:],
                                rhs=dAhl[:, c, part, 4 * g:4 * g + 4].unsqueeze(2).to_broadcast([128, 4, 128]),
                                start=False, stop=False), r=[b_hl, b_const], w=[b_PB[abk]], inc=False)
                        pe(lambda: nc.tensor.matmul(PB[abk][:], lhsT=ident[:],
                                                    rhs=NEGU4[:].rearrange("p h l -> p (h l)"), start=False, stop=True),
                           r=[b_const, b_ptab], w=[b_PB[abk]])
                        st[g] = (gq, abk)

                    def ymm(g):
                        gq, abk = st[g]
                        act(lambda: nc.scalar.activation(out=LTg[gq][:].rearrange("p h l -> p (h l)"), in_=PB[abk][:],
                                                         func=AF.Exp), r=[b_PB[abk]], w=[b_LT[gq]])
                        dve(lambda: nc.vector.tensor_tensor(out=MTg[gq][:], in0=LTg[gq][:],
                                                            in1=CBs[q][:, g:g + 1, :].to_broadcast([128, 4, 128]),
                                                            op=ALU.mult),
                            r=[b_LT[gq], b_CBs[q]], w=[b_MT[gq]])
                        for hh in range(4):
                            hd = 4 * g + hh
                            yb = 4 + hd // 8
                            pe(lambda: nc.tensor.matmul(Yv[hd // 8][:, hd % 8, :], lhsT=MTg[gq][:, hh, :],
                                                        rhs=xd[:, hd, :], start=(hd % 8 == 0), stop=False),
                               r=[b_MT[gq], b_xds[q]], w=[b_PB[yb]], inc=False)
                            pe(lambda: nc.tensor.matmul(Yv[hd // 8][:, hd % 8, :], lhsT=Dg[:, hd, :],
                                                        rhs=X_tm[:, c, hd * 64:(hd + 1) * 64], start=False,
                                                        stop=(hd % 8 == 7)),
                               r=[b_ptab, b_X], w=[b_PB[yb]], inc=(hh == 3))

                    acumb(0)
                    acumb(1)
                    yield
                    ymm(0)
                    yield
                    acumb(2)
                    ymm(1)
                    yield
                    acumb(3)
                    ymm(2)
                    yield
                    ymm(3)
                    yield

                def tail(c):
                    q = c % 2
                    csl = slice(c * 128, (c + 1) * 128)
                    ea, eAS, b_sm = eas[q], eASs[q], b_sms[q]
                    xdd = xdds[q]
                    if c > 0:
                        tk = None
                        for gi, g in enumerate((0, 2, 1, 3)):
                            p0 = (g % 2) * 64
                            ob_ = 6 + g // 2
                            tk2 = pe(lambda: nc.tensor.matmul(PB[ob_][:, (g % 2) * 256:(g % 2 + 1) * 256],
                                                              lhsT=CT[p0:p0 + 64, g // 2, csl],
                                                              rhs=S_bf[p0:p0 + 64, g // 2, :], start=(g % 2 == 0),
                                                              stop=(g % 2 == 1)),
                                     r=[b_X, b_Sbf], w=[b_PB[ob_]], inc=(gi >= 1),
                                     selfwait=(tk if gi == 2 else None))
                            if gi == 1:
                                tk = tk2
                        for hb in range(2):
                            dve(lambda: nc.vector.tensor_tensor(
                                out=ysb[:, hb * 8:(hb + 1) * 8, :],
                                in0=PB[6 + hb][:].rearrange("p (h d) -> p h d", d=64),
                                in1=ea[:, hb * 8:(hb + 1) * 8].unsqueeze(2).to_broadcast([128, 8, 64]), op=ALU.mult),
                                r=[b_PB[6 + hb], b_sm], w=[b_y])
                            dve(lambda: nc.vector.tensor_tensor(out=ysb[:, hb * 8:(hb + 1) * 8, :], in0=Yv[hb],
                                                                in1=ysb[:, hb * 8:(hb + 1) * 8, :], op=ALU.add),
                                r=[b_PB[4 + hb]], w=[b_y])
                    else:
                        for hb in range(2):
                            dve(lambda: nc.vector.tensor_copy(out=ysb[:, hb * 8:(hb + 1) * 8, :], in_=Yv[hb]),
                                r=[b_PB[4 + hb]], w=[b_y])
                    yield
                    dve(lambda: nc.vector.tensor_tensor(out=ysb[:].rearrange("p h d -> p (h d)"),
                                                        in0=ysb[:].rearrange("p h d -> p (h d)"), in1=szb[q][:],
                                                        op=ALU.mult), r=[b_szb[q]], w=[b_y])
                    yf = ysb[:].rearrange("p h d -> p (h d)")
                    for g in range(4):
                        act(lambda: nc.scalar.activation(out=junkf[:], in_=yf[:, g * 256:(g + 1) * 256], func=AF.Square,
                                                         accum_out=ssq[:, g:g + 1]), r=[b_y], w=[b_ob])
                    pool(lambda: nc.gpsimd.tensor_scalar(out=rs4[:], in0=ssq[:], scalar1=1.0 / 256, scalar2=EPS,
                                                         op0=ALU.mult, op1=ALU.add), w=[b_ob])
                    pool(lambda: nc.gpsimd.tensor_tensor(out=rs4[:], in0=rs4[:], in1=mhalf[:, 0:4], op=ALU.pow),
                         r=[b_const], w=[b_ob])
                    yield
                    if c < NT - 1:
                        for g in range(4):
                            p0 = (g % 2) * 64
                            pe(lambda: nc.tensor.matmul(PB[7][p0:p0 + 64, (g // 2) * 256:(g // 2 + 1) * 256],
                                                        lhsT=B_tm[:, c, g * 64:(g + 1) * 64],
                                                        rhs=xdd[:, 4 * g:4 * g + 4, :].rearrange("p h d -> p (h d)"),
                                                        start=(g < 2), stop=(g >= 2)),
                               r=[b_X, b_xdds[q]], w=[b_PB[7]], inc=(g == 3))
                        Sv = S_sb[:].rearrange("p s (h d) -> p (s h) d", d=64)
                        if c == 0:
                            dve(lambda: nc.vector.tensor_copy(out=S_sb[:].rearrange("p s f -> p (s f)"), in_=PB[7][:]),
                                r=[b_PB[7]], w=[b_S])
                        else:
                            dve(lambda: nc.vector.tensor_tensor(
                                out=Sv, in0=Sv,
                                in1=eAS[:].rearrange("p s h -> p (s h)").unsqueeze(2).to_broadcast([128, 8, 64]),
                                op=ALU.mult), r=[b_sm, b_Sbf], w=[b_S])
                            dve(lambda: nc.vector.tensor_tensor(out=S_sb[:].rearrange("p s f -> p (s f)"),
                                                                in0=S_sb[:].rearrange("p s f -> p (s f)"),
                                                                in1=PB[7][:], op=ALU.add),
                                r=[b_PB[7]], w=[b_S])
                        act(lambda: nc.scalar.copy(out=S_bf[:], in_=S_sb[:]), r=[b_S], w=[b_Sbf])
                    yield
                    for g in range(4):
                        dve(lambda: nc.vector.scalar_tensor_tensor(out=ob_sb[:, g * 256:(g + 1) * 256],
                                                                   in0=yf[:, g * 256:(g + 1) * 256],
                                                                   scalar=rs4[:, g:g + 1],
                                                                   in1=snw_b[:, g * 256:(g + 1) * 256],
                                                                   op0=ALU.mult, op1=ALU.mult),
                            r=[b_y, b_ptab], w=[b_ob])
                    yield
                    tv = bfv(7)
                    for cc in range(8):
                        pe(lambda: nc.tensor.transpose(out=tv[:, cc, :], in_=ob_sb[:, cc * 128:(cc + 1) * 128],
                                                       identity=ident[:]),
                           r=[b_ob, b_const], w=[b_PB[7]], inc=(cc == 7))
                    act(lambda: nc.scalar.copy(out=obT[:, :, csl], in_=tv[:, :, :]), r=[b_PB[7]], w=[b_obT[c]])

                def front(c):
                    yield from head(c)
                    yield from groups(c)

                def interleave(gens):
                    gens = list(gens)
                    while gens:
                        for g_ in list(gens):
                            try:
                                next(g_)
                            except StopIteration:
                                gens.remove(g_)

                interleave([front(0)])
                for c in range(NT):
                    gl = [tail(c)]
                    if c + 1 < NT:
                        gl.append(front(c + 1))
                    interleave(gl)
                    if c == NT - 2:
                        dpool(out=wG0[:, :, 0, :], in_=win_d[:, 4180:4180 + 128].rearrange("(k p) n -> p k n", p=128),
                              w=[b_wslot[0]])
                        dpool(out=wG0[:, :, 1, :], in_=win_d[:, 4180 + 1024:4180 + 1152]
                              .rearrange("(k p) n -> p k n", p=128), w=[b_wslot[0]])
                        dpool(out=Wpa0, in_=wpa_d[:, 0:128].rearrange("(k p) n -> p k n", p=128), w=[b_wslot[0]])
                        dpool(out=Wpb0, in_=wpb_d[:, 0:128].rearrange("(k p) n -> p k n", p=128), w=[b_wslot[0]])
                if "obT" in dbg:
                    k.dump("d_obT", obT[:], b_obT)
                k.barrier()
        if stop_after.startswith("B"):
            return nc, k


        with ExitStack() as esM:
            Wout = k.sb("Wout", [128, 8, D], BF16, esM)
            b_W = Buf()
            gbT = k.sb("gbT_sb", [128, 16], F32, esM)
            fnw_b = k.sb("fnw_b", [128, D], F32, esM)
            b_ct = Buf()
            dsync(out=gbT[:], in_=gbT_d[:, :], w=[b_ct])
            dsync(out=fnw_b[:], in_=fnw_d[0, :].partition_broadcast(128), w=[b_ct])
            mT = k.sb("mT", [128, 8, S], BF16, esM)
            b_mT = [Buf() for _ in range(4)]
            with ExitStack() as esM1:
                wG = [k.sb(f"wG{j}", [128, 8, 2, 128], BF16, esM1) for j in range(2)]
                Wpa = [k.sb(f"Wpa{j}", [128, 4, 128], BF16, esM1) for j in range(2)]
                Wpb = [k.sb(f"Wpb{j}", [128, 8, 128], BF16, esM1) for j in range(2)]
                b_wG = [Buf(), Buf()]
                gA = [k.sb(f"gA{j}", [128, 512], F32, esM1) for j in range(2)]
                gB = [k.sb(f"gB{j}", [128, 512], F32, esM1) for j in range(2)]
                b_g = [Buf(), Buf()]
                t1 = [k.sb(f"t1{j}", [128, 512], F32, esM1) for j in range(2)]
                t2 = [k.sb(f"t2{j}", [128, 512], F32, esM1) for j in range(2)]
                b_t = [Buf(), Buf()]
                it = 0
                for m in range(8):
                    wq = m % 2
                    if m == 0:
                        gw_, pa_, pb_, bw_ = wG0, Wpa0, Wpb0, b_wslot[0]
                    else:
                        gw_, pa_, pb_, bw_ = wG[wq][:], Wpa[wq][:], Wpb[wq][:], b_wG[wq]
                        dpool(out=gw_[:, :, 0, :], in_=win_d[:, 4180 + m * 128:4180 + (m + 1) * 128]
                              .rearrange("(k p) n -> p k n", p=128), w=[bw_])
                        dpool(out=gw_[:, :, 1, :], in_=win_d[:, 4180 + (8 + m) * 128:4180 + (9 + m) * 128]
                              .rearrange("(k p) n -> p k n", p=128), w=[bw_])
                        dpool(out=pa_, in_=wpa_d[:, m * 128:(m + 1) * 128].rearrange("(k p) n -> p k n", p=128),
                              w=[bw_])
                        dpool(out=pb_, in_=wpb_d[:, m * 128:(m + 1) * 128].rearrange("(k p) n -> p k n", p=128),
                              w=[bw_])
                    if m == 0:
                        for hf in range(2):
                            cs_ = slice(hf * 512, (hf + 1) * 512)
                            dpool(out=Wout[:, :, cs_], in_=wout_d[:, cs_].rearrange("(k p) n -> p k n", p=128),
                                  w=[b_W])
                    for tc in range(4):
                        q = it % 2
                        it += 1
                        b0 = 4 * q
                        ts_ = slice(tc * 512, (tc + 1) * 512)
                        for kc in range(8):
                            pe(lambda: nc.tensor.matmul(PB[b0][:], lhsT=gw_[:, kc, 0, :], rhs=hT[:, kc, ts_],
                                                        start=(kc == 0), stop=(kc == 7)),
                               r=b_hT[tc * 4:(tc + 1) * 4] + [bw_], w=[b_PB[b0]], inc=(kc == 7))
                        for kc in range(8):
                            pe(lambda: nc.tensor.matmul(PB[b0 + 1][:], lhsT=gw_[:, kc, 1, :], rhs=hT[:, kc, ts_],
                                                        start=(kc == 0), stop=(kc == 7)),
                               r=b_hT[tc * 4:(tc + 1) * 4] + [bw_], w=[b_PB[b0 + 1]], inc=(kc == 7))
                        for kc in range(4):
                            pe(lambda: nc.tensor.matmul(PB[b0 + 2][:], lhsT=pa_[:, kc, :],
                                                        rhs=oaT[:, kc, ts_], start=(kc == 0), stop=(kc == 3)),
                               r=b_oaT[tc * 4:(tc + 1) * 4] + [bw_], w=[b_PB[b0 + 2]], inc=(kc == 3))
                        for kc in range(8):
                            pe(lambda: nc.tensor.matmul(PB[b0 + 3][:], lhsT=pb_[:, kc, :],
                                                        rhs=obT[:, kc, ts_], start=(kc == 0), stop=(kc == 7)),
                               r=b_obT[tc * 4:(tc + 1) * 4] + [bw_], w=[b_PB[b0 + 3]], inc=(kc == 7))
                        act(lambda: nc.scalar.activation(out=gA[q][:], in_=PB[b0][:], func=AF.Sigmoid,
                                                         bias=gbT[:, m:m + 1]), r=[b_PB[b0], b_ct], w=[b_g[q]])
                        act(lambda: nc.scalar.activation(out=gB[q][:], in_=PB[b0 + 1][:], func=AF.Sigmoid,
                                                         bias=gbT[:, 8 + m:9 + m]), r=[b_PB[b0 + 1], b_ct], w=[b_g[q]])
                        dve(lambda: nc.vector.tensor_tensor(out=t1[q][:], in0=PB[b0 + 2][:], in1=gA[q][:], op=ALU.mult),
                            r=[b_PB[b0 + 2], b_g[q]], w=[b_t[q]])
                        dve(lambda: nc.vector.tensor_tensor(out=t2[q][:], in0=PB[b0 + 3][:], in1=gB[q][:], op=ALU.mult),
                            r=[b_PB[b0 + 3], b_g[q]], w=[b_t[q]])
                        dve(lambda: nc.vector.tensor_tensor(out=mT[:, m, ts_], in0=t1[q][:], in1=t2[q][:], op=ALU.add),
                            r=[b_t[q]], w=[b_mT[tc]])
                k.barrier()
            if "mT" in dbg:
                k.dump("d_mT", mT[:], b_mT)
            ck("Cm")
            xr = [k.sb(f"xr{j}", [128, D], F32, esM) for j in range(2)]
            b_xr = [Buf(), Buf()]
            xo = [k.sb(f"xo{j}", [128, D], F32, esM) for j in range(2)]
            b_xo = [Buf(), Buf()]
            fo = [k.sb(f"fo{j}", [128, D], F32, esM) for j in range(2)]
            b_fo = [Buf(), Buf()]
            junkc = k.sb("junkc", [128, D], BF16, esM)
            fss = k.sb("fss", [128, NT], F32, esM)
            b_fs = Buf()
            dpool(out=xr[0][:], in_=x_d[0:128, :], w=[b_xr[0]])
            b_fsi = [Buf() for _ in range(NT)]

            def out1(i):
                q = i % 2
                tsl = slice(i * 128, (i + 1) * 128)
                if i + 1 < NT:
                    dpool(out=xr[1 - q][:], in_=x_d[(i + 1) * 128:(i + 2) * 128, :], w=[b_xr[1 - q]])
                for hf in range(2):
                    bk = 2 * q + hf
                    for kc in range(8):
                        pe(lambda: nc.tensor.matmul(PB[bk][:], lhsT=mT[:, kc, tsl], rhs=Wout[:, kc, hf * 512:(hf + 1) * 512],
                                                    start=(kc == 0), stop=(kc == 7)),
                           r=[b_mT[i // 4], b_W], w=[b_PB[bk]], inc=(kc == 7))
                    dve(lambda: nc.vector.tensor_tensor(out=xo[q][:, hf * 512:(hf + 1) * 512], in0=PB[bk][:],
                                                        in1=xr[q][:, hf * 512:(hf + 1) * 512], op=ALU.add),
                        r=[b_PB[bk], b_xr[q]], w=[b_xo[q]])
                act(lambda: nc.scalar.activation(out=junkc[:], in_=xo[q][:], func=AF.Square, accum_out=fss[:, i:i + 1]),
                    r=[b_xo[q]], w=[b_fs, b_fsi[i]])
                act(lambda: nc.scalar.activation(out=fss[:, i:i + 1], in_=fss[:, i:i + 1], func=AF.Sqrt, scale=1.0 / D,
                                                 bias=EPS), w=[b_fsi[i]])

            def out2(i):
                q = i % 2
                tsl = slice(i * 128, (i + 1) * 128)
                dve(lambda: nc.vector.reciprocal(out=fss[:, i:i + 1], in_=fss[:, i:i + 1]), w=[b_fsi[i]])
                dve(lambda: nc.vector.scalar_tensor_tensor(out=fo[q][:], in0=xo[q][:], scalar=fss[:, i:i + 1],
                                                           in1=fnw_b[:], op0=ALU.mult, op1=ALU.mult),
                    r=[b_xo[q], b_ct, b_fsi[i]], w=[b_fo[q]])
                dsync(out=out_d[tsl, :], in_=fo[q][:], r=[b_fo[q]])

            out1(0)
            for i in range(NT):
                if i + 1 < NT:
                    out1(i + 1)
                out2(i)
            k.barrier()

        k.barrier()
    return nc, k


_NC_CACHE = {}


def kernel(x, positions, norm_w, w_in, gate_bias, conv_w, conv_b, dt_bias, a_log, d_skip,
           ssm_norm_w, w_branch_a, w_branch_b, w_out, final_norm_w):
    f32 = np.float32
    x = np.asarray(x, dtype=f32)
    positions = np.asarray(positions).astype(np.int32)
    nb = x.shape[0]
    assert nb == 8 and x.shape[1] == S and x.shape[2] == D
    if "nc" not in _NC_CACHE:
        _NC_CACHE["nc"] = build()[0]
    nc = _NC_CACHE["nc"]
    invf = (500000.0 ** (-np.arange(0, 16, 2, dtype=f32) / 16)).astype(f32)
    shared = {
        "invf": np.ascontiguousarray(np.broadcast_to(invf, (128, 8))),
        "norm_w": np.ascontiguousarray(np.asarray(norm_w, f32).reshape(1, D)),
        "w_in": np.ascontiguousarray(np.asarray(w_in, f32)[0]),
        "cwT": np.ascontiguousarray(np.asarray(conv_w, f32)[0].reshape(4, 12, 128).transpose(2, 1, 0)),
        "cbT": np.ascontiguousarray(np.asarray(conv_b, f32)[0].reshape(12, 128).T),
        "dt_bias": np.ascontiguousarray(np.asarray(dt_bias, f32).reshape(1, 16)),
        "a_log": np.ascontiguousarray(np.asarray(a_log, f32).reshape(1, 16)),
        "d_skip": np.ascontiguousarray(np.asarray(d_skip, f32).reshape(1, 16)),
        "ssm_norm_w": np.ascontiguousarray(np.asarray(ssm_norm_w, f32).reshape(1, D)),
        "gbT": np.ascontiguousarray(np.asarray(gate_bias, f32)[0].reshape(16, 128).T),
        "w_branch_a": np.ascontiguousarray(np.asarray(w_branch_a, f32)[0]),
        "w_branch_b": np.ascontiguousarray(np.asarray(w_branch_b, f32)[0]),
        "w_out": np.ascontiguousarray(np.asarray(w_out, f32)[0]),
        "final_norm_w": np.ascontiguousarray(np.asarray(final_norm_w, f32).reshape(1, D)),
    }
    in_maps = []
    for b in range(nb):
        m = dict(shared)
        m["x"] = np.ascontiguousarray(x[b])
        m["posT"] = np.ascontiguousarray(positions[b].reshape(NT, 128).T)
        in_maps.append(m)
    res = run_bass_kernel_spmd(nc, in_maps, core_ids=list(range(nb)))
    out = np.stack([np.asarray(res.results[b]["out"], dtype=f32) for b in range(nb)], axis=0)
    return out
```

```python
import numpy as np
import math
import concourse.bass as bass
import concourse.mybir as mybir
from concourse.bass_utils import run_bass_kernel_spmd
from contextlib import ExitStack

F32 = mybir.dt.float32
BF16 = mybir.dt.bfloat16
I32 = mybir.dt.int32
ALU = mybir.AluOpType
AF = mybir.ActivationFunctionType
AX = mybir.AxisListType

S = 2048
D = 1024
NT = 16
INW = 6228
EPS = 1e-6
NITER = 10
A_SRC = [(0, 512, 0), (512, 640, 512), (1280, 1536, 640), (1536, 1600, 896), (1600, 1604, 960),
         (640, 768, 964), (768, 1280, 1092)]
NA = 1604


class Buf:
    __slots__ = ("w", "r", "name", "excl")

    def __init__(self, name="", excl=False):
        self.w = {}
        self.r = {}
        self.name = name
        self.excl = excl


def _merge(deps, d):
    for k, (s, v) in d.items():
        if k not in deps or deps[k][1] < v:
            deps[k] = (s, v)


class Eng:
    def __init__(self, K, name, eng, selfdep=True):
        self.K = K
        self.name = name
        self.eng = eng
        self.selfdep = selfdep
        self.sem = K.es.enter_context(K.nc.semaphore("s_" + name))
        self.cnt = 0
        self.waited = {}
        self.pending = False

    def wait_deps(self, deps):
        for k, (s, v) in deps.items():
            if k == self.name and not self.selfdep:
                continue
            if self.waited.get(k, 0) < v:
                self.eng.wait_ge(s, v)
                self.waited[k] = v

    def __call__(self, fn, r=(), w=(), inc=True, extra=(), selfwait=None):
        if selfwait is not None:
            assert selfwait[0] == self.name
            if self.waited.get("self", 0) < selfwait[2]:
                self.eng.wait_ge(selfwait[1], selfwait[2])
                self.waited["self"] = selfwait[2]
        w = list(w) + [b for b in r if b.excl]
        r = [b for b in r if not b.excl]
        deps = {}
        for b in r:
            _merge(deps, b.w)
        for b in w:
            _merge(deps, b.w)
            _merge(deps, b.r)
        for t in extra:
            _merge(deps, {t[0]: (t[1], t[2])})
        self.wait_deps(deps)
        ins = fn()
        if inc:
            self.cnt += 1
            ins.then_inc(self.sem, 1)
            tok = (self.sem, self.cnt)
            self.pending = False
        else:
            tok = (self.sem, self.cnt + 1)
            self.pending = True
        for b in r:
            _merge(b.r, {self.name: tok})
        for b in w:
            b.w = {self.name: tok}
            b.r = {}
        return (self.name,) + tok


class DmaQ:
    def __init__(self, K, name, waiter, nsem=8):
        self.K = K
        self.name = name
        self.waiter = waiter
        self.sems = [K.es.enter_context(K.nc.semaphore(f"d_{name}{j}")) for j in range(nsem)]
        self.vals = [0] * nsem
        self.idx = 0

    def __call__(self, out, in_, r=(), w=(), extra=(), **kw):
        deps = {}
        for b in r:
            _merge(deps, b.w)
        for b in w:
            _merge(deps, b.w)
            _merge(deps, b.r)
        for t in extra:
            _merge(deps, {t[0]: (t[1], t[2])})
        k = self.idx
        self.idx = (k + 1) % len(self.sems)
        key = f"{self.name}{k}"
        if self.vals[k] > 0:
            _merge(deps, {key: (self.sems[k], self.vals[k])})
        self.waiter.wait_deps(deps)
        ins = self.waiter.eng.dma_start(out=out, in_=in_, **kw)
        self.vals[k] += 16
        ins.then_inc(self.sems[k], 16)
        tok = (self.sems[k], self.vals[k])
        for b in r:
            _merge(b.r, {key: tok})
        for b in w:
            b.w = {key: tok}
            b.r = {}
        return (key,) + tok


class K:
    def __init__(self, nc, es):
        self.nc = nc
        self.es = es
        self.pe = Eng(self, "pe", nc.tensor, selfdep=False)
        self.act = Eng(self, "act", nc.scalar)
        self.dve = Eng(self, "dve", nc.vector)
        self.pool = Eng(self, "pool", nc.gpsimd)
        self.sp = Eng(self, "sp", nc.sync)
        self.engs = [self.pe, self.act, self.dve, self.pool, self.sp]
        self.dsync = DmaQ(self, "qs", self.sp, nsem=8)
        self.dpool = DmaQ(self, "qp", self.pool, nsem=8)
        self.dqs = [self.dsync, self.dpool]
        self.dumps = []

    def sb(self, name, shape, dt, es=None):
        t = (es or self.es).enter_context(self.nc.sbuf_tensor(name, list(shape), dt))
        return t

    def ps(self, name, shape, dt, es=None):
        return (es or self.es).enter_context(self.nc.psum_tensor(name, list(shape), dt))

    def barrier(self):
        deps = {}
        for e in self.engs:
            assert not e.pending, e.name
            if e.cnt > 0:
                deps[e.name] = (e.sem, e.cnt)
        for q in self.dqs:
            for j, s in enumerate(q.sems):
                if q.vals[j] > 0:
                    deps[f"{q.name}{j}"] = (s, q.vals[j])
        for e in self.engs:
            e.wait_deps(deps)

    def dump(self, name, ap, buf):
        d = self.nc.dram_tensor(name, list(ap.shape), ap.dtype, kind="ExternalOutput").ap()
        self.dsync(out=d, in_=ap, r=(buf if isinstance(buf, (list, tuple)) else [buf]))
        self.dumps.append(name)


class Stop(Exception):
    pass


def build(stop_after="all", dbg=()):
    try:
        return _build(stop_after, dbg)
    except Stop as s:
        return s.args


def _build(stop_after="all", dbg=()):
    nc = bass.Bass("TRN2", target_bir_lowering=False)
    dbg = set(dbg)
    x_d = nc.dram_tensor("x", [S, D], F32, kind="ExternalInput").ap()
    posT_d = nc.dram_tensor("posT", [128, NT], I32, kind="ExternalInput").ap()
    invf_d = nc.dram_tensor("invf", [128, 8], F32, kind="ExternalInput").ap()
    normw_d = nc.dram_tensor("norm_w", [1, D], F32, kind="ExternalInput").ap()
    win_d = nc.dram_tensor("w_in", [D, INW], F32, kind="ExternalInput").ap()
    out_d = nc.dram_tensor("out", [S, D], F32, kind="ExternalOutput").ap()
    cwT_d = nc.dram_tensor("cwT", [128, 12, 4], F32, kind="ExternalInput").ap()
    cbT_d = nc.dram_tensor("cbT", [128, 12], F32, kind="ExternalInput").ap()
    dtb_d = nc.dram_tensor("dt_bias", [1, 16], F32, kind="ExternalInput").ap()
    alog_d = nc.dram_tensor("a_log", [1, 16], F32, kind="ExternalInput").ap()
    dsk_d = nc.dram_tensor("d_skip", [1, 16], F32, kind="ExternalInput").ap()
    snw_d = nc.dram_tensor("ssm_norm_w", [1, D], F32, kind="ExternalInput").ap()
    gbT_d = nc.dram_tensor("gbT", [128, 16], F32, kind="ExternalInput").ap()
    wpa_d = nc.dram_tensor("w_branch_a", [512, D], F32, kind="ExternalInput").ap()
    wpb_d = nc.dram_tensor("w_branch_b", [D, D], F32, kind="ExternalInput").ap()
    wout_d = nc.dram_tensor("w_out", [D, D], F32, kind="ExternalInput").ap()
    fnw_d = nc.dram_tensor("final_norm_w", [1, D], F32, kind="ExternalInput").ap()

    with ExitStack() as es:
        k = K(nc, es)
        pe, act, dve, pool, sp = k.pe, k.act, k.dve, k.pool, k.sp
        dsync, dpool = k.dsync, k.dpool

        def ck(name):
            if stop_after == name:
                k.barrier()
                raise Stop(nc, k)

        ident = k.sb("ident", [128, 128], BF16)
        Uf = k.sb("Uf", [128, 128], F32)
        Ub = k.sb("Ub", [128, 128], BF16)
        cbias = k.sb("cbias", [128, 128], F32)
        pow2 = k.sb("pow2", [128, NITER + 2], F32)
        negU = k.sb("negU", [128, 128], BF16)
        negUb = k.sb("negUb", [128, 128], BF16)
        mhalf = k.sb("mhalf", [128, 16], F32)
        b_const = Buf("const")
        pool(lambda: nc.gpsimd.memset(ident[:], 1.0), w=[b_const])
        pool(lambda: nc.gpsimd.affine_select(out=ident[:], in_=ident[:], pattern=[[-1, 128]],
                                             compare_op=ALU.is_equal, fill=0.0, base=0, channel_multiplier=1),
             w=[b_const])
        pool(lambda: nc.gpsimd.memset(Uf[:], 1.0), w=[b_const])
        pool(lambda: nc.gpsimd.affine_select(out=Uf[:], in_=Uf[:], pattern=[[1, 128]], compare_op=ALU.is_ge,
                                             fill=0.0, base=0, channel_multiplier=-1), w=[b_const])
        pool(lambda: nc.gpsimd.tensor_copy(out=Ub[:], in_=Uf[:]), w=[b_const])
        pool(lambda: nc.gpsimd.tensor_scalar(out=negUb[:], in0=Ub[:], scalar1=-1.0, scalar2=None, op0=ALU.mult),
             w=[b_const])
        pool(lambda: nc.gpsimd.memset(mhalf[:], -0.5), w=[b_const])
        pool(lambda: nc.gpsimd.memset(negU[:], 0.0), w=[b_const])
        pool(lambda: nc.gpsimd.affine_select(out=negU[:], in_=negU[:], pattern=[[1, 128]], compare_op=ALU.is_ge,
                                             fill=-30000.0, base=0, channel_multiplier=-1), w=[b_const])
        pool(lambda: nc.gpsimd.memset(cbias[:], 0.0), w=[b_const])
        pool(lambda: nc.gpsimd.affine_select(out=cbias[:], in_=cbias[:], pattern=[[-1, 128]], compare_op=ALU.is_ge,
                                             fill=-1e30, base=0, channel_multiplier=1), w=[b_const])
        for j in range(NITER + 2):
            pool(lambda: nc.gpsimd.memset(pow2[:, j:j + 1], 2.0 ** (-j)), w=[b_const])
        pool(lambda: nc.gpsimd.memset(pow2[:, 0:1], 1.0), w=[b_const])

        hT = k.sb("hT", [128, 8, S], BF16)
        b_hT = [Buf(f"hT{i}") for i in range(NT)]
        wBC = k.sb("wBC", [128, 8, 1024], BF16)
        b_wslot = [Buf(), Buf()]
        wG0 = wBC[:, :, 0:256].rearrange("p k (h n) -> p k h n", h=2)
        Wpb0 = wBC[:, :, 256:384]
        Wpa0 = wBC[:, 0:4, 384:512]
        CONV0 = 2628
        oaT = k.sb("oaT", [128, 4, S], BF16)
        b_oaT = [Buf() for _ in range(NT)]
        esWA = ExitStack()
        wA = k.sb("wA", [128, 8, NA], BF16, esWA)
        b_wA = Buf()
        for (c0, c1, dst) in A_SRC:
            dpool(out=wA[:, :, dst:dst + (c1 - c0)],
                  in_=win_d[:, c0:c1].rearrange("(k p) n -> p k n", p=128), w=[b_wA])

        with ExitStack() as es0:
            normw_b = k.sb("normw_b", [128, D], F32, es0)
            b_normw = Buf()
            dsync(out=normw_b[:], in_=normw_d[0, :].partition_broadcast(128), w=[b_normw])
            xt = k.sb("xt_all", [128, NT, D], F32, es0)
            b_xt = [Buf() for _ in range(NT)]
            xn = [k.sb(f"xn{j}", [128, D], BF16, es0) for j in range(3)]
            b_xn = [Buf(), Buf(), Buf()]
            junk = k.sb("junk0", [128, D], BF16, es0)
            b_junk = Buf()
            ss = k.sb("ss", [128, NT], F32, es0)
            sd = k.sb("sd", [128, NT], F32, es0)
            rstd = k.sb("rstd", [128, NT], F32, es0)
            b_ssg = [Buf() for _ in range(4)]
            pt = [k.ps(f"pt{j}", [128, 8, 128], BF16, es0) for j in range(2)]
            b_pt = [Buf(excl=True), Buf(excl=True)]
            for i in range(NT):
                (dsync if i % 2 == 0 else dsync)(out=xt[:, i, :], in_=x_d[i * 128:(i + 1) * 128, :], w=[b_xt[i]])

            def p0_stats(gq):
                for i in range(4 * gq, 4 * gq + 4):
                    act(lambda: nc.scalar.activation(out=junk[:], in_=xt[:, i, :], func=AF.Square,
                                                     accum_out=ss[:, i:i + 1]),
                        r=[b_xt[i]], w=[b_junk, b_ssg[gq]])
                act(lambda: nc.scalar.activation(out=sd[:, 4 * gq:4 * gq + 4], in_=ss[:, 4 * gq:4 * gq + 4],
                                                 func=AF.Sqrt, scale=1.0 / D, bias=EPS), w=[b_ssg[gq]])
                dve(lambda: nc.vector.reciprocal(out=rstd[:, 4 * gq:4 * gq + 4], in_=sd[:, 4 * gq:4 * gq + 4]),
                    w=[b_ssg[gq]])

            def p0_apply(gq):
                for i in range(4 * gq, 4 * gq + 4):
                    j = i % 3
                    jp = i % 2
                    dve(lambda: nc.vector.scalar_tensor_tensor(out=xn[j][:], in0=xt[:, i, :], scalar=rstd[:, i:i + 1],
                                                               in1=normw_b[:], op0=ALU.mult, op1=ALU.mult),
                        r=[b_xt[i], b_ssg[gq], b_normw], w=[b_xn[j]])
                    for c in range(8):
                        pe(lambda: nc.tensor.transpose(out=pt[jp][:, c, :], in_=xn[j][:, c * 128:(c + 1) * 128],
                                                       identity=ident[:]),
                           r=[b_xn[j], b_const], w=[b_pt[jp]], inc=(c == 7))
                    act(lambda: nc.scalar.copy(out=hT[:, :, i * 128:(i + 1) * 128], in_=pt[jp][:]),
                        r=[b_pt[jp]], w=[b_hT[i]])

            p0_stats(0)
            for gq in range(4):
                if gq + 1 < 4:
                    p0_stats(gq + 1)
                p0_apply(gq)
            k.barrier()
        if "hT" in dbg:
            k.dump("d_hT", hT[:], b_hT[NT - 1])
        if stop_after == "p0":
            k.barrier()
            return nc, k


        PB = [k.ps(f"pb{j}", [128, 512], F32) for j in range(8)]
        b_PB = [Buf(f"pb{j}", excl=True) for j in range(8)]

        def bfv(j):
            return PB[j][:].bitcast(BF16).rearrange("p (s t) -> p s t", t=128)

        with ExitStack() as esA:
            dpool(out=wBC[:, :, 0:512], in_=win_d[:, CONV0:CONV0 + 512].rearrange("(k p) n -> p k n", p=128),
                  w=[b_wslot[0]])
            dpool(out=wBC[:, :, 512:1024], in_=win_d[:, CONV0 + 512:CONV0 + 1024].rearrange("(k p) n -> p k n", p=128),
                  w=[b_wslot[1]])
            posi = k.sb("posi", [128, NT], I32, esA)
            posf = k.sb("posf", [128, NT], F32, esA)
            invf = k.sb("invf_sb", [128, 8], F32, esA)
            ang = k.sb("ang", [128, NT, 8], F32, esA)
            cos_t = k.sb("cos_t", [128, NT, 8], F32, esA)
            sin_t = k.sb("sin_t", [128, NT, 8], F32, esA)
            ry = k.sb("ry", [128, NT, 8], F32, esA)
            rki = k.sb("rki", [128, NT, 8], I32, esA)
            rkf = k.sb("rkf", [128, NT, 8], F32, esA)
            rg = k.sb("rg", [128, NT, 8], F32, esA)
            b_tab = Buf()
            dsync(out=posi[:], in_=posT_d[:, :], w=[b_tab])
            dsync(out=invf[:], in_=invf_d[:, :], w=[b_tab])
            dve(lambda: nc.vector.tensor_copy(out=posf[:], in_=posi[:]), r=[b_tab], w=[b_tab])
            dve(lambda: nc.vector.tensor_tensor(out=ang[:], in0=posf[:].unsqueeze(2).to_broadcast([128, NT, 8]),
                                                in1=invf[:].unsqueeze(1).to_broadcast([128, NT, 8]), op=ALU.mult),
                r=[b_tab], w=[b_tab])
            TWO_PI = 2.0 * math.pi
            for (dst_t, off) in ((sin_t, 0.0), (cos_t, 0.25)):
                dve(lambda: nc.vector.tensor_scalar(out=ry[:], in0=ang[:], scalar1=1.0 / TWO_PI, scalar2=off,
                                                    op0=ALU.mult, op1=ALU.add), r=[b_tab], w=[b_tab])
                dve(lambda: nc.vector.tensor_copy(out=rki[:], in_=ry[:]), r=[b_tab], w=[b_tab])
                dve(lambda: nc.vector.tensor_copy(out=rkf[:], in_=rki[:]), r=[b_tab], w=[b_tab])
                dve(lambda: nc.vector.tensor_tensor(out=ry[:], in0=ry[:], in1=rkf[:], op=ALU.subtract),
                    r=[b_tab], w=[b_tab])
                dve(lambda: nc.vector.tensor_scalar(out=rg[:], in0=ry[:], scalar1=0.5, scalar2=None, op0=ALU.is_ge),
                    r=[b_tab], w=[b_tab])
                dve(lambda: nc.vector.tensor_tensor(out=ry[:], in0=ry[:], in1=rg[:], op=ALU.subtract),
                    r=[b_tab], w=[b_tab])
                dve(lambda: nc.vector.tensor_scalar(out=rg[:], in0=ry[:], scalar1=-0.5, scalar2=None, op0=ALU.is_lt),
                    r=[b_tab], w=[b_tab])
                dve(lambda: nc.vector.tensor_tensor(out=ry[:], in0=ry[:], in1=rg[:], op=ALU.add),
                    r=[b_tab], w=[b_tab])
                act(lambda: nc.scalar.activation(out=dst_t[:], in_=ry[:], func=AF.Sin, scale=TWO_PI * (1.0 - 1e-6)),
                    r=[b_tab], w=[b_tab])
            ck("Atab")
            if "rope" in dbg:
                k.dump("d_cos", cos_t[:], b_tab)
                k.dump("d_sin", sin_t[:], b_tab)

            qk_sb = [k.sb(f"qk_sb{j}", [128, 15, 64], BF16, esA) for j in range(2)]
            b_qk = [Buf(), Buf()]
            rt = [k.sb(f"rt{j}", [128, 15, 8], F32, esA) for j in range(4)]
            b_rt = Buf()
            rsrc = k.sb("rsrc", [128, 15, 16], F32, esA)
            b_rsrc = Buf()
            QT = [k.sb(f"QT{j}", [128, 12, 128], BF16, esA) for j in range(2)]
            b_QT = [Buf(), Buf()]
            KT = k.sb("KT", [128, 2, S], BF16, esA)
            KIT = k.sb("KIT", [128, S], BF16, esA)
            b_KT = [Buf() for _ in range(NT)]
            for j_ in range(2):
                pool(lambda: nc.gpsimd.memset(QT[j_][64:128], 0.0), w=[b_QT[j_]])
            pool(lambda: nc.gpsimd.memset(KT[64:128], 0.0), w=b_KT)
            pool(lambda: nc.gpsimd.memset(KIT[64:128], 0.0), w=b_KT)
            Vaug = k.sb("Vaug", [128, NT, 2, 65], BF16, esA)
            b_V = [Buf() for _ in range(NT)]
            wv = k.sb("wv", [128, NT, 4], F32, esA)
            b_wv = [Buf() for _ in range(NT)]
            sza = [k.sb(f"sza{j}", [128, 512], F32, esA) for j in range(2)]
            b_sza = [Buf(), Buf()]
            sc = [k.sb(f"sc{j}", [128, S], F32, esA) for j in range(2)]
            b_sc = [Buf(), Buf()]
            rl = [k.sb(f"rl{j}", [128, S], F32, esA) for j in range(2)]
            b_rl = [Buf(), Buf()]
            junkb = k.sb("junkb", [128, S], BF16, esA)
            b_junkb = Buf()
            m01 = k.sb("m01", [128, S], BF16, esA)
            b_m01 = Buf()
            maskT = [k.sb(f"maskT{j}", [128, NT, 128], BF16, esA) for j in range(2)]
            b_maskT = [Buf(), Buf()]
            Bv = k.sb("Bv", [128, 1], F32, esA)
            Bk = k.sb("Bk", [128, NITER + 2], F32, esA)
            mid = [k.sb(f"mid{j}", [128, 1], F32, esA) for j in range(2)]
            cnt = k.sb("cnt", [128, 1], F32, esA)
            dd = k.sb("dd", [128, 1], F32, esA)
            thr = k.sb("thr", [128, NT], F32, esA)
            b_bis = Buf()
            Eb = [k.sb(f"Eb{j}", [128, 512], BF16, esA) for j in range(3)]
            b_Eb = [Buf(), Buf(), Buf()]
            rinv = k.sb("rinv", [128, 4], F32, esA)
            otmp = k.sb("otmp", [128, 4, 64], F32, esA)
            rinv8 = k.sb("rinv8", [128, 8], F32, esA)
            otmp8 = k.sb("otmp8", [128, 8, 64], F32, esA)
            b_otmp = Buf()
            oa_sb = k.sb("oa_sb", [128, 512], BF16, esA)
            b_oa = Buf()
            pool(lambda: nc.gpsimd.memset(Vaug[:], 1.0), w=b_V)

            A_BANK = [(0, 0, 512), (1, 512, 452), (2, 964, 512), (3, 1476, 128)]
            T0v = bfv(4)
            T1v = bfv(5)
            ctr = {"st": 0, "ix": 0, "e": 0}

            def hdr_(i):
                return i % 2, slice(i * 128, (i + 1) * 128), (i + 1) * 128, i >= 2

            def stage1(i):
                j, tsl, L, masked = hdr_(i)

                for (bk, c0, n) in A_BANK:
                    for kc in range(8):
                        pe(lambda: nc.tensor.matmul(PB[bk][:, 0:n], lhsT=hT[:, kc, tsl], rhs=wA[:, kc, c0:c0 + n],
                                                    start=(kc == 0), stop=(kc == 7)),
                           r=[b_hT[i], b_wA], w=[b_PB[bk]], inc=(kc == 7))
                ck(f"Aproj{i}")
                p0v = PB[0][:, 0:512].rearrange("p (h d) -> p h d", d=64)
                p1v = PB[1][:, 0:448].rearrange("p (h d) -> p h d", d=64)
                act(lambda: nc.scalar.copy(out=qk_sb[j][:, 0:8, 16:64], in_=p0v[:, :, 16:64]),
                    r=[b_PB[0]], w=[b_qk[j]])
                act(lambda: nc.scalar.copy(out=qk_sb[j][:, 8:15, 16:64], in_=p1v[:, :, 16:64]),
                    r=[b_PB[1]], w=[b_qk[j]])
                ck(f"Ae1_{i}")
                act(lambda: nc.scalar.copy(out=rsrc[:, 0:8, :], in_=p0v[:, :, 0:16]), r=[b_PB[0]], w=[b_rsrc])
                act(lambda: nc.scalar.copy(out=rsrc[:, 8:15, :], in_=p1v[:, :, 0:16]), r=[b_PB[1]], w=[b_rsrc])
                nh = 15
                cb = cos_t[:, i:i + 1, :].to_broadcast([128, nh, 8])
                sb_ = sin_t[:, i:i + 1, :].to_broadcast([128, nh, 8])
                x1 = rsrc[:, :, 0:8]
                x2 = rsrc[:, :, 8:16]
                dve(lambda: nc.vector.tensor_tensor(out=rt[0][:], in0=x1, in1=cb, op=ALU.mult),
                    r=[b_rsrc, b_tab], w=[b_rt])
                dve(lambda: nc.vector.tensor_tensor(out=rt[1][:], in0=x2, in1=sb_, op=ALU.mult),
                    r=[b_rsrc, b_tab], w=[b_rt])
                dve(lambda: nc.vector.tensor_tensor(out=qk_sb[j][:, :, 0:8], in0=rt[0][:], in1=rt[1][:], op=ALU.subtract),
                    r=[b_rt], w=[b_qk[j]])
                dve(lambda: nc.vector.tensor_tensor(out=rt[2][:], in0=x2, in1=cb, op=ALU.mult),
                    r=[b_rsrc, b_tab], w=[b_rt])
                dve(lambda: nc.vector.tensor_tensor(out=rt[3][:], in0=x1, in1=sb_, op=ALU.mult),
                    r=[b_rsrc, b_tab], w=[b_rt])
                dve(lambda: nc.vector.tensor_tensor(out=qk_sb[j][:, :, 8:16], in0=rt[2][:], in1=rt[3][:], op=ALU.add),
                    r=[b_rt], w=[b_qk[j]])
                ck(f"Ae2_{i}")
                act(lambda: nc.scalar.copy(out=wv[:, i, :], in_=PB[1][:, 448:452]), r=[b_PB[1]], w=[b_wv[i]])
                act(lambda: nc.scalar.copy(out=Vaug[:, i, :, 0:64],
                                           in_=PB[2][:, 0:128].rearrange("p (g d) -> p g d", d=64)),
                    r=[b_PB[2]], w=[b_V[i]])
                ck(f"Ae3_{i}")
                act(lambda: nc.scalar.activation(out=sza[j][:, 0:384], in_=PB[2][:, 128:512], func=AF.Silu),
                    r=[b_PB[2]], w=[b_sza[j]])
                act(lambda: nc.scalar.activation(out=sza[j][:, 384:512], in_=PB[3][:, 0:128], func=AF.Silu),
                    r=[b_PB[3]], w=[b_sza[j]])
                ck(f"Aevac{i}")
                for h in range(15):
                    tv, sl, bk = (T0v, h, 4) if h < 8 else (T1v, h - 8, 5)
                    pe(lambda: nc.tensor.transpose(out=tv[0:64, sl, :], in_=qk_sb[j][:, h, :], identity=ident[:]),
                       r=[b_qk[j], b_const], w=[b_PB[bk]], inc=(h == 7 or h == 14))
                act(lambda: nc.scalar.copy(out=QT[j][0:64, 0:8, :], in_=T0v[0:64, :, :]), r=[b_PB[4]], w=[b_QT[j]])
                act(lambda: nc.scalar.copy(out=KT[0:64, :, tsl], in_=T1v[0:64, 0:2, :]), r=[b_PB[5]], w=[b_KT[i]])
                act(lambda: nc.scalar.copy(out=QT[j][0:64, 8:12, :], in_=T1v[0:64, 2:6, :]), r=[b_PB[5]], w=[b_QT[j]])
                act(lambda: nc.scalar.copy(out=KIT[0:64, tsl], in_=T1v[0:64, 6, :]), r=[b_PB[5]], w=[b_KT[i]])


            def stage2(i):
                j, tsl, L, masked = hdr_(i)
                if not masked:
                    return

                nch = (L + 511) // 512
                for h in range(4):
                    q = h % 2
                    for c in range(nch):
                        c0 = c * 512
                        n = min(512, L - c0)
                        bk = ctr["ix"] % 2
                        ctr["ix"] += 1
                        pe(lambda: nc.tensor.matmul(PB[bk][:, 0:n], lhsT=QT[j][:, 8 + h, :], rhs=KIT[:, c0:c0 + n],
                                                    start=True, stop=True),
                           r=[b_QT[j]] + b_KT[0:i + 1], w=[b_PB[bk]])
                        act(lambda: nc.scalar.activation(out=rl[q][:, c0:c0 + n], in_=PB[bk][:, 0:n], func=AF.Relu),
                            r=[b_PB[bk]], w=[b_rl[q]])
                    if h == 0:
                        dve(lambda: nc.vector.tensor_scalar(out=sc[j][:, 0:L], in0=rl[q][:, 0:L],
                                                            scalar1=wv[:, i, 0:1], scalar2=None, op0=ALU.mult),
                            r=[b_rl[q], b_wv[i]], w=[b_sc[j]])
                    else:
                        dve(lambda: nc.vector.scalar_tensor_tensor(out=sc[j][:, 0:L], in0=rl[q][:, 0:L],
                                                                   scalar=wv[:, i, h:h + 1], in1=sc[j][:, 0:L],
                                                                   op0=ALU.mult, op1=ALU.add),
                            r=[b_rl[q], b_wv[i]], w=[b_sc[j]])


            def bisect(i):
                j, tsl, L, masked = hdr_(i)
                if not masked:
                    return
                yield

                dve(lambda: nc.vector.tensor_reduce(out=Bv[:], in_=sc[j][:, 0:L], axis=AX.X, op=ALU.max,
                                                    apply_absolute_value=True),
                    r=[b_sc[j]], w=[b_bis])
                dve(lambda: nc.vector.tensor_tensor(out=sc[j][:, L - 128:L], in0=sc[j][:, L - 128:L],
                                                    in1=cbias[:], op=ALU.add),
                    r=[b_const], w=[b_sc[j]])
                dve(lambda: nc.vector.tensor_scalar(out=Bk[:], in0=pow2[:], scalar1=Bv[:, 0:1], scalar2=None,
                                                    op0=ALU.mult), r=[b_const], w=[b_bis])
                dve(lambda: nc.vector.memset(mid[0][:], 0.0), w=[b_bis])
                for it in range(NITER):
                    ma, mb = mid[it % 2], mid[(it + 1) % 2]
                    dve(lambda: nc.vector.tensor_scalar(out=junkb[:, 0:L], in0=sc[j][:, 0:L], scalar1=ma[:, 0:1],
                                                        scalar2=None, op0=ALU.is_ge, op1=ALU.add,
                                                        accum_out=cnt[:, 0:1]),
                        r=[b_sc[j]], w=[b_bis, b_junkb])
                    dve(lambda: nc.vector.tensor_scalar(out=dd[:], in0=cnt[:], scalar1=255.5,
                                                        scalar2=Bk[:, it:it + 1], op0=ALU.is_ge, op1=ALU.mult),
                        w=[b_bis])
                    dve(lambda: nc.vector.tensor_scalar(out=mb[:], in0=dd[:], scalar1=Bk[:, it + 1:it + 2],
                                                        scalar2=ma[:, 0:1], op0=ALU.subtract, op1=ALU.add),
                        w=[b_bis])
                    yield
                mfin = mid[NITER % 2]
                dve(lambda: nc.vector.tensor_tensor(out=thr[:, i:i + 1], in0=mfin[:], in1=Bk[:, NITER:NITER + 1],
                                                    op=ALU.subtract), w=[b_bis])
                dve(lambda: nc.vector.tensor_scalar(out=m01[:, 0:L], in0=sc[j][:, 0:L], scalar1=thr[:, i:i + 1],
                                                    scalar2=None, op0=ALU.is_ge),
                    r=[b_sc[j], b_bis], w=[b_m01])
                if ("sc%d" % i) in dbg:
                    k.dump("d_sc", sc[j][:, 0:L], b_sc[j])
                    k.dump("d_thr", thr[:, i:i + 1], b_bis)


            def masktr(i):
                j, tsl, L, masked = hdr_(i)
                if not masked:
                    return

                for jb in range(i + 1):
                    tv, sl, bk = (T0v, jb, 4) if jb < 8 else (T1v, jb - 8, 5)
                    last = (jb == i) or (jb == 7)
                    pe(lambda: nc.tensor.transpose(out=tv[:, sl, :], in_=m01[:, jb * 128:(jb + 1) * 128],
                                                   identity=ident[:]),
                       r=[b_m01, b_const], w=[b_PB[bk]], inc=last)
                n0 = min(i + 1, 8)
                act(lambda: nc.scalar.activation(out=maskT[j][:, 0:n0, :], in_=T0v[:, 0:n0, :], func=AF.Identity,
                                                 scale=30000.0, bias=-30000.0),
                    r=[b_PB[4]], w=[b_maskT[j]])
                if i + 1 > 8:
                    act(lambda: nc.scalar.activation(out=maskT[j][:, 8:i + 1, :], in_=T1v[:, 0:i + 1 - 8, :],
                                                     func=AF.Identity, scale=30000.0, bias=-30000.0),
                        r=[b_PB[5]], w=[b_maskT[j]])


            def attn_main(i):
                j, tsl, L, masked = hdr_(i)
                nkt = i + 1
                for g in range(2):
                    Ov = PB[6 + g][:, 0:260].rearrange("p (h d) -> p h d", d=65)

                    def st_mm(jb):
                        bk = 2 + (ctr["st"] % 2)
                        ctr["st"] += 1
                        has_mask = masked or jb == i
                        pe(lambda: nc.tensor.matmul(PB[bk][:].rearrange("p (h t) -> p h t", t=128),
                                                    lhsT=KT[:, g, jb * 128:(jb + 1) * 128],
                                                    rhs=QT[j][:, 4 * g:4 * g + 4, :], start=True, stop=(not has_mask)),
                           r=[b_QT[j], b_KT[jb]], w=[b_PB[bk]], inc=(not has_mask))
                        if has_mask:
                            if masked:
                                mb_ap = maskT[j][:, jb:jb + 1, :].to_broadcast([128, 4, 128])
                                rd = [b_maskT[j], b_const]
                            else:
                                mb_ap = negU[:].unsqueeze(1).to_broadcast([128, 4, 128])
                                rd = [b_const]
                            pe(lambda: nc.tensor.matmul(PB[bk][:].rearrange("p (h t) -> p h t", t=128),
                                                        lhsT=ident[:], rhs=mb_ap, start=False, stop=True),
                               r=rd, w=[b_PB[bk]])
                        return bk

                    def exp_pv(jb, bk):
                        e = ctr["e"] % 3
                        ctr["e"] += 1
                        act(lambda: nc.scalar.activation(out=Eb[e][:], in_=PB[bk][:], func=AF.Exp, scale=0.125),
                            r=[b_PB[bk]], w=[b_Eb[e]])
                        for hh in range(4):
                            pe(lambda: nc.tensor.matmul(Ov[:, hh, :], lhsT=Eb[e][:, hh * 128:(hh + 1) * 128],
                                                        rhs=Vaug[:, jb, g, :], start=(jb == 0 and hh == 0),
                                                        stop=(jb == i and hh == 3)),
                               r=[b_Eb[e], b_V[jb]], w=[b_PB[6 + g]], inc=(hh == 3))

                    bks = {0: st_mm(0)}
                    for jb in range(nkt):
                        if jb + 1 < nkt:
                            bks[jb + 1] = st_mm(jb + 1)
                        exp_pv(jb, bks[jb])

            def attn_fin(i):
                j, tsl, L, masked = hdr_(i)
                for g in range(2):
                    Ov = PB[6 + g][:, 0:260].rearrange("p (h d) -> p h d", d=65)
                    dve(lambda: nc.vector.reciprocal(out=rinv8[:, 4 * g:4 * g + 4].unsqueeze(2), in_=Ov[:, :, 64:65]),
                        r=[b_PB[6 + g]], w=[b_otmp])
                    dve(lambda: nc.vector.tensor_tensor(out=otmp8[:, 4 * g:4 * g + 4, :], in0=Ov[:, :, 0:64],
                                                        in1=rinv8[:, 4 * g:4 * g + 4].unsqueeze(2)
                                                        .to_broadcast([128, 4, 64]), op=ALU.mult),
                        r=[b_PB[6 + g]], w=[b_otmp])
                dve(lambda: nc.vector.tensor_tensor(out=oa_sb[:], in0=otmp8[:].rearrange("p h d -> p (h d)"),
                                                    in1=sza[j][:], op=ALU.mult),
                    r=[b_otmp, b_sza[j]], w=[b_oa])


            def fin_tr(i):
                j, tsl, L, masked = hdr_(i)
                for c in range(4):
                    pe(lambda: nc.tensor.transpose(out=T0v[:, c, :], in_=oa_sb[:, c * 128:(c + 1) * 128],
                                                   identity=ident[:]),
                       r=[b_oa, b_const], w=[b_PB[4]], inc=(c == 3))
                act(lambda: nc.scalar.copy(out=oaT[:, :, tsl], in_=T0v[:, 0:4, :]), r=[b_PB[4]], w=[b_oaT[i]])


            nA = NT if "skipA" not in dbg else 0
            pend = None
            for i in range(nA + 2):
                if i < nA:
                    stage1(i)
                if pend is not None:
                    for _ in pend:
                        pass
                    pend = None
                if i < nA:
                    stage2(i)
                if 1 <= i <= nA:
                    masktr(i - 1)
                if 2 <= i <= nA + 1:
                    fin_tr(i - 2)
                if 1 <= i <= nA:
                    attn_main(i - 1)
                if i < nA:
                    g_ = bisect(i)
                    nsteps = (NITER - 3) if (i + 1 < nA) else 10 ** 9
                    done_ = False
                    for _s in range(nsteps):
                        try:
                            next(g_)
                        except StopIteration:
                            done_ = True
                            break
                    if not done_:
                        pend = g_
                if 1 <= i <= nA:
                    attn_fin(i - 1)
            i = nA - 1
            j = i % 2
            L = (i + 1) * 128

            if "qk" in dbg:
                k.dump("d_KT", KT[:, :, 0:L], b_KT)
                k.dump("d_KIT", KIT[:, 0:L], b_KT)
                k.dump("d_QT", QT[j][:], b_QT[j])
                k.dump("d_V", Vaug[:, 0:i + 1], b_V)
            if "oaT" in dbg:
                k.dump("d_oaT", oaT[:, :, 0:L], b_oaT)
            k.barrier()
        esWA.close()
        if stop_after.startswith("A"):
            return nc, k


        obT = k.sb("obT", [128, 8, S], BF16)
        b_obT = [Buf() for _ in range(NT)]
        with ExitStack() as esB:
            wdt = k.sb("wdt", [128, 8, 16], BF16, esB)
            b_wdt = Buf()
            dpool(out=wdt[:], in_=win_d[:, 4164:4180].rearrange("(k p) n -> p k n", p=128), w=[b_wdt])
            X_tm = k.sb("X_tm", [128, NT, 1024], BF16, esB)
            B_tm = k.sb("B_tm", [128, NT, 256], BF16, esB)
            BT = k.sb("BT", [128, 2, S], BF16, esB)
            CT = k.sb("CT", [128, 2, S], BF16, esB)
            b_X = Buf()
            dtb_b = k.sb("dtb_b", [128, 16], F32, esB)
            a_b = k.sb("a_b", [128, 16], F32, esB)
            dsk_b = k.sb("dsk_b", [128, 16], F32, esB)
            snw_b = k.sb("snw_b", [128, D], F32, esB)
            dt_all = k.sb("dt_all", [128, NT, 16], F32, esB)
            dA_all = k.sb("dA_all", [128, NT, 16], F32, esB)
            spt = [k.sb(f"spt{j}", [128, NT, 16], F32, esB) for j in range(3)]
            ones_f = k.sb("ones_f", [128, 128], F32, esB)
            NEGU4 = k.sb("NEGU4", [128, 4, 128], BF16, esB)
            Dg = k.sb("Dg", [128, 16, 128], BF16, esB)
            b_ptab = Buf()
            dsync(out=dtb_b[:], in_=dtb_d[0, :].partition_broadcast(128), w=[b_ptab])
            dsync(out=a_b[:], in_=alog_d[0, :].partition_broadcast(128), w=[b_ptab])
            dsync(out=dsk_b[:], in_=dsk_d[0, :].partition_broadcast(128), w=[b_ptab])
            dsync(out=snw_b[:], in_=snw_d[0, :].partition_broadcast(128), w=[b_ptab])
            pool(lambda: nc.gpsimd.memset(ones_f[:], 1.0), w=[b_ptab])
            pool(lambda: nc.gpsimd.memset(NEGU4[:], 0.0), w=[b_ptab])
            pool(lambda: nc.gpsimd.affine_select(out=NEGU4[:], in_=NEGU4[:], pattern=[[0, 4], [1, 128]],
                                                 compare_op=ALU.is_ge, fill=-1.0e4, base=0, channel_multiplier=-1),
                 w=[b_ptab])
            act(lambda: nc.scalar.activation(out=a_b[:], in_=a_b[:], func=AF.Exp), w=[b_ptab])
            dve(lambda: nc.vector.tensor_scalar(out=a_b[:], in0=a_b[:], scalar1=-1.0, scalar2=None, op0=ALU.mult),
                w=[b_ptab])
            dve(lambda: nc.vector.tensor_tensor(out=Dg[:], in0=ident[:].unsqueeze(1).to_broadcast([128, 16, 128]),
                                                in1=dsk_b[:].unsqueeze(2).to_broadcast([128, 16, 128]), op=ALU.mult),
                r=[b_const], w=[b_ptab])
            ck("Btab")

            for i in range(NT):
                for kc in range(8):
                    pe(lambda: nc.tensor.matmul(PB[0][:, i * 16:(i + 1) * 16], lhsT=hT[:, kc, i * 128:(i + 1) * 128],
                                                rhs=wdt[:, kc, :], start=(kc == 0), stop=(kc == 7)),
                       r=[b_hT[i], b_wdt], w=[b_PB[0]], inc=(kc == 7))
            dve(lambda: nc.vector.tensor_tensor(out=spt[0][:], in0=PB[0][:, 0:256].rearrange("p (i h) -> p i h", h=16),
                                                in1=dtb_b[:].unsqueeze(1).to_broadcast([128, NT, 16]), op=ALU.add),
                r=[b_PB[0]], w=[b_ptab])
            dve(lambda: nc.vector.tensor_scalar(out=spt[2][:], in0=spt[0][:], scalar1=-1.0, scalar2=None, op0=ALU.mult),
                w=[b_ptab])
            dve(lambda: nc.vector.tensor_tensor(out=spt[1][:], in0=spt[0][:], in1=spt[2][:], op=ALU.max), w=[b_ptab])
            act(lambda: nc.scalar.activation(out=spt[1][:], in_=spt[1][:], func=AF.Exp, scale=-1.0), w=[b_ptab])
            act(lambda: nc.scalar.activation(out=spt[1][:], in_=spt[1][:], func=AF.Ln, bias=1.0), w=[b_ptab])
            dve(lambda: nc.vector.tensor_scalar(out=spt[2][:], in0=spt[0][:], scalar1=0.0, scalar2=None, op0=ALU.max),
                w=[b_ptab])
            dve(lambda: nc.vector.tensor_tensor(out=dt_all[:], in0=spt[2][:], in1=spt[1][:], op=ALU.add), w=[b_ptab])
            dve(lambda: nc.vector.tensor_tensor(out=dA_all[:], in0=dt_all[:],
                                                in1=a_b[:].unsqueeze(1).to_broadcast([128, NT, 16]), op=ALU.mult),
                w=[b_ptab])
            if "dt" in dbg:
                k.dump("d_dt", dt_all[:], b_ptab)
            ck("Bdt")

            with ExitStack() as esC:
                cwT = k.sb("cwT_sb", [128, 12, 4], F32, esC)
                cbT = k.sb("cbT_sb", [128, 12], F32, esC)
                b_cw = Buf()
                dsync(out=cwT[:], in_=cwT_d[:, :, :], w=[b_cw])
                dsync(out=cbT[:], in_=cbT_d[:, :], w=[b_cw])
                pre = [k.sb(f"pre{j}", [128, S + 3], F32, esC) for j in range(2)]
                b_pre = [Buf(), Buf()]
                accs = [k.sb(f"acc{j}", [128, S], F32, esC) for j in range(2)]
                b_accs = [Buf(), Buf()]
                xs_fm = k.sb("xs_fm", [128, S], BF16, esC)
                b_xs = Buf()
                for q in range(2):
                    pool(lambda: nc.gpsimd.memset(pre[q][:, 0:3], 0.0), w=[b_pre[q]])
                b_xs2 = [Buf(), Buf()]
                b_cv = [Buf() for _ in range(12)]

                def cproj(m):
                    q = m % 2
                    slot = (m // 4) % 2
                    wc0 = slot * 512 + (m % 4) * 128
                    for tc in range(4):
                        for kc in range(8):
                            pe(lambda: nc.tensor.matmul(PB[tc][:], lhsT=wBC[:, kc, wc0:wc0 + 128],
                                                        rhs=hT[:, kc, tc * 512:(tc + 1) * 512],
                                                        start=(kc == 0), stop=(kc == 7)),
                               r=b_hT[tc * 4:(tc + 1) * 4] + [b_wslot[slot]], w=[b_PB[tc]], inc=(kc == 7))
                        act(lambda: nc.scalar.copy(out=pre[q][:, 3 + tc * 512:3 + (tc + 1) * 512], in_=PB[tc][:]),
                            r=[b_PB[tc]], w=[b_pre[q]])

                def cpost(m):
                    q = m % 2
                    acc = accs[q]
                    b_acc = b_accs[q]
                    dve(lambda: nc.vector.tensor_scalar(out=acc[:], in0=pre[q][:, 0:S], scalar1=cwT[:, m, 0:1],
                                                        scalar2=None, op0=ALU.mult),
                        r=[b_pre[q], b_cw], w=[b_acc])
                    for kk in range(1, 4):
                        dve(lambda: nc.vector.scalar_tensor_tensor(out=acc[:], in0=pre[q][:, kk:kk + S],
                                                                   scalar=cwT[:, m, kk:kk + 1], in1=acc[:],
                                                                   op0=ALU.mult, op1=ALU.add),
                            r=[b_pre[q], b_cw], w=[b_acc])
                    if m < 10:
                        dst = xs_fm[:] if m < 8 else BT[:, m - 8, :]
                        bdst = b_xs if m < 8 else b_cv[m]
                        act(lambda: nc.scalar.activation(out=dst, in_=acc[:], func=AF.Silu, bias=cbT[:, m:m + 1]),
                            r=[b_acc, b_cw], w=[bdst])
                        for half in range(2):
                            bk = 4 + half
                            tv = bfv(bk)
                            for s8 in range(8):
                                ti_ = half * 8 + s8
                                in_ap = (xs_fm[:, ti_ * 128:(ti_ + 1) * 128] if m < 8
                                         else BT[:, m - 8, ti_ * 128:(ti_ + 1) * 128])
                                pe(lambda: nc.tensor.transpose(out=tv[:, s8, :], in_=in_ap, identity=ident[:]),
                                   r=[bdst, b_const], w=[b_PB[bk]], inc=(s8 == 7))
                            if m < 8:
                                act(lambda: nc.scalar.copy(out=X_tm[:, half * 8:(half + 1) * 8, m * 128:(m + 1) * 128],
                                                           in_=tv[:, :, :]), r=[b_PB[bk]], w=[b_cv[m]])
                            else:
                                act(lambda: nc.scalar.copy(out=B_tm[:, half * 8:(half + 1) * 8,
                                                                    (m - 8) * 128:(m - 7) * 128],
                                                           in_=tv[:, :, :]), r=[b_PB[bk]], w=[b_cv[m]])
                    else:
                        act(lambda: nc.scalar.activation(out=CT[:, m - 10, :], in_=acc[:], func=AF.Silu,
                                                         bias=cbT[:, m:m + 1]),
                            r=[b_acc, b_cw], w=[b_cv[m]])

                def wreload(m_done):
                    if m_done == 3:
                        dpool(out=wBC[:, :, 0:512], in_=win_d[:, CONV0 + 1024:CONV0 + 1536]
                              .rearrange("(k p) n -> p k n", p=128), w=[b_wslot[0]])
                    elif m_done == 7:
                        dpool(out=wBC[:, :, 512:1024], in_=win_d[:, 1604:2116]
                              .rearrange("(k p) n -> p k n", p=128), w=[b_wslot[1]])
                    elif m_done == 11:
                        dpool(out=wBC[:, :, 0:512], in_=win_d[:, 2116:2628]
                              .rearrange("(k p) n -> p k n", p=128), w=[b_wslot[0]])

                cproj(0)
                wreload(0)
                for m in range(12):
                    if m + 1 < 12:
                        cproj(m + 1)
                        wreload(m + 1)
                    cpost(m)
                    ck(f"Bconv{m}")
                b_X.w = {}
                for bb in b_cv:
                    _merge(b_X.w, bb.w)
                if "conv" in dbg:
                    k.dump("d_Xtm", X_tm[:], b_X)
                    k.dump("d_Btm", B_tm[:], b_X)
                    k.dump("d_BT", BT[:], b_X)
                    k.dump("d_CT", CT[:], b_X)
                ck("Bconv")
                k.barrier()

            with ExitStack() as esS:
                ones_b = k.sb("ones_b", [128, 128], BF16, esS)
                dAhl = k.sb("dAhl", [128, NT, 2, 16], BF16, esS)
                dAres = spt[0]
                b_hl = Buf()
                pool(lambda: nc.gpsimd.memset(ones_b[:], 1.0), w=[b_hl])
                dve(lambda: nc.vector.tensor_copy(out=dAhl[:, :, 0, :], in_=dA_all[:]), r=[b_ptab], w=[b_hl])
                dve(lambda: nc.vector.tensor_tensor(out=dAres[:], in0=dA_all[:], in1=dAhl[:, :, 0, :], op=ALU.subtract),
                    r=[b_ptab], w=[b_hl])
                dve(lambda: nc.vector.tensor_copy(out=dAhl[:, :, 1, :], in_=dAres[:]), w=[b_hl])
                szb = [k.sb(f"szb{j}", [128, 1024], BF16, esS) for j in range(2)]
                b_szb = [Buf(), Buf()]
                smalls = [k.sb(f"small{j}", [128, 32], F32, esS) for j in range(2)]
                nacums = [k.sb(f"nacum{j}", [128, 16], F32, esS) for j in range(2)]
                eas = [k.sb(f"ea{j}", [128, 16], F32, esS) for j in range(2)]
                decs = [k.sb(f"dec{j}", [128, 16], F32, esS) for j in range(2)]
                dtds = [k.sb(f"dtd{j}", [128, 16], F32, esS) for j in range(2)]
                eASs = [k.sb(f"eAS{j}", [128, 2, 4], F32, esS) for j in range(2)]
                b_sms = [Buf(), Buf()]
                LTg = [k.sb(f"LTg{j}", [128, 4, 128], F32, esS) for j in range(2)]
                b_LT = [Buf(), Buf()]
                MTg = [k.sb(f"MTg{j}", [128, 4, 128], BF16, esS) for j in range(2)]
                b_MT = [Buf(), Buf()]
                CBs = [k.sb("CBs0", [128, 4, 128], F32, esS)] * 2
                b_CBs = [Buf()] * 2
                xds = [k.sb(f"xd{j}", [128, 16, 64], BF16, esS) for j in range(2)]
                xdds = [k.sb(f"xdd{j}", [128, 16, 64], BF16, esS) for j in range(2)]
                b_xds = [Buf(), Buf()]
                b_xdds = [Buf(), Buf()]
                ysb = k.sb("ysb", [128, 16, 64], F32, esS)
                b_y = Buf()
                ssq = k.sb("ssq", [128, 4], F32, esS)
                rs4 = k.sb("rs4", [128, 4], F32, esS)
                junkf = k.sb("junkf", [128, 256], F32, esS)
                ob_sb = k.sb("ob_sb", [128, 1024], BF16, esS)
                b_ob = Buf()
                S_sb = k.sb("S_sb", [128, 2, 256], F32, esS)
                S_bf = k.sb("S_bf", [128, 2, 256], BF16, esS)
                b_S = Buf()
                b_Sbf = Buf()
                gctr = {"g": 0, "d": 0}
                Yv = [PB[4][:].rearrange("p (h d) -> p h d", d=64), PB[5][:].rearrange("p (h d) -> p h d", d=64)]

                def head(c):
                    q = c % 2
                    csl = slice(c * 128, (c + 1) * 128)
                    small, nacum, ea, dec, dtd, eAS, b_sm = smalls[q], nacums[q], eas[q], decs[q], dtds[q], eASs[q], b_sms[q]
                    pe(lambda: nc.tensor.matmul(PB[2][:, 0:16], lhsT=Uf[:], rhs=dA_all[:, c, :], start=True, stop=False),
                       r=[b_const, b_ptab], w=[b_PB[2]], inc=False)
                    pe(lambda: nc.tensor.matmul(PB[2][:, 16:32], lhsT=ones_f[:], rhs=dA_all[:, c, :], start=False,
                                                stop=True), r=[b_ptab], w=[b_PB[2]])
                    CBv = PB[3][:].rearrange("p (g l) -> p g l", l=128)
                    tk = None
                    for gi, g in enumerate((0, 2, 1, 3)):
                        p0 = (g % 2) * 64
                        tk2 = pe(lambda: nc.tensor.matmul(CBv[:, g, :], lhsT=BT[p0:p0 + 64, g // 2, csl],
                                                          rhs=CT[p0:p0 + 64, g // 2, csl], start=(gi == 0),
                                                          stop=(gi == 3)),
                                 r=[b_X], w=[b_PB[3]], inc=(gi == 1 or gi == 3), selfwait=(tk if gi == 2 else None))
                        if gi == 1:
                            tk = tk2
                    for hb in range(2):
                        for kc in range(8):
                            pe(lambda: nc.tensor.matmul(PB[hb][:], lhsT=hT[:, kc, csl],
                                                        rhs=wBC[:, kc, (1 - hb) * 512:(2 - hb) * 512],
                                                        start=(kc == 0), stop=(kc == 7)),
                               r=[b_hT[c], b_wslot[1 - hb]], w=[b_PB[hb]], inc=(kc == 7))
                    yield
                    dve(lambda: nc.vector.tensor_copy(out=small[:], in_=PB[2][:, 0:32]), r=[b_PB[2]], w=[b_sm])
                    acum = small[:, 0:16]
                    atot = small[:, 16:32]
                    dve(lambda: nc.vector.tensor_tensor(out=dec[:], in0=atot, in1=acum, op=ALU.subtract), w=[b_sm])
                    act(lambda: nc.scalar.activation(out=ea[:], in_=acum, func=AF.Exp), w=[b_sm])
                    act(lambda: nc.scalar.activation(out=dec[:], in_=dec[:], func=AF.Exp), w=[b_sm])
                    atv = small[:, 16:32].rearrange("p (s f h) -> p s f h", s=2, f=2)
                    act(lambda: nc.scalar.activation(out=eAS[0:64], in_=atv[0:64, :, 0, :], func=AF.Exp), w=[b_sm])
                    act(lambda: nc.scalar.activation(out=eAS[64:128], in_=atv[64:128, :, 1, :], func=AF.Exp), w=[b_sm])
                    act(lambda: nc.scalar.copy(out=CBs[q][:], in_=CBv), r=[b_PB[3]], w=[b_CBs[q]])
                    for hb in range(2):
                        act(lambda: nc.scalar.activation(out=szb[q][:, hb * 512:(hb + 1) * 512], in_=PB[hb][:],
                                                         func=AF.Silu), r=[b_PB[hb]], w=[b_szb[q]])
                    dve(lambda: nc.vector.tensor_tensor(out=dtd[:], in0=dt_all[:, c, :], in1=dec[:], op=ALU.mult),
                        r=[b_ptab], w=[b_sm])
                    Xc = X_tm[:, c, :].rearrange("p (h d) -> p h d", d=64)
                    dve(lambda: nc.vector.tensor_tensor(out=xds[q][:], in0=Xc,
                                                        in1=dt_all[:, c, :].unsqueeze(2).to_broadcast([128, 16, 64]),
                                                        op=ALU.mult), r=[b_X, b_ptab], w=[b_xds[q]])
                    dve(lambda: nc.vector.tensor_tensor(out=xdds[q][:], in0=Xc,
                                                        in1=dtd[:].unsqueeze(2).to_broadcast([128, 16, 64]),
                                                        op=ALU.mult), r=[b_X, b_sm], w=[b_xdds[q]])

                    yield

                def groups(c):
                    q = c % 2
                    csl = slice(c * 128, (c + 1) * 128)
                    nacum, b_sm = nacums[q], b_sms[q]
                    xd = xds[q]
                    st = {}

                    def acumb(g):
                        gq = gctr["g"] % 2
                        gctr["g"] += 1
                        abk = 2 if gq == 0 else 6
                        first = True
                        for hh in range(4):
                            hd = 4 * g + hh
                            for part in range(2):
                                pe(lambda: nc.tensor.matmul(PB[abk][:, hh * 128:(hh + 1) * 128],
                                                            lhsT=dAhl[:, c, part, hd:hd + 1].to_broadcast([128, 128]),
                                                            rhs=Ub[:], start=first, stop=False),
                                   r=[b_hl, b_const], w=[b_PB[abk]], inc=False)
                                first = False
                        for part in range(2):
                            pe(lambda: nc.tensor.matmul(
                                PB[abk][:].rearrange("p (h l) -> p h l", l=128), lhsT=negUb[:],
                                rhs=dAhl[:, c, part, 4 * g:4 * g + 4].unsqueeze(2).to_broadcast([128, 4, 128]),
                                start=False, stop=False), r=[b_hl, b_const], w=[b_PB[abk]], inc=False)
                        pe(lambda: nc.tensor.matmul(PB[abk][:], lhsT=ident[:],
                                                    rhs=NEGU4[:].rearrange("p h l -> p (h l)"), start=False, stop=True),
                           r=[b_const, b_ptab], w=[b_PB[abk]])
                        st[g] = (gq, abk)

                    def ymm(g):
                        gq, abk = st[g]
                        act(lambda: nc.scalar.activation(out=LTg[gq][:].rearrange("p h l -> p (h l)"), in_=PB[abk][:],
                                                         func=AF.Exp), r=[b_PB[abk]], w=[b_LT[gq]])
                        dve(lambda: nc.vector.tensor_tensor(out=MTg[gq][:], in0=LTg[gq][:],
                                                            in1=CBs[q][:, g:g + 1, :].to_broadcast([128, 4, 128]),
                                                            op=ALU.mult),
                            r=[b_LT[gq], b_CBs[q]], w=[b_MT[gq]])
                        for hh in range(4):
                            hd = 4 * g + hh
                            yb = 4 + hd // 8
                            pe(lambda: nc.tensor.matmul(Yv[hd // 8][:, hd % 8, :], lhsT=MTg[gq][:, hh, :],
                                                        rhs=xd[:, hd, :], start=(hd % 8 == 0), stop=False),
                               r=[b_MT[gq], b_xds[q]], w=[b_PB[yb]], inc=False)
                            pe(lambda: nc.tensor.matmul(Yv[hd // 8][:, hd % 8, :], lhsT=Dg[:, hd, :],
                                                        rhs=X_tm[:, c, hd * 64:(hd + 1) * 64], start=False,
                                                        stop=(hd % 8 == 7)),
                               r=[b_ptab, b_X], w=[b_PB[yb]], inc=(hh == 3))

                    acumb(0)
                    acumb(1)
                    yield
                    ymm(0)
                    yield
                    acumb(2)
                    ymm(1)
                    yield
                    acumb(3)
                    ymm(2)
                    yield
                    ymm(3)
                    yield

                def tail(c):
                    q = c % 2
                    csl = slice(c * 128, (c + 1) * 128)
                    ea, eAS, b_sm = eas[q], eASs[q], b_sms[q]
                    xdd = xdds[q]
                    if c > 0:
                        tk = None
                        for gi, g in enumerate((0, 2, 1, 3)):
                            p0 = (g % 2) * 64
                            ob_ = 6 + g // 2
                            tk2 = pe(lambda: nc.tensor.matmul(PB[ob_][:, (g % 2) * 256:(g % 2 + 1) * 256],
                                                              lhsT=CT[p0:p0 + 64, g // 2, csl],
                                                              rhs=S_bf[p0:p0 + 64, g // 2, :], start=(g % 2 == 0),
                                                              stop=(g % 2 == 1)),
                                     r=[b_X, b_Sbf], w=[b_PB[ob_]], inc=(gi >= 1),
                                     selfwait=(tk if gi == 2 else None))
                            if gi == 1:
                                tk = tk2
                        for hb in range(2):
                            dve(lambda: nc.vector.tensor_tensor(
                                out=ysb[:, hb * 8:(hb + 1) * 8, :],
                                in0=PB[6 + hb][:].rearrange("p (h d) -> p h d", d=64),
                                in1=ea[:, hb * 8:(hb + 1) * 8].unsqueeze(2).to_broadcast([128, 8, 64]), op=ALU.mult),
                                r=[b_PB[6 + hb], b_sm], w=[b_y])
                            dve(lambda: nc.vector.tensor_tensor(out=ysb[:, hb * 8:(hb + 1) * 8, :], in0=Yv[hb],
                                                                in1=ysb[:, hb * 8:(hb + 1) * 8, :], op=ALU.add),
                                r=[b_PB[4 + hb]], w=[b_y])
                    else:
                        for hb in range(2):
                            dve(lambda: nc.vector.tensor_copy(out=ysb[:, hb * 8:(hb + 1) * 8, :], in_=Yv[hb]),
                                r=[b_PB[4 + hb]], w=[b_y])
                    yield
                    dve(lambda: nc.vector.tensor_tensor(out=ysb[:].rearrange("p h d -> p (h d)"),
                                                        in0=ysb[:].rearrange("p h d -> p (h d)"), in1=szb[q][:],
                                                        op=ALU.mult), r=[b_szb[q]], w=[b_y])
                    yf = ysb[:].rearrange("p h d -> p (h d)")
                    for g in range(4):
                        act(lambda: nc.scalar.activation(out=junkf[:], in_=yf[:, g * 256:(g + 1) * 256], func=AF.Square,
                                                         accum_out=ssq[:, g:g + 1]), r=[b_y], w=[b_ob])
                    pool(lambda: nc.gpsimd.tensor_scalar(out=rs4[:], in0=ssq[:], scalar1=1.0 / 256, scalar2=EPS,
                                                         op0=ALU.mult, op1=ALU.add), w=[b_ob])
                    pool(lambda: nc.gpsimd.tensor_tensor(out=rs4[:], in0=rs4[:], in1=mhalf[:, 0:4], op=ALU.pow),
                         r=[b_const], w=[b_ob])
                    yield
                    if c < NT - 1:
                        for g in range(4):
                            p0 = (g % 2) * 64
                            pe(lambda: nc.tensor.matmul(PB[7][p0:p0 + 64, (g // 2) * 256:(g // 2 + 1) * 256],
                                                        lhsT=B_tm[:, c, g * 64:(g + 1) * 64],
                                                        rhs=xdd[:, 4 * g:4 * g + 4, :].rearrange("p h d -> p (h d)"),
                                                        start=(g < 2), stop=(g >= 2)),
                               r=[b_X, b_xdds[q]], w=[b_PB[7]], inc=(g == 3))
                        Sv = S_sb[:].rearrange("p s (h d) -> p (s h) d", d=64)
                        if c == 0:
                            dve(lambda: nc.vector.tensor_copy(out=S_sb[:].rearrange("p s f -> p (s f)"), in_=PB[7][:]),
                                r=[b_PB[7]], w=[b_S])
                        else:
                            dve(lambda: nc.vector.tensor_tensor(
                                out=Sv, in0=Sv,
                                in1=eAS[:].rearrange("p s h -> p (s h)").unsqueeze(2).to_broadcast([128, 8, 64]),
                                op=ALU.mult), r=[b_sm, b_Sbf], w=[b_S])
                            dve(lambda: nc.vector.tensor_tensor(out=S_sb[:].rearrange("p s f -> p (s f)"),
                                                                in0=S_sb[:].rearrange("p s f -> p (s f)"),
                                                                in1=PB[7][:], op=ALU.add),
                                r=[b_PB[7]], w=[b_S])
                        act(lambda: nc.scalar.copy(out=S_bf[:], in_=S_sb[:]), r=[b_S], w=[b_Sbf])
                    yield
                    for g in range(4):
                        dve(lambda: nc.vector.scalar_tensor_tensor(out=ob_sb[:, g * 256:(g + 1) * 256],
                                                                   in0=yf[:, g * 256:(g + 1) * 256],
                                                                   scalar=rs4[:, g:g + 1],
                                                                   in1=snw_b[:, g * 256:(g + 1) * 256],
                                                                   op0=ALU.mult, op1=ALU.mult),
                            r=[b_y, b_ptab], w=[b_ob])
                    yield
                    tv = bfv(7)
                    for cc in range(8):
                        pe(lambda: nc.tensor.transpose(out=tv[:, cc, :], in_=ob_sb[:, cc * 128:(cc + 1) * 128],
                                                       identity=ident[:]),
                           r=[b_ob, b_const], w=[b_PB[7]], inc=(cc == 7))
                    act(lambda: nc.scalar.copy(out=obT[:, :, csl], in_=tv[:, :, :]), r=[b_PB[7]], w=[b_obT[c]])

                def front(c):
                    yield from head(c)
                    yield from groups(c)

                def interleave(gens):
                    gens = list(gens)
                    while gens:
                        for g_ in list(gens):
                            try:
                                next(g_)
                            except StopIteration:
                                gens.remove(g_)

                interleave([front(0)])
                for c in range(NT):
                    gl = [tail(c)]
                    if c + 1 < NT:
                        gl.append(front(c + 1))
                    interleave(gl)
                    if c == NT - 2:
                        dpool(out=wG0[:, :, 0, :], in_=win_d[:, 4180:4180 + 128].rearrange("(k p) n -> p k n", p=128),
                              w=[b_wslot[0]])
                        dpool(out=wG0[:, :, 1, :], in_=win_d[:, 4180 + 1024:4180 + 1152]
                              .rearrange("(k p) n -> p k n", p=128), w=[b_wslot[0]])
                        dpool(out=Wpa0, in_=wpa_d[:, 0:128].rearrange("(k p) n -> p k n", p=128), w=[b_wslot[0]])
                        dpool(out=Wpb0, in_=wpb_d[:, 0:128].rearrange("(k p) n -> p k n", p=128), w=[b_wslot[0]])
                if "obT" in dbg:
                    k.dump("d_obT", obT[:], b_obT)
                k.barrier()
        if stop_after.startswith("B"):
            return nc, k


        with ExitStack() as esM:
            Wout = k.sb("Wout", [128, 8, D], BF16, esM)
            b_W = Buf()
            gbT = k.sb("gbT_sb", [128, 16], F32, esM)
            fnw_b = k.sb("fnw_b", [128, D], F32, esM)
            b_ct = Buf()
            dsync(out=gbT[:], in_=gbT_d[:, :], w=[b_ct])
            dsync(out=fnw_b[:], in_=fnw_d[0, :].partition_broadcast(128), w=[b_ct])
            mT = k.sb("mT", [128, 8, S], BF16, esM)
            b_mT = [Buf() for _ in range(4)]
            with ExitStack() as esM1:
                wG = [k.sb(f"wG{j}", [128, 8, 2, 128], BF16, esM1) for j in range(2)]
                Wpa = [k.sb(f"Wpa{j}", [128, 4, 128], BF16, esM1) for j in range(2)]
                Wpb = [k.sb(f"Wpb{j}", [128, 8, 128], BF16, esM1) for j in range(2)]
                b_wG = [Buf(), Buf()]
                gA = [k.sb(f"gA{j}", [128, 512], F32, esM1) for j in range(2)]
                gB = [k.sb(f"gB{j}", [128, 512], F32, esM1) for j in range(2)]
                b_g = [Buf(), Buf()]
                t1 = [k.sb(f"t1{j}", [128, 512], F32, esM1) for j in range(2)]
                t2 = [k.sb(f"t2{j}", [128, 512], F32, esM1) for j in range(2)]
                b_t = [Buf(), Buf()]
                it = 0
                for m in range(8):
                    wq = m % 2
                    if m == 0:
                        gw_, pa_, pb_, bw_ = wG0, Wpa0, Wpb0, b_wslot[0]
                    else:
                        gw_, pa_, pb_, bw_ = wG[wq][:], Wpa[wq][:], Wpb[wq][:], b_wG[wq]
                        dpool(out=gw_[:, :, 0, :], in_=win_d[:, 4180 + m * 128:4180 + (m + 1) * 128]
                              .rearrange("(k p) n -> p k n", p=128), w=[bw_])
                        dpool(out=gw_[:, :, 1, :], in_=win_d[:, 4180 + (8 + m) * 128:4180 + (9 + m) * 128]
                              .rearrange("(k p) n -> p k n", p=128), w=[bw_])
                        dpool(out=pa_, in_=wpa_d[:, m * 128:(m + 1) * 128].rearrange("(k p) n -> p k n", p=128),
                              w=[bw_])
                        dpool(out=pb_, in_=wpb_d[:, m * 128:(m + 1) * 128].rearrange("(k p) n -> p k n", p=128),
                              w=[bw_])
                    if m == 0:
                        for hf in range(2):
                            cs_ = slice(hf * 512, (hf + 1) * 512)
                            dpool(out=Wout[:, :, cs_], in_=wout_d[:, cs_].rearrange("(k p) n -> p k n", p=128),
                                  w=[b_W])
                    for tc in range(4):
                        q = it % 2
                        it += 1
                        b0 = 4 * q
                        ts_ = slice(tc * 512, (tc + 1) * 512)
                        for kc in range(8):
                            pe(lambda: nc.tensor.matmul(PB[b0][:], lhsT=gw_[:, kc, 0, :], rhs=hT[:, kc, ts_],
                                                        start=(kc == 0), stop=(kc == 7)),
                               r=b_hT[tc * 4:(tc + 1) * 4] + [bw_], w=[b_PB[b0]], inc=(kc == 7))
                        for kc in range(8):
                            pe(lambda: nc.tensor.matmul(PB[b0 + 1][:], lhsT=gw_[:, kc, 1, :], rhs=hT[:, kc, ts_],
                                                        start=(kc == 0), stop=(kc == 7)),
                               r=b_hT[tc * 4:(tc + 1) * 4] + [bw_], w=[b_PB[b0 + 1]], inc=(kc == 7))
                        for kc in range(4):
                            pe(lambda: nc.tensor.matmul(PB[b0 + 2][:], lhsT=pa_[:, kc, :],
                                                        rhs=oaT[:, kc, ts_], start=(kc == 0), stop=(kc == 3)),
                               r=b_oaT[tc * 4:(tc + 1) * 4] + [bw_], w=[b_PB[b0 + 2]], inc=(kc == 3))
                        for kc in range(8):
                            pe(lambda: nc.tensor.matmul(PB[b0 + 3][:], lhsT=pb_[:, kc, :],
                                                        rhs=obT[:, kc, ts_], start=(kc == 0), stop=(kc == 7)),
                               r=b_obT[tc * 4:(tc + 1) * 4] + [bw_], w=[b_PB[b0 + 3]], inc=(kc == 7))
                        act(lambda: nc.scalar.activation(out=gA[q][:], in_=PB[b0][:], func=AF.Sigmoid,
                                                         bias=gbT[:, m:m + 1]), r=[b_PB[b0], b_ct], w=[b_g[q]])
                        act(lambda: nc.scalar.activation(out=gB[q][:], in_=PB[b0 + 1][:], func=AF.Sigmoid,
                                                         bias=gbT[:, 8 + m:9 + m]), r=[b_PB[b0 + 1], b_ct], w=[b_g[q]])
                        dve(lambda: nc.vector.tensor_tensor(out=t1[q][:], in0=PB[b0 + 2][:], in1=gA[q][:], op=ALU.mult),
                            r=[b_PB[b0 + 2], b_g[q]], w=[b_t[q]])
                        dve(lambda: nc.vector.tensor_tensor(out=t2[q][:], in0=PB[b0 + 3][:], in1=gB[q][:], op=ALU.mult),
                            r=[b_PB[b0 + 3], b_g[q]], w=[b_t[q]])
                        dve(lambda: nc.vector.tensor_tensor(out=mT[:, m, ts_], in0=t1[q][:], in1=t2[q][:], op=ALU.add),
                            r=[b_t[q]], w=[b_mT[tc]])
                k.barrier()
            if "mT" in dbg:
                k.dump("d_mT", mT[:], b_mT)
            ck("Cm")
            xr = [k.sb(f"xr{j}", [128, D], F32, esM) for j in range(2)]
            b_xr = [Buf(), Buf()]
            xo = [k.sb(f"xo{j}", [128, D], F32, esM) for j in range(2)]
            b_xo = [Buf(), Buf()]
            fo = [k.sb(f"fo{j}", [128, D], F32, esM) for j in range(2)]
            b_fo = [Buf(), Buf()]
            junkc = k.sb("junkc", [128, D], BF16, esM)
            fss = k.sb("fss", [128, NT], F32, esM)
            b_fs = Buf()
            dpool(out=xr[0][:], in_=x_d[0:128, :], w=[b_xr[0]])
            b_fsi = [Buf() for _ in range(NT)]

            def out1(i):
                q = i % 2
                tsl = slice(i * 128, (i + 1) * 128)
                if i + 1 < NT:
                    dpool(out=xr[1 - q][:], in_=x_d[(i + 1) * 128:(i + 2) * 128, :], w=[b_xr[1 - q]])
                for hf in range(2):
                    bk = 2 * q + hf
                    for kc in range(8):
                        pe(lambda: nc.tensor.matmul(PB[bk][:], lhsT=mT[:, kc, tsl], rhs=Wout[:, kc, hf * 512:(hf + 1) * 512],
                                                    start=(kc == 0), stop=(kc == 7)),
                           r=[b_mT[i // 4], b_W], w=[b_PB[bk]], inc=(kc == 7))
                    dve(lambda: nc.vector.tensor_tensor(out=xo[q][:, hf * 512:(hf + 1) * 512], in0=PB[bk][:],
                                                        in1=xr[q][:, hf * 512:(hf + 1) * 512], op=ALU.add),
                        r=[b_PB[bk], b_xr[q]], w=[b_xo[q]])
                act(lambda: nc.scalar.activation(out=junkc[:], in_=xo[q][:], func=AF.Square, accum_out=fss[:, i:i + 1]),
                    r=[b_xo[q]], w=[b_fs, b_fsi[i]])
                act(lambda: nc.scalar.activation(out=fss[:, i:i + 1], in_=fss[:, i:i + 1], func=AF.Sqrt, scale=1.0 / D,
                                                 bias=EPS), w=[b_fsi[i]])

            def out2(i):
                q = i % 2
                tsl = slice(i * 128, (i + 1) * 128)
                dve(lambda: nc.vector.reciprocal(out=fss[:, i:i + 1], in_=fss[:, i:i + 1]), w=[b_fsi[i]])
                dve(lambda: nc.vector.scalar_tensor_tensor(out=fo[q][:], in0=xo[q][:], scalar=fss[:, i:i + 1],
                                                           in1=fnw_b[:], op0=ALU.mult, op1=ALU.mult),
                    r=[b_xo[q], b_ct, b_fsi[i]], w=[b_fo[q]])
                dsync(out=out_d[tsl, :], in_=fo[q][:], r=[b_fo[q]])

            out1(0)
            for i in range(NT):
                if i + 1 < NT:
                    out1(i + 1)
                out2(i)
            k.barrier()

        k.barrier()
    return nc, k


_NC_CACHE = {}


def kernel(x, positions, norm_w, w_in, gate_bias, conv_w, conv_b, dt_bias, a_log, d_skip,
           ssm_norm_w, w_branch_a, w_branch_b, w_out, final_norm_w):
    f32 = np.float32
    x = np.asarray(x, dtype=f32)
    positions = np.asarray(positions).astype(np.int32)
    nb = x.shape[0]
    assert nb == 8 and x.shape[1] == S and x.shape[2] == D
    if "nc" not in _NC_CACHE:
        _NC_CACHE["nc"] = build()[0]
    nc = _NC_CACHE["nc"]
    invf = (500000.0 ** (-np.arange(0, 16, 2, dtype=f32) / 16)).astype(f32)
    shared = {
        "invf": np.ascontiguousarray(np.broadcast_to(invf, (128, 8))),
        "norm_w": np.ascontiguousarray(np.asarray(norm_w, f32).reshape(1, D)),
        "w_in": np.ascontiguousarray(np.asarray(w_in, f32)[0]),
        "cwT": np.ascontiguousarray(np.asarray(conv_w, f32)[0].reshape(4, 12, 128).transpose(2, 1, 0)),
        "cbT": np.ascontiguousarray(np.asarray(conv_b, f32)[0].reshape(12, 128).T),
        "dt_bias": np.ascontiguousarray(np.asarray(dt_bias, f32).reshape(1, 16)),
        "a_log": np.ascontiguousarray(np.asarray(a_log, f32).reshape(1, 16)),
        "d_skip": np.ascontiguousarray(np.asarray(d_skip, f32).reshape(1, 16)),
        "ssm_norm_w": np.ascontiguousarray(np.asarray(ssm_norm_w, f32).reshape(1, D)),
        "gbT": np.ascontiguousarray(np.asarray(gate_bias, f32)[0].reshape(16, 128).T),
        "w_branch_a": np.ascontiguousarray(np.asarray(w_branch_a, f32)[0]),
        "w_branch_b": np.ascontiguousarray(np.asarray(w_branch_b, f32)[0]),
        "w_out": np.ascontiguousarray(np.asarray(w_out, f32)[0]),
        "final_norm_w": np.ascontiguousarray(np.asarray(final_norm_w, f32).reshape(1, D)),
    }
    in_maps = []
    for b in range(nb):
        m = dict(shared)
        m["x"] = np.ascontiguousarray(x[b])
        m["posT"] = np.ascontiguousarray(positions[b].reshape(NT, 128).T)
        in_maps.append(m)
    res = run_bass_kernel_spmd(nc, in_maps, core_ids=list(range(nb)))
    out = np.stack([np.asarray(res.results[b]["out"], dtype=f32) for b in range(nb)], axis=0)
    return out
```

```python
import numpy as np
import math
import concourse.bass as bass
import concourse.mybir as mybir
from concourse.bass_utils import run_bass_kernel_spmd
from contextlib import ExitStack

F32 = mybir.dt.float32
BF16 = mybir.dt.bfloat16
I32 = mybir.dt.int32
ALU = mybir.AluOpType
AF = mybir.ActivationFunctionType
AX = mybir.AxisListType

S = 2048
D = 1024
NT = 16
INW = 6228
EPS = 1e-6
NITER = 9
A_SRC = [(0, 512, 0), (512, 640, 512), (1280, 1536, 640), (1536, 1600, 896), (1600, 1604, 960),
         (640, 768, 964), (768, 1280, 1092)]
NA = 1604


class Buf:
    __slots__ = ("w", "r", "name", "excl")

    def __init__(self, name="", excl=False):
        self.w = {}
        self.r = {}
        self.name = name
        self.excl = excl


def _merge(deps, d):
    for k, (s, v) in d.items():
        if k not in deps or deps[k][1] < v:
            deps[k] = (s, v)


class Eng:
    def __init__(self, K, name, eng, selfdep=True):
        self.K = K
        self.name = name
        self.eng = eng
        self.selfdep = selfdep
        self.sem = K.es.enter_context(K.nc.semaphore("s_" + name))
        self.cnt = 0
        self.waited = {}
        self.pending = False

    def wait_deps(self, deps):
        for k, (s, v) in deps.items():
            if k == self.name and not self.selfdep:
                continue
            if self.waited.get(k, 0) < v:
                self.eng.wait_ge(s, v)
                self.waited[k] = v

    def __call__(self, fn, r=(), w=(), inc=True, extra=(), selfwait=None):
        if selfwait is not None:
            assert selfwait[0] == self.name
            if self.waited.get("self", 0) < selfwait[2]:
                self.eng.wait_ge(selfwait[1], selfwait[2])
                self.waited["self"] = selfwait[2]
        w = list(w) + [b for b in r if b.excl]
        r = [b for b in r if not b.excl]
        deps = {}
        for b in r:
            _merge(deps, b.w)
        for b in w:
            _merge(deps, b.w)
            _merge(deps, b.r)
        for t in extra:
            _merge(deps, {t[0]: (t[1], t[2])})
        self.wait_deps(deps)
        ins = fn()
        if inc:
            self.cnt += 1
            ins.then_inc(self.sem, 1)
            tok = (self.sem, self.cnt)
            self.pending = False
        else:
            tok = (self.sem, self.cnt + 1)
            self.pending = True
        for b in r:
            _merge(b.r, {self.name: tok})
        for b in w:
            b.w = {self.name: tok}
            b.r = {}
        return (self.name,) + tok


class DmaQ:
    def __init__(self, K, name, waiter, nsem=8):
        self.K = K
        self.name = name
        self.waiter = waiter
        self.sems = [K.es.enter_context(K.nc.semaphore(f"d_{name}{j}")) for j in range(nsem)]
        self.vals = [0] * nsem
        self.idx = 0

    def __call__(self, out, in_, r=(), w=(), extra=(), **kw):
        deps = {}
        for b in r:
            _merge(deps, b.w)
        for b in w:
            _merge(deps, b.w)
            _merge(deps, b.r)
        for t in extra:
            _merge(deps, {t[0]: (t[1], t[2])})
        k = self.idx
        self.idx = (k + 1) % len(self.sems)
        key = f"{self.name}{k}"
        if self.vals[k] > 0:
            _merge(deps, {key: (self.sems[k], self.vals[k])})
        self.waiter.wait_deps(deps)
        ins = self.waiter.eng.dma_start(out=out, in_=in_, **kw)
        self.vals[k] += 16
        ins.then_inc(self.sems[k], 16)
        tok = (self.sems[k], self.vals[k])
        for b in r:
            _merge(b.r, {key: tok})
        for b in w:
            b.w = {key: tok}
            b.r = {}
        return (key,) + tok


class K:
    def __init__(self, nc, es):
        self.nc = nc
        self.es = es
        self.pe = Eng(self, "pe", nc.tensor, selfdep=False)
        self.act = Eng(self, "act", nc.scalar)
        self.dve = Eng(self, "dve", nc.vector)
        self.pool = Eng(self, "pool", nc.gpsimd)
        self.sp = Eng(self, "sp", nc.sync)
        self.engs = [self.pe, self.act, self.dve, self.pool, self.sp]
        self.dsync = DmaQ(self, "qs", self.sp, nsem=8)
        self.dpool = DmaQ(self, "qp", self.pool, nsem=8)
        self.dqs = [self.dsync, self.dpool]
        self.dumps = []

    def sb(self, name, shape, dt, es=None):
        t = (es or self.es).enter_context(self.nc.sbuf_tensor(name, list(shape), dt))
        return t

    def ps(self, name, shape, dt, es=None):
        return (es or self.es).enter_context(self.nc.psum_tensor(name, list(shape), dt))

    def barrier(self):
        deps = {}
        for e in self.engs:
            assert not e.pending, e.name
            if e.cnt > 0:
                deps[e.name] = (e.sem, e.cnt)
        for q in self.dqs:
            for j, s in enumerate(q.sems):
                if q.vals[j] > 0:
                    deps[f"{q.name}{j}"] = (s, q.vals[j])
        for e in self.engs:
            e.wait_deps(deps)

    def dump(self, name, ap, buf):
        d = self.nc.dram_tensor(name, list(ap.shape), ap.dtype, kind="ExternalOutput").ap()
        self.dsync(out=d, in_=ap, r=(buf if isinstance(buf, (list, tuple)) else [buf]))
        self.dumps.append(name)


class Stop(Exception):
    pass


def build(stop_after="all", dbg=()):
    try:
        return _build(stop_after, dbg)
    except Stop as s:
        return s.args


def _build(stop_after="all", dbg=()):
    nc = bass.Bass("TRN2", target_bir_lowering=False)
    dbg = set(dbg)
    x_d = nc.dram_tensor("x", [S, D], F32, kind="ExternalInput").ap()
    posT_d = nc.dram_tensor("posT", [128, NT], I32, kind="ExternalInput").ap()
    invf_d = nc.dram_tensor("invf", [128, 8], F32, kind="ExternalInput").ap()
    normw_d = nc.dram_tensor("norm_w", [1, D], F32, kind="ExternalInput").ap()
    win_d = nc.dram_tensor("w_in", [D, INW], F32, kind="ExternalInput").ap()
    out_d = nc.dram_tensor("out", [S, D], F32, kind="ExternalOutput").ap()
    cwT_d = nc.dram_tensor("cwT", [128, 12, 4], F32, kind="ExternalInput").ap()
    cbT_d = nc.dram_tensor("cbT", [128, 12], F32, kind="ExternalInput").ap()
    dtb_d = nc.dram_tensor("dt_bias", [1, 16], F32, kind="ExternalInput").ap()
    alog_d = nc.dram_tensor("a_log", [1, 16], F32, kind="ExternalInput").ap()
    dsk_d = nc.dram_tensor("d_skip", [1, 16], F32, kind="ExternalInput").ap()
    snw_d = nc.dram_tensor("ssm_norm_w", [1, D], F32, kind="ExternalInput").ap()
    gbT_d = nc.dram_tensor("gbT", [128, 16], F32, kind="ExternalInput").ap()
    wpa_d = nc.dram_tensor("w_branch_a", [512, D], F32, kind="ExternalInput").ap()
    wpb_d = nc.dram_tensor("w_branch_b", [D, D], F32, kind="ExternalInput").ap()
    wout_d = nc.dram_tensor("w_out", [D, D], F32, kind="ExternalInput").ap()
    fnw_d = nc.dram_tensor("final_norm_w", [1, D], F32, kind="ExternalInput").ap()

    with ExitStack() as es:
        k = K(nc, es)
        pe, act, dve, pool, sp = k.pe, k.act, k.dve, k.pool, k.sp
        dsync, dpool = k.dsync, k.dpool

        def ck(name):
            if stop_after == name:
                k.barrier()
                raise Stop(nc, k)

        ident = k.sb("ident", [128, 128], BF16)
        Uf = k.sb("Uf", [128, 128], F32)
        Ub = k.sb("Ub", [128, 128], BF16)
        cbias = k.sb("cbias", [128, 128], F32)
        pow2 = k.sb("pow2", [128, NITER + 2], F32)
        negU = k.sb("negU", [128, 128], BF16)
        negUb = k.sb("negUb", [128, 128], BF16)
        mhalf = k.sb("mhalf", [128, 16], F32)
        b_const = Buf("const")
        pool(lambda: nc.gpsimd.memset(ident[:], 1.0), w=[b_const])
        pool(lambda: nc.gpsimd.affine_select(out=ident[:], in_=ident[:], pattern=[[-1, 128]],
                                             compare_op=ALU.is_equal, fill=0.0, base=0, channel_multiplier=1),
             w=[b_const])
        pool(lambda: nc.gpsimd.memset(Uf[:], 1.0), w=[b_const])
        pool(lambda: nc.gpsimd.affine_select(out=Uf[:], in_=Uf[:], pattern=[[1, 128]], compare_op=ALU.is_ge,
                                             fill=0.0, base=0, channel_multiplier=-1), w=[b_const])
        pool(lambda: nc.gpsimd.tensor_copy(out=Ub[:], in_=Uf[:]), w=[b_const])
        pool(lambda: nc.gpsimd.tensor_scalar(out=negUb[:], in0=Ub[:], scalar1=-1.0, scalar2=None, op0=ALU.mult),
             w=[b_const])
        pool(lambda: nc.gpsimd.memset(mhalf[:], -0.5), w=[b_const])
        pool(lambda: nc.gpsimd.memset(negU[:], 0.0), w=[b_const])
        pool(lambda: nc.gpsimd.affine_select(out=negU[:], in_=negU[:], pattern=[[1, 128]], compare_op=ALU.is_ge,
                                             fill=-30000.0, base=0, channel_multiplier=-1), w=[b_const])
        pool(lambda: nc.gpsimd.memset(cbias[:], 0.0), w=[b_const])
        pool(lambda: nc.gpsimd.affine_select(out=cbias[:], in_=cbias[:], pattern=[[-1, 128]], compare_op=ALU.is_ge,
                                             fill=-1e30, base=0, channel_multiplier=1), w=[b_const])
        for j in range(NITER + 2):
            pool(lambda: nc.gpsimd.memset(pow2[:, j:j + 1], 2.0 ** (-j)), w=[b_const])
        pool(lambda: nc.gpsimd.memset(pow2[:, 0:1], 1.0), w=[b_const])

        hT = k.sb("hT", [128, 8, S], BF16)
        b_hT = [Buf(f"hT{i}") for i in range(NT)]
        wBC = k.sb("wBC", [128, 8, 1024], BF16)
        b_wslot = [Buf(), Buf()]
        wG0 = wBC[:, :, 0:256].rearrange("p k (h n) -> p k h n", h=2)
        Wpb0 = wBC[:, :, 256:384]
        Wpa0 = wBC[:, 0:4, 384:512]
        CONV0 = 2628
        oaT = k.sb("oaT", [128, 4, S], BF16)
        b_oaT = [Buf() for _ in range(NT)]
        esWA = ExitStack()
        wA = k.sb("wA", [128, 8, NA], BF16, esWA)
        b_wA = Buf()
        for (c0, c1, dst) in A_SRC:
            dpool(out=wA[:, :, dst:dst + (c1 - c0)],
                  in_=win_d[:, c0:c1].rearrange("(k p) n -> p k n", p=128), w=[b_wA])

        with ExitStack() as es0:
            normw_b = k.sb("normw_b", [128, D], F32, es0)
            b_normw = Buf()
            dsync(out=normw_b[:], in_=normw_d[0, :].partition_broadcast(128), w=[b_normw])
            xt = k.sb("xt_all", [128, NT, D], F32, es0)
            b_xt = [Buf() for _ in range(NT)]
            xn = [k.sb(f"xn{j}", [128, D], BF16, es0) for j in range(3)]
            b_xn = [Buf(), Buf(), Buf()]
            junk = k.sb("junk0", [128, D], BF16, es0)
            b_junk = Buf()
            ss = k.sb("ss", [128, NT], F32, es0)
            sd = k.sb("sd", [128, NT], F32, es0)
            rstd = k.sb("rstd", [128, NT], F32, es0)
            b_ssg = [Buf() for _ in range(4)]
            pt = [k.ps(f"pt{j}", [128, 8, 128], BF16, es0) for j in range(2)]
            b_pt = [Buf(excl=True), Buf(excl=True)]
            for i in range(NT):
                (dsync if i % 2 == 0 else dsync)(out=xt[:, i, :], in_=x_d[i * 128:(i + 1) * 128, :], w=[b_xt[i]])

            def p0_stats(gq):
                for i in range(4 * gq, 4 * gq + 4):
                    act(lambda: nc.scalar.activation(out=junk[:], in_=xt[:, i, :], func=AF.Square,
                                                     accum_out=ss[:, i:i + 1]),
                        r=[b_xt[i]], w=[b_junk, b_ssg[gq]])
                act(lambda: nc.scalar.activation(out=sd[:, 4 * gq:4 * gq + 4], in_=ss[:, 4 * gq:4 * gq + 4],
                                                 func=AF.Sqrt, scale=1.0 / D, bias=EPS), w=[b_ssg[gq]])
                dve(lambda: nc.vector.reciprocal(out=rstd[:, 4 * gq:4 * gq + 4], in_=sd[:, 4 * gq:4 * gq + 4]),
                    w=[b_ssg[gq]])

            def p0_apply(gq):
                for i in range(4 * gq, 4 * gq + 4):
                    j = i % 3
                    jp = i % 2
                    dve(lambda: nc.vector.scalar_tensor_tensor(out=xn[j][:], in0=xt[:, i, :], scalar=rstd[:, i:i + 1],
                                                               in1=normw_b[:], op0=ALU.mult, op1=ALU.mult),
                        r=[b_xt[i], b_ssg[gq], b_normw], w=[b_xn[j]])
                    for c in range(8):
                        pe(lambda: nc.tensor.transpose(out=pt[jp][:, c, :], in_=xn[j][:, c * 128:(c + 1) * 128],
                                                       identity=ident[:]),
                           r=[b_xn[j], b_const], w=[b_pt[jp]], inc=(c == 7))
                    act(lambda: nc.scalar.copy(out=hT[:, :, i * 128:(i + 1) * 128], in_=pt[jp][:]),
                        r=[b_pt[jp]], w=[b_hT[i]])

            p0_stats(0)
            for gq in range(4):
                if gq + 1 < 4:
                    p0_stats(gq + 1)
                p0_apply(gq)
            k.barrier()
        if "hT" in dbg:
            k.dump("d_hT", hT[:], b_hT[NT - 1])
        if stop_after == "p0":
            k.barrier()
            return nc, k


        PB = [k.ps(f"pb{j}", [128, 512], F32) for j in range(8)]
        b_PB = [Buf(f"pb{j}", excl=True) for j in range(8)]

        def bfv(j):
            return PB[j][:].bitcast(BF16).rearrange("p (s t) -> p s t", t=128)

        with ExitStack() as esA:
            dpool(out=wBC[:, :, 0:512], in_=win_d[:, CONV0:CONV0 + 512].rearrange("(k p) n -> p k n", p=128),
                  w=[b_wslot[0]])
            dpool(out=wBC[:, :, 512:1024], in_=win_d[:, CONV0 + 512:CONV0 + 1024].rearrange("(k p) n -> p k n", p=128),
                  w=[b_wslot[1]])
            posi = k.sb("posi", [128, NT], I32, esA)
            posf = k.sb("posf", [128, NT], F32, esA)
            invf = k.sb("invf_sb", [128, 8], F32, esA)
            ang = k.sb("ang", [128, NT, 8], F32, esA)
            cos_t = k.sb("cos_t", [128, NT, 8], F32, esA)
            sin_t = k.sb("sin_t", [128, NT, 8], F32, esA)
            ry = k.sb("ry", [128, NT, 8], F32, esA)
            rki = k.sb("rki", [128, NT, 8], I32, esA)
            rkf = k.sb("rkf", [128, NT, 8], F32, esA)
            rg = k.sb("rg", [128, NT, 8], F32, esA)
            b_tab = Buf()
            dsync(out=posi[:], in_=posT_d[:, :], w=[b_tab])
            dsync(out=invf[:], in_=invf_d[:, :], w=[b_tab])
            dve(lambda: nc.vector.tensor_copy(out=posf[:], in_=posi[:]), r=[b_tab], w=[b_tab])
            dve(lambda: nc.vector.tensor_tensor(out=ang[:], in0=posf[:].unsqueeze(2).to_broadcast([128, NT, 8]),
                                                in1=invf[:].unsqueeze(1).to_broadcast([128, NT, 8]), op=ALU.mult),
                r=[b_tab], w=[b_tab])
            TWO_PI = 2.0 * math.pi
            for (dst_t, off) in ((sin_t, 0.0), (cos_t, 0.25)):
                dve(lambda: nc.vector.tensor_scalar(out=ry[:], in0=ang[:], scalar1=1.0 / TWO_PI, scalar2=off,
                                                    op0=ALU.mult, op1=ALU.add), r=[b_tab], w=[b_tab])
                dve(lambda: nc.vector.tensor_copy(out=rki[:], in_=ry[:]), r=[b_tab], w=[b_tab])
                dve(lambda: nc.vector.tensor_copy(out=rkf[:], in_=rki[:]), r=[b_tab], w=[b_tab])
                dve(lambda: nc.vector.tensor_tensor(out=ry[:], in0=ry[:], in1=rkf[:], op=ALU.subtract),
                    r=[b_tab], w=[b_tab])
                dve(lambda: nc.vector.tensor_scalar(out=rg[:], in0=ry[:], scalar1=0.5, scalar2=None, op0=ALU.is_ge),
                    r=[b_tab], w=[b_tab])
                dve(lambda: nc.vector.tensor_tensor(out=ry[:], in0=ry[:], in1=rg[:], op=ALU.subtract),
                    r=[b_tab], w=[b_tab])
                dve(lambda: nc.vector.tensor_scalar(out=rg[:], in0=ry[:], scalar1=-0.5, scalar2=None, op0=ALU.is_lt),
                    r=[b_tab], w=[b_tab])
                dve(lambda: nc.vector.tensor_tensor(out=ry[:], in0=ry[:], in1=rg[:], op=ALU.add),
                    r=[b_tab], w=[b_tab])
                act(lambda: nc.scalar.activation(out=dst_t[:], in_=ry[:], func=AF.Sin, scale=TWO_PI * (1.0 - 1e-6)),
                    r=[b_tab], w=[b_tab])
            ck("Atab")
            if "rope" in dbg:
                k.dump("d_cos", cos_t[:], b_tab)
                k.dump("d_sin", sin_t[:], b_tab)

            qk_sb = [k.sb(f"qk_sb{j}", [128, 15, 64], BF16, esA) for j in range(2)]
            b_qk = [Buf(), Buf()]
            rt = [k.sb(f"rt{j}", [128, 15, 8], F32, esA) for j in range(4)]
            b_rt = Buf()
            rsrc = k.sb("rsrc", [128, 15, 16], F32, esA)
            b_rsrc = Buf()
            QT = [k.sb(f"QT{j}", [128, 12, 128], BF16, esA) for j in range(2)]
            b_QT = [Buf(), Buf()]
            KT = k.sb("KT", [128, 2, S], BF16, esA)
            KIT = k.sb("KIT", [128, S], BF16, esA)
            b_KT = [Buf() for _ in range(NT)]
            for j_ in range(2):
                pool(lambda: nc.gpsimd.memset(QT[j_][64:128], 0.0), w=[b_QT[j_]])
            pool(lambda: nc.gpsimd.memset(KT[64:128], 0.0), w=b_KT)
            pool(lambda: nc.gpsimd.memset(KIT[64:128], 0.0), w=b_KT)
            Vaug = k.sb("Vaug", [128, NT, 2, 65], BF16, esA)
            b_V = [Buf() for _ in range(NT)]
            wv = k.sb("wv", [128, NT, 4], F32, esA)
            b_wv = [Buf() for _ in range(NT)]
            sza = [k.sb(f"sza{j}", [128, 512], F32, esA) for j in range(2)]
            b_sza = [Buf(), Buf()]
            sc = [k.sb(f"sc{j}", [128, S], F32, esA) for j in range(2)]
            b_sc = [Buf(), Buf()]
            rl = [k.sb(f"rl{j}", [128, S], F32, esA) for j in range(2)]
            b_rl = [Buf(), Buf()]
            junkb = k.sb("junkb", [128, S], BF16, esA)
            b_junkb = Buf()
            m01 = k.sb("m01", [128, S], BF16, esA)
            b_m01 = Buf()
            maskT = [k.sb(f"maskT{j}", [128, NT, 128], BF16, esA) for j in range(2)]
            b_maskT = [Buf(), Buf()]
            Bv = k.sb("Bv", [128, 1], F32, esA)
            Bk = k.sb("Bk", [128, NITER + 2], F32, esA)
            mid = [k.sb(f"mid{j}", [128, 1], F32, esA) for j in range(2)]
            cnt = k.sb("cnt", [128, 1], F32, esA)
            dd = k.sb("dd", [128, 1], F32, esA)
            thr = k.sb("thr", [128, NT], F32, esA)
            b_bis = Buf()
            Eb = [k.sb(f"Eb{j}", [128, 512], BF16, esA) for j in range(3)]
            b_Eb = [Buf(), Buf(), Buf()]
            rinv = k.sb("rinv", [128, 4], F32, esA)
            otmp = k.sb("otmp", [128, 4, 64], F32, esA)
            rinv8 = k.sb("rinv8", [128, 8], F32, esA)
            otmp8 = k.sb("otmp8", [128, 8, 64], F32, esA)
            b_otmp = Buf()
            oa_sb = k.sb("oa_sb", [128, 512], BF16, esA)
            b_oa = Buf()
            pool(lambda: nc.gpsimd.memset(Vaug[:], 1.0), w=b_V)

            A_BANK = [(0, 0, 512), (1, 512, 452), (2, 964, 512), (3, 1476, 128)]
            T0v = bfv(4)
            T1v = bfv(5)
            ctr = {"st": 0, "ix": 0, "e": 0}

            def hdr_(i):
                return i % 2, slice(i * 128, (i + 1) * 128), (i + 1) * 128, i >= 2

            def stage1(i):
                j, tsl, L, masked = hdr_(i)

                for (bk, c0, n) in A_BANK:
                    for kc in range(8):
                        pe(lambda: nc.tensor.matmul(PB[bk][:, 0:n], lhsT=hT[:, kc, tsl], rhs=wA[:, kc, c0:c0 + n],
                                                    start=(kc == 0), stop=(kc == 7)),
                           r=[b_hT[i], b_wA], w=[b_PB[bk]], inc=(kc == 7))
                ck(f"Aproj{i}")
                p0v = PB[0][:, 0:512].rearrange("p (h d) -> p h d", d=64)
                p1v = PB[1][:, 0:448].rearrange("p (h d) -> p h d", d=64)
                act(lambda: nc.scalar.copy(out=qk_sb[j][:, 0:8, 16:64], in_=p0v[:, :, 16:64]),
                    r=[b_PB[0]], w=[b_qk[j]])
                act(lambda: nc.scalar.copy(out=qk_sb[j][:, 8:15, 16:64], in_=p1v[:, :, 16:64]),
                    r=[b_PB[1]], w=[b_qk[j]])
                ck(f"Ae1_{i}")
                act(lambda: nc.scalar.copy(out=rsrc[:, 0:8, :], in_=p0v[:, :, 0:16]), r=[b_PB[0]], w=[b_rsrc])
                act(lambda: nc.scalar.copy(out=rsrc[:, 8:15, :], in_=p1v[:, :, 0:16]), r=[b_PB[1]], w=[b_rsrc])
                nh = 15
                cb = cos_t[:, i:i + 1, :].to_broadcast([128, nh, 8])
                sb_ = sin_t[:, i:i + 1, :].to_broadcast([128, nh, 8])
                x1 = rsrc[:, :, 0:8]
                x2 = rsrc[:, :, 8:16]
                dve(lambda: nc.vector.tensor_tensor(out=rt[0][:], in0=x1, in1=cb, op=ALU.mult),
                    r=[b_rsrc, b_tab], w=[b_rt])
                dve(lambda: nc.vector.tensor_tensor(out=rt[1][:], in0=x2, in1=sb_, op=ALU.mult),
                    r=[b_rsrc, b_tab], w=[b_rt])
                dve(lambda: nc.vector.tensor_tensor(out=qk_sb[j][:, :, 0:8], in0=rt[0][:], in1=rt[1][:], op=ALU.subtract),
                    r=[b_rt], w=[b_qk[j]])
                dve(lambda: nc.vector.tensor_tensor(out=rt[2][:], in0=x2, in1=cb, op=ALU.mult),
                    r=[b_rsrc, b_tab], w=[b_rt])
                dve(lambda: nc.vector.tensor_tensor(out=rt[3][:], in0=x1, in1=sb_, op=ALU.mult),
                    r=[b_rsrc, b_tab], w=[b_rt])
                dve(lambda: nc.vector.tensor_tensor(out=qk_sb[j][:, :, 8:16], in0=rt[2][:], in1=rt[3][:], op=ALU.add),
                    r=[b_rt], w=[b_qk[j]])
                ck(f"Ae2_{i}")
                act(lambda: nc.scalar.copy(out=wv[:, i, :], in_=PB[1][:, 448:452]), r=[b_PB[1]], w=[b_wv[i]])
                act(lambda: nc.scalar.copy(out=Vaug[:, i, :, 0:64],
                                           in_=PB[2][:, 0:128].rearrange("p (g d) -> p g d", d=64)),
                    r=[b_PB[2]], w=[b_V[i]])
                ck(f"Ae3_{i}")
                act(lambda: nc.scalar.activation(out=sza[j][:, 0:384], in_=PB[2][:, 128:512], func=AF.Silu),
                    r=[b_PB[2]], w=[b_sza[j]])
                act(lambda: nc.scalar.activation(out=sza[j][:, 384:512], in_=PB[3][:, 0:128], func=AF.Silu),
                    r=[b_PB[3]], w=[b_sza[j]])
                ck(f"Aevac{i}")
                for h in range(15):
                    tv, sl, bk = (T0v, h, 4) if h < 8 else (T1v, h - 8, 5)
                    pe(lambda: nc.tensor.transpose(out=tv[0:64, sl, :], in_=qk_sb[j][:, h, :], identity=ident[:]),
                       r=[b_qk[j], b_const], w=[b_PB[bk]], inc=(h == 7 or h == 14))
                act(lambda: nc.scalar.copy(out=QT[j][0:64, 0:8, :], in_=T0v[0:64, :, :]), r=[b_PB[4]], w=[b_QT[j]])
                act(lambda: nc.scalar.copy(out=KT[0:64, :, tsl], in_=T1v[0:64, 0:2, :]), r=[b_PB[5]], w=[b_KT[i]])
                act(lambda: nc.scalar.copy(out=QT[j][0:64, 8:12, :], in_=T1v[0:64, 2:6, :]), r=[b_PB[5]], w=[b_QT[j]])
                act(lambda: nc.scalar.copy(out=KIT[0:64, tsl], in_=T1v[0:64, 6, :]), r=[b_PB[5]], w=[b_KT[i]])


            def stage2(i):
                j, tsl, L, masked = hdr_(i)
                if not masked:
                    return

                nch = (L + 511) // 512
                for h in range(4):
                    q = h % 2
                    for c in range(nch):
                        c0 = c * 512
                        n = min(512, L - c0)
                        bk = ctr["ix"] % 2
                        ctr["ix"] += 1
                        pe(lambda: nc.tensor.matmul(PB[bk][:, 0:n], lhsT=QT[j][:, 8 + h, :], rhs=KIT[:, c0:c0 + n],
                                                    start=True, stop=True),
                           r=[b_QT[j]] + b_KT[0:i + 1], w=[b_PB[bk]])
                        act(lambda: nc.scalar.activation(out=rl[q][:, c0:c0 + n], in_=PB[bk][:, 0:n], func=AF.Relu),
                            r=[b_PB[bk]], w=[b_rl[q]])
                    if h == 0:
                        dve(lambda: nc.vector.tensor_scalar(out=sc[j][:, 0:L], in0=rl[q][:, 0:L],
                                                            scalar1=wv[:, i, 0:1], scalar2=None, op0=ALU.mult),
                            r=[b_rl[q], b_wv[i]], w=[b_sc[j]])
                    else:
                        dve(lambda: nc.vector.scalar_tensor_tensor(out=sc[j][:, 0:L], in0=rl[q][:, 0:L],
                                                                   scalar=wv[:, i, h:h + 1], in1=sc[j][:, 0:L],
                                                                   op0=ALU.mult, op1=ALU.add),
                            r=[b_rl[q], b_wv[i]], w=[b_sc[j]])


            def bisect(i):
                j, tsl, L, masked = hdr_(i)
                if not masked:
                    return
                yield

                dve(lambda: nc.vector.tensor_reduce(out=Bv[:], in_=sc[j][:, 0:L], axis=AX.X, op=ALU.max,
                                                    apply_absolute_value=True),
                    r=[b_sc[j]], w=[b_bis])
                dve(lambda: nc.vector.tensor_tensor(out=sc[j][:, L - 128:L], in0=sc[j][:, L - 128:L],
                                                    in1=cbias[:], op=ALU.add),
                    r=[b_const], w=[b_sc[j]])
                dve(lambda: nc.vector.tensor_scalar(out=Bk[:], in0=pow2[:], scalar1=Bv[:, 0:1], scalar2=None,
                                                    op0=ALU.mult), r=[b_const], w=[b_bis])
                dve(lambda: nc.vector.memset(mid[0][:], 0.0), w=[b_bis])
                for it in range(NITER):
                    ma, mb = mid[it % 2], mid[(it + 1) % 2]
                    dve(lambda: nc.vector.tensor_scalar(out=junkb[:, 0:L], in0=sc[j][:, 0:L], scalar1=ma[:, 0:1],
                                                        scalar2=None, op0=ALU.is_ge, op1=ALU.add,
                                                        accum_out=cnt[:, 0:1]),
                        r=[b_sc[j]], w=[b_bis, b_junkb])
                    dve(lambda: nc.vector.tensor_scalar(out=dd[:], in0=cnt[:], scalar1=255.5,
                                                        scalar2=Bk[:, it:it + 1], op0=ALU.is_ge, op1=ALU.mult),
                        w=[b_bis])
                    dve(lambda: nc.vector.tensor_scalar(out=mb[:], in0=dd[:], scalar1=Bk[:, it + 1:it + 2],
                                                        scalar2=ma[:, 0:1], op0=ALU.subtract, op1=ALU.add),
                        w=[b_bis])
                    yield
                mfin = mid[NITER % 2]
                dve(lambda: nc.vector.tensor_tensor(out=thr[:, i:i + 1], in0=mfin[:], in1=Bk[:, NITER:NITER + 1],
                                                    op=ALU.subtract), w=[b_bis])
                dve(lambda: nc.vector.tensor_scalar(out=m01[:, 0:L], in0=sc[j][:, 0:L], scalar1=thr[:, i:i + 1],
                                                    scalar2=None, op0=ALU.is_ge),
                    r=[b_sc[j], b_bis], w=[b_m01])
                if ("sc%d" % i) in dbg:
                    k.dump("d_sc", sc[j][:, 0:L], b_sc[j])
                    k.dump("d_thr", thr[:, i:i + 1], b_bis)


            def masktr(i):
                j, tsl, L, masked = hdr_(i)
                if not masked:
                    return

                for jb in range(i + 1):
                    tv, sl, bk = (T0v, jb, 4) if jb < 8 else (T1v, jb - 8, 5)
                    last = (jb == i) or (jb == 7)
                    pe(lambda: nc.tensor.transpose(out=tv[:, sl, :], in_=m01[:, jb * 128:(jb + 1) * 128],
                                                   identity=ident[:]),
                       r=[b_m01, b_const], w=[b_PB[bk]], inc=last)
                n0 = min(i + 1, 8)
                act(lambda: nc.scalar.activation(out=maskT[j][:, 0:n0, :], in_=T0v[:, 0:n0, :], func=AF.Identity,
                                                 scale=30000.0, bias=-30000.0),
                    r=[b_PB[4]], w=[b_maskT[j]])
                if i + 1 > 8:
                    act(lambda: nc.scalar.activation(out=maskT[j][:, 8:i + 1, :], in_=T1v[:, 0:i + 1 - 8, :],
                                                     func=AF.Identity, scale=30000.0, bias=-30000.0),
                        r=[b_PB[5]], w=[b_maskT[j]])


            def attn_main(i):
                j, tsl, L, masked = hdr_(i)
                nkt = i + 1
                for g in range(2):
                    Ov = PB[6 + g][:, 0:260].rearrange("p (h d) -> p h d", d=65)

                    def st_mm(jb):
                        bk = 2 + (ctr["st"] % 2)
                        ctr["st"] += 1
                        has_mask = masked or jb == i
                        pe(lambda: nc.tensor.matmul(PB[bk][:].rearrange("p (h t) -> p h t", t=128),
                                                    lhsT=KT[:, g, jb * 128:(jb + 1) * 128],
                                                    rhs=QT[j][:, 4 * g:4 * g + 4, :], start=True, stop=(not has_mask)),
                           r=[b_QT[j], b_KT[jb]], w=[b_PB[bk]], inc=(not has_mask))
                        if has_mask:
                            if masked:
                                mb_ap = maskT[j][:, jb:jb + 1, :].to_broadcast([128, 4, 128])
                                rd = [b_maskT[j], b_const]
                            else:
                                mb_ap = negU[:].unsqueeze(1).to_broadcast([128, 4, 128])
                                rd = [b_const]
                            pe(lambda: nc.tensor.matmul(PB[bk][:].rearrange("p (h t) -> p h t", t=128),
                                                        lhsT=ident[:], rhs=mb_ap, start=False, stop=True),
                               r=rd, w=[b_PB[bk]])
                        return bk

                    def exp_pv(jb, bk):
                        e = ctr["e"] % 3
                        ctr["e"] += 1
                        act(lambda: nc.scalar.activation(out=Eb[e][:], in_=PB[bk][:], func=AF.Exp, scale=0.125),
                            r=[b_PB[bk]], w=[b_Eb[e]])
                        for hh in range(4):
                            pe(lambda: nc.tensor.matmul(Ov[:, hh, :], lhsT=Eb[e][:, hh * 128:(hh + 1) * 128],
                                                        rhs=Vaug[:, jb, g, :], start=(jb == 0 and hh == 0),
                                                        stop=(jb == i and hh == 3)),
                               r=[b_Eb[e], b_V[jb]], w=[b_PB[6 + g]], inc=(hh == 3))

                    bks = {0: st_mm(0)}
                    for jb in range(nkt):
                        if jb + 1 < nkt:
                            bks[jb + 1] = st_mm(jb + 1)
                        exp_pv(jb, bks[jb])

            def attn_fin(i):
                j, tsl, L, masked = hdr_(i)
                for g in range(2):
                    Ov = PB[6 + g][:, 0:260].rearrange("p (h d) -> p h d", d=65)
                    dve(lambda: nc.vector.reciprocal(out=rinv8[:, 4 * g:4 * g + 4].unsqueeze(2), in_=Ov[:, :, 64:65]),
                        r=[b_PB[6 + g]], w=[b_otmp])
                    dve(lambda: nc.vector.tensor_tensor(out=otmp8[:, 4 * g:4 * g + 4, :], in0=Ov[:, :, 0:64],
                                                        in1=rinv8[:, 4 * g:4 * g + 4].unsqueeze(2)
                                                        .to_broadcast([128, 4, 64]), op=ALU.mult),
                        r=[b_PB[6 + g]], w=[b_otmp])
                dve(lambda: nc.vector.tensor_tensor(out=oa_sb[:], in0=otmp8[:].rearrange("p h d -> p (h d)"),
                                                    in1=sza[j][:], op=ALU.mult),
                    r=[b_otmp, b_sza[j]], w=[b_oa])


            def fin_tr(i):
                j, tsl, L, masked = hdr_(i)
                for c in range(4):
                    pe(lambda: nc.tensor.transpose(out=T0v[:, c, :], in_=oa_sb[:, c * 128:(c + 1) * 128],
                                                   identity=ident[:]),
                       r=[b_oa, b_const], w=[b_PB[4]], inc=(c == 3))
                act(lambda: nc.scalar.copy(out=oaT[:, :, tsl], in_=T0v[:, 0:4, :]), r=[b_PB[4]], w=[b_oaT[i]])


            nA = NT if "skipA" not in dbg else 0
            pend = None
            for i in range(nA + 2):
                if i < nA:
                    stage1(i)
                if pend is not None:
                    for _ in pend:
                        pass
                    pend = None
                if i < nA:
                    stage2(i)
                if 1 <= i <= nA:
                    masktr(i - 1)
                if 2 <= i <= nA + 1:
                    fin_tr(i - 2)
                if 1 <= i <= nA:
                    attn_main(i - 1)
                if i < nA:
                    g_ = bisect(i)
                    nsteps = (NITER - 3) if (i + 1 < nA) else 10 ** 9
                    done_ = False
                    for _s in range(nsteps):
                        try:
                            next(g_)
                        except StopIteration:
                            done_ = True
                            break
                    if not done_:
                        pend = g_
                if 1 <= i <= nA:
                    attn_fin(i - 1)
            i = nA - 1
            j = i % 2
            L = (i + 1) * 128

            if "qk" in dbg:
                k.dump("d_KT", KT[:, :, 0:L], b_KT)
                k.dump("d_KIT", KIT[:, 0:L], b_KT)
                k.dump("d_QT", QT[j][:], b_QT[j])
                k.dump("d_V", Vaug[:, 0:i + 1], b_V)
            if "oaT" in dbg:
                k.dump("d_oaT", oaT[:, :, 0:L], b_oaT)
            k.barrier()
        esWA.close()
        if stop_after.startswith("A"):
            return nc, k


        obT = k.sb("obT", [128, 8, S], BF16)
        b_obT = [Buf() for _ in range(NT)]
        with ExitStack() as esB:
            wdt = k.sb("wdt", [128, 8, 16], BF16, esB)
            b_wdt = Buf()
            dpool(out=wdt[:], in_=win_d[:, 4164:4180].rearrange("(k p) n -> p k n", p=128), w=[b_wdt])
            X_tm = k.sb("X_tm", [128, NT, 1024], BF16, esB)
            B_tm = k.sb("B_tm", [128, NT, 256], BF16, esB)
            BT = k.sb("BT", [128, 2, S], BF16, esB)
            CT = k.sb("CT", [128, 2, S], BF16, esB)
            b_X = Buf()
            dtb_b = k.sb("dtb_b", [128, 16], F32, esB)
            a_b = k.sb("a_b", [128, 16], F32, esB)
            dsk_b = k.sb("dsk_b", [128, 16], F32, esB)
            snw_b = k.sb("snw_b", [128, D], F32, esB)
            dt_all = k.sb("dt_all", [128, NT, 16], F32, esB)
            dA_all = k.sb("dA_all", [128, NT, 16], F32, esB)
            spt = [k.sb(f"spt{j}", [128, NT, 16], F32, esB) for j in range(3)]
            ones_f = k.sb("ones_f", [128, 128], F32, esB)
            NEGU4 = k.sb("NEGU4", [128, 4, 128], BF16, esB)
            Dg = k.sb("Dg", [128, 16, 128], BF16, esB)
            b_ptab = Buf()
            dsync(out=dtb_b[:], in_=dtb_d[0, :].partition_broadcast(128), w=[b_ptab])
            dsync(out=a_b[:], in_=alog_d[0, :].partition_broadcast(128), w=[b_ptab])
            dsync(out=dsk_b[:], in_=dsk_d[0, :].partition_broadcast(128), w=[b_ptab])
            dsync(out=snw_b[:], in_=snw_d[0, :].partition_broadcast(128), w=[b_ptab])
            pool(lambda: nc.gpsimd.memset(ones_f[:], 1.0), w=[b_ptab])
            pool(lambda: nc.gpsimd.memset(NEGU4[:], 0.0), w=[b_ptab])
            pool(lambda: nc.gpsimd.affine_select(out=NEGU4[:], in_=NEGU4[:], pattern=[[0, 4], [1, 128]],
                                                 compare_op=ALU.is_ge, fill=-1.0e4, base=0, channel_multiplier=-1),
                 w=[b_ptab])
            act(lambda: nc.scalar.activation(out=a_b[:], in_=a_b[:], func=AF.Exp), w=[b_ptab])
            dve(lambda: nc.vector.tensor_scalar(out=a_b[:], in0=a_b[:], scalar1=-1.0, scalar2=None, op0=ALU.mult),
                w=[b_ptab])
            dve(lambda: nc.vector.tensor_tensor(out=Dg[:], in0=ident[:].unsqueeze(1).to_broadcast([128, 16, 128]),
                                                in1=dsk_b[:].unsqueeze(2).to_broadcast([128, 16, 128]), op=ALU.mult),
                r=[b_const], w=[b_ptab])
            ck("Btab")

            for i in range(NT):
                for kc in range(8):
                    pe(lambda: nc.tensor.matmul(PB[0][:, i * 16:(i + 1) * 16], lhsT=hT[:, kc, i * 128:(i + 1) * 128],
                                                rhs=wdt[:, kc, :], start=(kc == 0), stop=(kc == 7)),
                       r=[b_hT[i], b_wdt], w=[b_PB[0]], inc=(kc == 7))
            dve(lambda: nc.vector.tensor_tensor(out=spt[0][:], in0=PB[0][:, 0:256].rearrange("p (i h) -> p i h", h=16),
                                                in1=dtb_b[:].unsqueeze(1).to_broadcast([128, NT, 16]), op=ALU.add),
                r=[b_PB[0]], w=[b_ptab])
            dve(lambda: nc.vector.tensor_scalar(out=spt[2][:], in0=spt[0][:], scalar1=-1.0, scalar2=None, op0=ALU.mult),
                w=[b_ptab])
            dve(lambda: nc.vector.tensor_tensor(out=spt[1][:], in0=spt[0][:], in1=spt[2][:], op=ALU.max), w=[b_ptab])
            act(lambda: nc.scalar.activation(out=spt[1][:], in_=spt[1][:], func=AF.Exp, scale=-1.0), w=[b_ptab])
            act(lambda: nc.scalar.activation(out=spt[1][:], in_=spt[1][:], func=AF.Ln, bias=1.0), w=[b_ptab])
            dve(lambda: nc.vector.tensor_scalar(out=spt[2][:], in0=spt[0][:], scalar1=0.0, scalar2=None, op0=ALU.max),
                w=[b_ptab])
            dve(lambda: nc.vector.tensor_tensor(out=dt_all[:], in0=spt[2][:], in1=spt[1][:], op=ALU.add), w=[b_ptab])
            dve(lambda: nc.vector.tensor_tensor(out=dA_all[:], in0=dt_all[:],
                                                in1=a_b[:].unsqueeze(1).to_broadcast([128, NT, 16]), op=ALU.mult),
                w=[b_ptab])
            if "dt" in dbg:
                k.dump("d_dt", dt_all[:], b_ptab)
            ck("Bdt")

            with ExitStack() as esC:
                cwT = k.sb("cwT_sb", [128, 12, 4], F32, esC)
                cbT = k.sb("cbT_sb", [128, 12], F32, esC)
                b_cw = Buf()
                dsync(out=cwT[:], in_=cwT_d[:, :, :], w=[b_cw])
                dsync(out=cbT[:], in_=cbT_d[:, :], w=[b_cw])
                pre = [k.sb(f"pre{j}", [128, S + 3], F32, esC) for j in range(2)]
                b_pre = [Buf(), Buf()]
                accs = [k.sb(f"acc{j}", [128, S], F32, esC) for j in range(2)]
                b_accs = [Buf(), Buf()]
                xs_fm = k.sb("xs_fm", [128, S], BF16, esC)
                b_xs = Buf()
                for q in range(2):
                    pool(lambda: nc.gpsimd.memset(pre[q][:, 0:3], 0.0), w=[b_pre[q]])
                b_xs2 = [Buf(), Buf()]
                b_cv = [Buf() for _ in range(12)]

                def cproj(m):
                    q = m % 2
                    slot = (m // 4) % 2
                    wc0 = slot * 512 + (m % 4) * 128
                    for tc in range(4):
                        for kc in range(8):
                            pe(lambda: nc.tensor.matmul(PB[tc][:], lhsT=wBC[:, kc, wc0:wc0 + 128],
                                                        rhs=hT[:, kc, tc * 512:(tc + 1) * 512],
                                                        start=(kc == 0), stop=(kc == 7)),
                               r=b_hT[tc * 4:(tc + 1) * 4] + [b_wslot[slot]], w=[b_PB[tc]], inc=(kc == 7))
                        act(lambda: nc.scalar.copy(out=pre[q][:, 3 + tc * 512:3 + (tc + 1) * 512], in_=PB[tc][:]),
                            r=[b_PB[tc]], w=[b_pre[q]])

                def cpost(m):
                    q = m % 2
                    acc = accs[q]
                    b_acc = b_accs[q]
                    dve(lambda: nc.vector.tensor_scalar(out=acc[:], in0=pre[q][:, 0:S], scalar1=cwT[:, m, 0:1],
                                                        scalar2=None, op0=ALU.mult),
                        r=[b_pre[q], b_cw], w=[b_acc])
                    for kk in range(1, 4):
                        dve(lambda: nc.vector.scalar_tensor_tensor(out=acc[:], in0=pre[q][:, kk:kk + S],
                                                                   scalar=cwT[:, m, kk:kk + 1], in1=acc[:],
                                                                   op0=ALU.mult, op1=ALU.add),
                            r=[b_pre[q], b_cw], w=[b_acc])
                    if m < 10:
                        dst = xs_fm[:] if m < 8 else BT[:, m - 8, :]
                        bdst = b_xs if m < 8 else b_cv[m]
                        act(lambda: nc.scalar.activation(out=dst, in_=acc[:], func=AF.Silu, bias=cbT[:, m:m + 1]),
                            r=[b_acc, b_cw], w=[bdst])
                        for half in range(2):
                            bk = 4 + half
                            tv = bfv(bk)
                            for s8 in range(8):
                                ti_ = half * 8 + s8
                                in_ap = (xs_fm[:, ti_ * 128:(ti_ + 1) * 128] if m < 8
                                         else BT[:, m - 8, ti_ * 128:(ti_ + 1) * 128])
                                pe(lambda: nc.tensor.transpose(out=tv[:, s8, :], in_=in_ap, identity=ident[:]),
                                   r=[bdst, b_const], w=[b_PB[bk]], inc=(s8 == 7))
                            if m < 8:
                                act(lambda: nc.scalar.copy(out=X_tm[:, half * 8:(half + 1) * 8, m * 128:(m + 1) * 128],
                                                           in_=tv[:, :, :]), r=[b_PB[bk]], w=[b_cv[m]])
                            else:
                                act(lambda: nc.scalar.copy(out=B_tm[:, half * 8:(half + 1) * 8,
                                                                    (m - 8) * 128:(m - 7) * 128],
                                                           in_=tv[:, :, :]), r=[b_PB[bk]], w=[b_cv[m]])
                    else:
                        act(lambda: nc.scalar.activation(out=CT[:, m - 10, :], in_=acc[:], func=AF.Silu,
                                                         bias=cbT[:, m:m + 1]),
                            r=[b_acc, b_cw], w=[b_cv[m]])

                def wreload(m_done):
                    if m_done == 3:
                        dpool(out=wBC[:, :, 0:512], in_=win_d[:, CONV0 + 1024:CONV0 + 1536]
                              .rearrange("(k p) n -> p k n", p=128), w=[b_wslot[0]])
                    elif m_done == 7:
                        dpool(out=wBC[:, :, 512:1024], in_=win_d[:, 1604:2116]
                              .rearrange("(k p) n -> p k n", p=128), w=[b_wslot[1]])
                    elif m_done == 11:
                        dpool(out=wBC[:, :, 0:512], in_=win_d[:, 2116:2628]
                              .rearrange("(k p) n -> p k n", p=128), w=[b_wslot[0]])

                cproj(0)
                wreload(0)
                for m in range(12):
                    if m + 1 < 12:
                        cproj(m + 1)
                        wreload(m + 1)
                    cpost(m)
                    ck(f"Bconv{m}")
                b_X.w = {}
                for bb in b_cv:
                    _merge(b_X.w, bb.w)
                if "conv" in dbg:
                    k.dump("d_Xtm", X_tm[:], b_X)
                    k.dump("d_Btm", B_tm[:], b_X)
                    k.dump("d_BT", BT[:], b_X)
                    k.dump("d_CT", CT[:], b_X)
                ck("Bconv")
                k.barrier()

            with ExitStack() as esS:
                ones_b = k.sb("ones_b", [128, 128], BF16, esS)
                dAhl = k.sb("dAhl", [128, NT, 2, 16], BF16, esS)
                dAres = spt[0]
                b_hl = Buf()
                pool(lambda: nc.gpsimd.memset(ones_b[:], 1.0), w=[b_hl])
                dve(lambda: nc.vector.tensor_copy(out=dAhl[:, :, 0, :], in_=dA_all[:]), r=[b_ptab], w=[b_hl])
                dve(lambda: nc.vector.tensor_tensor(out=dAres[:], in0=dA_all[:], in1=dAhl[:, :, 0, :], op=ALU.subtract),
                    r=[b_ptab], w=[b_hl])
                dve(lambda: nc.vector.tensor_copy(out=dAhl[:, :, 1, :], in_=dAres[:]), w=[b_hl])
                szb = [k.sb(f"szb{j}", [128, 1024], BF16, esS) for j in range(2)]
                b_szb = [Buf(), Buf()]
                smalls = [k.sb(f"small{j}", [128, 32], F32, esS) for j in range(2)]
                nacums = [k.sb(f"nacum{j}", [128, 16], F32, esS) for j in range(2)]
                eas = [k.sb(f"ea{j}", [128, 16], F32, esS) for j in range(2)]
                decs = [k.sb(f"dec{j}", [128, 16], F32, esS) for j in range(2)]
                dtds = [k.sb(f"dtd{j}", [128, 16], F32, esS) for j in range(2)]
                eASs = [k.sb(f"eAS{j}", [128, 2, 4], F32, esS) for j in range(2)]
                b_sms = [Buf(), Buf()]
                LTg = [k.sb(f"LTg{j}", [128, 4, 128], F32, esS) for j in range(2)]
                b_LT = [Buf(), Buf()]
                MTg = [k.sb(f"MTg{j}", [128, 4, 128], BF16, esS) for j in range(2)]
                b_MT = [Buf(), Buf()]
                CBs = [k.sb("CBs0", [128, 4, 128], F32, esS)] * 2
                b_CBs = [Buf()] * 2
                xds = [k.sb(f"xd{j}", [128, 16, 64], BF16, esS) for j in range(2)]
                xdds = [k.sb(f"xdd{j}", [128, 16, 64], BF16, esS) for j in range(2)]
                b_xds = [Buf(), Buf()]
                b_xdds = [Buf(), Buf()]
                ysb = k.sb("ysb", [128, 16, 64], F32, esS)
                b_y = Buf()
                ssq = k.sb("ssq", [128, 4], F32, esS)
                rs4 = k.sb("rs4", [128, 4], F32, esS)
                junkf = k.sb("junkf", [128, 256], F32, esS)
                ob_sb = k.sb("ob_sb", [128, 1024], BF16, esS)
                b_ob = Buf()
                S_sb = k.sb("S_sb", [128, 2, 256], F32, esS)
                S_bf = k.sb("S_bf", [128, 2, 256], BF16, esS)
                b_S = Buf()
                b_Sbf = Buf()
                gctr = {"g": 0, "d": 0}
                Yv = [PB[4][:].rearrange("p (h d) -> p h d", d=64), PB[5][:].rearrange("p (h d) -> p h d", d=64)]

                def head(c):
                    q = c % 2
                    csl = slice(c * 128, (c + 1) * 128)
                    small, nacum, ea, dec, dtd, eAS, b_sm = smalls[q], nacums[q], eas[q], decs[q], dtds[q], eASs[q], b_sms[q]
                    pe(lambda: nc.tensor.matmul(PB[2][:, 0:16], lhsT=Uf[:], rhs=dA_all[:, c, :], start=True, stop=False),
                       r=[b_const, b_ptab], w=[b_PB[2]], inc=False)
                    pe(lambda: nc.tensor.matmul(PB[2][:, 16:32], lhsT=ones_f[:], rhs=dA_all[:, c, :], start=False,
                                                stop=True), r=[b_ptab], w=[b_PB[2]])
                    CBv = PB[3][:].rearrange("p (g l) -> p g l", l=128)
                    tk = None
                    for gi, g in enumerate((0, 2, 1, 3)):
                        p0 = (g % 2) * 64
                        tk2 = pe(lambda: nc.tensor.matmul(CBv[:, g, :], lhsT=BT[p0:p0 + 64, g // 2, csl],
                                                          rhs=CT[p0:p0 + 64, g // 2, csl], start=(gi == 0),
                                                          stop=(gi == 3)),
                                 r=[b_X], w=[b_PB[3]], inc=(gi == 1 or gi == 3), selfwait=(tk if gi == 2 else None))
                        if gi == 1:
                            tk = tk2
                    for hb in range(2):
                        for kc in range(8):
                            pe(lambda: nc.tensor.matmul(PB[hb][:], lhsT=hT[:, kc, csl],
                                                        rhs=wBC[:, kc, (1 - hb) * 512:(2 - hb) * 512],
                                                        start=(kc == 0), stop=(kc == 7)),
                               r=[b_hT[c], b_wslot[1 - hb]], w=[b_PB[hb]], inc=(kc == 7))
                    yield
                    dve(lambda: nc.vector.tensor_copy(out=small[:], in_=PB[2][:, 0:32]), r=[b_PB[2]], w=[b_sm])
                    acum = small[:, 0:16]
                    atot = small[:, 16:32]
                    dve(lambda: nc.vector.tensor_tensor(out=dec[:], in0=atot, in1=acum, op=ALU.subtract), w=[b_sm])
                    act(lambda: nc.scalar.activation(out=ea[:], in_=acum, func=AF.Exp), w=[b_sm])
                    act(lambda: nc.scalar.activation(out=dec[:], in_=dec[:], func=AF.Exp), w=[b_sm])
                    atv = small[:, 16:32].rearrange("p (s f h) -> p s f h", s=2, f=2)
                    act(lambda: nc.scalar.activation(out=eAS[0:64], in_=atv[0:64, :, 0, :], func=AF.Exp), w=[b_sm])
                    act(lambda: nc.scalar.activation(out=eAS[64:128], in_=atv[64:128, :, 1, :], func=AF.Exp), w=[b_sm])
                    act(lambda: nc.scalar.copy(out=CBs[q][:], in_=CBv), r=[b_PB[3]], w=[b_CBs[q]])
                    for hb in range(2):
                        act(lambda: nc.scalar.activation(out=szb[q][:, hb * 512:(hb + 1) * 512], in_=PB[hb][:],
                                                         func=AF.Silu), r=[b_PB[hb]], w=[b_szb[q]])
                    dve(lambda: nc.vector.tensor_tensor(out=dtd[:], in0=dt_all[:, c, :], in1=dec[:], op=ALU.mult),
                        r=[b_ptab], w=[b_sm])
                    Xc = X_tm[:, c, :].rearrange("p (h d) -> p h d", d=64)
                    dve(lambda: nc.vector.tensor_tensor(out=xds[q][:], in0=Xc,
                                                        in1=dt_all[:, c, :].unsqueeze(2).to_broadcast([128, 16, 64]),
                                                        op=ALU.mult), r=[b_X, b_ptab], w=[b_xds[q]])
                    dve(lambda: nc.vector.tensor_tensor(out=xdds[q][:], in0=Xc,
                                                        in1=dtd[:].unsqueeze(2).to_broadcast([128, 16, 64]),
                                                        op=ALU.mult), r=[b_X, b_sm], w=[b_xdds[q]])

                    yield

                def groups(c):
                    q = c % 2
                    csl = slice(c * 128, (c + 1) * 128)
                    nacum, b_sm = nacums[q], b_sms[q]
                    xd = xds[q]
                    st = {}

                    def acumb(g):
                        gq = gctr["g"] % 2
                        gctr["g"] += 1
                        abk = 2 if gq == 0 else 6
                        first = True
                        for hh in range(4):
                            hd = 4 * g + hh
                            for part in range(2):
                                pe(lambda: nc.tensor.matmul(PB[abk][:, hh * 128:(hh + 1) * 128],
                                                            lhsT=dAhl[:, c, part, hd:hd + 1].to_broadcast([128, 128]),
                                                            rhs=Ub[:], start=first, stop=False),
                                   r=[b_hl, b_const], w=[b_PB[abk]], inc=False)
                                first = False
                        for part in range(2):
                            pe(lambda: nc.tensor.matmul(
                                PB[abk][:].rearrange("p (h l) -> p h l", l=128), lhsT=negUb[:],
                                rhs=dAhl[:, c, part, 4 * g:4 * g + 4].unsqueeze(2).to_broadcast([128, 4, 128]),
                                start=False, stop=False), r=[b_hl, b_const], w=[b_PB[abk]], inc=False)
                        pe(lambda: nc.tensor.matmul(PB[abk][:], lhsT=ident[:],
                                                    rhs=NEGU4[:].rearrange("p h l -> p (h l)"), start=False, stop=True),
                           r=[b_const, b_ptab], w=[b_PB[abk]])
                        st[g] = (gq, abk)

                    def ymm(g):
                        gq, abk = st[g]
                        act(lambda: nc.scalar.activation(out=LTg[gq][:].rearrange("p h l -> p (h l)"), in_=PB[abk][:],
                                                         func=AF.Exp), r=[b_PB[abk]], w=[b_LT[gq]])
                        dve(lambda: nc.vector.tensor_tensor(out=MTg[gq][:], in0=LTg[gq][:],
                                                            in1=CBs[q][:, g:g + 1, :].to_broadcast([128, 4, 128]),
                                                            op=ALU.mult),
                            r=[b_LT[gq], b_CBs[q]], w=[b_MT[gq]])
                        for hh in range(4):
                            hd = 4 * g + hh
                            yb = 4 + hd // 8
                            pe(lambda: nc.tensor.matmul(Yv[hd // 8][:, hd % 8, :], lhsT=MTg[gq][:, hh, :],
                                                        rhs=xd[:, hd, :], start=(hd % 8 == 0), stop=False),
                               r=[b_MT[gq], b_xds[q]], w=[b_PB[yb]], inc=False)
                            pe(lambda: nc.tensor.matmul(Yv[hd // 8][:, hd % 8, :], lhsT=Dg[:, hd, :],
                                                        rhs=X_tm[:, c, hd * 64:(hd + 1) * 64], start=False,
                                                        stop=(hd % 8 == 7)),
                               r=[b_ptab, b_X], w=[b_PB[yb]], inc=(hh == 3))

                    acumb(0)
                    acumb(1)
                    yield
                    ymm(0)
                    yield
                    acumb(2)
                    ymm(1)
                    yield
                    acumb(3)
                    ymm(2)
                    yield
                    ymm(3)
                    yield

                def tail(c):
                    q = c % 2
                    csl = slice(c * 128, (c + 1) * 128)
                    ea, eAS, b_sm = eas[q], eASs[q], b_sms[q]
                    xdd = xdds[q]
                    if c > 0:
                        tk = None
                        for gi, g in enumerate((0, 2, 1, 3)):
                            p0 = (g % 2) * 64
                            ob_ = 6 + g // 2
                            tk2 = pe(lambda: nc.tensor.matmul(PB[ob_][:, (g % 2) * 256:(g % 2 + 1) * 256],
                                                              lhsT=CT[p0:p0 + 64, g // 2, csl],
                                                              rhs=S_bf[p0:p0 + 64, g // 2, :], start=(g % 2 == 0),
                                                              stop=(g % 2 == 1)),
                                     r=[b_X, b_Sbf], w=[b_PB[ob_]], inc=(gi >= 1),
                                     selfwait=(tk if gi == 2 else None))
                            if gi == 1:
                                tk = tk2
                        for hb in range(2):
                            dve(lambda: nc.vector.tensor_tensor(
                                out=ysb[:, hb * 8:(hb + 1) * 8, :],
                                in0=PB[6 + hb][:].rearrange("p (h d) -> p h d", d=64),
                                in1=ea[:, hb * 8:(hb + 1) * 8].unsqueeze(2).to_broadcast([128, 8, 64]), op=ALU.mult),
                                r=[b_PB[6 + hb], b_sm], w=[b_y])
                            dve(lambda: nc.vector.tensor_tensor(out=ysb[:, hb * 8:(hb + 1) * 8, :], in0=Yv[hb],
                                                                in1=ysb[:, hb * 8:(hb + 1) * 8, :], op=ALU.add),
                                r=[b_PB[4 + hb]], w=[b_y])
                    else:
                        for hb in range(2):
                            dve(lambda: nc.vector.tensor_copy(out=ysb[:, hb * 8:(hb + 1) * 8, :], in_=Yv[hb]),
                                r=[b_PB[4 + hb]], w=[b_y])
                    yield
                    dve(lambda: nc.vector.tensor_tensor(out=ysb[:].rearrange("p h d -> p (h d)"),
                                                        in0=ysb[:].rearrange("p h d -> p (h d)"), in1=szb[q][:],
                                                        op=ALU.mult), r=[b_szb[q]], w=[b_y])
                    yf = ysb[:].rearrange("p h d -> p (h d)")
                    for g in range(4):
                        act(lambda: nc.scalar.activation(out=junkf[:], in_=yf[:, g * 256:(g + 1) * 256], func=AF.Square,
                                                         accum_out=ssq[:, g:g + 1]), r=[b_y], w=[b_ob])
                    pool(lambda: nc.gpsimd.tensor_scalar(out=rs4[:], in0=ssq[:], scalar1=1.0 / 256, scalar2=EPS,
                                                         op0=ALU.mult, op1=ALU.add), w=[b_ob])
                    pool(lambda: nc.gpsimd.tensor_tensor(out=rs4[:], in0=rs4[:], in1=mhalf[:, 0:4], op=ALU.pow),
                         r=[b_const], w=[b_ob])
                    yield
                    if c < NT - 1:
                        for g in range(4):
                            p0 = (g % 2) * 64
                            pe(lambda: nc.tensor.matmul(PB[7][p0:p0 + 64, (g // 2) * 256:(g // 2 + 1) * 256],
                                                        lhsT=B_tm[:, c, g * 64:(g + 1) * 64],
                                                        rhs=xdd[:, 4 * g:4 * g + 4, :].rearrange("p h d -> p (h d)"),
                                                        start=(g < 2), stop=(g >= 2)),
                               r=[b_X, b_xdds[q]], w=[b_PB[7]], inc=(g == 3))
                        Sv = S_sb[:].rearrange("p s (h d) -> p (s h) d", d=64)
                        if c == 0:
                            dve(lambda: nc.vector.tensor_copy(out=S_sb[:].rearrange("p s f -> p (s f)"), in_=PB[7][:]),
                                r=[b_PB[7]], w=[b_S])
                        else:
                            dve(lambda: nc.vector.tensor_tensor(
                                out=Sv, in0=Sv,
                                in1=eAS[:].rearrange("p s h -> p (s h)").unsqueeze(2).to_broadcast([128, 8, 64]),
                                op=ALU.mult), r=[b_sm, b_Sbf], w=[b_S])
                            dve(lambda: nc.vector.tensor_tensor(out=S_sb[:].rearrange("p s f -> p (s f)"),
                                                                in0=S_sb[:].rearrange("p s f -> p (s f)"),
                                                                in1=PB[7][:], op=ALU.add),
                                r=[b_PB[7]], w=[b_S])
                        act(lambda: nc.scalar.copy(out=S_bf[:], in_=S_sb[:]), r=[b_S], w=[b_Sbf])
                    yield
                    for g in range(4):
                        dve(lambda: nc.vector.scalar_tensor_tensor(out=ob_sb[:, g * 256:(g + 1) * 256],
                                                                   in0=yf[:, g * 256:(g + 1) * 256],
                                                                   scalar=rs4[:, g:g + 1],
                                                                   in1=snw_b[:, g * 256:(g + 1) * 256],
                                                                   op0=ALU.mult, op1=ALU.mult),
                            r=[b_y, b_ptab], w=[b_ob])
                    yield
                    tv = bfv(7)
                    for cc in range(8):
                        pe(lambda: nc.tensor.transpose(out=tv[:, cc, :], in_=ob_sb[:, cc * 128:(cc + 1) * 128],
                                                       identity=ident[:]),
                           r=[b_ob, b_const], w=[b_PB[7]], inc=(cc == 7))
                    act(lambda: nc.scalar.copy(out=obT[:, :, csl], in_=tv[:, :, :]), r=[b_PB[7]], w=[b_obT[c]])

                def front(c):
                    yield from head(c)
                    yield from groups(c)

                def interleave(gens):
                    gens = list(gens)
                    while gens:
                        for g_ in list(gens):
                            try:
                                next(g_)
                            except StopIteration:
                                gens.remove(g_)

                interleave([front(0)])
                for c in range(NT):
                    gl = [tail(c)]
                    if c + 1 < NT:
                        gl.append(front(c + 1))
                    interleave(gl)
                    if c == NT - 2:
                        dpool(out=wG0[:, :, 0, :], in_=win_d[:, 4180:4180 + 128].rearrange("(k p) n -> p k n", p=128),
                              w=[b_wslot[0]])
                        dpool(out=wG0[:, :, 1, :], in_=win_d[:, 4180 + 1024:4180 + 1152]
                              .rearrange("(k p) n -> p k n", p=128), w=[b_wslot[0]])
                        dpool(out=Wpa0, in_=wpa_d[:, 0:128].rearrange("(k p) n -> p k n", p=128), w=[b_wslot[0]])
                        dpool(out=Wpb0, in_=wpb_d[:, 0:128].rearrange("(k p) n -> p k n", p=128), w=[b_wslot[0]])
                if "obT" in dbg:
                    k.dump("d_obT", obT[:], b_obT)
                k.barrier()
        if stop_after.startswith("B"):
            return nc, k


        with ExitStack() as esM:
            Wout = k.sb("Wout", [128, 8, D], BF16, esM)
            b_W = Buf()
            gbT = k.sb("gbT_sb", [128, 16], F32, esM)
            fnw_b = k.sb("fnw_b", [128, D], F32, esM)
            b_ct = Buf()
            dsync(out=gbT[:], in_=gbT_d[:, :], w=[b_ct])
            dsync(out=fnw_b[:], in_=fnw_d[0, :].partition_broadcast(128), w=[b_ct])
            mT = k.sb("mT", [128, 8, S], BF16, esM)
            b_mT = [Buf() for _ in range(4)]
            with ExitStack() as esM1:
                wG = [k.sb(f"wG{j}", [128, 8, 2, 128], BF16, esM1) for j in range(2)]
                Wpa = [k.sb(f"Wpa{j}", [128, 4, 128], BF16, esM1) for j in range(2)]
                Wpb = [k.sb(f"Wpb{j}", [128, 8, 128], BF16, esM1) for j in range(2)]
                b_wG = [Buf(), Buf()]
                gA = [k.sb(f"gA{j}", [128, 512], F32, esM1) for j in range(2)]
                gB = [k.sb(f"gB{j}", [128, 512], F32, esM1) for j in range(2)]
                b_g = [Buf(), Buf()]
                t1 = [k.sb(f"t1{j}", [128, 512], F32, esM1) for j in range(2)]
                t2 = [k.sb(f"t2{j}", [128, 512], F32, esM1) for j in range(2)]
                b_t = [Buf(), Buf()]
                it = 0
                for m in range(8):
                    wq = m % 2
                    if m == 0:
                        gw_, pa_, pb_, bw_ = wG0, Wpa0, Wpb0, b_wslot[0]
                    else:
                        gw_, pa_, pb_, bw_ = wG[wq][:], Wpa[wq][:], Wpb[wq][:], b_wG[wq]
                        dpool(out=gw_[:, :, 0, :], in_=win_d[:, 4180 + m * 128:4180 + (m + 1) * 128]
                              .rearrange("(k p) n -> p k n", p=128), w=[bw_])
                        dpool(out=gw_[:, :, 1, :], in_=win_d[:, 4180 + (8 + m) * 128:4180 + (9 + m) * 128]
                              .rearrange("(k p) n -> p k n", p=128), w=[bw_])
                        dpool(out=pa_, in_=wpa_d[:, m * 128:(m + 1) * 128].rearrange("(k p) n -> p k n", p=128),
                              w=[bw_])
                        dpool(out=pb_, in_=wpb_d[:, m * 128:(m + 1) * 128].rearrange("(k p) n -> p k n", p=128),
                              w=[bw_])
                    if m == 0:
                        for hf in range(2):
                            cs_ = slice(hf * 512, (hf + 1) * 512)
                            dpool(out=Wout[:, :, cs_], in_=wout_d[:, cs_].rearrange("(k p) n -> p k n", p=128),
                                  w=[b_W])
                    for tc in range(4):
                        q = it % 2
                        it += 1
                        b0 = 4 * q
                        ts_ = slice(tc * 512, (tc + 1) * 512)
                        for kc in range(8):
                            pe(lambda: nc.tensor.matmul(PB[b0][:], lhsT=gw_[:, kc, 0, :], rhs=hT[:, kc, ts_],
                                                        start=(kc == 0), stop=(kc == 7)),
                               r=b_hT[tc * 4:(tc + 1) * 4] + [bw_], w=[b_PB[b0]], inc=(kc == 7))
                        for kc in range(8):
                            pe(lambda: nc.tensor.matmul(PB[b0 + 1][:], lhsT=gw_[:, kc, 1, :], rhs=hT[:, kc, ts_],
                                                        start=(kc == 0), stop=(kc == 7)),
                               r=b_hT[tc * 4:(tc + 1) * 4] + [bw_], w=[b_PB[b0 + 1]], inc=(kc == 7))
                        for kc in range(4):
                            pe(lambda: nc.tensor.matmul(PB[b0 + 2][:], lhsT=pa_[:, kc, :],
                                                        rhs=oaT[:, kc, ts_], start=(kc == 0), stop=(kc == 3)),
                               r=b_oaT[tc * 4:(tc + 1) * 4] + [bw_], w=[b_PB[b0 + 2]], inc=(kc == 3))
                        for kc in range(8):
                            pe(lambda: nc.tensor.matmul(PB[b0 + 3][:], lhsT=pb_[:, kc, :],
                                                        rhs=obT[:, kc, ts_], start=(kc == 0), stop=(kc == 7)),
                               r=b_obT[tc * 4:(tc + 1) * 4] + [bw_], w=[b_PB[b0 + 3]], inc=(kc == 7))
                        act(lambda: nc.scalar.activation(out=gA[q][:], in_=PB[b0][:], func=AF.Sigmoid,
                                                         bias=gbT[:, m:m + 1]), r=[b_PB[b0], b_ct], w=[b_g[q]])
                        act(lambda: nc.scalar.activation(out=gB[q][:], in_=PB[b0 + 1][:], func=AF.Sigmoid,
                                                         bias=gbT[:, 8 + m:9 + m]), r=[b_PB[b0 + 1], b_ct], w=[b_g[q]])
                        dve(lambda: nc.vector.tensor_tensor(out=t1[q][:], in0=PB[b0 + 2][:], in1=gA[q][:], op=ALU.mult),
                            r=[b_PB[b0 + 2], b_g[q]], w=[b_t[q]])
                        dve(lambda: nc.vector.tensor_tensor(out=t2[q][:], in0=PB[b0 + 3][:], in1=gB[q][:], op=ALU.mult),
                            r=[b_PB[b0 + 3], b_g[q]], w=[b_t[q]])
                        dve(lambda: nc.vector.tensor_tensor(out=mT[:, m, ts_], in0=t1[q][:], in1=t2[q][:], op=ALU.add),
                            r=[b_t[q]], w=[b_mT[tc]])
                k.barrier()
            if "mT" in dbg:
                k.dump("d_mT", mT[:], b_mT)
            ck("Cm")
            xr = [k.sb(f"xr{j}", [128, D], F32, esM) for j in range(2)]
            b_xr = [Buf(), Buf()]
            xo = [k.sb(f"xo{j}", [128, D], F32, esM) for j in range(2)]
            b_xo = [Buf(), Buf()]
            fo = [k.sb(f"fo{j}", [128, D], F32, esM) for j in range(2)]
            b_fo = [Buf(), Buf()]
            junkc = k.sb("junkc", [128, D], BF16, esM)
            fss = k.sb("fss", [128, NT], F32, esM)
            b_fs = Buf()
            dpool(out=xr[0][:], in_=x_d[0:128, :], w=[b_xr[0]])
            b_fsi = [Buf() for _ in range(NT)]

            def out1(i):
                q = i % 2
                tsl = slice(i * 128, (i + 1) * 128)
                if i + 1 < NT:
                    dpool(out=xr[1 - q][:], in_=x_d[(i + 1) * 128:(i + 2) * 128, :], w=[b_xr[1 - q]])
                for hf in range(2):
                    bk = 2 * q + hf
                    for kc in range(8):
                        pe(lambda: nc.tensor.matmul(PB[bk][:], lhsT=mT[:, kc, tsl], rhs=Wout[:, kc, hf * 512:(hf + 1) * 512],
                                                    start=(kc == 0), stop=(kc == 7)),
                           r=[b_mT[i // 4], b_W], w=[b_PB[bk]], inc=(kc == 7))
                    dve(lambda: nc.vector.tensor_tensor(out=xo[q][:, hf * 512:(hf + 1) * 512], in0=PB[bk][:],
                                                        in1=xr[q][:, hf * 512:(hf + 1) * 512], op=ALU.add),
                        r=[b_PB[bk], b_xr[q]], w=[b_xo[q]])
                act(lambda: nc.scalar.activation(out=junkc[:], in_=xo[q][:], func=AF.Square, accum_out=fss[:, i:i + 1]),
                    r=[b_xo[q]], w=[b_fs, b_fsi[i]])
                act(lambda: nc.scalar.activation(out=fss[:, i:i + 1], in_=fss[:, i:i + 1], func=AF.Sqrt, scale=1.0 / D,
                                                 bias=EPS), w=[b_fsi[i]])

            def out2(i):
                q = i % 2
                tsl = slice(i * 128, (i + 1) * 128)
                dve(lambda: nc.vector.reciprocal(out=fss[:, i:i + 1], in_=fss[:, i:i + 1]), w=[b_fsi[i]])
                dve(lambda: nc.vector.scalar_tensor_tensor(out=fo[q][:], in0=xo[q][:], scalar=fss[:, i:i + 1],
                                                           in1=fnw_b[:], op0=ALU.mult, op1=ALU.mult),
                    r=[b_xo[q], b_ct, b_fsi[i]], w=[b_fo[q]])
                dsync(out=out_d[tsl, :], in_=fo[q][:], r=[b_fo[q]])

            out1(0)
            for i in range(NT):
                if i + 1 < NT:
                    out1(i + 1)
                out2(i)
            k.barrier()

        k.barrier()
    return nc, k


_NC_CACHE = {}


def kernel(x, positions, norm_w, w_in, gate_bias, conv_w, conv_b, dt_bias, a_log, d_skip,
           ssm_norm_w, w_branch_a, w_branch_b, w_out, final_norm_w):
    f32 = np.float32
    x = np.asarray(x, dtype=f32)
    positions = np.asarray(positions).astype(np.int32)
    nb = x.shape[0]
    assert nb == 8 and x.shape[1] == S and x.shape[2] == D
    if "nc" not in _NC_CACHE:
        _NC_CACHE["nc"] = build()[0]
    nc = _NC_CACHE["nc"]
    invf = (500000.0 ** (-np.arange(0, 16, 2, dtype=f32) / 16)).astype(f32)
    shared = {
        "invf": np.ascontiguousarray(np.broadcast_to(invf, (128, 8))),
        "norm_w": np.ascontiguousarray(np.asarray(norm_w, f32).reshape(1, D)),
        "w_in": np.ascontiguousarray(np.asarray(w_in, f32)[0]),
        "cwT": np.ascontiguousarray(np.asarray(conv_w, f32)[0].reshape(4, 12, 128).transpose(2, 1, 0)),
        "cbT": np.ascontiguousarray(np.asarray(conv_b, f32)[0].reshape(12, 128).T),
        "dt_bias": np.ascontiguousarray(np.asarray(dt_bias, f32).reshape(1, 16)),
        "a_log": np.ascontiguousarray(np.asarray(a_log, f32).reshape(1, 16)),
        "d_skip": np.ascontiguousarray(np.asarray(d_skip, f32).reshape(1, 16)),
        "ssm_norm_w": np.ascontiguousarray(np.asarray(ssm_norm_w, f32).reshape(1, D)),
        "gbT": np.ascontiguousarray(np.asarray(gate_bias, f32)[0].reshape(16, 128).T),
        "w_branch_a": np.ascontiguousarray(np.asarray(w_branch_a, f32)[0]),
        "w_branch_b": np.ascontiguousarray(np.asarray(w_branch_b, f32)[0]),
        "w_out": np.ascontiguousarray(np.asarray(w_out, f32)[0]),
        "final_norm_w": np.ascontiguousarray(np.asarray(final_norm_w, f32).reshape(1, D)),
    }
    in_maps = []
    for b in range(nb):
        m = dict(shared)
        m["x"] = np.ascontiguousarray(x[b])
        m["posT"] = np.ascontiguousarray(positions[b].reshape(NT, 128).T)
        in_maps.append(m)
    res = run_bass_kernel_spmd(nc, in_maps, core_ids=list(range(nb)))
    out = np.stack([np.asarray(res.results[b]["out"], dtype=f32) for b in range(nb)], axis=0)
    return out
```

```python
import numpy as np
import math
import concourse.bass as bass
import concourse.mybir as mybir
from concourse.bass_utils import run_bass_kernel_spmd
from contextlib import ExitStack

F32 = mybir.dt.float32
BF16 = mybir.dt.bfloat16
I32 = mybir.dt.int32
ALU = mybir.AluOpType
AF = mybir.ActivationFunctionType
AX = mybir.AxisListType

S = 2048
D = 1024
NT = 16
INW = 6228
EPS = 1e-6
NITER = 10
A_SRC = [(0, 512, 0), (512, 640, 512), (1280, 1536, 640), (1536, 1600, 896), (1600, 1604, 960),
         (640, 768, 964), (768, 1280, 1092)]
NA = 1604


class Buf:
    __slots__ = ("w", "r", "name", "excl")

    def __init__(self, name="", excl=False):
        self.w = {}
        self.r = {}
        self.name = name
        self.excl = excl


def _merge(deps, d):
    for k, (s, v) in d.items():
        if k not in deps or deps[k][1] < v:
            deps[k] = (s, v)


class Eng:
    def __init__(self, K, name, eng, selfdep=True):
        self.K = K
        self.name = name
        self.eng = eng
        self.selfdep = selfdep
        self.sem = K.es.enter_context(K.nc.semaphore("s_" + name))
        self.cnt = 0
        self.waited = {}
        self.pending = False

    def wait_deps(self, deps):
        for k, (s, v) in deps.items():
            if k == self.name and not self.selfdep:
                continue
            if self.waited.get(k, 0) < v:
                self.eng.wait_ge(s, v)
                self.waited[k] = v

    def __call__(self, fn, r=(), w=(), inc=True, extra=(), selfwait=None):
        if selfwait is not None:
            assert selfwait[0] == self.name
            if self.waited.get("self", 0) < selfwait[2]:
                self.eng.wait_ge(selfwait[1], selfwait[2])
                self.waited["self"] = selfwait[2]
        w = list(w) + [b for b in r if b.excl]
        r = [b for b in r if not b.excl]
        deps = {}
        for b in r:
            _merge(deps, b.w)
        for b in w:
            _merge(deps, b.w)
            _merge(deps, b.r)
        for t in extra:
            _merge(deps, {t[0]: (t[1], t[2])})
        self.wait_deps(deps)
        ins = fn()
        if inc:
            self.cnt += 1
            ins.then_inc(self.sem, 1)
            tok = (self.sem, self.cnt)
            self.pending = False
        else:
            tok = (self.sem, self.cnt + 1)
            self.pending = True
        for b in r:
            _merge(b.r, {self.name: tok})
        for b in w:
            b.w = {self.name: tok}
            b.r = {}
        return (self.name,) + tok


class DmaQ:
    def __init__(self, K, name, waiter, nsem=8):
        self.K = K
        self.name = name
        self.waiter = waiter
        self.sems = [K.es.enter_context(K.nc.semaphore(f"d_{name}{j}")) for j in range(nsem)]
        self.vals = [0] * nsem
        self.idx = 0

    def __call__(self, out, in_, r=(), w=(), extra=(), **kw):
        deps = {}
        for b in r:
            _merge(deps, b.w)
        for b in w:
            _merge(deps, b.w)
            _merge(deps, b.r)
        for t in extra:
            _merge(deps, {t[0]: (t[1], t[2])})
        k = self.idx
        self.idx = (k + 1) % len(self.sems)
        key = f"{self.name}{k}"
        if self.vals[k] > 0:
            _merge(deps, {key: (self.sems[k], self.vals[k])})
        self.waiter.wait_deps(deps)
        ins = self.waiter.eng.dma_start(out=out, in_=in_, **kw)
        self.vals[k] += 16
        ins.then_inc(self.sems[k], 16)
        tok = (self.sems[k], self.vals[k])
        for b in r:
            _merge(b.r, {key: tok})
        for b in w:
            b.w = {key: tok}
            b.r = {}
        return (key,) + tok


class K:
    def __init__(self, nc, es):
        self.nc = nc
        self.es = es
        self.pe = Eng(self, "pe", nc.tensor, selfdep=False)
        self.act = Eng(self, "act", nc.scalar)
        self.dve = Eng(self, "dve", nc.vector)
        self.pool = Eng(self, "pool", nc.gpsimd)
        self.sp = Eng(self, "sp", nc.sync)
        self.engs = [self.pe, self.act, self.dve, self.pool, self.sp]
        self.dsync = DmaQ(self, "qs", self.sp, nsem=8)
        self.dpool = DmaQ(self, "qp", self.pool, nsem=8)
        self.dqs = [self.dsync, self.dpool]
        self.dumps = []

    def sb(self, name, shape, dt, es=None):
        t = (es or self.es).enter_context(self.nc.sbuf_tensor(name, list(shape), dt))
        return t

    def ps(self, name, shape, dt, es=None):
        return (es or self.es).enter_context(self.nc.psum_tensor(name, list(shape), dt))

    def barrier(self):
        deps = {}
        for e in self.engs:
            assert not e.pending, e.name
            if e.cnt > 0:
                deps[e.name] = (e.sem, e.cnt)
        for q in self.dqs:
            for j, s in enumerate(q.sems):
                if q.vals[j] > 0:
                    deps[f"{q.name}{j}"] = (s, q.vals[j])
        for e in self.engs:
            e.wait_deps(deps)

    def dump(self, name, ap, buf):
        d = self.nc.dram_tensor(name, list(ap.shape), ap.dtype, kind="ExternalOutput").ap()
        self.dsync(out=d, in_=ap, r=(buf if isinstance(buf, (list, tuple)) else [buf]))
        self.dumps.append(name)


class Stop(Exception):
    pass


def build(stop_after="all", dbg=()):
    try:
        return _build(stop_after, dbg)
    except Stop as s:
        return s.args


def _build(stop_after="all", dbg=()):
    nc = bass.Bass("TRN2", target_bir_lowering=False)
    dbg = set(dbg)
    x_d = nc.dram_tensor("x", [S, D], F32, kind="ExternalInput").ap()
    posT_d = nc.dram_tensor("posT", [128, NT], I32, kind="ExternalInput").ap()
    invf_d = nc.dram_tensor("invf", [128, 8], F32, kind="ExternalInput").ap()
    normw_d = nc.dram_tensor("norm_w", [1, D], F32, kind="ExternalInput").ap()
    win_d = nc.dram_tensor("w_in", [D, INW], F32, kind="ExternalInput").ap()
    out_d = nc.dram_tensor("out", [S, D], F32, kind="ExternalOutput").ap()
    cwT_d = nc.dram_tensor("cwT", [128, 12, 4], F32, kind="ExternalInput").ap()
    cbT_d = nc.dram_tensor("cbT", [128, 12], F32, kind="ExternalInput").ap()
    dtb_d = nc.dram_tensor("dt_bias", [1, 16], F32, kind="ExternalInput").ap()
    alog_d = nc.dram_tensor("a_log", [1, 16], F32, kind="ExternalInput").ap()
    dsk_d = nc.dram_tensor("d_skip", [1, 16], F32, kind="ExternalInput").ap()
    snw_d = nc.dram_tensor("ssm_norm_w", [1, D], F32, kind="ExternalInput").ap()
    gbT_d = nc.dram_tensor("gbT", [128, 16], F32, kind="ExternalInput").ap()
    wpa_d = nc.dram_tensor("w_branch_a", [512, D], F32, kind="ExternalInput").ap()
    wpb_d = nc.dram_tensor("w_branch_b", [D, D], F32, kind="ExternalInput").ap()
    wout_d = nc.dram_tensor("w_out", [D, D], F32, kind="ExternalInput").ap()
    fnw_d = nc.dram_tensor("final_norm_w", [1, D], F32, kind="ExternalInput").ap()

    with ExitStack() as es:
        k = K(nc, es)
        pe, act, dve, pool, sp = k.pe, k.act, k.dve, k.pool, k.sp
        dsync, dpool = k.dsync, k.dpool

        def ck(name):
            if stop_after == name:
                k.barrier()
                raise Stop(nc, k)

        ident = k.sb("ident", [128, 128], BF16)
        Uf = k.sb("Uf", [128, 128], F32)
        Ub = k.sb("Ub", [128, 128], BF16)
        cbias = k.sb("cbias", [128, 128], F32)
        pow2 = k.sb("pow2", [128, NITER + 2], F32)
        negU = k.sb("negU", [128, 128], BF16)
        negUb = k.sb("negUb", [128, 128], BF16)
        mhalf = k.sb("mhalf", [128, 16], F32)
        b_const = Buf("const")
        pool(lambda: nc.gpsimd.memset(ident[:], 1.0), w=[b_const])
        pool(lambda: nc.gpsimd.affine_select(out=ident[:], in_=ident[:], pattern=[[-1, 128]],
                                             compare_op=ALU.is_equal, fill=0.0, base=0, channel_multiplier=1),
             w=[b_const])
        pool(lambda: nc.gpsimd.memset(Uf[:], 1.0), w=[b_const])
        pool(lambda: nc.gpsimd.affine_select(out=Uf[:], in_=Uf[:], pattern=[[1, 128]], compare_op=ALU.is_ge,
                                             fill=0.0, base=0, channel_multiplier=-1), w=[b_const])
        pool(lambda: nc.gpsimd.tensor_copy(out=Ub[:], in_=Uf[:]), w=[b_const])
        pool(lambda: nc.gpsimd.tensor_scalar(out=negUb[:], in0=Ub[:], scalar1=-1.0, scalar2=None, op0=ALU.mult),
             w=[b_const])
        pool(lambda: nc.gpsimd.memset(mhalf[:], -0.5), w=[b_const])
        pool(lambda: nc.gpsimd.memset(negU[:], 0.0), w=[b_const])
        pool(lambda: nc.gpsimd.affine_select(out=negU[:], in_=negU[:], pattern=[[1, 128]], compare_op=ALU.is_ge,
                                             fill=-30000.0, base=0, channel_multiplier=-1), w=[b_const])
        pool(lambda: nc.gpsimd.memset(cbias[:], 0.0), w=[b_const])
        pool(lambda: nc.gpsimd.affine_select(out=cbias[:], in_=cbias[:], pattern=[[-1, 128]], compare_op=ALU.is_ge,
                                             fill=-1e30, base=0, channel_multiplier=1), w=[b_const])
        for j in range(NITER + 2):
            pool(lambda: nc.gpsimd.memset(pow2[:, j:j + 1], 2.0 ** (-j)), w=[b_const])
        pool(lambda: nc.gpsimd.memset(pow2[:, 0:1], 1.0), w=[b_const])

        hT = k.sb("hT", [128, 8, S], BF16)
        b_hT = [Buf(f"hT{i}") for i in range(NT)]
        wBC = k.sb("wBC", [128, 8, 1024], BF16)
        b_wslot = [Buf(), Buf()]
        wG0 = wBC[:, :, 0:256].rearrange("p k (h n) -> p k h n", h=2)
        Wpb0 = wBC[:, :, 256:384]
        Wpa0 = wBC[:, 0:4, 384:512]
        CONV0 = 2628
        oaT = k.sb("oaT", [128, 4, S], BF16)
        b_oaT = [Buf() for _ in range(NT)]
        esWA = ExitStack()
        wA = k.sb("wA", [128, 8, NA], BF16, esWA)
        b_wA = Buf()
        for (c0, c1, dst) in A_SRC:
            dpool(out=wA[:, :, dst:dst + (c1 - c0)],
                  in_=win_d[:, c0:c1].rearrange("(k p) n -> p k n", p=128), w=[b_wA])

        with ExitStack() as es0:
            normw_b = k.sb("normw_b", [128, D], F32, es0)
            b_normw = Buf()
            dsync(out=normw_b[:], in_=normw_d[0, :].partition_broadcast(128), w=[b_normw])
            xt = k.sb("xt_all", [128, NT, D], F32, es0)
            b_xt = [Buf() for _ in range(NT)]
            xn = [k.sb(f"xn{j}", [128, D], BF16, es0) for j in range(3)]
            b_xn = [Buf(), Buf(), Buf()]
            junk = k.sb("junk0", [128, D], BF16, es0)
            b_junk = Buf()
            ss = k.sb("ss", [128, NT], F32, es0)
            sd = k.sb("sd", [128, NT], F32, es0)
            rstd = k.sb("rstd", [128, NT], F32, es0)
            b_ssg = [Buf() for _ in range(4)]
            pt = [k.ps(f"pt{j}", [128, 8, 128], BF16, es0) for j in range(2)]
            b_pt = [Buf(excl=True), Buf(excl=True)]
            for i in range(NT):
                (dsync if i % 2 == 0 else dsync)(out=xt[:, i, :], in_=x_d[i * 128:(i + 1) * 128, :], w=[b_xt[i]])

            def p0_stats(gq):
                for i in range(4 * gq, 4 * gq + 4):
                    act(lambda: nc.scalar.activation(out=junk[:], in_=xt[:, i, :], func=AF.Square,
                                                     accum_out=ss[:, i:i + 1]),
                        r=[b_xt[i]], w=[b_junk, b_ssg[gq]])
                act(lambda: nc.scalar.activation(out=sd[:, 4 * gq:4 * gq + 4], in_=ss[:, 4 * gq:4 * gq + 4],
                                                 func=AF.Sqrt, scale=1.0 / D, bias=EPS), w=[b_ssg[gq]])
                dve(lambda: nc.vector.reciprocal(out=rstd[:, 4 * gq:4 * gq + 4], in_=sd[:, 4 * gq:4 * gq + 4]),
                    w=[b_ssg[gq]])

            def p0_apply(gq):
                for i in range(4 * gq, 4 * gq + 4):
                    j = i % 3
                    jp = i % 2
                    dve(lambda: nc.vector.scalar_tensor_tensor(out=xn[j][:], in0=xt[:, i, :], scalar=rstd[:, i:i + 1],
                                                               in1=normw_b[:], op0=ALU.mult, op1=ALU.mult),
                        r=[b_xt[i], b_ssg[gq], b_normw], w=[b_xn[j]])
                    for c in range(8):
                        pe(lambda: nc.tensor.transpose(out=pt[jp][:, c, :], in_=xn[j][:, c * 128:(c + 1) * 128],
                                                       identity=ident[:]),
                           r=[b_xn[j], b_const], w=[b_pt[jp]], inc=(c == 7))
                    act(lambda: nc.scalar.copy(out=hT[:, :, i * 128:(i + 1) * 128], in_=pt[jp][:]),
                        r=[b_pt[jp]], w=[b_hT[i]])

            p0_stats(0)
            for gq in range(4):
                if gq + 1 < 4:
                    p0_stats(gq + 1)
                p0_apply(gq)
            k.barrier()
        if "hT" in dbg:
            k.dump("d_hT", hT[:], b_hT[NT - 1])
        if stop_after == "p0":
            k.barrier()
            return nc, k


        PB = [k.ps(f"pb{j}", [128, 512], F32) for j in range(8)]
        b_PB = [Buf(f"pb{j}", excl=True) for j in range(8)]

        def bfv(j):
            return PB[j][:].bitcast(BF16).rearrange("p (s t) -> p s t", t=128)

        with ExitStack() as esA:
            dpool(out=wBC[:, :, 0:512], in_=win_d[:, CONV0:CONV0 + 512].rearrange("(k p) n -> p k n", p=128),
                  w=[b_wslot[0]])
            dpool(out=wBC[:, :, 512:1024], in_=win_d[:, CONV0 + 512:CONV0 + 1024].rearrange("(k p) n -> p k n", p=128),
                  w=[b_wslot[1]])
            posi = k.sb("posi", [128, NT], I32, esA)
            posf = k.sb("posf", [128, NT], F32, esA)
            invf = k.sb("invf_sb", [128, 8], F32, esA)
            ang = k.sb("ang", [128, NT, 8], F32, esA)
            cos_t = k.sb("cos_t", [128, NT, 8], F32, esA)
            sin_t = k.sb("sin_t", [128, NT, 8], F32, esA)
            ry = k.sb("ry", [128, NT, 8], F32, esA)
            rki = k.sb("rki", [128, NT, 8], I32, esA)
            rkf = k.sb("rkf", [128, NT, 8], F32, esA)
            rg = k.sb("rg", [128, NT, 8], F32, esA)
            b_tab = Buf()
            dsync(out=posi[:], in_=posT_d[:, :], w=[b_tab])
            dsync(out=invf[:], in_=invf_d[:, :], w=[b_tab])
            dve(lambda: nc.vector.tensor_copy(out=posf[:], in_=posi[:]), r=[b_tab], w=[b_tab])
            dve(lambda: nc.vector.tensor_tensor(out=ang[:], in0=posf[:].unsqueeze(2).to_broadcast([128, NT, 8]),
                                                in1=invf[:].unsqueeze(1).to_broadcast([128, NT, 8]), op=ALU.mult),
                r=[b_tab], w=[b_tab])
            TWO_PI = 2.0 * math.pi
            for (dst_t, off) in ((sin_t, 0.0), (cos_t, 0.25)):
                dve(lambda: nc.vector.tensor_scalar(out=ry[:], in0=ang[:], scalar1=1.0 / TWO_PI, scalar2=off,
                                                    op0=ALU.mult, op1=ALU.add), r=[b_tab], w=[b_tab])
                dve(lambda: nc.vector.tensor_copy(out=rki[:], in_=ry[:]), r=[b_tab], w=[b_tab])
                dve(lambda: nc.vector.tensor_copy(out=rkf[:], in_=rki[:]), r=[b_tab], w=[b_tab])
                dve(lambda: nc.vector.tensor_tensor(out=ry[:], in0=ry[:], in1=rkf[:], op=ALU.subtract),
                    r=[b_tab], w=[b_tab])
                dve(lambda: nc.vector.tensor_scalar(out=rg[:], in0=ry[:], scalar1=0.5, scalar2=None, op0=ALU.is_ge),
                    r=[b_tab], w=[b_tab])
                dve(lambda: nc.vector.tensor_tensor(out=ry[:], in0=ry[:], in1=rg[:], op=ALU.subtract),
                    r=[b_tab], w=[b_tab])
                dve(lambda: nc.vector.tensor_scalar(out=rg[:], in0=ry[:], scalar1=-0.5, scalar2=None, op0=ALU.is_lt),
                    r=[b_tab], w=[b_tab])
                dve(lambda: nc.vector.tensor_tensor(out=ry[:], in0=ry[:], in1=rg[:], op=ALU.add),
                    r=[b_tab], w=[b_tab])
                act(lambda: nc.scalar.activation(out=dst_t[:], in_=ry[:], func=AF.Sin, scale=TWO_PI * (1.0 - 1e-6)),
                    r=[b_tab], w=[b_tab])
            ck("Atab")
            if "rope" in dbg:
                k.dump("d_cos", cos_t[:], b_tab)
                k.dump("d_sin", sin_t[:], b_tab)

            qk_sb = [k.sb(f"qk_sb{j}", [128, 15, 64], BF16, esA) for j in range(2)]
            b_qk = [Buf(), Buf()]
            rt = [k.sb(f"rt{j}", [128, 15, 8], F32, esA) for j in range(4)]
            b_rt = Buf()
            rsrc = k.sb("rsrc", [128, 15, 16], F32, esA)
            b_rsrc = Buf()
            QT = [k.sb(f"QT{j}", [128, 12, 128], BF16, esA) for j in range(2)]
            b_QT = [Buf(), Buf()]
            KT = k.sb("KT", [128, 2, S], BF16, esA)
            KIT = k.sb("KIT", [128, S], BF16, esA)
            b_KT = [Buf() for _ in range(NT)]
            for j_ in range(2):
                pool(lambda: nc.gpsimd.memset(QT[j_][64:128], 0.0), w=[b_QT[j_]])
            pool(lambda: nc.gpsimd.memset(KT[64:128], 0.0), w=b_KT)
            pool(lambda: nc.gpsimd.memset(KIT[64:128], 0.0), w=b_KT)
            Vaug = k.sb("Vaug", [128, NT, 2, 65], BF16, esA)
            b_V = [Buf() for _ in range(NT)]
            wv = k.sb("wv", [128, NT, 4], F32, esA)
            b_wv = [Buf() for _ in range(NT)]
            sza = [k.sb(f"sza{j}", [128, 512], F32, esA) for j in range(2)]
            b_sza = [Buf(), Buf()]
            sc = [k.sb(f"sc{j}", [128, S], F32, esA) for j in range(2)]
            b_sc = [Buf(), Buf()]
            rl = [k.sb(f"rl{j}", [128, S], F32, esA) for j in range(2)]
            b_rl = [Buf(), Buf()]
            junkb = k.sb("junkb", [128, S], BF16, esA)
            b_junkb = Buf()
            m01 = k.sb("m01", [128, S], BF16, esA)
            b_m01 = Buf()
            maskT = [k.sb(f"maskT{j}", [128, NT, 128], BF16, esA) for j in range(2)]
            b_maskT = [Buf(), Buf()]
            Bv = k.sb("Bv", [128, 1], F32, esA)
            Bk = k.sb("Bk", [128, NITER + 2], F32, esA)
            mid = [k.sb(f"mid{j}", [128, 1], F32, esA) for j in range(2)]
            cnt = k.sb("cnt", [128, 1], F32, esA)
            dd = k.sb("dd", [128, 1], F32, esA)
            thr = k.sb("thr", [128, NT], F32, esA)
            b_bis = Buf()
            Eb = [k.sb(f"Eb{j}", [128, 512], BF16, esA) for j in range(3)]
            b_Eb = [Buf(), Buf(), Buf()]
            rinv = k.sb("rinv", [128, 4], F32, esA)
            otmp = k.sb("otmp", [128, 4, 64], F32, esA)
            rinv8 = k.sb("rinv8", [128, 8], F32, esA)
            otmp8 = k.sb("otmp8", [128, 8, 64], F32, esA)
            b_otmp = Buf()
            oa_sb = k.sb("oa_sb", [128, 512], BF16, esA)
            b_oa = Buf()
            pool(lambda: nc.gpsimd.memset(Vaug[:], 1.0), w=b_V)

            A_BANK = [(0, 0, 512), (1, 512, 452), (2, 964, 512), (3, 1476, 128)]
            T0v = bfv(4)
            T1v = bfv(5)
            ctr = {"st": 0, "ix": 0, "e": 0}

            def hdr_(i):
                return i % 2, slice(i * 128, (i + 1) * 128), (i + 1) * 128, i >= 2

            def stage1(i):
                j, tsl, L, masked = hdr_(i)

                for (bk, c0, n) in A_BANK:
                    for kc in range(8):
                        pe(lambda: nc.tensor.matmul(PB[bk][:, 0:n], lhsT=hT[:, kc, tsl], rhs=wA[:, kc, c0:c0 + n],
                                                    start=(kc == 0), stop=(kc == 7)),
                           r=[b_hT[i], b_wA], w=[b_PB[bk]], inc=(kc == 7))
                ck(f"Aproj{i}")
                p0v = PB[0][:, 0:512].rearrange("p (h d) -> p h d", d=64)
                p1v = PB[1][:, 0:448].rearrange("p (h d) -> p h d", d=64)
                act(lambda: nc.scalar.copy(out=qk_sb[j][:, 0:8, 16:64], in_=p0v[:, :, 16:64]),
                    r=[b_PB[0]], w=[b_qk[j]])
                act(lambda: nc.scalar.copy(out=qk_sb[j][:, 8:15, 16:64], in_=p1v[:, :, 16:64]),
                    r=[b_PB[1]], w=[b_qk[j]])
                ck(f"Ae1_{i}")
                act(lambda: nc.scalar.copy(out=rsrc[:, 0:8, :], in_=p0v[:, :, 0:16]), r=[b_PB[0]], w=[b_rsrc])
                act(lambda: nc.scalar.copy(out=rsrc[:, 8:15, :], in_=p1v[:, :, 0:16]), r=[b_PB[1]], w=[b_rsrc])
                nh = 15
                cb = cos_t[:, i:i + 1, :].to_broadcast([128, nh, 8])
                sb_ = sin_t[:, i:i + 1, :].to_broadcast([128, nh, 8])
                x1 = rsrc[:, :, 0:8]
                x2 = rsrc[:, :, 8:16]
                dve(lambda: nc.vector.tensor_tensor(out=rt[0][:], in0=x1, in1=cb, op=ALU.mult),
                    r=[b_rsrc, b_tab], w=[b_rt])
                dve(lambda: nc.vector.tensor_tensor(out=rt[1][:], in0=x2, in1=sb_, op=ALU.mult),
                    r=[b_rsrc, b_tab], w=[b_rt])
                dve(lambda: nc.vector.tensor_tensor(out=qk_sb[j][:, :, 0:8], in0=rt[0][:], in1=rt[1][:], op=ALU.subtract),
                    r=[b_rt], w=[b_qk[j]])
                dve(lambda: nc.vector.tensor_tensor(out=rt[2][:], in0=x2, in1=cb, op=ALU.mult),
                    r=[b_rsrc, b_tab], w=[b_rt])
                dve(lambda: nc.vector.tensor_tensor(out=rt[3][:], in0=x1, in1=sb_, op=ALU.mult),
                    r=[b_rsrc, b_tab], w=[b_rt])
                dve(lambda: nc.vector.tensor_tensor(out=qk_sb[j][:, :, 8:16], in0=rt[2][:], in1=rt[3][:], op=ALU.add),
                    r=[b_rt], w=[b_qk[j]])
                ck(f"Ae2_{i}")
                act(lambda: nc.scalar.copy(out=wv[:, i, :], in_=PB[1][:, 448:452]), r=[b_PB[1]], w=[b_wv[i]])
                act(lambda: nc.scalar.copy(out=Vaug[:, i, :, 0:64],
                                           in_=PB[2][:, 0:128].rearrange("p (g d) -> p g d", d=64)),
                    r=[b_PB[2]], w=[b_V[i]])
                ck(f"Ae3_{i}")
                act(lambda: nc.scalar.activation(out=sza[j][:, 0:384], in_=PB[2][:, 128:512], func=AF.Silu),
                    r=[b_PB[2]], w=[b_sza[j]])
                act(lambda: nc.scalar.activation(out=sza[j][:, 384:512], in_=PB[3][:, 0:128], func=AF.Silu),
                    r=[b_PB[3]], w=[b_sza[j]])
                ck(f"Aevac{i}")
                for h in range(15):
                    tv, sl, bk = (T0v, h, 4) if h < 8 else (T1v, h - 8, 5)
                    pe(lambda: nc.tensor.transpose(out=tv[0:64, sl, :], in_=qk_sb[j][:, h, :], identity=ident[:]),
                       r=[b_qk[j], b_const], w=[b_PB[bk]], inc=(h == 7 or h == 14))
                act(lambda: nc.scalar.copy(out=QT[j][0:64, 0:8, :], in_=T0v[0:64, :, :]), r=[b_PB[4]], w=[b_QT[j]])
                act(lambda: nc.scalar.copy(out=KT[0:64, :, tsl], in_=T1v[0:64, 0:2, :]), r=[b_PB[5]], w=[b_KT[i]])
                act(lambda: nc.scalar.copy(out=QT[j][0:64, 8:12, :], in_=T1v[0:64, 2:6, :]), r=[b_PB[5]], w=[b_QT[j]])
                act(lambda: nc.scalar.copy(out=KIT[0:64, tsl], in_=T1v[0:64, 6, :]), r=[b_PB[5]], w=[b_KT[i]])


            def stage2(i):
                j, tsl, L, masked = hdr_(i)
                if not masked:
                    return

                nch = (L + 511) // 512
                for h in range(4):
                    q = h % 2
                    for c in range(nch):
                        c0 = c * 512
                        n = min(512, L - c0)
                        bk = ctr["ix"] % 2
                        ctr["ix"] += 1
                        pe(lambda: nc.tensor.matmul(PB[bk][:, 0:n], lhsT=QT[j][:, 8 + h, :], rhs=KIT[:, c0:c0 + n],
                                                    start=True, stop=True),
                           r=[b_QT[j]] + b_KT[0:i + 1], w=[b_PB[bk]])
                        act(lambda: nc.scalar.activation(out=rl[q][:, c0:c0 + n], in_=PB[bk][:, 0:n], func=AF.Relu),
                            r=[b_PB[bk]], w=[b_rl[q]])
                    if h == 0:
                        dve(lambda: nc.vector.tensor_scalar(out=sc[j][:, 0:L], in0=rl[q][:, 0:L],
                                                            scalar1=wv[:, i, 0:1], scalar2=None, op0=ALU.mult),
                            r=[b_rl[q], b_wv[i]], w=[b_sc[j]])
                    else:
                        dve(lambda: nc.vector.scalar_tensor_tensor(out=sc[j][:, 0:L], in0=rl[q][:, 0:L],
                                                                   scalar=wv[:, i, h:h + 1], in1=sc[j][:, 0:L],
                                                                   op0=ALU.mult, op1=ALU.add),
                            r=[b_rl[q], b_wv[i]], w=[b_sc[j]])


            def bisect(i):
                j, tsl, L, masked = hdr_(i)
                if not masked:
                    return
                yield

                dve(lambda: nc.vector.tensor_reduce(out=Bv[:], in_=sc[j][:, 0:L], axis=AX.X, op=ALU.max,
                                                    apply_absolute_value=True),
                    r=[b_sc[j]], w=[b_bis])
                dve(lambda: nc.vector.tensor_tensor(out=sc[j][:, L - 128:L], in0=sc[j][:, L - 128:L],
                                                    in1=cbias[:], op=ALU.add),
                    r=[b_const], w=[b_sc[j]])
                dve(lambda: nc.vector.tensor_scalar(out=Bk[:], in0=pow2[:], scalar1=Bv[:, 0:1], scalar2=None,
                                                    op0=ALU.mult), r=[b_const], w=[b_bis])
                dve(lambda: nc.vector.memset(mid[0][:], 0.0), w=[b_bis])
                for it in range(NITER):
                    ma, mb = mid[it % 2], mid[(it + 1) % 2]
                    dve(lambda: nc.vector.tensor_scalar(out=junkb[:, 0:L], in0=sc[j][:, 0:L], scalar1=ma[:, 0:1],
                                                        scalar2=None, op0=ALU.is_ge, op1=ALU.add,
                                                        accum_out=cnt[:, 0:1]),
                        r=[b_sc[j]], w=[b_bis, b_junkb])
                    dve(lambda: nc.vector.tensor_scalar(out=dd[:], in0=cnt[:], scalar1=255.5,
                                                        scalar2=Bk[:, it:it + 1], op0=ALU.is_ge, op1=ALU.mult),
                        w=[b_bis])
                    dve(lambda: nc.vector.tensor_scalar(out=mb[:], in0=dd[:], scalar1=Bk[:, it + 1:it + 2],
                                                        scalar2=ma[:, 0:1], op0=ALU.subtract, op1=ALU.add),
                        w=[b_bis])
                    yield
                mfin = mid[NITER % 2]
                dve(lambda: nc.vector.tensor_tensor(out=thr[:, i:i + 1], in0=mfin[:], in1=Bk[:, NITER:NITER + 1],
                                                    op=ALU.subtract), w=[b_bis])
                dve(lambda: nc.vector.tensor_scalar(out=m01[:, 0:L], in0=sc[j][:, 0:L], scalar1=thr[:, i:i + 1],
                                                    scalar2=None, op0=ALU.is_ge),
                    r=[b_sc[j], b_bis], w=[b_m01])
                if ("sc%d" % i) in dbg:
                    k.dump("d_sc", sc[j][:, 0:L], b_sc[j])
                    k.dump("d_thr", thr[:, i:i + 1], b_bis)


            def masktr(i):
                j, tsl, L, masked = hdr_(i)
                if not masked:
                    return

                for jb in range(i + 1):
                    tv, sl, bk = (T0v, jb, 4) if jb < 8 else (T1v, jb - 8, 5)
                    last = (jb == i) or (jb == 7)
                    pe(lambda: nc.tensor.transpose(out=tv[:, sl, :], in_=m01[:, jb * 128:(jb + 1) * 128],
                                                   identity=ident[:]),
                       r=[b_m01, b_const], w=[b_PB[bk]], inc=last)
                n0 = min(i + 1, 8)
                act(lambda: nc.scalar.activation(out=maskT[j][:, 0:n0, :], in_=T0v[:, 0:n0, :], func=AF.Identity,
                                                 scale=30000.0, bias=-30000.0),
                    r=[b_PB[4]], w=[b_maskT[j]])
                if i + 1 > 8:
                    act(lambda: nc.scalar.activation(out=maskT[j][:, 8:i + 1, :], in_=T1v[:, 0:i + 1 - 8, :],
                                                     func=AF.Identity, scale=30000.0, bias=-30000.0),
                        r=[b_PB[5]], w=[b_maskT[j]])


            def attn_main(i):
                j, tsl, L, masked = hdr_(i)
                nkt = i + 1
                for g in range(2):
                    Ov = PB[6 + g][:, 0:260].rearrange("p (h d) -> p h d", d=65)

                    def st_mm(jb):
                        bk = 2 + (ctr["st"] % 2)
                        ctr["st"] += 1
                        has_mask = masked or jb == i
                        pe(lambda: nc.tensor.matmul(PB[bk][:].rearrange("p (h t) -> p h t", t=128),
                                                    lhsT=KT[:, g, jb * 128:(jb + 1) * 128],
                                                    rhs=QT[j][:, 4 * g:4 * g + 4, :], start=True, stop=(not has_mask)),
                           r=[b_QT[j], b_KT[jb]], w=[b_PB[bk]], inc=(not has_mask))
                        if has_mask:
                            if masked:
                                mb_ap = maskT[j][:, jb:jb + 1, :].to_broadcast([128, 4, 128])
                                rd = [b_maskT[j], b_const]
                            else:
                                mb_ap = negU[:].unsqueeze(1).to_broadcast([128, 4, 128])
                                rd = [b_const]
                            pe(lambda: nc.tensor.matmul(PB[bk][:].rearrange("p (h t) -> p h t", t=128),
                                                        lhsT=ident[:], rhs=mb_ap, start=False, stop=True),
                               r=rd, w=[b_PB[bk]])
                        return bk

                    def exp_pv(jb, bk):
                        e = ctr["e"] % 3
                        ctr["e"] += 1
                        act(lambda: nc.scalar.activation(out=Eb[e][:], in_=PB[bk][:], func=AF.Exp, scale=0.125),
                            r=[b_PB[bk]], w=[b_Eb[e]])
                        for hh in range(4):
                            pe(lambda: nc.tensor.matmul(Ov[:, hh, :], lhsT=Eb[e][:, hh * 128:(hh + 1) * 128],
                                                        rhs=Vaug[:, jb, g, :], start=(jb == 0 and hh == 0),
                                                        stop=(jb == i and hh == 3)),
                               r=[b_Eb[e], b_V[jb]], w=[b_PB[6 + g]], inc=(hh == 3))

                    bks = {0: st_mm(0)}
                    for jb in range(nkt):
                        if jb + 1 < nkt:
                            bks[jb + 1] = st_mm(jb + 1)
                        exp_pv(jb, bks[jb])

            def attn_fin(i):
                j, tsl, L, masked = hdr_(i)
                for g in range(2):
                    Ov = PB[6 + g][:, 0:260].rearrange("p (h d) -> p h d", d=65)
                    dve(lambda: nc.vector.reciprocal(out=rinv8[:, 4 * g:4 * g + 4].unsqueeze(2), in_=Ov[:, :, 64:65]),
                        r=[b_PB[6 + g]], w=[b_otmp])
                    dve(lambda: nc.vector.tensor_tensor(out=otmp8[:, 4 * g:4 * g + 4, :], in0=Ov[:, :, 0:64],
                                                        in1=rinv8[:, 4 * g:4 * g + 4].unsqueeze(2)
                                                        .to_broadcast([128, 4, 64]), op=ALU.mult),
                        r=[b_PB[6 + g]], w=[b_otmp])
                dve(lambda: nc.vector.tensor_tensor(out=oa_sb[:], in0=otmp8[:].rearrange("p h d -> p (h d)"),
                                                    in1=sza[j][:], op=ALU.mult),
                    r=[b_otmp, b_sza[j]], w=[b_oa])


            def fin_tr(i):
                j, tsl, L, masked = hdr_(i)
                for c in range(4):
                    pe(lambda: nc.tensor.transpose(out=T0v[:, c, :], in_=oa_sb[:, c * 128:(c + 1) * 128],
                                                   identity=ident[:]),
                       r=[b_oa, b_const], w=[b_PB[4]], inc=(c == 3))
                act(lambda: nc.scalar.copy(out=oaT[:, :, tsl], in_=T0v[:, 0:4, :]), r=[b_PB[4]], w=[b_oaT[i]])


            nA = NT if "skipA" not in dbg else 0
            pend = None
            for i in range(nA + 2):
                if i < nA:
                    stage1(i)
                if pend is not None:
                    for _ in pend:
                        pass
                    pend = None
                if i < nA:
                    stage2(i)
                if 1 <= i <= nA:
                    masktr(i - 1)
                if 2 <= i <= nA + 1:
                    fin_tr(i - 2)
                if 1 <= i <= nA:
                    attn_main(i - 1)
                if i < nA:
                    g_ = bisect(i)
                    nsteps = (NITER - 2) if (i + 1 < nA) else 10 ** 9
                    done_ = False
                    for _s in range(nsteps):
                        try:
                            next(g_)
                        except StopIteration:
                            done_ = True
                            break
                    if not done_:
                        pend = g_
                if 1 <= i <= nA:
                    attn_fin(i - 1)
            i = nA - 1
            j = i % 2
            L = (i + 1) * 128

            if "qk" in dbg:
                k.dump("d_KT", KT[:, :, 0:L], b_KT)
                k.dump("d_KIT", KIT[:, 0:L], b_KT)
                k.dump("d_QT", QT[j][:], b_QT[j])
                k.dump("d_V", Vaug[:, 0:i + 1], b_V)
            if "oaT" in dbg:
                k.dump("d_oaT", oaT[:, :, 0:L], b_oaT)
            k.barrier()
        esWA.close()
        if stop_after.startswith("A"):
            return nc, k


        obT = k.sb("obT", [128, 8, S], BF16)
        b_obT = [Buf() for _ in range(NT)]
        with ExitStack() as esB:
            wdt = k.sb("wdt", [128, 8, 16], BF16, esB)
            b_wdt = Buf()
            dpool(out=wdt[:], in_=win_d[:, 4164:4180].rearrange("(k p) n -> p k n", p=128), w=[b_wdt])
            X_tm = k.sb("X_tm", [128, NT, 1024], BF16, esB)
            B_tm = k.sb("B_tm", [128, NT, 256], BF16, esB)
            BT = k.sb("BT", [128, 2, S], BF16, esB)
            CT = k.sb("CT", [128, 2, S], BF16, esB)
            b_X = Buf()
            dtb_b = k.sb("dtb_b", [128, 16], F32, esB)
            a_b = k.sb("a_b", [128, 16], F32, esB)
            dsk_b = k.sb("dsk_b", [128, 16], F32, esB)
            snw_b = k.sb("snw_b", [128, D], F32, esB)
            dt_all = k.sb("dt_all", [128, NT, 16], F32, esB)
            dA_all = k.sb("dA_all", [128, NT, 16], F32, esB)
            spt = [k.sb(f"spt{j}", [128, NT, 16], F32, esB) for j in range(3)]
            ones_f = k.sb("ones_f", [128, 128], F32, esB)
            NEGU4 = k.sb("NEGU4", [128, 4, 128], BF16, esB)
            Dg = k.sb("Dg", [128, 16, 128], BF16, esB)
            b_ptab = Buf()
            dsync(out=dtb_b[:], in_=dtb_d[0, :].partition_broadcast(128), w=[b_ptab])
            dsync(out=a_b[:], in_=alog_d[0, :].partition_broadcast(128), w=[b_ptab])
            dsync(out=dsk_b[:], in_=dsk_d[0, :].partition_broadcast(128), w=[b_ptab])
            dsync(out=snw_b[:], in_=snw_d[0, :].partition_broadcast(128), w=[b_ptab])
            pool(lambda: nc.gpsimd.memset(ones_f[:], 1.0), w=[b_ptab])
            pool(lambda: nc.gpsimd.memset(NEGU4[:], 0.0), w=[b_ptab])
            pool(lambda: nc.gpsimd.affine_select(out=NEGU4[:], in_=NEGU4[:], pattern=[[0, 4], [1, 128]],
                                                 compare_op=ALU.is_ge, fill=-1.0e4, base=0, channel_multiplier=-1),
                 w=[b_ptab])
            act(lambda: nc.scalar.activation(out=a_b[:], in_=a_b[:], func=AF.Exp), w=[b_ptab])
            dve(lambda: nc.vector.tensor_scalar(out=a_b[:], in0=a_b[:], scalar1=-1.0, scalar2=None, op0=ALU.mult),
                w=[b_ptab])
            dve(lambda: nc.vector.tensor_tensor(out=Dg[:], in0=ident[:].unsqueeze(1).to_broadcast([128, 16, 128]),
                                                in1=dsk_b[:].unsqueeze(2).to_broadcast([128, 16, 128]), op=ALU.mult),
                r=[b_const], w=[b_ptab])
            ck("Btab")

            for i in range(NT):
                for kc in range(8):
                    pe(lambda: nc.tensor.matmul(PB[0][:, i * 16:(i + 1) * 16], lhsT=hT[:, kc, i * 128:(i + 1) * 128],
                                                rhs=wdt[:, kc, :], start=(kc == 0), stop=(kc == 7)),
                       r=[b_hT[i], b_wdt], w=[b_PB[0]], inc=(kc == 7))
            dve(lambda: nc.vector.tensor_tensor(out=spt[0][:], in0=PB[0][:, 0:256].rearrange("p (i h) -> p i h", h=16),
                                                in1=dtb_b[:].unsqueeze(1).to_broadcast([128, NT, 16]), op=ALU.add),
                r=[b_PB[0]], w=[b_ptab])
            dve(lambda: nc.vector.tensor_scalar(out=spt[2][:], in0=spt[0][:], scalar1=-1.0, scalar2=None, op0=ALU.mult),
                w=[b_ptab])
            dve(lambda: nc.vector.tensor_tensor(out=spt[1][:], in0=spt[0][:], in1=spt[2][:], op=ALU.max), w=[b_ptab])
            act(lambda: nc.scalar.activation(out=spt[1][:], in_=spt[1][:], func=AF.Exp, scale=-1.0), w=[b_ptab])
            act(lambda: nc.scalar.activation(out=spt[1][:], in_=spt[1][:], func=AF.Ln, bias=1.0), w=[b_ptab])
            dve(lambda: nc.vector.tensor_scalar(out=spt[2][:], in0=spt[0][:], scalar1=0.0, scalar2=None, op0=ALU.max),
                w=[b_ptab])
            dve(lambda: nc.vector.tensor_tensor(out=dt_all[:], in0=spt[2][:], in1=spt[1][:], op=ALU.add), w=[b_ptab])
            dve(lambda: nc.vector.tensor_tensor(out=dA_all[:], in0=dt_all[:],
                                                in1=a_b[:].unsqueeze(1).to_broadcast([128, NT, 16]), op=ALU.mult),
                w=[b_ptab])
            if "dt" in dbg:
                k.dump("d_dt", dt_all[:], b_ptab)
            ck("Bdt")

            with ExitStack() as esC:
                cwT = k.sb("cwT_sb", [128, 12, 4], F32, esC)
                cbT = k.sb("cbT_sb", [128, 12], F32, esC)
                b_cw = Buf()
                dsync(out=cwT[:], in_=cwT_d[:, :, :], w=[b_cw])
                dsync(out=cbT[:], in_=cbT_d[:, :], w=[b_cw])
                pre = [k.sb(f"pre{j}", [128, S + 3], F32, esC) for j in range(2)]
                b_pre = [Buf(), Buf()]
                accs = [k.sb(f"acc{j}", [128, S], F32, esC) for j in range(2)]
                b_accs = [Buf(), Buf()]
                xs_fm = k.sb("xs_fm", [128, S], BF16, esC)
                b_xs = Buf()
                for q in range(2):
                    pool(lambda: nc.gpsimd.memset(pre[q][:, 0:3], 0.0), w=[b_pre[q]])
                b_xs2 = [Buf(), Buf()]
                b_cv = [Buf() for _ in range(12)]

                def cproj(m):
                    q = m % 2
                    slot = (m // 4) % 2
                    wc0 = slot * 512 + (m % 4) * 128
                    for tc in range(4):
                        for kc in range(8):
                            pe(lambda: nc.tensor.matmul(PB[tc][:], lhsT=wBC[:, kc, wc0:wc0 + 128],
                                                        rhs=hT[:, kc, tc * 512:(tc + 1) * 512],
                                                        start=(kc == 0), stop=(kc == 7)),
                               r=b_hT[tc * 4:(tc + 1) * 4] + [b_wslot[slot]], w=[b_PB[tc]], inc=(kc == 7))
                        act(lambda: nc.scalar.copy(out=pre[q][:, 3 + tc * 512:3 + (tc + 1) * 512], in_=PB[tc][:]),
                            r=[b_PB[tc]], w=[b_pre[q]])

                def cpost(m):
                    q = m % 2
                    acc = accs[q]
                    b_acc = b_accs[q]
                    dve(lambda: nc.vector.tensor_scalar(out=acc[:], in0=pre[q][:, 0:S], scalar1=cwT[:, m, 0:1],
                                                        scalar2=None, op0=ALU.mult),
                        r=[b_pre[q], b_cw], w=[b_acc])
                    for kk in range(1, 4):
                        dve(lambda: nc.vector.scalar_tensor_tensor(out=acc[:], in0=pre[q][:, kk:kk + S],
                                                                   scalar=cwT[:, m, kk:kk + 1], in1=acc[:],
                                                                   op0=ALU.mult, op1=ALU.add),
                            r=[b_pre[q], b_cw], w=[b_acc])
                    if m < 10:
                        dst = xs_fm[:] if m < 8 else BT[:, m - 8, :]
                        bdst = b_xs if m < 8 else b_cv[m]
                        act(lambda: nc.scalar.activation(out=dst, in_=acc[:], func=AF.Silu, bias=cbT[:, m:m + 1]),
                            r=[b_acc, b_cw], w=[bdst])
                        for half in range(2):
                            bk = 4 + half
                            tv = bfv(bk)
                            for s8 in range(8):
                                ti_ = half * 8 + s8
                                in_ap = (xs_fm[:, ti_ * 128:(ti_ + 1) * 128] if m < 8
                                         else BT[:, m - 8, ti_ * 128:(ti_ + 1) * 128])
                                pe(lambda: nc.tensor.transpose(out=tv[:, s8, :], in_=in_ap, identity=ident[:]),
                                   r=[bdst, b_const], w=[b_PB[bk]], inc=(s8 == 7))
                            if m < 8:
                                act(lambda: nc.scalar.copy(out=X_tm[:, half * 8:(half + 1) * 8, m * 128:(m + 1) * 128],
                                                           in_=tv[:, :, :]), r=[b_PB[bk]], w=[b_cv[m]])
                            else:
                                act(lambda: nc.scalar.copy(out=B_tm[:, half * 8:(half + 1) * 8,
                                                                    (m - 8) * 128:(m - 7) * 128],
                                                           in_=tv[:, :, :]), r=[b_PB[bk]], w=[b_cv[m]])
                    else:
                        act(lambda: nc.scalar.activation(out=CT[:, m - 10, :], in_=acc[:], func=AF.Silu,
                                                         bias=cbT[:, m:m + 1]),
                            r=[b_acc, b_cw], w=[b_cv[m]])

                def wreload(m_done):
                    if m_done == 3:
                        dpool(out=wBC[:, :, 0:512], in_=win_d[:, CONV0 + 1024:CONV0 + 1536]
                              .rearrange("(k p) n -> p k n", p=128), w=[b_wslot[0]])
                    elif m_done == 7:
                        dpool(out=wBC[:, :, 512:1024], in_=win_d[:, 1604:2116]
                              .rearrange("(k p) n -> p k n", p=128), w=[b_wslot[1]])
                    elif m_done == 11:
                        dpool(out=wBC[:, :, 0:512], in_=win_d[:, 2116:2628]
                              .rearrange("(k p) n -> p k n", p=128), w=[b_wslot[0]])

                cproj(0)
                wreload(0)
                for m in range(12):
                    if m + 1 < 12:
                        cproj(m + 1)
                        wreload(m + 1)
                    cpost(m)
                    ck(f"Bconv{m}")
                b_X.w = {}
                for bb in b_cv:
                    _merge(b_X.w, bb.w)
                if "conv" in dbg:
                    k.dump("d_Xtm", X_tm[:], b_X)
                    k.dump("d_Btm", B_tm[:], b_X)
                    k.dump("d_BT", BT[:], b_X)
                    k.dump("d_CT", CT[:], b_X)
                ck("Bconv")
                k.barrier()

            with ExitStack() as esS:
                ones_b = k.sb("ones_b", [128, 128], BF16, esS)
                dAhl = k.sb("dAhl", [128, NT, 2, 16], BF16, esS)
                dAres = spt[0]
                b_hl = Buf()
                pool(lambda: nc.gpsimd.memset(ones_b[:], 1.0), w=[b_hl])
                dve(lambda: nc.vector.tensor_copy(out=dAhl[:, :, 0, :], in_=dA_all[:]), r=[b_ptab], w=[b_hl])
                dve(lambda: nc.vector.tensor_tensor(out=dAres[:], in0=dA_all[:], in1=dAhl[:, :, 0, :], op=ALU.subtract),
                    r=[b_ptab], w=[b_hl])
                dve(lambda: nc.vector.tensor_copy(out=dAhl[:, :, 1, :], in_=dAres[:]), w=[b_hl])
                szb = [k.sb(f"szb{j}", [128, 1024], BF16, esS) for j in range(2)]
                b_szb = [Buf(), Buf()]
                smalls = [k.sb(f"small{j}", [128, 32], F32, esS) for j in range(2)]
                nacums = [k.sb(f"nacum{j}", [128, 16], F32, esS) for j in range(2)]
                eas = [k.sb(f"ea{j}", [128, 16], F32, esS) for j in range(2)]
                decs = [k.sb(f"dec{j}", [128, 16], F32, esS) for j in range(2)]
                dtds = [k.sb(f"dtd{j}", [128, 16], F32, esS) for j in range(2)]
                eASs = [k.sb(f"eAS{j}", [128, 2, 4], F32, esS) for j in range(2)]
                b_sms = [Buf(), Buf()]
                LTg = [k.sb(f"LTg{j}", [128, 4, 128], F32, esS) for j in range(2)]
                b_LT = [Buf(), Buf()]
                MTg = [k.sb(f"MTg{j}", [128, 4, 128], BF16, esS) for j in range(2)]
                b_MT = [Buf(), Buf()]
                CBs = [k.sb("CBs0", [128, 4, 128], F32, esS)] * 2
                b_CBs = [Buf()] * 2
                xds = [k.sb(f"xd{j}", [128, 16, 64], BF16, esS) for j in range(2)]
                xdds = [k.sb(f"xdd{j}", [128, 16, 64], BF16, esS) for j in range(2)]
                b_xds = [Buf(), Buf()]
                b_xdds = [Buf(), Buf()]
                ysb = k.sb("ysb", [128, 16, 64], F32, esS)
                b_y = Buf()
                ssq = k.sb("ssq", [128, 4], F32, esS)
                rs4 = k.sb("rs4", [128, 4], F32, esS)
                junkf = k.sb("junkf", [128, 256], F32, esS)
                ob_sb = k.sb("ob_sb", [128, 1024], BF16, esS)
                b_ob = Buf()
                S_sb = k.sb("S_sb", [128, 2, 256], F32, esS)
                S_bf = k.sb("S_bf", [128, 2, 256], BF16, esS)
                b_S = Buf()
                b_Sbf = Buf()
                gctr = {"g": 0, "d": 0}
                Yv = [PB[4][:].rearrange("p (h d) -> p h d", d=64), PB[5][:].rearrange("p (h d) -> p h d", d=64)]

                def head(c):
                    q = c % 2
                    csl = slice(c * 128, (c + 1) * 128)
                    small, nacum, ea, dec, dtd, eAS, b_sm = smalls[q], nacums[q], eas[q], decs[q], dtds[q], eASs[q], b_sms[q]
                    pe(lambda: nc.tensor.matmul(PB[2][:, 0:16], lhsT=Uf[:], rhs=dA_all[:, c, :], start=True, stop=False),
                       r=[b_const, b_ptab], w=[b_PB[2]], inc=False)
                    pe(lambda: nc.tensor.matmul(PB[2][:, 16:32], lhsT=ones_f[:], rhs=dA_all[:, c, :], start=False,
                                                stop=True), r=[b_ptab], w=[b_PB[2]])
                    CBv = PB[3][:].rearrange("p (g l) -> p g l", l=128)
                    tk = None
                    for gi, g in enumerate((0, 2, 1, 3)):
                        p0 = (g % 2) * 64
                        tk2 = pe(lambda: nc.tensor.matmul(CBv[:, g, :], lhsT=BT[p0:p0 + 64, g // 2, csl],
                                                          rhs=CT[p0:p0 + 64, g // 2, csl], start=(gi == 0),
                                                          stop=(gi == 3)),
                                 r=[b_X], w=[b_PB[3]], inc=(gi == 1 or gi == 3), selfwait=(tk if gi == 2 else None))
                        if gi == 1:
                            tk = tk2
                    for hb in range(2):
                        for kc in range(8):
                            pe(lambda: nc.tensor.matmul(PB[hb][:], lhsT=hT[:, kc, csl],
                                                        rhs=wBC[:, kc, (1 - hb) * 512:(2 - hb) * 512],
                                                        start=(kc == 0), stop=(kc == 7)),
                               r=[b_hT[c], b_wslot[1 - hb]], w=[b_PB[hb]], inc=(kc == 7))
                    yield
                    dve(lambda: nc.vector.tensor_copy(out=small[:], in_=PB[2][:, 0:32]), r=[b_PB[2]], w=[b_sm])
                    acum = small[:, 0:16]
                    atot = small[:, 16:32]
                    dve(lambda: nc.vector.tensor_tensor(out=dec[:], in0=atot, in1=acum, op=ALU.subtract), w=[b_sm])
                    act(lambda: nc.scalar.activation(out=ea[:], in_=acum, func=AF.Exp), w=[b_sm])
                    act(lambda: nc.scalar.activation(out=dec[:], in_=dec[:], func=AF.Exp), w=[b_sm])
                    atv = small[:, 16:32].rearrange("p (s f h) -> p s f h", s=2, f=2)
                    act(lambda: nc.scalar.activation(out=eAS[0:64], in_=atv[0:64, :, 0, :], func=AF.Exp), w=[b_sm])
                    act(lambda: nc.scalar.activation(out=eAS[64:128], in_=atv[64:128, :, 1, :], func=AF.Exp), w=[b_sm])
                    act(lambda: nc.scalar.copy(out=CBs[q][:], in_=CBv), r=[b_PB[3]], w=[b_CBs[q]])
                    for hb in range(2):
                        act(lambda: nc.scalar.activation(out=szb[q][:, hb * 512:(hb + 1) * 512], in_=PB[hb][:],
                                                         func=AF.Silu), r=[b_PB[hb]], w=[b_szb[q]])
                    dve(lambda: nc.vector.tensor_tensor(out=dtd[:], in0=dt_all[:, c, :], in1=dec[:], op=ALU.mult),
                        r=[b_ptab], w=[b_sm])
                    Xc = X_tm[:, c, :].rearrange("p (h d) -> p h d", d=64)
                    dve(lambda: nc.vector.tensor_tensor(out=xds[q][:], in0=Xc,
                                                        in1=dt_all[:, c, :].unsqueeze(2).to_broadcast([128, 16, 64]),
                                                        op=ALU.mult), r=[b_X, b_ptab], w=[b_xds[q]])
                    dve(lambda: nc.vector.tensor_tensor(out=xdds[q][:], in0=Xc,
                                                        in1=dtd[:].unsqueeze(2).to_broadcast([128, 16, 64]),
                                                        op=ALU.mult), r=[b_X, b_sm], w=[b_xdds[q]])

                    yield

                def groups(c):
                    q = c % 2
                    csl = slice(c * 128, (c + 1) * 128)
                    nacum, b_sm = nacums[q], b_sms[q]
                    xd = xds[q]
                    st = {}

                    def acumb(g):
                        gq = gctr["g"] % 2
                        gctr["g"] += 1
                        abk = 2 if gq == 0 else 6
                        first = True
                        for hh in range(4):
                            hd = 4 * g + hh
                            for part in range(2):
                                pe(lambda: nc.tensor.matmul(PB[abk][:, hh * 128:(hh + 1) * 128],
                                                            lhsT=dAhl[:, c, part, hd:hd + 1].to_broadcast([128, 128]),
                                                            rhs=Ub[:], start=first, stop=False),
                                   r=[b_hl, b_const], w=[b_PB[abk]], inc=False)
                                first = False
                        for part in range(2):
                            pe(lambda: nc.tensor.matmul(
                                PB[abk][:].rearrange("p (h l) -> p h l", l=128), lhsT=negUb[:],
                                rhs=dAhl[:, c, part, 4 * g:4 * g + 4].unsqueeze(2).to_broadcast([128, 4, 128]),
                                start=False, stop=False), r=[b_hl, b_const], w=[b_PB[abk]], inc=False)
                        pe(lambda: nc.tensor.matmul(PB[abk][:], lhsT=ident[:],
                                                    rhs=NEGU4[:].rearrange("p h l -> p (h l)"), start=False, stop=True),
                           r=[b_const, b_ptab], w=[b_PB[abk]])
                        st[g] = (gq, abk)

                    def ymm(g):
                        gq, abk = st[g]
                        act(lambda: nc.scalar.activation(out=LTg[gq][:].rearrange("p h l -> p (h l)"), in_=PB[abk][:],
                                                         func=AF.Exp), r=[b_PB[abk]], w=[b_LT[gq]])
                        dve(lambda: nc.vector.tensor_tensor(out=MTg[gq][:], in0=LTg[gq][:],
                                                            in1=CBs[q][:, g:g + 1, :].to_broadcast([128, 4, 128]),
                                                            op=ALU.mult),
                            r=[b_LT[gq], b_CBs[q]], w=[b_MT[gq]])
                        for hh in range(4):
                            hd = 4 * g + hh
                            yb = 4 + hd // 8
                            pe(lambda: nc.tensor.matmul(Yv[hd // 8][:, hd % 8, :], lhsT=MTg[gq][:, hh, :],
                                                        rhs=xd[:, hd, :], start=(hd % 8 == 0), stop=False),
                               r=[b_MT[gq], b_xds[q]], w=[b_PB[yb]], inc=False)
                            pe(lambda: nc.tensor.matmul(Yv[hd // 8][:, hd % 8, :], lhsT=Dg[:, hd, :],
                                                        rhs=X_tm[:, c, hd * 64:(hd + 1) * 64], start=False,
                                                        stop=(hd % 8 == 7)),
                               r=[b_ptab, b_X], w=[b_PB[yb]], inc=(hh == 3))

                    acumb(0)
                    acumb(1)
                    yield
                    ymm(0)
                    yield
                    acumb(2)
                    ymm(1)
                    yield
                    acumb(3)
                    ymm(2)
                    yield
                    ymm(3)
                    yield

                def tail(c):
                    q = c % 2
                    csl = slice(c * 128, (c + 1) * 128)
                    ea, eAS, b_sm = eas[q], eASs[q], b_sms[q]
                    xdd = xdds[q]
                    if c > 0:
                        tk = None
                        for gi, g in enumerate((0, 2, 1, 3)):
                            p0 = (g % 2) * 64
                            ob_ = 6 + g // 2
                            tk2 = pe(lambda: nc.tensor.matmul(PB[ob_][:, (g % 2) * 256:(g % 2 + 1) * 256],
                                                              lhsT=CT[p0:p0 + 64, g // 2, csl],
                                                              rhs=S_bf[p0:p0 + 64, g // 2, :], start=(g % 2 == 0),
                                                              stop=(g % 2 == 1)),
                                     r=[b_X, b_Sbf], w=[b_PB[ob_]], inc=(gi >= 1),
                                     selfwait=(tk if gi == 2 else None))
                            if gi == 1:
                                tk = tk2
                        for hb in range(2):
                            dve(lambda: nc.vector.tensor_tensor(
                                out=ysb[:, hb * 8:(hb + 1) * 8, :],
                                in0=PB[6 + hb][:].rearrange("p (h d) -> p h d", d=64),
                                in1=ea[:, hb * 8:(hb + 1) * 8].unsqueeze(2).to_broadcast([128, 8, 64]), op=ALU.mult),
                                r=[b_PB[6 + hb], b_sm], w=[b_y])
                            dve(lambda: nc.vector.tensor_tensor(out=ysb[:, hb * 8:(hb + 1) * 8, :], in0=Yv[hb],
                                                                in1=ysb[:, hb * 8:(hb + 1) * 8, :], op=ALU.add),
                                r=[b_PB[4 + hb]], w=[b_y])
                    else:
                        for hb in range(2):
                            dve(lambda: nc.vector.tensor_copy(out=ysb[:, hb * 8:(hb + 1) * 8, :], in_=Yv[hb]),
                                r=[b_PB[4 + hb]], w=[b_y])
                    yield
                    dve(lambda: nc.vector.tensor_tensor(out=ysb[:].rearrange("p h d -> p (h d)"),
                                                        in0=ysb[:].rearrange("p h d -> p (h d)"), in1=szb[q][:],
                                                        op=ALU.mult), r=[b_szb[q]], w=[b_y])
                    yf = ysb[:].rearrange("p h d -> p (h d)")
                    for g in range(4):
                        act(lambda: nc.scalar.activation(out=junkf[:], in_=yf[:, g * 256:(g + 1) * 256], func=AF.Square,
                                                         accum_out=ssq[:, g:g + 1]), r=[b_y], w=[b_ob])
                    pool(lambda: nc.gpsimd.tensor_scalar(out=rs4[:], in0=ssq[:], scalar1=1.0 / 256, scalar2=EPS,
                                                         op0=ALU.mult, op1=ALU.add), w=[b_ob])
                    pool(lambda: nc.gpsimd.tensor_tensor(out=rs4[:], in0=rs4[:], in1=mhalf[:, 0:4], op=ALU.pow),
                         r=[b_const], w=[b_ob])
                    yield
                    if c < NT - 1:
                        for g in range(4):
                            p0 = (g % 2) * 64
                            pe(lambda: nc.tensor.matmul(PB[7][p0:p0 + 64, (g // 2) * 256:(g // 2 + 1) * 256],
                                                        lhsT=B_tm[:, c, g * 64:(g + 1) * 64],
                                                        rhs=xdd[:, 4 * g:4 * g + 4, :].rearrange("p h d -> p (h d)"),
                                                        start=(g < 2), stop=(g >= 2)),
                               r=[b_X, b_xdds[q]], w=[b_PB[7]], inc=(g == 3))
                        Sv = S_sb[:].rearrange("p s (h d) -> p (s h) d", d=64)
                        if c == 0:
                            dve(lambda: nc.vector.tensor_copy(out=S_sb[:].rearrange("p s f -> p (s f)"), in_=PB[7][:]),
                                r=[b_PB[7]], w=[b_S])
                        else:
                            dve(lambda: nc.vector.tensor_tensor(
                                out=Sv, in0=Sv,
                                in1=eAS[:].rearrange("p s h -> p (s h)").unsqueeze(2).to_broadcast([128, 8, 64]),
                                op=ALU.mult), r=[b_sm, b_Sbf], w=[b_S])
                            dve(lambda: nc.vector.tensor_tensor(out=S_sb[:].rearrange("p s f -> p (s f)"),
                                                                in0=S_sb[:].rearrange("p s f -> p (s f)"),
                                                                in1=PB[7][:], op=ALU.add),
                                r=[b_PB[7]], w=[b_S])
                        act(lambda: nc.scalar.copy(out=S_bf[:], in_=S_sb[:]), r=[b_S], w=[b_Sbf])
                    yield
                    for g in range(4):
                        dve(lambda: nc.vector.scalar_tensor_tensor(out=ob_sb[:, g * 256:(g + 1) * 256],
                                                                   in0=yf[:, g * 256:(g + 1) * 256],
                                                                   scalar=rs4[:, g:g + 1],
                                                                   in1=snw_b[:, g * 256:(g + 1) * 256],
                                                                   op0=ALU.mult, op1=ALU.mult),
                            r=[b_y, b_ptab], w=[b_ob])
                    yield
                    tv = bfv(7)
                    for cc in range(8):
                        pe(lambda: nc.tensor.transpose(out=tv[:, cc, :], in_=ob_sb[:, cc * 128:(cc + 1) * 128],
                                                       identity=ident[:]),
                           r=[b_ob, b_const], w=[b_PB[7]], inc=(cc == 7))
                    act(lambda: nc.scalar.copy(out=obT[:, :, csl], in_=tv[:, :, :]), r=[b_PB[7]], w=[b_obT[c]])

                def front(c):
                    yield from head(c)
                    yield from groups(c)

                def interleave(gens):
                    gens = list(gens)
                    while gens:
                        for g_ in list(gens):
                            try:
                                next(g_)
                            except StopIteration:
                                gens.remove(g_)

                interleave([front(0)])
                for c in range(NT):
                    gl = [tail(c)]
                    if c + 1 < NT:
                        gl.append(front(c + 1))
                    interleave(gl)
                    if c == NT - 2:
                        dpool(out=wG0[:, :, 0, :], in_=win_d[:, 4180:4180 + 128].rearrange("(k p) n -> p k n", p=128),
                              w=[b_wslot[0]])
                        dpool(out=wG0[:, :, 1, :], in_=win_d[:, 4180 + 1024:4180 + 1152]
                              .rearrange("(k p) n -> p k n", p=128), w=[b_wslot[0]])
                        dpool(out=Wpa0, in_=wpa_d[:, 0:128].rearrange("(k p) n -> p k n", p=128), w=[b_wslot[0]])
                        dpool(out=Wpb0, in_=wpb_d[:, 0:128].rearrange("(k p) n -> p k n", p=128), w=[b_wslot[0]])
                if "obT" in dbg:
                    k.dump("d_obT", obT[:], b_obT)
                k.barrier()
        if stop_after.startswith("B"):
            return nc, k


        with ExitStack() as esM:
            Wout = k.sb("Wout", [128, 8, D], BF16, esM)
            b_W = Buf()
            gbT = k.sb("gbT_sb", [128, 16], F32, esM)
            fnw_b = k.sb("fnw_b", [128, D], F32, esM)
            b_ct = Buf()
            dsync(out=gbT[:], in_=gbT_d[:, :], w=[b_ct])
            dsync(out=fnw_b[:], in_=fnw_d[0, :].partition_broadcast(128), w=[b_ct])
            mT = k.sb("mT", [128, 8, S], BF16, esM)
            b_mT = [Buf() for _ in range(4)]
            with ExitStack() as esM1:
                wG = [k.sb(f"wG{j}", [128, 8, 2, 128], BF16, esM1) for j in range(2)]
                Wpa = [k.sb(f"Wpa{j}", [128, 4, 128], BF16, esM1) for j in range(2)]
                Wpb = [k.sb(f"Wpb{j}", [128, 8, 128], BF16, esM1) for j in range(2)]
                b_wG = [Buf(), Buf()]
                gA = [k.sb(f"gA{j}", [128, 512], F32, esM1) for j in range(2)]
                gB = [k.sb(f"gB{j}", [128, 512], F32, esM1) for j in range(2)]
                b_g = [Buf(), Buf()]
                t1 = [k.sb(f"t1{j}", [128, 512], F32, esM1) for j in range(2)]
                t2 = [k.sb(f"t2{j}", [128, 512], F32, esM1) for j in range(2)]
                b_t = [Buf(), Buf()]
                it = 0
                for m in range(8):
                    wq = m % 2
                    if m == 0:
                        gw_, pa_, pb_, bw_ = wG0, Wpa0, Wpb0, b_wslot[0]
                    else:
                        gw_, pa_, pb_, bw_ = wG[wq][:], Wpa[wq][:], Wpb[wq][:], b_wG[wq]
                        dpool(out=gw_[:, :, 0, :], in_=win_d[:, 4180 + m * 128:4180 + (m + 1) * 128]
                              .rearrange("(k p) n -> p k n", p=128), w=[bw_])
                        dpool(out=gw_[:, :, 1, :], in_=win_d[:, 4180 + (8 + m) * 128:4180 + (9 + m) * 128]
                              .rearrange("(k p) n -> p k n", p=128), w=[bw_])
                        dpool(out=pa_, in_=wpa_d[:, m * 128:(m + 1) * 128].rearrange("(k p) n -> p k n", p=128),
                              w=[bw_])
                        dpool(out=pb_, in_=wpb_d[:, m * 128:(m + 1) * 128].rearrange("(k p) n -> p k n", p=128),
                              w=[bw_])
                    if m == 0:
                        for hf in range(2):
                            cs_ = slice(hf * 512, (hf + 1) * 512)
                            dpool(out=Wout[:, :, cs_], in_=wout_d[:, cs_].rearrange("(k p) n -> p k n", p=128),
                                  w=[b_W])
                    for tc in range(4):
                        q = it % 2
                        it += 1
                        b0 = 4 * q
                        ts_ = slice(tc * 512, (tc + 1) * 512)
                        for kc in range(8):
                            pe(lambda: nc.tensor.matmul(PB[b0][:], lhsT=gw_[:, kc, 0, :], rhs=hT[:, kc, ts_],
                                                        start=(kc == 0), stop=(kc == 7)),
                               r=b_hT[tc * 4:(tc + 1) * 4] + [bw_], w=[b_PB[b0]], inc=(kc == 7))
                        for kc in range(8):
                            pe(lambda: nc.tensor.matmul(PB[b0 + 1][:], lhsT=gw_[:, kc, 1, :], rhs=hT[:, kc, ts_],
                                                        start=(kc == 0), stop=(kc == 7)),
                               r=b_hT[tc * 4:(tc + 1) * 4] + [bw_], w=[b_PB[b0 + 1]], inc=(kc == 7))
                        for kc in range(4):
                            pe(lambda: nc.tensor.matmul(PB[b0 + 2][:], lhsT=pa_[:, kc, :],
                                                        rhs=oaT[:, kc, ts_], start=(kc == 0), stop=(kc == 3)),
                               r=b_oaT[tc * 4:(tc + 1) * 4] + [bw_], w=[b_PB[b0 + 2]], inc=(kc == 3))
                        for kc in range(8):
                            pe(lambda: nc.tensor.matmul(PB[b0 + 3][:], lhsT=pb_[:, kc, :],
                                                        rhs=obT[:, kc, ts_], start=(kc == 0), stop=(kc == 7)),
                               r=b_obT[tc * 4:(tc + 1) * 4] + [bw_], w=[b_PB[b0 + 3]], inc=(kc == 7))
                        act(lambda: nc.scalar.activation(out=gA[q][:], in_=PB[b0][:], func=AF.Sigmoid,
                                                         bias=gbT[:, m:m + 1]), r=[b_PB[b0], b_ct], w=[b_g[q]])
                        act(lambda: nc.scalar.activation(out=gB[q][:], in_=PB[b0 + 1][:], func=AF.Sigmoid,
                                                         bias=gbT[:, 8 + m:9 + m]), r=[b_PB[b0 + 1], b_ct], w=[b_g[q]])
                        dve(lambda: nc.vector.tensor_tensor(out=t1[q][:], in0=PB[b0 + 2][:], in1=gA[q][:], op=ALU.mult),
                            r=[b_PB[b0 + 2], b_g[q]], w=[b_t[q]])
                        dve(lambda: nc.vector.tensor_tensor(out=t2[q][:], in0=PB[b0 + 3][:], in1=gB[q][:], op=ALU.mult),
                            r=[b_PB[b0 + 3], b_g[q]], w=[b_t[q]])
                        dve(lambda: nc.vector.tensor_tensor(out=mT[:, m, ts_], in0=t1[q][:], in1=t2[q][:], op=ALU.add),
                            r=[b_t[q]], w=[b_mT[tc]])
                k.barrier()
            if "mT" in dbg:
                k.dump("d_mT", mT[:], b_mT)
            ck("Cm")
            xr = [k.sb(f"xr{j}", [128, D], F32, esM) for j in range(2)]
            b_xr = [Buf(), Buf()]
            xo = [k.sb(f"xo{j}", [128, D], F32, esM) for j in range(2)]
            b_xo = [Buf(), Buf()]
            fo = [k.sb(f"fo{j}", [128, D], F32, esM) for j in range(2)]
            b_fo = [Buf(), Buf()]
            junkc = k.sb("junkc", [128, D], BF16, esM)
            fss = k.sb("fss", [128, NT], F32, esM)
            b_fs = Buf()
            dpool(out=xr[0][:], in_=x_d[0:128, :], w=[b_xr[0]])
            b_fsi = [Buf() for _ in range(NT)]

            def out1(i):
                q = i % 2
                tsl = slice(i * 128, (i + 1) * 128)
                if i + 1 < NT:
                    dpool(out=xr[1 - q][:], in_=x_d[(i + 1) * 128:(i + 2) * 128, :], w=[b_xr[1 - q]])
                for hf in range(2):
                    bk = 2 * q + hf
                    for kc in range(8):
                        pe(lambda: nc.tensor.matmul(PB[bk][:], lhsT=mT[:, kc, tsl], rhs=Wout[:, kc, hf * 512:(hf + 1) * 512],
                                                    start=(kc == 0), stop=(kc == 7)),
                           r=[b_mT[i // 4], b_W], w=[b_PB[bk]], inc=(kc == 7))
                    dve(lambda: nc.vector.tensor_tensor(out=xo[q][:, hf * 512:(hf + 1) * 512], in0=PB[bk][:],
                                                        in1=xr[q][:, hf * 512:(hf + 1) * 512], op=ALU.add),
                        r=[b_PB[bk], b_xr[q]], w=[b_xo[q]])
                act(lambda: nc.scalar.activation(out=junkc[:], in_=xo[q][:], func=AF.Square, accum_out=fss[:, i:i + 1]),
                    r=[b_xo[q]], w=[b_fs, b_fsi[i]])
                act(lambda: nc.scalar.activation(out=fss[:, i:i + 1], in_=fss[:, i:i + 1], func=AF.Sqrt, scale=1.0 / D,
                                                 bias=EPS), w=[b_fsi[i]])

            def out2(i):
                q = i % 2
                tsl = slice(i * 128, (i + 1) * 128)
                dve(lambda: nc.vector.reciprocal(out=fss[:, i:i + 1], in_=fss[:, i:i + 1]), w=[b_fsi[i]])
                dve(lambda: nc.vector.scalar_tensor_tensor(out=fo[q][:], in0=xo[q][:], scalar=fss[:, i:i + 1],
                                                           in1=fnw_b[:], op0=ALU.mult, op1=ALU.mult),
                    r=[b_xo[q], b_ct, b_fsi[i]], w=[b_fo[q]])
                dsync(out=out_d[tsl, :], in_=fo[q][:], r=[b_fo[q]])

            out1(0)
            for i in range(NT):
                if i + 1 < NT:
                    out1(i + 1)
                out2(i)
            k.barrier()

        k.barrier()
    return nc, k


_NC_CACHE = {}


def kernel(x, positions, norm_w, w_in, gate_bias, conv_w, conv_b, dt_bias, a_log, d_skip,
           ssm_norm_w, w_branch_a, w_branch_b, w_out, final_norm_w):
    f32 = np.float32
    x = np.asarray(x, dtype=f32)
    positions = np.asarray(positions).astype(np.int32)
    nb = x.shape[0]
    assert nb == 8 and x.shape[1] == S and x.shape[2] == D
    if "nc" not in _NC_CACHE:
        _NC_CACHE["nc"] = build()[0]
    nc = _NC_CACHE["nc"]
    invf = (500000.0 ** (-np.arange(0, 16, 2, dtype=f32) / 16)).astype(f32)
    shared = {
        "invf": np.ascontiguousarray(np.broadcast_to(invf, (128, 8))),
        "norm_w": np.ascontiguousarray(np.asarray(norm_w, f32).reshape(1, D)),
        "w_in": np.ascontiguousarray(np.asarray(w_in, f32)[0]),
        "cwT": np.ascontiguousarray(np.asarray(conv_w, f32)[0].reshape(4, 12, 128).transpose(2, 1, 0)),
        "cbT": np.ascontiguousarray(np.asarray(conv_b, f32)[0].reshape(12, 128).T),
        "dt_bias": np.ascontiguousarray(np.asarray(dt_bias, f32).reshape(1, 16)),
        "a_log": np.ascontiguousarray(np.asarray(a_log, f32).reshape(1, 16)),
        "d_skip": np.ascontiguousarray(np.asarray(d_skip, f32).reshape(1, 16)),
        "ssm_norm_w": np.ascontiguousarray(np.asarray(ssm_norm_w, f32).reshape(1, D)),
        "gbT": np.ascontiguousarray(np.asarray(gate_bias, f32)[0].reshape(16, 128).T),
        "w_branch_a": np.ascontiguousarray(np.asarray(w_branch_a, f32)[0]),
        "w_branch_b": np.ascontiguousarray(np.asarray(w_branch_b, f32)[0]),
        "w_out": np.ascontiguousarray(np.asarray(w_out, f32)[0]),
        "final_norm_w": np.ascontiguousarray(np.asarray(final_norm_w, f32).reshape(1, D)),
    }
    in_maps = []
    for b in range(nb):
        m = dict(shared)
        m["x"] = np.ascontiguousarray(x[b])
        m["posT"] = np.ascontiguousarray(positions[b].reshape(NT, 128).T)
        in_maps.append(m)
    res = run_bass_kernel_spmd(nc, in_maps, core_ids=list(range(nb)))
    out = np.stack([np.asarray(res.results[b]["out"], dtype=f32) for b in range(nb)], axis=0)
    return out
```

```python
import numpy as np
import math
import concourse.bass as bass
import concourse.mybir as mybir
from concourse.bass_utils import run_bass_kernel_spmd
from contextlib import ExitStack

F32 = mybir.dt.float32
BF16 = mybir.dt.bfloat16
I32 = mybir.dt.int32
ALU = mybir.AluOpType
AF = mybir.ActivationFunctionType
AX = mybir.AxisListType

S = 2048
D = 1024
NT = 16
INW = 6228
EPS = 1e-6
NITER = 10
A_SRC = [(0, 512, 0), (512, 640, 512), (1280, 1536, 640), (1536, 1600, 896), (1600, 1604, 960),
         (640, 768, 964), (768, 1280, 1092)]
NA = 1604


class Buf:
    __slots__ = ("w", "r", "name", "excl")

    def __init__(self, name="", excl=False):
        self.w = {}
        self.r = {}
        self.name = name
        self.excl = excl


def _merge(deps, d):
    for k, (s, v) in d.items():
        if k not in deps or deps[k][1] < v:
            deps[k] = (s, v)


class Eng:
    def __init__(self, K, name, eng, selfdep=True):
        self.K = K
        self.name = name
        self.eng = eng
        self.selfdep = selfdep
        self.sem = K.es.enter_context(K.nc.semaphore("s_" + name))
        self.cnt = 0
        self.waited = {}
        self.pending = False

    def wait_deps(self, deps):
        for k, (s, v) in deps.items():
            if k == self.name and not self.selfdep:
                continue
            if self.waited.get(k, 0) < v:
                self.eng.wait_ge(s, v)
                self.waited[k] = v

    def __call__(self, fn, r=(), w=(), inc=True, extra=(), selfwait=None):
        if selfwait is not None:
            assert selfwait[0] == self.name
            if self.waited.get("self", 0) < selfwait[2]:
                self.eng.wait_ge(selfwait[1], selfwait[2])
                self.waited["self"] = selfwait[2]
        w = list(w) + [b for b in r if b.excl]
        r = [b for b in r if not b.excl]
        deps = {}
        for b in r:
            _merge(deps, b.w)
        for b in w:
            _merge(deps, b.w)
            _merge(deps, b.r)
        for t in extra:
            _merge(deps, {t[0]: (t[1], t[2])})
        self.wait_deps(deps)
        ins = fn()
        if inc:
            self.cnt += 1
            ins.then_inc(self.sem, 1)
            tok = (self.sem, self.cnt)
            self.pending = False
        else:
            tok = (self.sem, self.cnt + 1)
            self.pending = True
        for b in r:
            _merge(b.r, {self.name: tok})
        for b in w:
            b.w = {self.name: tok}
            b.r = {}
        return (self.name,) + tok


class DmaQ:
    def __init__(self, K, name, waiter, nsem=8):
        self.K = K
        self.name = name
        self.waiter = waiter
        self.sems = [K.es.enter_context(K.nc.semaphore(f"d_{name}{j}")) for j in range(nsem)]
        self.vals = [0] * nsem
        self.idx = 0

    def __call__(self, out, in_, r=(), w=(), extra=(), **kw):
        deps = {}
        for b in r:
            _merge(deps, b.w)
        for b in w:
            _merge(deps, b.w)
            _merge(deps, b.r)
        for t in extra:
            _merge(deps, {t[0]: (t[1], t[2])})
        k = self.idx
        self.idx = (k + 1) % len(self.sems)
        key = f"{self.name}{k}"
        if self.vals[k] > 0:
            _merge(deps, {key: (self.sems[k], self.vals[k])})
        self.waiter.wait_deps(deps)
        ins = self.waiter.eng.dma_start(out=out, in_=in_, **kw)
        self.vals[k] += 16
        ins.then_inc(self.sems[k], 16)
        tok = (self.sems[k], self.vals[k])
        for b in r:
            _merge(b.r, {key: tok})
        for b in w:
            b.w = {key: tok}
            b.r = {}
        return (key,) + tok


class K:
    def __init__(self, nc, es):
        self.nc = nc
        self.es = es
        self.pe = Eng(self, "pe", nc.tensor, selfdep=False)
        self.act = Eng(self, "act", nc.scalar)
        self.dve = Eng(self, "dve", nc.vector)
        self.pool = Eng(self, "pool", nc.gpsimd)
        self.sp = Eng(self, "sp", nc.sync)
        self.engs = [self.pe, self.act, self.dve, self.pool, self.sp]
        self.dsync = DmaQ(self, "qs", self.sp, nsem=8)
        self.dpool = DmaQ(self, "qp", self.pool, nsem=8)
        self.dqs = [self.dsync, self.dpool]
        self.dumps = []

    def sb(self, name, shape, dt, es=None):
        t = (es or self.es).enter_context(self.nc.sbuf_tensor(name, list(shape), dt))
        return t

    def ps(self, name, shape, dt, es=None):
        return (es or self.es).enter_context(self.nc.psum_tensor(name, list(shape), dt))

    def barrier(self):
        deps = {}
        for e in self.engs:
            assert not e.pending, e.name
            if e.cnt > 0:
                deps[e.name] = (e.sem, e.cnt)
        for q in self.dqs:
            for j, s in enumerate(q.sems):
                if q.vals[j] > 0:
                    deps[f"{q.name}{j}"] = (s, q.vals[j])
        for e in self.engs:
            e.wait_deps(deps)

    def dump(self, name, ap, buf):
        d = self.nc.dram_tensor(name, list(ap.shape), ap.dtype, kind="ExternalOutput").ap()
        self.dsync(out=d, in_=ap, r=(buf if isinstance(buf, (list, tuple)) else [buf]))
        self.dumps.append(name)


class Stop(Exception):
    pass


def build(stop_after="all", dbg=()):
    try:
        return _build(stop_after, dbg)
    except Stop as s:
        return s.args


def _build(stop_after="all", dbg=()):
    nc = bass.Bass("TRN2", target_bir_lowering=False)
    dbg = set(dbg)
    x_d = nc.dram_tensor("x", [S, D], F32, kind="ExternalInput").ap()
    posT_d = nc.dram_tensor("posT", [128, NT], I32, kind="ExternalInput").ap()
    invf_d = nc.dram_tensor("invf", [128, 8], F32, kind="ExternalInput").ap()
    normw_d = nc.dram_tensor("norm_w", [1, D], F32, kind="ExternalInput").ap()
    win_d = nc.dram_tensor("w_in", [D, INW], F32, kind="ExternalInput").ap()
    out_d = nc.dram_tensor("out", [S, D], F32, kind="ExternalOutput").ap()
    cwT_d = nc.dram_tensor("cwT", [128, 12, 4], F32, kind="ExternalInput").ap()
    cbT_d = nc.dram_tensor("cbT", [128, 12], F32, kind="ExternalInput").ap()
    dtb_d = nc.dram_tensor("dt_bias", [1, 16], F32, kind="ExternalInput").ap()
    alog_d = nc.dram_tensor("a_log", [1, 16], F32, kind="ExternalInput").ap()
    dsk_d = nc.dram_tensor("d_skip", [1, 16], F32, kind="ExternalInput").ap()
    snw_d = nc.dram_tensor("ssm_norm_w", [1, D], F32, kind="ExternalInput").ap()
    gbT_d = nc.dram_tensor("gbT", [128, 16], F32, kind="ExternalInput").ap()
    wpa_d = nc.dram_tensor("w_branch_a", [512, D], F32, kind="ExternalInput").ap()
    wpb_d = nc.dram_tensor("w_branch_b", [D, D], F32, kind="ExternalInput").ap()
    wout_d = nc.dram_tensor("w_out", [D, D], F32, kind="ExternalInput").ap()
    fnw_d = nc.dram_tensor("final_norm_w", [1, D], F32, kind="ExternalInput").ap()

    with ExitStack() as es:
        k = K(nc, es)
        pe, act, dve, pool, sp = k.pe, k.act, k.dve, k.pool, k.sp
        dsync, dpool = k.dsync, k.dpool

        def ck(name):
            if stop_after == name:
                k.barrier()
                raise Stop(nc, k)

        ident = k.sb("ident", [128, 128], BF16)
        Uf = k.sb("Uf", [128, 128], F32)
        Ub = k.sb("Ub", [128, 128], BF16)
        cbias = k.sb("cbias", [128, 128], F32)
        pow2 = k.sb("pow2", [128, NITER + 2], F32)
        negU = k.sb("negU", [128, 128], BF16)
        negUb = k.sb("negUb", [128, 128], BF16)
        mhalf = k.sb("mhalf", [128, 16], F32)
        b_const = Buf("const")
        pool(lambda: nc.gpsimd.memset(ident[:], 1.0), w=[b_const])
        pool(lambda: nc.gpsimd.affine_select(out=ident[:], in_=ident[:], pattern=[[-1, 128]],
                                             compare_op=ALU.is_equal, fill=0.0, base=0, channel_multiplier=1),
             w=[b_const])
        pool(lambda: nc.gpsimd.memset(Uf[:], 1.0), w=[b_const])
        pool(lambda: nc.gpsimd.affine_select(out=Uf[:], in_=Uf[:], pattern=[[1, 128]], compare_op=ALU.is_ge,
                                             fill=0.0, base=0, channel_multiplier=-1), w=[b_const])
        pool(lambda: nc.gpsimd.tensor_copy(out=Ub[:], in_=Uf[:]), w=[b_const])
        pool(lambda: nc.gpsimd.tensor_scalar(out=negUb[:], in0=Ub[:], scalar1=-1.0, scalar2=None, op0=ALU.mult),
             w=[b_const])
        pool(lambda: nc.gpsimd.memset(mhalf[:], -0.5), w=[b_const])
        pool(lambda: nc.gpsimd.memset(negU[:], 0.0), w=[b_const])
        pool(lambda: nc.gpsimd.affine_select(out=negU[:], in_=negU[:], pattern=[[1, 128]], compare_op=ALU.is_ge,
                                             fill=-30000.0, base=0, channel_multiplier=-1), w=[b_const])
        pool(lambda: nc.gpsimd.memset(cbias[:], 0.0), w=[b_const])
        pool(lambda: nc.gpsimd.affine_select(out=cbias[:], in_=cbias[:], pattern=[[-1, 128]], compare_op=ALU.is_ge,
                                             fill=-1e30, base=0, channel_multiplier=1), w=[b_const])
        for j in range(NITER + 2):
            pool(lambda: nc.gpsimd.memset(pow2[:, j:j + 1], 2.0 ** (-j)), w=[b_const])
        pool(lambda: nc.gpsimd.memset(pow2[:, 0:1], 1.0), w=[b_const])

        hT = k.sb("hT", [128, 8, S], BF16)
        b_hT = [Buf(f"hT{i}") for i in range(NT)]
        wBC = k.sb("wBC", [128, 8, 1024], BF16)
        b_wslot = [Buf(), Buf()]
        wG0 = wBC[:, :, 0:256].rearrange("p k (h n) -> p k h n", h=2)
        Wpb0 = wBC[:, :, 256:384]
        Wpa0 = wBC[:, 0:4, 384:512]
        CONV0 = 2628
        oaT = k.sb("oaT", [128, 4, S], BF16)
        b_oaT = [Buf() for _ in range(NT)]
        esWA = ExitStack()
        wA = k.sb("wA", [128, 8, NA], BF16, esWA)
        b_wA = Buf()
        for (c0, c1, dst) in A_SRC:
            dpool(out=wA[:, :, dst:dst + (c1 - c0)],
                  in_=win_d[:, c0:c1].rearrange("(k p) n -> p k n", p=128), w=[b_wA])

        with ExitStack() as es0:
            normw_b = k.sb("normw_b", [128, D], F32, es0)
            b_normw = Buf()
            dsync(out=normw_b[:], in_=normw_d[0, :].partition_broadcast(128), w=[b_normw])
            xt = k.sb("xt_all", [128, NT, D], F32, es0)
            b_xt = [Buf() for _ in range(NT)]
            xn = [k.sb(f"xn{j}", [128, D], BF16, es0) for j in range(3)]
            b_xn = [Buf(), Buf(), Buf()]
            junk = k.sb("junk0", [128, D], BF16, es0)
            b_junk = Buf()
            ss = k.sb("ss", [128, NT], F32, es0)
            sd = k.sb("sd", [128, NT], F32, es0)
            rstd = k.sb("rstd", [128, NT], F32, es0)
            b_ssg = [Buf() for _ in range(4)]
            pt = [k.ps(f"pt{j}", [128, 8, 128], BF16, es0) for j in range(2)]
            b_pt = [Buf(excl=True), Buf(excl=True)]
            for i in range(NT):
                (dsync if i % 2 == 0 else dsync)(out=xt[:, i, :], in_=x_d[i * 128:(i + 1) * 128, :], w=[b_xt[i]])

            def p0_stats(gq):
                for i in range(4 * gq, 4 * gq + 4):
                    act(lambda: nc.scalar.activation(out=junk[:], in_=xt[:, i, :], func=AF.Square,
                                                     accum_out=ss[:, i:i + 1]),
                        r=[b_xt[i]], w=[b_junk, b_ssg[gq]])
                act(lambda: nc.scalar.activation(out=sd[:, 4 * gq:4 * gq + 4], in_=ss[:, 4 * gq:4 * gq + 4],
                                                 func=AF.Sqrt, scale=1.0 / D, bias=EPS), w=[b_ssg[gq]])
                dve(lambda: nc.vector.reciprocal(out=rstd[:, 4 * gq:4 * gq + 4], in_=sd[:, 4 * gq:4 * gq + 4]),
                    w=[b_ssg[gq]])

            def p0_apply(gq):
                for i in range(4 * gq, 4 * gq + 4):
                    j = i % 3
                    jp = i % 2
                    dve(lambda: nc.vector.scalar_tensor_tensor(out=xn[j][:], in0=xt[:, i, :], scalar=rstd[:, i:i + 1],
                                                               in1=normw_b[:], op0=ALU.mult, op1=ALU.mult),
                        r=[b_xt[i], b_ssg[gq], b_normw], w=[b_xn[j]])
                    for c in range(8):
                        pe(lambda: nc.tensor.transpose(out=pt[jp][:, c, :], in_=xn[j][:, c * 128:(c + 1) * 128],
                                                       identity=ident[:]),
                           r=[b_xn[j], b_const], w=[b_pt[jp]], inc=(c == 7))
                    act(lambda: nc.scalar.copy(out=hT[:, :, i * 128:(i + 1) * 128], in_=pt[jp][:]),
                        r=[b_pt[jp]], w=[b_hT[i]])

            p0_stats(0)
            for gq in range(4):
                if gq + 1 < 4:
                    p0_stats(gq + 1)
                p0_apply(gq)
            k.barrier()
        if "hT" in dbg:
            k.dump("d_hT", hT[:], b_hT[NT - 1])
        if stop_after == "p0":
            k.barrier()
            return nc, k


        PB = [k.ps(f"pb{j}", [128, 512], F32) for j in range(8)]
        b_PB = [Buf(f"pb{j}", excl=True) for j in range(8)]

        def bfv(j):
            return PB[j][:].bitcast(BF16).rearrange("p (s t) -> p s t", t=128)

        with ExitStack() as esA:
            dpool(out=wBC[:, :, 0:512], in_=win_d[:, CONV0:CONV0 + 512].rearrange("(k p) n -> p k n", p=128),
                  w=[b_wslot[0]])
            dpool(out=wBC[:, :, 512:1024], in_=win_d[:, CONV0 + 512:CONV0 + 1024].rearrange("(k p) n -> p k n", p=128),
                  w=[b_wslot[1]])
            posi = k.sb("posi", [128, NT], I32, esA)
            posf = k.sb("posf", [128, NT], F32, esA)
            invf = k.sb("invf_sb", [128, 8], F32, esA)
            ang = k.sb("ang", [128, NT, 8], F32, esA)
            cos_t = k.sb("cos_t", [128, NT, 8], F32, esA)
            sin_t = k.sb("sin_t", [128, NT, 8], F32, esA)
            ry = k.sb("ry", [128, NT, 8], F32, esA)
            rki = k.sb("rki", [128, NT, 8], I32, esA)
            rkf = k.sb("rkf", [128, NT, 8], F32, esA)
            rg = k.sb("rg", [128, NT, 8], F32, esA)
            b_tab = Buf()
            dsync(out=posi[:], in_=posT_d[:, :], w=[b_tab])
            dsync(out=invf[:], in_=invf_d[:, :], w=[b_tab])
            dve(lambda: nc.vector.tensor_copy(out=posf[:], in_=posi[:]), r=[b_tab], w=[b_tab])
            dve(lambda: nc.vector.tensor_tensor(out=ang[:], in0=posf[:].unsqueeze(2).to_broadcast([128, NT, 8]),
                                                in1=invf[:].unsqueeze(1).to_broadcast([128, NT, 8]), op=ALU.mult),
                r=[b_tab], w=[b_tab])
            TWO_PI = 2.0 * math.pi
            for (dst_t, off) in ((sin_t, 0.0), (cos_t, 0.25)):
                dve(lambda: nc.vector.tensor_scalar(out=ry[:], in0=ang[:], scalar1=1.0 / TWO_PI, scalar2=off,
                                                    op0=ALU.mult, op1=ALU.add), r=[b_tab], w=[b_tab])
                dve(lambda: nc.vector.tensor_copy(out=rki[:], in_=ry[:]), r=[b_tab], w=[b_tab])
                dve(lambda: nc.vector.tensor_copy(out=rkf[:], in_=rki[:]), r=[b_tab], w=[b_tab])
                dve(lambda: nc.vector.tensor_tensor(out=ry[:], in0=ry[:], in1=rkf[:], op=ALU.subtract),
                    r=[b_tab], w=[b_tab])
                dve(lambda: nc.vector.tensor_scalar(out=rg[:], in0=ry[:], scalar1=0.5, scalar2=None, op0=ALU.is_ge),
                    r=[b_tab], w=[b_tab])
                dve(lambda: nc.vector.tensor_tensor(out=ry[:], in0=ry[:], in1=rg[:], op=ALU.subtract),
                    r=[b_tab], w=[b_tab])
                dve(lambda: nc.vector.tensor_scalar(out=rg[:], in0=ry[:], scalar1=-0.5, scalar2=None, op0=ALU.is_lt),
                    r=[b_tab], w=[b_tab])
                dve(lambda: nc.vector.tensor_tensor(out=ry[:], in0=ry[:], in1=rg[:], op=ALU.add),
                    r=[b_tab], w=[b_tab])
                act(lambda: nc.scalar.activation(out=dst_t[:], in_=ry[:], func=AF.Sin, scale=TWO_PI * (1.0 - 1e-6)),
                    r=[b_tab], w=[b_tab])
            ck("Atab")
            if "rope" in dbg:
                k.dump("d_cos", cos_t[:], b_tab)
                k.dump("d_sin", sin_t[:], b_tab)

            qk_sb = [k.sb(f"qk_sb{j}", [128, 15, 64], BF16, esA) for j in range(2)]
            b_qk = [Buf(), Buf()]
            rt = [k.sb(f"rt{j}", [128, 15, 8], F32, esA) for j in range(4)]
            b_rt = Buf()
            rsrc = k.sb("rsrc", [128, 15, 16], F32, esA)
            b_rsrc = Buf()
            QT = [k.sb(f"QT{j}", [128, 12, 128], BF16, esA) for j in range(2)]
            b_QT = [Buf(), Buf()]
            KT = k.sb("KT", [128, 2, S], BF16, esA)
            KIT = k.sb("KIT", [128, S], BF16, esA)
            b_KT = [Buf() for _ in range(NT)]
            for j_ in range(2):
                pool(lambda: nc.gpsimd.memset(QT[j_][64:128], 0.0), w=[b_QT[j_]])
            pool(lambda: nc.gpsimd.memset(KT[64:128], 0.0), w=b_KT)
            pool(lambda: nc.gpsimd.memset(KIT[64:128], 0.0), w=b_KT)
            Vaug = k.sb("Vaug", [128, NT, 2, 65], BF16, esA)
            b_V = [Buf() for _ in range(NT)]
            wv = k.sb("wv", [128, NT, 4], F32, esA)
            b_wv = [Buf() for _ in range(NT)]
            sza = [k.sb(f"sza{j}", [128, 512], F32, esA) for j in range(2)]
            b_sza = [Buf(), Buf()]
            sc = [k.sb(f"sc{j}", [128, S], F32, esA) for j in range(2)]
            b_sc = [Buf(), Buf()]
            rl = [k.sb(f"rl{j}", [128, S], F32, esA) for j in range(2)]
            b_rl = [Buf(), Buf()]
            junkb = k.sb("junkb", [128, S], BF16, esA)
            b_junkb = Buf()
            m01 = k.sb("m01", [128, S], BF16, esA)
            b_m01 = Buf()
            maskT = [k.sb(f"maskT{j}", [128, NT, 128], BF16, esA) for j in range(2)]
            b_maskT = [Buf(), Buf()]
            Bv = k.sb("Bv", [128, 1], F32, esA)
            Bk = k.sb("Bk", [128, NITER + 2], F32, esA)
            mid = [k.sb(f"mid{j}", [128, 1], F32, esA) for j in range(2)]
            cnt = k.sb("cnt", [128, 1], F32, esA)
            dd = k.sb("dd", [128, 1], F32, esA)
            thr = k.sb("thr", [128, NT], F32, esA)
            b_bis = Buf()
            Eb = [k.sb(f"Eb{j}", [128, 512], BF16, esA) for j in range(3)]
            b_Eb = [Buf(), Buf(), Buf()]
            rinv = k.sb("rinv", [128, 4], F32, esA)
            otmp = k.sb("otmp", [128, 4, 64], F32, esA)
            rinv8 = k.sb("rinv8", [128, 8], F32, esA)
            otmp8 = k.sb("otmp8", [128, 8, 64], F32, esA)
            b_otmp = Buf()
            oa_sb = k.sb("oa_sb", [128, 512], BF16, esA)
            b_oa = Buf()
            pool(lambda: nc.gpsimd.memset(Vaug[:], 1.0), w=b_V)

            A_BANK = [(0, 0, 512), (1, 512, 452), (2, 964, 512), (3, 1476, 128)]
            T0v = bfv(4)
            T1v = bfv(5)
            ctr = {"st": 0, "ix": 0, "e": 0}

            def hdr_(i):
                return i % 2, slice(i * 128, (i + 1) * 128), (i + 1) * 128, i >= 2

            def stage1(i):
                j, tsl, L, masked = hdr_(i)

                for (bk, c0, n) in A_BANK:
                    for kc in range(8):
                        pe(lambda: nc.tensor.matmul(PB[bk][:, 0:n], lhsT=hT[:, kc, tsl], rhs=wA[:, kc, c0:c0 + n],
                                                    start=(kc == 0), stop=(kc == 7)),
                           r=[b_hT[i], b_wA], w=[b_PB[bk]], inc=(kc == 7))
                ck(f"Aproj{i}")
                p0v = PB[0][:, 0:512].rearrange("p (h d) -> p h d", d=64)
                p1v = PB[1][:, 0:448].rearrange("p (h d) -> p h d", d=64)
                act(lambda: nc.scalar.copy(out=qk_sb[j][:, 0:8, 16:64], in_=p0v[:, :, 16:64]),
                    r=[b_PB[0]], w=[b_qk[j]])
                act(lambda: nc.scalar.copy(out=qk_sb[j][:, 8:15, 16:64], in_=p1v[:, :, 16:64]),
                    r=[b_PB[1]], w=[b_qk[j]])
                ck(f"Ae1_{i}")
                act(lambda: nc.scalar.copy(out=rsrc[:, 0:8, :], in_=p0v[:, :, 0:16]), r=[b_PB[0]], w=[b_rsrc])
                act(lambda: nc.scalar.copy(out=rsrc[:, 8:15, :], in_=p1v[:, :, 0:16]), r=[b_PB[1]], w=[b_rsrc])
                nh = 15
                cb = cos_t[:, i:i + 1, :].to_broadcast([128, nh, 8])
                sb_ = sin_t[:, i:i + 1, :].to_broadcast([128, nh, 8])
                x1 = rsrc[:, :, 0:8]
                x2 = rsrc[:, :, 8:16]
                dve(lambda: nc.vector.tensor_tensor(out=rt[0][:], in0=x1, in1=cb, op=ALU.mult),
                    r=[b_rsrc, b_tab], w=[b_rt])
                dve(lambda: nc.vector.tensor_tensor(out=rt[1][:], in0=x2, in1=sb_, op=ALU.mult),
                    r=[b_rsrc, b_tab], w=[b_rt])
                dve(lambda: nc.vector.tensor_tensor(out=qk_sb[j][:, :, 0:8], in0=rt[0][:], in1=rt[1][:], op=ALU.subtract),
                    r=[b_rt], w=[b_qk[j]])
                dve(lambda: nc.vector.tensor_tensor(out=rt[2][:], in0=x2, in1=cb, op=ALU.mult),
                    r=[b_rsrc, b_tab], w=[b_rt])
                dve(lambda: nc.vector.tensor_tensor(out=rt[3][:], in0=x1, in1=sb_, op=ALU.mult),
                    r=[b_rsrc, b_tab], w=[b_rt])
                dve(lambda: nc.vector.tensor_tensor(out=qk_sb[j][:, :, 8:16], in0=rt[2][:], in1=rt[3][:], op=ALU.add),
                    r=[b_rt], w=[b_qk[j]])
                ck(f"Ae2_{i}")
                act(lambda: nc.scalar.copy(out=wv[:, i, :], in_=PB[1][:, 448:452]), r=[b_PB[1]], w=[b_wv[i]])
                act(lambda: nc.scalar.copy(out=Vaug[:, i, :, 0:64],
                                           in_=PB[2][:, 0:128].rearrange("p (g d) -> p g d", d=64)),
                    r=[b_PB[2]], w=[b_V[i]])
                ck(f"Ae3_{i}")
                act(lambda: nc.scalar.activation(out=sza[j][:, 0:384], in_=PB[2][:, 128:512], func=AF.Silu),
                    r=[b_PB[2]], w=[b_sza[j]])
                act(lambda: nc.scalar.activation(out=sza[j][:, 384:512], in_=PB[3][:, 0:128], func=AF.Silu),
                    r=[b_PB[3]], w=[b_sza[j]])
                ck(f"Aevac{i}")
                for h in range(15):
                    tv, sl, bk = (T0v, h, 4) if h < 8 else (T1v, h - 8, 5)
                    pe(lambda: nc.tensor.transpose(out=tv[0:64, sl, :], in_=qk_sb[j][:, h, :], identity=ident[:]),
                       r=[b_qk[j], b_const], w=[b_PB[bk]], inc=(h == 7 or h == 14))
                act(lambda: nc.scalar.copy(out=QT[j][0:64, 0:8, :], in_=T0v[0:64, :, :]), r=[b_PB[4]], w=[b_QT[j]])
                act(lambda: nc.scalar.copy(out=KT[0:64, :, tsl], in_=T1v[0:64, 0:2, :]), r=[b_PB[5]], w=[b_KT[i]])
                act(lambda: nc.scalar.copy(out=QT[j][0:64, 8:12, :], in_=T1v[0:64, 2:6, :]), r=[b_PB[5]], w=[b_QT[j]])
                act(lambda: nc.scalar.copy(out=KIT[0:64, tsl], in_=T1v[0:64, 6, :]), r=[b_PB[5]], w=[b_KT[i]])


            def stage2(i):
                j, tsl, L, masked = hdr_(i)
                if not masked:
                    return

                nch = (L + 511) // 512
                for h in range(4):
                    q = h % 2
                    for c in range(nch):
                        c0 = c * 512
                        n = min(512, L - c0)
                        bk = ctr["ix"] % 2
                        ctr["ix"] += 1
                        pe(lambda: nc.tensor.matmul(PB[bk][:, 0:n], lhsT=QT[j][:, 8 + h, :], rhs=KIT[:, c0:c0 + n],
                                                    start=True, stop=True),
                           r=[b_QT[j]] + b_KT[0:i + 1], w=[b_PB[bk]])
                        act(lambda: nc.scalar.activation(out=rl[q][:, c0:c0 + n], in_=PB[bk][:, 0:n], func=AF.Relu),
                            r=[b_PB[bk]], w=[b_rl[q]])
                    if h == 0:
                        dve(lambda: nc.vector.tensor_scalar(out=sc[j][:, 0:L], in0=rl[q][:, 0:L],
                                                            scalar1=wv[:, i, 0:1], scalar2=None, op0=ALU.mult),
                            r=[b_rl[q], b_wv[i]], w=[b_sc[j]])
                    else:
                        dve(lambda: nc.vector.scalar_tensor_tensor(out=sc[j][:, 0:L], in0=rl[q][:, 0:L],
                                                                   scalar=wv[:, i, h:h + 1], in1=sc[j][:, 0:L],
                                                                   op0=ALU.mult, op1=ALU.add),
                            r=[b_rl[q], b_wv[i]], w=[b_sc[j]])


            def bisect(i):
                j, tsl, L, masked = hdr_(i)
                if not masked:
                    return
                yield

                dve(lambda: nc.vector.tensor_reduce(out=Bv[:], in_=sc[j][:, 0:L], axis=AX.X, op=ALU.max,
                                                    apply_absolute_value=True),
                    r=[b_sc[j]], w=[b_bis])
                dve(lambda: nc.vector.tensor_tensor(out=sc[j][:, L - 128:L], in0=sc[j][:, L - 128:L],
                                                    in1=cbias[:], op=ALU.add),
                    r=[b_const], w=[b_sc[j]])
                dve(lambda: nc.vector.tensor_scalar(out=Bk[:], in0=pow2[:], scalar1=Bv[:, 0:1], scalar2=None,
                                                    op0=ALU.mult), r=[b_const], w=[b_bis])
                dve(lambda: nc.vector.memset(mid[0][:], 0.0), w=[b_bis])
                for it in range(NITER):
                    ma, mb = mid[it % 2], mid[(it + 1) % 2]
                    dve(lambda: nc.vector.tensor_scalar(out=junkb[:, 0:L], in0=sc[j][:, 0:L], scalar1=ma[:, 0:1],
                                                        scalar2=None, op0=ALU.is_ge, op1=ALU.add,
                                                        accum_out=cnt[:, 0:1]),
                        r=[b_sc[j]], w=[b_bis, b_junkb])
                    dve(lambda: nc.vector.tensor_scalar(out=dd[:], in0=cnt[:], scalar1=255.5,
                                                        scalar2=Bk[:, it:it + 1], op0=ALU.is_ge, op1=ALU.mult),
                        w=[b_bis])
                    dve(lambda: nc.vector.tensor_scalar(out=mb[:], in0=dd[:], scalar1=Bk[:, it + 1:it + 2],
                                                        scalar2=ma[:, 0:1], op0=ALU.subtract, op1=ALU.add),
                        w=[b_bis])
                    yield
                mfin = mid[NITER % 2]
                dve(lambda: nc.vector.tensor_tensor(out=thr[:, i:i + 1], in0=mfin[:], in1=Bk[:, NITER:NITER + 1],
                                                    op=ALU.subtract), w=[b_bis])
                dve(lambda: nc.vector.tensor_scalar(out=m01[:, 0:L], in0=sc[j][:, 0:L], scalar1=thr[:, i:i + 1],
                                                    scalar2=None, op0=ALU.is_ge),
                    r=[b_sc[j], b_bis], w=[b_m01])
                if ("sc%d" % i) in dbg:
                    k.dump("d_sc", sc[j][:, 0:L], b_sc[j])
                    k.dump("d_thr", thr[:, i:i + 1], b_bis)


            def masktr(i):
                j, tsl, L, masked = hdr_(i)
                if not masked:
                    return

                for jb in range(i + 1):
                    tv, sl, bk = (T0v, jb, 4) if jb < 8 else (T1v, jb - 8, 5)
                    last = (jb == i) or (jb == 7)
                    pe(lambda: nc.tensor.transpose(out=tv[:, sl, :], in_=m01[:, jb * 128:(jb + 1) * 128],
                                                   identity=ident[:]),
                       r=[b_m01, b_const], w=[b_PB[bk]], inc=last)
                n0 = min(i + 1, 8)
                act(lambda: nc.scalar.activation(out=maskT[j][:, 0:n0, :], in_=T0v[:, 0:n0, :], func=AF.Identity,
                                                 scale=30000.0, bias=-30000.0),
                    r=[b_PB[4]], w=[b_maskT[j]])
                if i + 1 > 8:
                    act(lambda: nc.scalar.activation(out=maskT[j][:, 8:i + 1, :], in_=T1v[:, 0:i + 1 - 8, :],
                                                     func=AF.Identity, scale=30000.0, bias=-30000.0),
                        r=[b_PB[5]], w=[b_maskT[j]])


            def attn_main(i):
                j, tsl, L, masked = hdr_(i)
                nkt = i + 1
                for g in range(2):
                    Ov = PB[6 + g][:, 0:260].rearrange("p (h d) -> p h d", d=65)

                    def st_mm(jb):
                        bk = (2, 3, 0, 1)[ctr["st"] % 4]
                        ctr["st"] += 1
                        has_mask = masked or jb == i
                        pe(lambda: nc.tensor.matmul(PB[bk][:].rearrange("p (h t) -> p h t", t=128),
                                                    lhsT=KT[:, g, jb * 128:(jb + 1) * 128],
                                                    rhs=QT[j][:, 4 * g:4 * g + 4, :], start=True, stop=(not has_mask)),
                           r=[b_QT[j], b_KT[jb]], w=[b_PB[bk]], inc=(not has_mask))
                        if has_mask:
                            if masked:
                                mb_ap = maskT[j][:, jb:jb + 1, :].to_broadcast([128, 4, 128])
                                rd = [b_maskT[j], b_const]
                            else:
                                mb_ap = negU[:].unsqueeze(1).to_broadcast([128, 4, 128])
                                rd = [b_const]
                            pe(lambda: nc.tensor.matmul(PB[bk][:].rearrange("p (h t) -> p h t", t=128),
                                                        lhsT=ident[:], rhs=mb_ap, start=False, stop=True),
                               r=rd, w=[b_PB[bk]])
                        return bk

                    def exp_pv(jb, bk):
                        e = ctr["e"] % 3
                        ctr["e"] += 1
                        act(lambda: nc.scalar.activation(out=Eb[e][:], in_=PB[bk][:], func=AF.Exp, scale=0.125),
                            r=[b_PB[bk]], w=[b_Eb[e]])
                        for hh in range(4):
                            pe(lambda: nc.tensor.matmul(Ov[:, hh, :], lhsT=Eb[e][:, hh * 128:(hh + 1) * 128],
                                                        rhs=Vaug[:, jb, g, :], start=(jb == 0 and hh == 0),
                                                        stop=(jb == i and hh == 3)),
                               r=[b_Eb[e], b_V[jb]], w=[b_PB[6 + g]], inc=(hh == 3))

                    LA = 2
                    bks = {}
                    for jb0 in range(min(LA, nkt)):
                        bks[jb0] = st_mm(jb0)
                    for jb in range(nkt):
                        if jb + LA < nkt:
                            bks[jb + LA] = st_mm(jb + LA)
                        exp_pv(jb, bks[jb])

            def attn_fin(i):
                j, tsl, L, masked = hdr_(i)
                for g in range(2):
                    Ov = PB[6 + g][:, 0:260].rearrange("p (h d) -> p h d", d=65)
                    dve(lambda: nc.vector.reciprocal(out=rinv8[:, 4 * g:4 * g + 4].unsqueeze(2), in_=Ov[:, :, 64:65]),
                        r=[b_PB[6 + g]], w=[b_otmp])
                    dve(lambda: nc.vector.tensor_tensor(out=otmp8[:, 4 * g:4 * g + 4, :], in0=Ov[:, :, 0:64],
                                                        in1=rinv8[:, 4 * g:4 * g + 4].unsqueeze(2)
                                                        .to_broadcast([128, 4, 64]), op=ALU.mult),
                        r=[b_PB[6 + g]], w=[b_otmp])
                dve(lambda: nc.vector.tensor_tensor(out=oa_sb[:], in0=otmp8[:].rearrange("p h d -> p (h d)"),
                                                    in1=sza[j][:], op=ALU.mult),
                    r=[b_otmp, b_sza[j]], w=[b_oa])


            def fin_tr(i):
                j, tsl, L, masked = hdr_(i)
                for c in range(4):
                    pe(lambda: nc.tensor.transpose(out=T0v[:, c, :], in_=oa_sb[:, c * 128:(c + 1) * 128],
                                                   identity=ident[:]),
                       r=[b_oa, b_const], w=[b_PB[4]], inc=(c == 3))
                act(lambda: nc.scalar.copy(out=oaT[:, :, tsl], in_=T0v[:, 0:4, :]), r=[b_PB[4]], w=[b_oaT[i]])


            nA = NT if "skipA" not in dbg else 0
            pend = None
            for i in range(nA + 2):
                if i < nA:
                    stage1(i)
                if pend is not None:
                    for _ in pend:
                        pass
                    pend = None
                if i < nA:
                    stage2(i)
                if 1 <= i <= nA:
                    masktr(i - 1)
                if 2 <= i <= nA + 1:
                    fin_tr(i - 2)
                if 1 <= i <= nA:
                    attn_main(i - 1)
                if i < nA:
                    g_ = bisect(i)
                    nsteps = (NITER - 2) if (i + 1 < nA) else 10 ** 9
                    done_ = False
                    for _s in range(nsteps):
                        try:
                            next(g_)
                        except StopIteration:
                            done_ = True
                            break
                    if not done_:
                        pend = g_
                if 1 <= i <= nA:
                    attn_fin(i - 1)
            i = nA - 1
            j = i % 2
            L = (i + 1) * 128

            if "qk" in dbg:
                k.dump("d_KT", KT[:, :, 0:L], b_KT)
                k.dump("d_KIT", KIT[:, 0:L], b_KT)
                k.dump("d_QT", QT[j][:], b_QT[j])
                k.dump("d_V", Vaug[:, 0:i + 1], b_V)
            if "oaT" in dbg:
                k.dump("d_oaT", oaT[:, :, 0:L], b_oaT)
            k.barrier()
        esWA.close()
        if stop_after.startswith("A"):
            return nc, k


        obT = k.sb("obT", [128, 8, S], BF16)
        b_obT = [Buf() for _ in range(NT)]
        with ExitStack() as esB:
            wdt = k.sb("wdt", [128, 8, 16], BF16, esB)
            b_wdt = Buf()
            dpool(out=wdt[:], in_=win_d[:, 4164:4180].rearrange("(k p) n -> p k n", p=128), w=[b_wdt])
            X_tm = k.sb("X_tm", [128, NT, 1024], BF16, esB)
            B_tm = k.sb("B_tm", [128, NT, 256], BF16, esB)
            BT = k.sb("BT", [128, 2, S], BF16, esB)
            CT = k.sb("CT", [128, 2, S], BF16, esB)
            b_X = Buf()
            dtb_b = k.sb("dtb_b", [128, 16], F32, esB)
            a_b = k.sb("a_b", [128, 16], F32, esB)
            dsk_b = k.sb("dsk_b", [128, 16], F32, esB)
            snw_b = k.sb("snw_b", [128, D], F32, esB)
            dt_all = k.sb("dt_all", [128, NT, 16], F32, esB)
            dA_all = k.sb("dA_all", [128, NT, 16], F32, esB)
            spt = [k.sb(f"spt{j}", [128, NT, 16], F32, esB) for j in range(3)]
            ones_f = k.sb("ones_f", [128, 128], F32, esB)
            NEGU4 = k.sb("NEGU4", [128, 4, 128], BF16, esB)
            Dg = k.sb("Dg", [128, 16, 128], BF16, esB)
            b_ptab = Buf()
            dsync(out=dtb_b[:], in_=dtb_d[0, :].partition_broadcast(128), w=[b_ptab])
            dsync(out=a_b[:], in_=alog_d[0, :].partition_broadcast(128), w=[b_ptab])
            dsync(out=dsk_b[:], in_=dsk_d[0, :].partition_broadcast(128), w=[b_ptab])
            dsync(out=snw_b[:], in_=snw_d[0, :].partition_broadcast(128), w=[b_ptab])
            pool(lambda: nc.gpsimd.memset(ones_f[:], 1.0), w=[b_ptab])
            pool(lambda: nc.gpsimd.memset(NEGU4[:], 0.0), w=[b_ptab])
            pool(lambda: nc.gpsimd.affine_select(out=NEGU4[:], in_=NEGU4[:], pattern=[[0, 4], [1, 128]],
                                                 compare_op=ALU.is_ge, fill=-1.0e4, base=0, channel_multiplier=-1),
                 w=[b_ptab])
            act(lambda: nc.scalar.activation(out=a_b[:], in_=a_b[:], func=AF.Exp), w=[b_ptab])
            dve(lambda: nc.vector.tensor_scalar(out=a_b[:], in0=a_b[:], scalar1=-1.0, scalar2=None, op0=ALU.mult),
                w=[b_ptab])
            dve(lambda: nc.vector.tensor_tensor(out=Dg[:], in0=ident[:].unsqueeze(1).to_broadcast([128, 16, 128]),
                                                in1=dsk_b[:].unsqueeze(2).to_broadcast([128, 16, 128]), op=ALU.mult),
                r=[b_const], w=[b_ptab])
            ck("Btab")

            for i in range(NT):
                for kc in range(8):
                    pe(lambda: nc.tensor.matmul(PB[0][:, i * 16:(i + 1) * 16], lhsT=hT[:, kc, i * 128:(i + 1) * 128],
                                                rhs=wdt[:, kc, :], start=(kc == 0), stop=(kc == 7)),
                       r=[b_hT[i], b_wdt], w=[b_PB[0]], inc=(kc == 7))
            dve(lambda: nc.vector.tensor_tensor(out=spt[0][:], in0=PB[0][:, 0:256].rearrange("p (i h) -> p i h", h=16),
                                                in1=dtb_b[:].unsqueeze(1).to_broadcast([128, NT, 16]), op=ALU.add),
                r=[b_PB[0]], w=[b_ptab])
            dve(lambda: nc.vector.tensor_scalar(out=spt[2][:], in0=spt[0][:], scalar1=-1.0, scalar2=None, op0=ALU.mult),
                w=[b_ptab])
            dve(lambda: nc.vector.tensor_tensor(out=spt[1][:], in0=spt[0][:], in1=spt[2][:], op=ALU.max), w=[b_ptab])
            act(lambda: nc.scalar.activation(out=spt[1][:], in_=spt[1][:], func=AF.Exp, scale=-1.0), w=[b_ptab])
            act(lambda: nc.scalar.activation(out=spt[1][:], in_=spt[1][:], func=AF.Ln, bias=1.0), w=[b_ptab])
            dve(lambda: nc.vector.tensor_scalar(out=spt[2][:], in0=spt[0][:], scalar1=0.0, scalar2=None, op0=ALU.max),
                w=[b_ptab])
            dve(lambda: nc.vector.tensor_tensor(out=dt_all[:], in0=spt[2][:], in1=spt[1][:], op=ALU.add), w=[b_ptab])
            dve(lambda: nc.vector.tensor_tensor(out=dA_all[:], in0=dt_all[:],
                                                in1=a_b[:].unsqueeze(1).to_broadcast([128, NT, 16]), op=ALU.mult),
                w=[b_ptab])
            if "dt" in dbg:
                k.dump("d_dt", dt_all[:], b_ptab)
            ck("Bdt")

            with ExitStack() as esC:
                cwT = k.sb("cwT_sb", [128, 12, 4], F32, esC)
                cbT = k.sb("cbT_sb", [128, 12], F32, esC)
                b_cw = Buf()
                dsync(out=cwT[:], in_=cwT_d[:, :, :], w=[b_cw])
                dsync(out=cbT[:], in_=cbT_d[:, :], w=[b_cw])
                pre = [k.sb(f"pre{j}", [128, S + 3], F32, esC) for j in range(2)]
                b_pre = [Buf(), Buf()]
                accs = [k.sb(f"acc{j}", [128, S], F32, esC) for j in range(2)]
                b_accs = [Buf(), Buf()]
                xs_fm = k.sb("xs_fm", [128, S], BF16, esC)
                b_xs = Buf()
                for q in range(2):
                    pool(lambda: nc.gpsimd.memset(pre[q][:, 0:3], 0.0), w=[b_pre[q]])
                b_xs2 = [Buf(), Buf()]
                b_cv = [Buf() for _ in range(12)]

                def cproj(m):
                    q = m % 2
                    slot = (m // 4) % 2
                    wc0 = slot * 512 + (m % 4) * 128
                    for tc in range(4):
                        for kc in range(8):
                            pe(lambda: nc.tensor.matmul(PB[tc][:], lhsT=wBC[:, kc, wc0:wc0 + 128],
                                                        rhs=hT[:, kc, tc * 512:(tc + 1) * 512],
                                                        start=(kc == 0), stop=(kc == 7)),
                               r=b_hT[tc * 4:(tc + 1) * 4] + [b_wslot[slot]], w=[b_PB[tc]], inc=(kc == 7))
                        act(lambda: nc.scalar.copy(out=pre[q][:, 3 + tc * 512:3 + (tc + 1) * 512], in_=PB[tc][:]),
                            r=[b_PB[tc]], w=[b_pre[q]])

                def cpost(m):
                    q = m % 2
                    acc = accs[q]
                    b_acc = b_accs[q]
                    dve(lambda: nc.vector.tensor_scalar(out=acc[:], in0=pre[q][:, 0:S], scalar1=cwT[:, m, 0:1],
                                                        scalar2=None, op0=ALU.mult),
                        r=[b_pre[q], b_cw], w=[b_acc])
                    for kk in range(1, 4):
                        dve(lambda: nc.vector.scalar_tensor_tensor(out=acc[:], in0=pre[q][:, kk:kk + S],
                                                                   scalar=cwT[:, m, kk:kk + 1], in1=acc[:],
                                                                   op0=ALU.mult, op1=ALU.add),
                            r=[b_pre[q], b_cw], w=[b_acc])
                    if m < 10:
                        dst = xs_fm[:] if m < 8 else BT[:, m - 8, :]
                        bdst = b_xs if m < 8 else b_cv[m]
                        act(lambda: nc.scalar.activation(out=dst, in_=acc[:], func=AF.Silu, bias=cbT[:, m:m + 1]),
                            r=[b_acc, b_cw], w=[bdst])
                        for half in range(2):
                            bk = 4 + half
                            tv = bfv(bk)
                            for s8 in range(8):
                                ti_ = half * 8 + s8
                                in_ap = (xs_fm[:, ti_ * 128:(ti_ + 1) * 128] if m < 8
                                         else BT[:, m - 8, ti_ * 128:(ti_ + 1) * 128])
                                pe(lambda: nc.tensor.transpose(out=tv[:, s8, :], in_=in_ap, identity=ident[:]),
                                   r=[bdst, b_const], w=[b_PB[bk]], inc=(s8 == 7))
                            if m < 8:
                                act(lambda: nc.scalar.copy(out=X_tm[:, half * 8:(half + 1) * 8, m * 128:(m + 1) * 128],
                                                           in_=tv[:, :, :]), r=[b_PB[bk]], w=[b_cv[m]])
                            else:
                                act(lambda: nc.scalar.copy(out=B_tm[:, half * 8:(half + 1) * 8,
                                                                    (m - 8) * 128:(m - 7) * 128],
                                                           in_=tv[:, :, :]), r=[b_PB[bk]], w=[b_cv[m]])
                    else:
                        act(lambda: nc.scalar.activation(out=CT[:, m - 10, :], in_=acc[:], func=AF.Silu,
                                                         bias=cbT[:, m:m + 1]),
                            r=[b_acc, b_cw], w=[b_cv[m]])

                def wreload(m_done):
                    if m_done == 3:
                        dpool(out=wBC[:, :, 0:512], in_=win_d[:, CONV0 + 1024:CONV0 + 1536]
                              .rearrange("(k p) n -> p k n", p=128), w=[b_wslot[0]])
                    elif m_done == 7:
                        dpool(out=wBC[:, :, 512:1024], in_=win_d[:, 1604:2116]
                              .rearrange("(k p) n -> p k n", p=128), w=[b_wslot[1]])
                    elif m_done == 11:
                        dpool(out=wBC[:, :, 0:512], in_=win_d[:, 2116:2628]
                              .rearrange("(k p) n -> p k n", p=128), w=[b_wslot[0]])

                cproj(0)
                wreload(0)
                for m in range(12):
                    if m + 1 < 12:
                        cproj(m + 1)
                        wreload(m + 1)
                    cpost(m)
                    ck(f"Bconv{m}")
                b_X.w = {}
                for bb in b_cv:
                    _merge(b_X.w, bb.w)
                if "conv" in dbg:
                    k.dump("d_Xtm", X_tm[:], b_X)
                    k.dump("d_Btm", B_tm[:], b_X)
                    k.dump("d_BT", BT[:], b_X)
                    k.dump("d_CT", CT[:], b_X)
                ck("Bconv")
                k.barrier()

            with ExitStack() as esS:
                ones_b = k.sb("ones_b", [128, 128], BF16, esS)
                dAhl = k.sb("dAhl", [128, NT, 2, 16], BF16, esS)
                dAres = spt[0]
                b_hl = Buf()
                pool(lambda: nc.gpsimd.memset(ones_b[:], 1.0), w=[b_hl])
                dve(lambda: nc.vector.tensor_copy(out=dAhl[:, :, 0, :], in_=dA_all[:]), r=[b_ptab], w=[b_hl])
                dve(lambda: nc.vector.tensor_tensor(out=dAres[:], in0=dA_all[:], in1=dAhl[:, :, 0, :], op=ALU.subtract),
                    r=[b_ptab], w=[b_hl])
                dve(lambda: nc.vector.tensor_copy(out=dAhl[:, :, 1, :], in_=dAres[:]), w=[b_hl])
                szb = [k.sb(f"szb{j}", [128, 1024], BF16, esS) for j in range(2)]
                b_szb = [Buf(), Buf()]
                smalls = [k.sb(f"small{j}", [128, 32], F32, esS) for j in range(2)]
                nacums = [k.sb(f"nacum{j}", [128, 16], F32, esS) for j in range(2)]
                eas = [k.sb(f"ea{j}", [128, 16], F32, esS) for j in range(2)]
                decs = [k.sb(f"dec{j}", [128, 16], F32, esS) for j in range(2)]
                dtds = [k.sb(f"dtd{j}", [128, 16], F32, esS) for j in range(2)]
                eASs = [k.sb(f"eAS{j}", [128, 2, 4], F32, esS) for j in range(2)]
                b_sms = [Buf(), Buf()]
                LTg = [k.sb(f"LTg{j}", [128, 4, 128], F32, esS) for j in range(2)]
                b_LT = [Buf(), Buf()]
                MTg = [k.sb(f"MTg{j}", [128, 4, 128], BF16, esS) for j in range(2)]
                b_MT = [Buf(), Buf()]
                CBs = [k.sb("CBs0", [128, 4, 128], F32, esS)] * 2
                b_CBs = [Buf()] * 2
                xds = [k.sb(f"xd{j}", [128, 16, 64], BF16, esS) for j in range(2)]
                xdds = [k.sb(f"xdd{j}", [128, 16, 64], BF16, esS) for j in range(2)]
                b_xds = [Buf(), Buf()]
                b_xdds = [Buf(), Buf()]
                ysb = k.sb("ysb", [128, 16, 64], F32, esS)
                b_y = Buf()
                ssq = k.sb("ssq", [128, 4], F32, esS)
                rs4 = k.sb("rs4", [128, 4], F32, esS)
                junkf = k.sb("junkf", [128, 256], F32, esS)
                ob_sb = k.sb("ob_sb", [128, 1024], BF16, esS)
                b_ob = Buf()
                S_sb = k.sb("S_sb", [128, 2, 256], F32, esS)
                S_bf = k.sb("S_bf", [128, 2, 256], BF16, esS)
                b_S = Buf()
                b_Sbf = Buf()
                gctr = {"g": 0, "d": 0}
                Yv = [PB[4][:].rearrange("p (h d) -> p h d", d=64), PB[5][:].rearrange("p (h d) -> p h d", d=64)]

                def head(c):
                    q = c % 2
                    csl = slice(c * 128, (c + 1) * 128)
                    small, nacum, ea, dec, dtd, eAS, b_sm = smalls[q], nacums[q], eas[q], decs[q], dtds[q], eASs[q], b_sms[q]
                    pe(lambda: nc.tensor.matmul(PB[2][:, 0:16], lhsT=Uf[:], rhs=dA_all[:, c, :], start=True, stop=False),
                       r=[b_const, b_ptab], w=[b_PB[2]], inc=False)
                    pe(lambda: nc.tensor.matmul(PB[2][:, 16:32], lhsT=ones_f[:], rhs=dA_all[:, c, :], start=False,
                                                stop=True), r=[b_ptab], w=[b_PB[2]])
                    CBv = PB[3][:].rearrange("p (g l) -> p g l", l=128)
                    tk = None
                    for gi, g in enumerate((0, 2, 1, 3)):
                        p0 = (g % 2) * 64
                        tk2 = pe(lambda: nc.tensor.matmul(CBv[:, g, :], lhsT=BT[p0:p0 + 64, g // 2, csl],
                                                          rhs=CT[p0:p0 + 64, g // 2, csl], start=(gi == 0),
                                                          stop=(gi == 3)),
                                 r=[b_X], w=[b_PB[3]], inc=(gi == 1 or gi == 3), selfwait=(tk if gi == 2 else None))
                        if gi == 1:
                            tk = tk2
                    for hb in range(2):
                        for kc in range(8):
                            pe(lambda: nc.tensor.matmul(PB[hb][:], lhsT=hT[:, kc, csl],
                                                        rhs=wBC[:, kc, (1 - hb) * 512:(2 - hb) * 512],
                                                        start=(kc == 0), stop=(kc == 7)),
                               r=[b_hT[c], b_wslot[1 - hb]], w=[b_PB[hb]], inc=(kc == 7))
                    yield
                    dve(lambda: nc.vector.tensor_copy(out=small[:], in_=PB[2][:, 0:32]), r=[b_PB[2]], w=[b_sm])
                    acum = small[:, 0:16]
                    atot = small[:, 16:32]
                    dve(lambda: nc.vector.tensor_tensor(out=dec[:], in0=atot, in1=acum, op=ALU.subtract), w=[b_sm])
                    act(lambda: nc.scalar.activation(out=ea[:], in_=acum, func=AF.Exp), w=[b_sm])
                    act(lambda: nc.scalar.activation(out=dec[:], in_=dec[:], func=AF.Exp), w=[b_sm])
                    atv = small[:, 16:32].rearrange("p (s f h) -> p s f h", s=2, f=2)
                    act(lambda: nc.scalar.activation(out=eAS[0:64], in_=atv[0:64, :, 0, :], func=AF.Exp), w=[b_sm])
                    act(lambda: nc.scalar.activation(out=eAS[64:128], in_=atv[64:128, :, 1, :], func=AF.Exp), w=[b_sm])
                    act(lambda: nc.scalar.copy(out=CBs[q][:], in_=CBv), r=[b_PB[3]], w=[b_CBs[q]])
                    for hb in range(2):
                        act(lambda: nc.scalar.activation(out=szb[q][:, hb * 512:(hb + 1) * 512], in_=PB[hb][:],
                                                         func=AF.Silu), r=[b_PB[hb]], w=[b_szb[q]])
                    dve(lambda: nc.vector.tensor_tensor(out=dtd[:], in0=dt_all[:, c, :], in1=dec[:], op=ALU.mult),
                        r=[b_ptab], w=[b_sm])
                    Xc = X_tm[:, c, :].rearrange("p (h d) -> p h d", d=64)
                    dve(lambda: nc.vector.tensor_tensor(out=xds[q][:], in0=Xc,
                                                        in1=dt_all[:, c, :].unsqueeze(2).to_broadcast([128, 16, 64]),
                                                        op=ALU.mult), r=[b_X, b_ptab], w=[b_xds[q]])
                    dve(lambda: nc.vector.tensor_tensor(out=xdds[q][:], in0=Xc,
                                                        in1=dtd[:].unsqueeze(2).to_broadcast([128, 16, 64]),
                                                        op=ALU.mult), r=[b_X, b_sm], w=[b_xdds[q]])

                    yield

                def groups(c):
                    q = c % 2
                    csl = slice(c * 128, (c + 1) * 128)
                    nacum, b_sm = nacums[q], b_sms[q]
                    xd = xds[q]
                    st = {}

                    def acumb(g):
                        gq = gctr["g"] % 2
                        gctr["g"] += 1
                        abk = 2 if gq == 0 else 6
                        first = True
                        for hh in range(4):
                            hd = 4 * g + hh
                            for part in range(2):
                                pe(lambda: nc.tensor.matmul(PB[abk][:, hh * 128:(hh + 1) * 128],
                                                            lhsT=dAhl[:, c, part, hd:hd + 1].to_broadcast([128, 128]),
                                                            rhs=Ub[:], start=first, stop=False),
                                   r=[b_hl, b_const], w=[b_PB[abk]], inc=False)
                                first = False
                        for part in range(2):
                            pe(lambda: nc.tensor.matmul(
                                PB[abk][:].rearrange("p (h l) -> p h l", l=128), lhsT=negUb[:],
                                rhs=dAhl[:, c, part, 4 * g:4 * g + 4].unsqueeze(2).to_broadcast([128, 4, 128]),
                                start=False, stop=False), r=[b_hl, b_const], w=[b_PB[abk]], inc=False)
                        pe(lambda: nc.tensor.matmul(PB[abk][:], lhsT=ident[:],
                                                    rhs=NEGU4[:].rearrange("p h l -> p (h l)"), start=False, stop=True),
                           r=[b_const, b_ptab], w=[b_PB[abk]])
                        st[g] = (gq, abk)

                    def ymm(g):
                        gq, abk = st[g]
                        act(lambda: nc.scalar.activation(out=LTg[gq][:].rearrange("p h l -> p (h l)"), in_=PB[abk][:],
                                                         func=AF.Exp), r=[b_PB[abk]], w=[b_LT[gq]])
                        dve(lambda: nc.vector.tensor_tensor(out=MTg[gq][:], in0=LTg[gq][:],
                                                            in1=CBs[q][:, g:g + 1, :].to_broadcast([128, 4, 128]),
                                                            op=ALU.mult),
                            r=[b_LT[gq], b_CBs[q]], w=[b_MT[gq]])
                        for hh in range(4):
                            hd = 4 * g + hh
                            yb = 4 + hd // 8
                            pe(lambda: nc.tensor.matmul(Yv[hd // 8][:, hd % 8, :], lhsT=MTg[gq][:, hh, :],
                                                        rhs=xd[:, hd, :], start=(hd % 8 == 0), stop=False),
                               r=[b_MT[gq], b_xds[q]], w=[b_PB[yb]], inc=False)
                            pe(lambda: nc.tensor.matmul(Yv[hd // 8][:, hd % 8, :], lhsT=Dg[:, hd, :],
                                                        rhs=X_tm[:, c, hd * 64:(hd + 1) * 64], start=False,
                                                        stop=(hd % 8 == 7)),
                               r=[b_ptab, b_X], w=[b_PB[yb]], inc=(hh == 3))

                    acumb(0)
                    acumb(1)
                    yield
                    ymm(0)
                    yield
                    acumb(2)
                    ymm(1)
                    yield
                    acumb(3)
                    ymm(2)
                    yield
                    ymm(3)
                    yield

                def tail(c):
                    q = c % 2
                    csl = slice(c * 128, (c + 1) * 128)
                    ea, eAS, b_sm = eas[q], eASs[q], b_sms[q]
                    xdd = xdds[q]
                    if c > 0:
                        tk = None
                        for gi, g in enumerate((0, 2, 1, 3)):
                            p0 = (g % 2) * 64
                            ob_ = 6 + g // 2
                            tk2 = pe(lambda: nc.tensor.matmul(PB[ob_][:, (g % 2) * 256:(g % 2 + 1) * 256],
                                                              lhsT=CT[p0:p0 + 64, g // 2, csl],
                                                              rhs=S_bf[p0:p0 + 64, g // 2, :], start=(g % 2 == 0),
                                                              stop=(g % 2 == 1)),
                                     r=[b_X, b_Sbf], w=[b_PB[ob_]], inc=(gi >= 1),
                                     selfwait=(tk if gi == 2 else None))
                            if gi == 1:
                                tk = tk2
                        for hb in range(2):
                            dve(lambda: nc.vector.tensor_tensor(
                                out=ysb[:, hb * 8:(hb + 1) * 8, :],
                                in0=PB[6 + hb][:].rearrange("p (h d) -> p h d", d=64),
                                in1=ea[:, hb * 8:(hb + 1) * 8].unsqueeze(2).to_broadcast([128, 8, 64]), op=ALU.mult),
                                r=[b_PB[6 + hb], b_sm], w=[b_y])
                            dve(lambda: nc.vector.tensor_tensor(out=ysb[:, hb * 8:(hb + 1) * 8, :], in0=Yv[hb],
                                                                in1=ysb[:, hb * 8:(hb + 1) * 8, :], op=ALU.add),
                                r=[b_PB[4 + hb]], w=[b_y])
                    else:
                        for hb in range(2):
                            dve(lambda: nc.vector.tensor_copy(out=ysb[:, hb * 8:(hb + 1) * 8, :], in_=Yv[hb]),
                                r=[b_PB[4 + hb]], w=[b_y])
                    yield
                    dve(lambda: nc.vector.tensor_tensor(out=ysb[:].rearrange("p h d -> p (h d)"),
                                                        in0=ysb[:].rearrange("p h d -> p (h d)"), in1=szb[q][:],
                                                        op=ALU.mult), r=[b_szb[q]], w=[b_y])
                    yf = ysb[:].rearrange("p h d -> p (h d)")
                    for g in range(4):
                        act(lambda: nc.scalar.activation(out=junkf[:], in_=yf[:, g * 256:(g + 1) * 256], func=AF.Square,
                                                         accum_out=ssq[:, g:g + 1]), r=[b_y], w=[b_ob])
                    pool(lambda: nc.gpsimd.tensor_scalar(out=rs4[:], in0=ssq[:], scalar1=1.0 / 256, scalar2=EPS,
                                                         op0=ALU.mult, op1=ALU.add), w=[b_ob])
                    pool(lambda: nc.gpsimd.tensor_tensor(out=rs4[:], in0=rs4[:], in1=mhalf[:, 0:4], op=ALU.pow),
                         r=[b_const], w=[b_ob])
                    yield
                    if c < NT - 1:
                        for g in range(4):
                            p0 = (g % 2) * 64
                            pe(lambda: nc.tensor.matmul(PB[7][p0:p0 + 64, (g // 2) * 256:(g // 2 + 1) * 256],
                                                        lhsT=B_tm[:, c, g * 64:(g + 1) * 64],
                                                        rhs=xdd[:, 4 * g:4 * g + 4, :].rearrange("p h d -> p (h d)"),
                                                        start=(g < 2), stop=(g >= 2)),
                               r=[b_X, b_xdds[q]], w=[b_PB[7]], inc=(g == 3))
                        Sv = S_sb[:].rearrange("p s (h d) -> p (s h) d", d=64)
                        if c == 0:
                            dve(lambda: nc.vector.tensor_copy(out=S_sb[:].rearrange("p s f -> p (s f)"), in_=PB[7][:]),
                                r=[b_PB[7]], w=[b_S])
                        else:
                            dve(lambda: nc.vector.tensor_tensor(
                                out=Sv, in0=Sv,
                                in1=eAS[:].rearrange("p s h -> p (s h)").unsqueeze(2).to_broadcast([128, 8, 64]),
                                op=ALU.mult), r=[b_sm, b_Sbf], w=[b_S])
                            dve(lambda: nc.vector.tensor_tensor(out=S_sb[:].rearrange("p s f -> p (s f)"),
                                                                in0=S_sb[:].rearrange("p s f -> p (s f)"),
                                                                in1=PB[7][:], op=ALU.add),
                                r=[b_PB[7]], w=[b_S])
                        act(lambda: nc.scalar.copy(out=S_bf[:], in_=S_sb[:]), r=[b_S], w=[b_Sbf])
                    yield
                    for g in range(4):
                        dve(lambda: nc.vector.scalar_tensor_tensor(out=ob_sb[:, g * 256:(g + 1) * 256],
                                                                   in0=yf[:, g * 256:(g + 1) * 256],
                                                                   scalar=rs4[:, g:g + 1],
                                                                   in1=snw_b[:, g * 256:(g + 1) * 256],
                                                                   op0=ALU.mult, op1=ALU.mult),
                            r=[b_y, b_ptab], w=[b_ob])
                    yield
                    tv = bfv(7)
                    for cc in range(8):
                        pe(lambda: nc.tensor.transpose(out=tv[:, cc, :], in_=ob_sb[:, cc * 128:(cc + 1) * 128],
                                                       identity=ident[:]),
                           r=[b_ob, b_const], w=[b_PB[7]], inc=(cc == 7))
                    act(lambda: nc.scalar.copy(out=obT[:, :, csl], in_=tv[:, :, :]), r=[b_PB[7]], w=[b_obT[c]])

                def front(c):
                    yield from head(c)
                    yield from groups(c)

                def interleave(gens):
                    gens = list(gens)
                    while gens:
                        for g_ in list(gens):
                            try:
                                next(g_)
                            except StopIteration:
                                gens.remove(g_)

                interleave([front(0)])
                for c in range(NT):
                    gl = [tail(c)]
                    if c + 1 < NT:
                        gl.append(front(c + 1))
                    interleave(gl)
                    if c == NT - 2:
                        dpool(out=wG0[:, :, 0, :], in_=win_d[:, 4180:4180 + 128].rearrange("(k p) n -> p k n", p=128),
                              w=[b_wslot[0]])
                        dpool(out=wG0[:, :, 1, :], in_=win_d[:, 4180 + 1024:4180 + 1152]
                              .rearrange("(k p) n -> p k n", p=128), w=[b_wslot[0]])
                        dpool(out=Wpa0, in_=wpa_d[:, 0:128].rearrange("(k p) n -> p k n", p=128), w=[b_wslot[0]])
                        dpool(out=Wpb0, in_=wpb_d[:, 0:128].rearrange("(k p) n -> p k n", p=128), w=[b_wslot[0]])
                if "obT" in dbg:
                    k.dump("d_obT", obT[:], b_obT)
                k.barrier()
        if stop_after.startswith("B"):
            return nc, k


        with ExitStack() as esM:
            Wout = k.sb("Wout", [128, 8, D], BF16, esM)
            b_W = Buf()
            gbT = k.sb("gbT_sb", [128, 16], F32, esM)
            fnw_b = k.sb("fnw_b", [128, D], F32, esM)
            b_ct = Buf()
            dsync(out=gbT[:], in_=gbT_d[:, :], w=[b_ct])
            dsync(out=fnw_b[:], in_=fnw_d[0, :].partition_broadcast(128), w=[b_ct])
            mT = k.sb("mT", [128, 8, S], BF16, esM)
            b_mT = [Buf() for _ in range(4)]
            with ExitStack() as esM1:
                wG = [k.sb(f"wG{j}", [128, 8, 2, 128], BF16, esM1) for j in range(2)]
                Wpa = [k.sb(f"Wpa{j}", [128, 4, 128], BF16, esM1) for j in range(2)]
                Wpb = [k.sb(f"Wpb{j}", [128, 8, 128], BF16, esM1) for j in range(2)]
                b_wG = [Buf(), Buf()]
                gA = [k.sb(f"gA{j}", [128, 512], F32, esM1) for j in range(2)]
                gB = [k.sb(f"gB{j}", [128, 512], F32, esM1) for j in range(2)]
                b_g = [Buf(), Buf()]
                t1 = [k.sb(f"t1{j}", [128, 512], F32, esM1) for j in range(2)]
                t2 = [k.sb(f"t2{j}", [128, 512], F32, esM1) for j in range(2)]
                b_t = [Buf(), Buf()]
                it = 0
                for m in range(8):
                    wq = m % 2
                    if m == 0:
                        gw_, pa_, pb_, bw_ = wG0, Wpa0, Wpb0, b_wslot[0]
                    else:
                        gw_, pa_, pb_, bw_ = wG[wq][:], Wpa[wq][:], Wpb[wq][:], b_wG[wq]
                        dpool(out=gw_[:, :, 0, :], in_=win_d[:, 4180 + m * 128:4180 + (m + 1) * 128]
                              .rearrange("(k p) n -> p k n", p=128), w=[bw_])
                        dpool(out=gw_[:, :, 1, :], in_=win_d[:, 4180 + (8 + m) * 128:4180 + (9 + m) * 128]
                              .rearrange("(k p) n -> p k n", p=128), w=[bw_])
                        dpool(out=pa_, in_=wpa_d[:, m * 128:(m + 1) * 128].rearrange("(k p) n -> p k n", p=128),
                              w=[bw_])
                        dpool(out=pb_, in_=wpb_d[:, m * 128:(m + 1) * 128].rearrange("(k p) n -> p k n", p=128),
                              w=[bw_])
                    if m == 0:
                        for hf in range(2):
                            cs_ = slice(hf * 512, (hf + 1) * 512)
                            dpool(out=Wout[:, :, cs_], in_=wout_d[:, cs_].rearrange("(k p) n -> p k n", p=128),
                                  w=[b_W])
                    for tc in range(4):
                        q = it % 2
                        it += 1
                        b0 = 4 * q
                        ts_ = slice(tc * 512, (tc + 1) * 512)
                        for kc in range(8):
                            pe(lambda: nc.tensor.matmul(PB[b0][:], lhsT=gw_[:, kc, 0, :], rhs=hT[:, kc, ts_],
                                                        start=(kc == 0), stop=(kc == 7)),
                               r=b_hT[tc * 4:(tc + 1) * 4] + [bw_], w=[b_PB[b0]], inc=(kc == 7))
                        for kc in range(8):
                            pe(lambda: nc.tensor.matmul(PB[b0 + 1][:], lhsT=gw_[:, kc, 1, :], rhs=hT[:, kc, ts_],
                                                        start=(kc == 0), stop=(kc == 7)),
                               r=b_hT[tc * 4:(tc + 1) * 4] + [bw_], w=[b_PB[b0 + 1]], inc=(kc == 7))
                        for kc in range(4):
                            pe(lambda: nc.tensor.matmul(PB[b0 + 2][:], lhsT=pa_[:, kc, :],
                                                        rhs=oaT[:, kc, ts_], start=(kc == 0), stop=(kc == 3)),
                               r=b_oaT[tc * 4:(tc + 1) * 4] + [bw_], w=[b_PB[b0 + 2]], inc=(kc == 3))
                        for kc in range(8):
                            pe(lambda: nc.tensor.matmul(PB[b0 + 3][:], lhsT=pb_[:, kc, :],
                                                        rhs=obT[:, kc, ts_], start=(kc == 0), stop=(kc == 7)),
                               r=b_obT[tc * 4:(tc + 1) * 4] + [bw_], w=[b_PB[b0 + 3]], inc=(kc == 7))
                        act(lambda: nc.scalar.activation(out=gA[q][:], in_=PB[b0][:], func=AF.Sigmoid,
                                                         bias=gbT[:, m:m + 1]), r=[b_PB[b0], b_ct], w=[b_g[q]])
                        act(lambda: nc.scalar.activation(out=gB[q][:], in_=PB[b0 + 1][:], func=AF.Sigmoid,
                                                         bias=gbT[:, 8 + m:9 + m]), r=[b_PB[b0 + 1], b_ct], w=[b_g[q]])
                        dve(lambda: nc.vector.tensor_tensor(out=t1[q][:], in0=PB[b0 + 2][:], in1=gA[q][:], op=ALU.mult),
                            r=[b_PB[b0 + 2], b_g[q]], w=[b_t[q]])
                        dve(lambda: nc.vector.tensor_tensor(out=t2[q][:], in0=PB[b0 + 3][:], in1=gB[q][:], op=ALU.mult),
                            r=[b_PB[b0 + 3], b_g[q]], w=[b_t[q]])
                        dve(lambda: nc.vector.tensor_tensor(out=mT[:, m, ts_], in0=t1[q][:], in1=t2[q][:], op=ALU.add),
                            r=[b_t[q]], w=[b_mT[tc]])
                k.barrier()
            if "mT" in dbg:
                k.dump("d_mT", mT[:], b_mT)
            ck("Cm")
            xr = [k.sb(f"xr{j}", [128, D], F32, esM) for j in range(2)]
            b_xr = [Buf(), Buf()]
            xo = [k.sb(f"xo{j}", [128, D], F32, esM) for j in range(2)]
            b_xo = [Buf(), Buf()]
            fo = [k.sb(f"fo{j}", [128, D], F32, esM) for j in range(2)]
            b_fo = [Buf(), Buf()]
            junkc = k.sb("junkc", [128, D], BF16, esM)
            fss = k.sb("fss", [128, NT], F32, esM)
            b_fs = Buf()
            dpool(out=xr[0][:], in_=x_d[0:128, :], w=[b_xr[0]])
            b_fsi = [Buf() for _ in range(NT)]

            def out1(i):
                q = i % 2
                tsl = slice(i * 128, (i + 1) * 128)
                if i + 1 < NT:
                    dpool(out=xr[1 - q][:], in_=x_d[(i + 1) * 128:(i + 2) * 128, :], w=[b_xr[1 - q]])
                for hf in range(2):
                    bk = 2 * q + hf
                    for kc in range(8):
                        pe(lambda: nc.tensor.matmul(PB[bk][:], lhsT=mT[:, kc, tsl], rhs=Wout[:, kc, hf * 512:(hf + 1) * 512],
                                                    start=(kc == 0), stop=(kc == 7)),
                           r=[b_mT[i // 4], b_W], w=[b_PB[bk]], inc=(kc == 7))
                    dve(lambda: nc.vector.tensor_tensor(out=xo[q][:, hf * 512:(hf + 1) * 512], in0=PB[bk][:],
                                                        in1=xr[q][:, hf * 512:(hf + 1) * 512], op=ALU.add),
                        r=[b_PB[bk], b_xr[q]], w=[b_xo[q]])
                act(lambda: nc.scalar.activation(out=junkc[:], in_=xo[q][:], func=AF.Square, accum_out=fss[:, i:i + 1]),
                    r=[b_xo[q]], w=[b_fs, b_fsi[i]])
                act(lambda: nc.scalar.activation(out=fss[:, i:i + 1], in_=fss[:, i:i + 1], func=AF.Sqrt, scale=1.0 / D,
                                                 bias=EPS), w=[b_fsi[i]])

            def out2(i):
                q = i % 2
                tsl = slice(i * 128, (i + 1) * 128)
                dve(lambda: nc.vector.reciprocal(out=fss[:, i:i + 1], in_=fss[:, i:i + 1]), w=[b_fsi[i]])
                dve(lambda: nc.vector.scalar_tensor_tensor(out=fo[q][:], in0=xo[q][:], scalar=fss[:, i:i + 1],
                                                           in1=fnw_b[:], op0=ALU.mult, op1=ALU.mult),
                    r=[b_xo[q], b_ct, b_fsi[i]], w=[b_fo[q]])
                dsync(out=out_d[tsl, :], in_=fo[q][:], r=[b_fo[q]])

            out1(0)
            for i in range(NT):
                if i + 1 < NT:
                    out1(i + 1)
                out2(i)
            k.barrier()

        k.barrier()
    return nc, k


_NC_CACHE = {}


def kernel(x, positions, norm_w, w_in, gate_bias, conv_w, conv_b, dt_bias, a_log, d_skip,
           ssm_norm_w, w_branch_a, w_branch_b, w_out, final_norm_w):
    f32 = np.float32
    x = np.asarray(x, dtype=f32)
    positions = np.asarray(positions).astype(np.int32)
    nb = x.shape[0]
    assert nb == 8 and x.shape[1] == S and x.shape[2] == D
    if "nc" not in _NC_CACHE:
        _NC_CACHE["nc"] = build()[0]
    nc = _NC_CACHE["nc"]
    invf = (500000.0 ** (-np.arange(0, 16, 2, dtype=f32) / 16)).astype(f32)
    shared = {
        "invf": np.ascontiguousarray(np.broadcast_to(invf, (128, 8))),
        "norm_w": np.ascontiguousarray(np.asarray(norm_w, f32).reshape(1, D)),
        "w_in": np.ascontiguousarray(np.asarray(w_in, f32)[0]),
        "cwT": np.ascontiguousarray(np.asarray(conv_w, f32)[0].reshape(4, 12, 128).transpose(2, 1, 0)),
        "cbT": np.ascontiguousarray(np.asarray(conv_b, f32)[0].reshape(12, 128).T),
        "dt_bias": np.ascontiguousarray(np.asarray(dt_bias, f32).reshape(1, 16)),
        "a_log": np.ascontiguousarray(np.asarray(a_log, f32).reshape(1, 16)),
        "d_skip": np.ascontiguousarray(np.asarray(d_skip, f32).reshape(1, 16)),
        "ssm_norm_w": np.ascontiguousarray(np.asarray(ssm_norm_w, f32).reshape(1, D)),
        "gbT": np.ascontiguousarray(np.asarray(gate_bias, f32)[0].reshape(16, 128).T),
        "w_branch_a": np.ascontiguousarray(np.asarray(w_branch_a, f32)[0]),
        "w_branch_b": np.ascontiguousarray(np.asarray(w_branch_b, f32)[0]),
        "w_out": np.ascontiguousarray(np.asarray(w_out, f32)[0]),
        "final_norm_w": np.ascontiguousarray(np.asarray(final_norm_w, f32).reshape(1, D)),
    }
    in_maps = []
    for b in range(nb):
        m = dict(shared)
        m["x"] = np.ascontiguousarray(x[b])
        m["posT"] = np.ascontiguousarray(positions[b].reshape(NT, 128).T)
        in_maps.append(m)
    res = run_bass_kernel_spmd(nc, in_maps, core_ids=list(range(nb)))
    out = np.stack([np.asarray(res.results[b]["out"], dtype=f32) for b in range(nb)], axis=0)
    return out
```

```python
import numpy as np
import math
import concourse.bass as bass
import concourse.mybir as mybir
from concourse.bass_utils import run_bass_kernel_spmd
from contextlib import ExitStack

F32 = mybir.dt.float32
BF16 = mybir.dt.bfloat16
I32 = mybir.dt.int32
ALU = mybir.AluOpType
AF = mybir.ActivationFunctionType
AX = mybir.AxisListType

S = 2048
D = 1024
NT = 16
INW = 6228
EPS = 1e-6
NITER = 10
A_SRC = [(0, 512, 0), (512, 640, 512), (1280, 1536, 640), (1536, 1600, 896), (1600, 1604, 960),
         (640, 768, 964), (768, 1280, 1092)]
NA = 1604


class Buf:
    __slots__ = ("w", "r", "name", "excl")

    def __init__(self, name="", excl=False):
        self.w = {}
        self.r = {}
        self.name = name
        self.excl = excl


def _merge(deps, d):
    for k, (s, v) in d.items():
        if k not in deps or deps[k][1] < v:
            deps[k] = (s, v)


class Eng:
    def __init__(self, K, name, eng, selfdep=True):
        self.K = K
        self.name = name
        self.eng = eng
        self.selfdep = selfdep
        self.sem = K.es.enter_context(K.nc.semaphore("s_" + name))
        self.cnt = 0
        self.waited = {}
        self.pending = False

    def wait_deps(self, deps):
        for k, (s, v) in deps.items():
            if k == self.name and not self.selfdep:
                continue
            if self.waited.get(k, 0) < v:
                self.eng.wait_ge(s, v)
                self.waited[k] = v

    def __call__(self, fn, r=(), w=(), inc=True, extra=(), selfwait=None):
        if selfwait is not None:
            assert selfwait[0] == self.name
            if self.waited.get("self", 0) < selfwait[2]:
                self.eng.wait_ge(selfwait[1], selfwait[2])
                self.waited["self"] = selfwait[2]
        w = list(w) + [b for b in r if b.excl]
        r = [b for b in r if not b.excl]
        deps = {}
        for b in r:
            _merge(deps, b.w)
        for b in w:
            _merge(deps, b.w)
            _merge(deps, b.r)
        for t in extra:
            _merge(deps, {t[0]: (t[1], t[2])})
        self.wait_deps(deps)
        ins = fn()
        if inc:
            self.cnt += 1
            ins.then_inc(self.sem, 1)
            tok = (self.sem, self.cnt)
            self.pending = False
        else:
            tok = (self.sem, self.cnt + 1)
            self.pending = True
        for b in r:
            _merge(b.r, {self.name: tok})
        for b in w:
            b.w = {self.name: tok}
            b.r = {}
        return (self.name,) + tok


class DmaQ:
    def __init__(self, K, name, waiter, nsem=8):
        self.K = K
        self.name = name
        self.waiter = waiter
        self.sems = [K.es.enter_context(K.nc.semaphore(f"d_{name}{j}")) for j in range(nsem)]
        self.vals = [0] * nsem
        self.idx = 0

    def __call__(self, out, in_, r=(), w=(), extra=(), **kw):
        deps = {}
        for b in r:
            _merge(deps, b.w)
        for b in w:
            _merge(deps, b.w)
            _merge(deps, b.r)
        for t in extra:
            _merge(deps, {t[0]: (t[1], t[2])})
        k = self.idx
        self.idx = (k + 1) % len(self.sems)
        key = f"{self.name}{k}"
        if self.vals[k] > 0:
            _merge(deps, {key: (self.sems[k], self.vals[k])})
        self.waiter.wait_deps(deps)
        ins = self.waiter.eng.dma_start(out=out, in_=in_, **kw)
        self.vals[k] += 16
        ins.then_inc(self.sems[k], 16)
        tok = (self.sems[k], self.vals[k])
        for b in r:
            _merge(b.r, {key: tok})
        for b in w:
            b.w = {key: tok}
            b.r = {}
        return (key,) + tok


class K:
    def __init__(self, nc, es):
        self.nc = nc
        self.es = es
        self.pe = Eng(self, "pe", nc.tensor, selfdep=False)
        self.act = Eng(self, "act", nc.scalar)
        self.dve = Eng(self, "dve", nc.vector)
        self.pool = Eng(self, "pool", nc.gpsimd)
        self.sp = Eng(self, "sp", nc.sync)
        self.engs = [self.pe, self.act, self.dve, self.pool, self.sp]
        self.dsync = DmaQ(self, "qs", self.sp, nsem=8)
        self.dpool = DmaQ(self, "qp", self.pool, nsem=8)
        self.dqs = [self.dsync, self.dpool]
        self.dumps = []

    def sb(self, name, shape, dt, es=None):
        t = (es or self.es).enter_context(self.nc.sbuf_tensor(name, list(shape), dt))
        return t

    def ps(self, name, shape, dt, es=None):
        return (es or self.es).enter_context(self.nc.psum_tensor(name, list(shape), dt))

    def barrier(self):
        deps = {}
        for e in self.engs:
            assert not e.pending, e.name
            if e.cnt > 0:
                deps[e.name] = (e.sem, e.cnt)
        for q in self.dqs:
            for j, s in enumerate(q.sems):
                if q.vals[j] > 0:
                    deps[f"{q.name}{j}"] = (s, q.vals[j])
        for e in self.engs:
            e.wait_deps(deps)

    def dump(self, name, ap, buf):
        d = self.nc.dram_tensor(name, list(ap.shape), ap.dtype, kind="ExternalOutput").ap()
        self.dsync(out=d, in_=ap, r=(buf if isinstance(buf, (list, tuple)) else [buf]))
        self.dumps.append(name)


class Stop(Exception):
    pass


def build(stop_after="all", dbg=()):
    try:
        return _build(stop_after, dbg)
    except Stop as s:
        return s.args


def _build(stop_after="all", dbg=()):
    nc = bass.Bass("TRN2", target_bir_lowering=False)
    dbg = set(dbg)
    x_d = nc.dram_tensor("x", [S, D], F32, kind="ExternalInput").ap()
    posT_d = nc.dram_tensor("posT", [128, NT], I32, kind="ExternalInput").ap()
    invf_d = nc.dram_tensor("invf", [128, 8], F32, kind="ExternalInput").ap()
    normw_d = nc.dram_tensor("norm_w", [1, D], F32, kind="ExternalInput").ap()
    win_d = nc.dram_tensor("w_in", [D, INW], F32, kind="ExternalInput").ap()
    out_d = nc.dram_tensor("out", [S, D], F32, kind="ExternalOutput").ap()
    cwT_d = nc.dram_tensor("cwT", [128, 12, 4], F32, kind="ExternalInput").ap()
    cbT_d = nc.dram_tensor("cbT", [128, 12], F32, kind="ExternalInput").ap()
    dtb_d = nc.dram_tensor("dt_bias", [1, 16], F32, kind="ExternalInput").ap()
    alog_d = nc.dram_tensor("a_log", [1, 16], F32, kind="ExternalInput").ap()
    dsk_d = nc.dram_tensor("d_skip", [1, 16], F32, kind="ExternalInput").ap()
    snw_d = nc.dram_tensor("ssm_norm_w", [1, D], F32, kind="ExternalInput").ap()
    gbT_d = nc.dram_tensor("gbT", [128, 16], F32, kind="ExternalInput").ap()
    wpa_d = nc.dram_tensor("w_branch_a", [512, D], F32, kind="ExternalInput").ap()
    wpb_d = nc.dram_tensor("w_branch_b", [D, D], F32, kind="ExternalInput").ap()
    wout_d = nc.dram_tensor("w_out", [D, D], F32, kind="ExternalInput").ap()
    fnw_d = nc.dram_tensor("final_norm_w", [1, D], F32, kind="ExternalInput").ap()

    with ExitStack() as es:
        k = K(nc, es)
        pe, act, dve, pool, sp = k.pe, k.act, k.dve, k.pool, k.sp
        dsync, dpool = k.dsync, k.dpool

        def ck(name):
            if stop_after == name:
                k.barrier()
                raise Stop(nc, k)

        ident = k.sb("ident", [128, 128], BF16)
        Uf = k.sb("Uf", [128, 128], F32)
        Ub = k.sb("Ub", [128, 128], BF16)
        cbias = k.sb("cbias", [128, 128], F32)
        pow2 = k.sb("pow2", [128, NITER + 2], F32)
        negU = k.sb("negU", [128, 128], BF16)
        negUb = k.sb("negUb", [128, 128], BF16)
        mhalf = k.sb("mhalf", [128, 16], F32)
        b_const = Buf("const")
        pool(lambda: nc.gpsimd.memset(ident[:], 1.0), w=[b_const])
        pool(lambda: nc.gpsimd.affine_select(out=ident[:], in_=ident[:], pattern=[[-1, 128]],
                                             compare_op=ALU.is_equal, fill=0.0, base=0, channel_multiplier=1),
             w=[b_const])
        pool(lambda: nc.gpsimd.memset(Uf[:], 1.0), w=[b_const])
        pool(lambda: nc.gpsimd.affine_select(out=Uf[:], in_=Uf[:], pattern=[[1, 128]], compare_op=ALU.is_ge,
                                             fill=0.0, base=0, channel_multiplier=-1), w=[b_const])
        pool(lambda: nc.gpsimd.tensor_copy(out=Ub[:], in_=Uf[:]), w=[b_const])
        pool(lambda: nc.gpsimd.tensor_scalar(out=negUb[:], in0=Ub[:], scalar1=-1.0, scalar2=None, op0=ALU.mult),
             w=[b_const])
        pool(lambda: nc.gpsimd.memset(mhalf[:], -0.5), w=[b_const])
        pool(lambda: nc.gpsimd.memset(negU[:], 0.0), w=[b_const])
        pool(lambda: nc.gpsimd.affine_select(out=negU[:], in_=negU[:], pattern=[[1, 128]], compare_op=ALU.is_ge,
                                             fill=-30000.0, base=0, channel_multiplier=-1), w=[b_const])
        pool(lambda: nc.gpsimd.memset(cbias[:], 0.0), w=[b_const])
        pool(lambda: nc.gpsimd.affine_select(out=cbias[:], in_=cbias[:], pattern=[[-1, 128]], compare_op=ALU.is_ge,
                                             fill=-1e30, base=0, channel_multiplier=1), w=[b_const])
        for j in range(NITER + 2):
            pool(lambda: nc.gpsimd.memset(pow2[:, j:j + 1], 2.0 ** (-j)), w=[b_const])
        pool(lambda: nc.gpsimd.memset(pow2[:, 0:1], 1.0), w=[b_const])

        hT = k.sb("hT", [128, 8, S], BF16)
        b_hT = [Buf(f"hT{i}") for i in range(NT)]
        wBC = k.sb("wBC", [128, 8, 1024], BF16)
        b_wslot = [Buf(), Buf()]
        wG0 = wBC[:, :, 0:256].rearrange("p k (h n) -> p k h n", h=2)
        Wpb0 = wBC[:, :, 256:384]
        Wpa0 = wBC[:, 0:4, 384:512]
        CONV0 = 2628
        oaT = k.sb("oaT", [128, 4, S], BF16)
        b_oaT = [Buf() for _ in range(NT)]
        esWA = ExitStack()
        wA = k.sb("wA", [128, 8, NA], BF16, esWA)
        b_wA = Buf()
        for (c0, c1, dst) in A_SRC:
            dpool(out=wA[:, :, dst:dst + (c1 - c0)],
                  in_=win_d[:, c0:c1].rearrange("(k p) n -> p k n", p=128), w=[b_wA])

        with ExitStack() as es0:
            normw_b = k.sb("normw_b", [128, D], F32, es0)
            b_normw = Buf()
            dsync(out=normw_b[:], in_=normw_d[0, :].partition_broadcast(128), w=[b_normw])
            xt = k.sb("xt_all", [128, NT, D], F32, es0)
            b_xt = [Buf() for _ in range(NT)]
            xn = [k.sb(f"xn{j}", [128, D], BF16, es0) for j in range(3)]
            b_xn = [Buf(), Buf(), Buf()]
            junk = k.sb("junk0", [128, D], BF16, es0)
            b_junk = Buf()
            ss = k.sb("ss", [128, NT], F32, es0)
            sd = k.sb("sd", [128, NT], F32, es0)
            rstd = k.sb("rstd", [128, NT], F32, es0)
            b_ssg = [Buf() for _ in range(4)]
            pt = [k.ps(f"pt{j}", [128, 8, 128], BF16, es0) for j in range(2)]
            b_pt = [Buf(excl=True), Buf(excl=True)]
            for i in range(NT):
                (dsync if i % 2 == 0 else dsync)(out=xt[:, i, :], in_=x_d[i * 128:(i + 1) * 128, :], w=[b_xt[i]])

            def p0_stats(gq):
                for i in range(4 * gq, 4 * gq + 4):
                    act(lambda: nc.scalar.activation(out=junk[:], in_=xt[:, i, :], func=AF.Square,
                                                     accum_out=ss[:, i:i + 1]),
                        r=[b_xt[i]], w=[b_junk, b_ssg[gq]])
                act(lambda: nc.scalar.activation(out=sd[:, 4 * gq:4 * gq + 4], in_=ss[:, 4 * gq:4 * gq + 4],
                                                 func=AF.Sqrt, scale=1.0 / D, bias=EPS), w=[b_ssg[gq]])
                dve(lambda: nc.vector.reciprocal(out=rstd[:, 4 * gq:4 * gq + 4], in_=sd[:, 4 * gq:4 * gq + 4]),
                    w=[b_ssg[gq]])

            def p0_apply(gq):
                for i in range(4 * gq, 4 * gq + 4):
                    j = i % 3
                    jp = i % 2
                    dve(lambda: nc.vector.scalar_tensor_tensor(out=xn[j][:], in0=xt[:, i, :], scalar=rstd[:, i:i + 1],
                                                               in1=normw_b[:], op0=ALU.mult, op1=ALU.mult),
                        r=[b_xt[i], b_ssg[gq], b_normw], w=[b_xn[j]])
                    for c in range(8):
                        pe(lambda: nc.tensor.transpose(out=pt[jp][:, c, :], in_=xn[j][:, c * 128:(c + 1) * 128],
                                                       identity=ident[:]),
                           r=[b_xn[j], b_const], w=[b_pt[jp]], inc=(c == 7))
                    act(lambda: nc.scalar.copy(out=hT[:, :, i * 128:(i + 1) * 128], in_=pt[jp][:]),
                        r=[b_pt[jp]], w=[b_hT[i]])

            p0_stats(0)
            for gq in range(4):
                if gq + 1 < 4:
                    p0_stats(gq + 1)
                p0_apply(gq)
            k.barrier()
        if "hT" in dbg:
            k.dump("d_hT", hT[:], b_hT[NT - 1])
        if stop_after == "p0":
            k.barrier()
            return nc, k


        PB = [k.ps(f"pb{j}", [128, 512], F32) for j in range(8)]
        b_PB = [Buf(f"pb{j}", excl=True) for j in range(8)]

        def bfv(j):
            return PB[j][:].bitcast(BF16).rearrange("p (s t) -> p s t", t=128)

        with ExitStack() as esA:
            dpool(out=wBC[:, :, 0:512], in_=win_d[:, CONV0:CONV0 + 512].rearrange("(k p) n -> p k n", p=128),
                  w=[b_wslot[0]])
            dpool(out=wBC[:, :, 512:1024], in_=win_d[:, CONV0 + 512:CONV0 + 1024].rearrange("(k p) n -> p k n", p=128),
                  w=[b_wslot[1]])
            posi = k.sb("posi", [128, NT], I32, esA)
            posf = k.sb("posf", [128, NT], F32, esA)
            invf = k.sb("invf_sb", [128, 8], F32, esA)
            ang = k.sb("ang", [128, NT, 8], F32, esA)
            cos_t = k.sb("cos_t", [128, NT, 8], F32, esA)
            sin_t = k.sb("sin_t", [128, NT, 8], F32, esA)
            ry = k.sb("ry", [128, NT, 8], F32, esA)
            rki = k.sb("rki", [128, NT, 8], I32, esA)
            rkf = k.sb("rkf", [128, NT, 8], F32, esA)
            rg = k.sb("rg", [128, NT, 8], F32, esA)
            b_tab = Buf()
            dsync(out=posi[:], in_=posT_d[:, :], w=[b_tab])
            dsync(out=invf[:], in_=invf_d[:, :], w=[b_tab])
            dve(lambda: nc.vector.tensor_copy(out=posf[:], in_=posi[:]), r=[b_tab], w=[b_tab])
            dve(lambda: nc.vector.tensor_tensor(out=ang[:], in0=posf[:].unsqueeze(2).to_broadcast([128, NT, 8]),
                                                in1=invf[:].unsqueeze(1).to_broadcast([128, NT, 8]), op=ALU.mult),
                r=[b_tab], w=[b_tab])
            TWO_PI = 2.0 * math.pi
            for (dst_t, off) in ((sin_t, 0.0), (cos_t, 0.25)):
                dve(lambda: nc.vector.tensor_scalar(out=ry[:], in0=ang[:], scalar1=1.0 / TWO_PI, scalar2=off,
                                                    op0=ALU.mult, op1=ALU.add), r=[b_tab], w=[b_tab])
                dve(lambda: nc.vector.tensor_copy(out=rki[:], in_=ry[:]), r=[b_tab], w=[b_tab])
                dve(lambda: nc.vector.tensor_copy(out=rkf[:], in_=rki[:]), r=[b_tab], w=[b_tab])
                dve(lambda: nc.vector.tensor_tensor(out=ry[:], in0=ry[:], in1=rkf[:], op=ALU.subtract),
                    r=[b_tab], w=[b_tab])
                dve(lambda: nc.vector.tensor_scalar(out=rg[:], in0=ry[:], scalar1=0.5, scalar2=None, op0=ALU.is_ge),
                    r=[b_tab], w=[b_tab])
                dve(lambda: nc.vector.tensor_tensor(out=ry[:], in0=ry[:], in1=rg[:], op=ALU.subtract),
                    r=[b_tab], w=[b_tab])
                dve(lambda: nc.vector.tensor_scalar(out=rg[:], in0=ry[:], scalar1=-0.5, scalar2=None, op0=ALU.is_lt),
                    r=[b_tab], w=[b_tab])
                dve(lambda: nc.vector.tensor_tensor(out=ry[:], in0=ry[:], in1=rg[:], op=ALU.add),
                    r=[b_tab], w=[b_tab])
                act(lambda: nc.scalar.activation(out=dst_t[:], in_=ry[:], func=AF.Sin, scale=TWO_PI * (1.0 - 1e-6)),
                    r=[b_tab], w=[b_tab])
            ck("Atab")
            if "rope" in dbg:
                k.dump("d_cos", cos_t[:], b_tab)
                k.dump("d_sin", sin_t[:], b_tab)

            qk_sb = [k.sb(f"qk_sb{j}", [128, 15, 64], BF16, esA) for j in range(2)]
            b_qk = [Buf(), Buf()]
            rt = [k.sb(f"rt{j}", [128, 15, 8], F32, esA) for j in range(4)]
            b_rt = Buf()
            rsrc = k.sb("rsrc", [128, 15, 16], F32, esA)
            b_rsrc = Buf()
            QT = [k.sb(f"QT{j}", [128, 12, 128], BF16, esA) for j in range(2)]
            b_QT = [Buf(), Buf()]
            KT = k.sb("KT", [128, 2, S], BF16, esA)
            KIT = k.sb("KIT", [128, S], BF16, esA)
            b_KT = [Buf() for _ in range(NT)]
            for j_ in range(2):
                pool(lambda: nc.gpsimd.memset(QT[j_][64:128], 0.0), w=[b_QT[j_]])
            pool(lambda: nc.gpsimd.memset(KT[64:128], 0.0), w=b_KT)
            pool(lambda: nc.gpsimd.memset(KIT[64:128], 0.0), w=b_KT)
            Vaug = k.sb("Vaug", [128, NT, 2, 65], BF16, esA)
            b_V = [Buf() for _ in range(NT)]
            wv = k.sb("wv", [128, NT, 4], F32, esA)
            b_wv = [Buf() for _ in range(NT)]
            sza = [k.sb(f"sza{j}", [128, 512], F32, esA) for j in range(2)]
            b_sza = [Buf(), Buf()]
            sc = [k.sb(f"sc{j}", [128, S], F32, esA) for j in range(2)]
            b_sc = [Buf(), Buf()]
            rl = [k.sb(f"rl{j}", [128, S], F32, esA) for j in range(2)]
            b_rl = [Buf(), Buf()]
            junkb = k.sb("junkb", [128, S], BF16, esA)
            b_junkb = Buf()
            m01 = k.sb("m01", [128, S], BF16, esA)
            b_m01 = Buf()
            maskT = [k.sb(f"maskT{j}", [128, NT, 128], BF16, esA) for j in range(2)]
            b_maskT = [Buf(), Buf()]
            Bv = k.sb("Bv", [128, 1], F32, esA)
            Bk = k.sb("Bk", [128, NITER + 2], F32, esA)
            mid = [k.sb(f"mid{j}", [128, 1], F32, esA) for j in range(2)]
            cnt = k.sb("cnt", [128, 1], F32, esA)
            dd = k.sb("dd", [128, 1], F32, esA)
            thr = k.sb("thr", [128, NT], F32, esA)
            b_bis = Buf()
            Eb = [k.sb(f"Eb{j}", [128, 512], BF16, esA) for j in range(3)]
            b_Eb = [Buf(), Buf(), Buf()]
            rinv = k.sb("rinv", [128, 4], F32, esA)
            otmp = k.sb("otmp", [128, 4, 64], F32, esA)
            rinv8 = k.sb("rinv8", [128, 8], F32, esA)
            otmp8 = k.sb("otmp8", [128, 8, 64], F32, esA)
            b_otmp = Buf()
            oa_sb = k.sb("oa_sb", [128, 512], BF16, esA)
            b_oa = Buf()
            pool(lambda: nc.gpsimd.memset(Vaug[:], 1.0), w=b_V)

            A_BANK = [(0, 0, 512), (1, 512, 452), (2, 964, 512), (3, 1476, 128)]
            T0v = bfv(4)
            T1v = bfv(5)
            ctr = {"st": 0, "ix": 0, "e": 0}

            def hdr_(i):
                return i % 2, slice(i * 128, (i + 1) * 128), (i + 1) * 128, i >= 2

            def stage1(i):
                j, tsl, L, masked = hdr_(i)

                for (bk, c0, n) in A_BANK:
                    for kc in range(8):
                        pe(lambda: nc.tensor.matmul(PB[bk][:, 0:n], lhsT=hT[:, kc, tsl], rhs=wA[:, kc, c0:c0 + n],
                                                    start=(kc == 0), stop=(kc == 7)),
                           r=[b_hT[i], b_wA], w=[b_PB[bk]], inc=(kc == 7))
                ck(f"Aproj{i}")
                p0v = PB[0][:, 0:512].rearrange("p (h d) -> p h d", d=64)
                p1v = PB[1][:, 0:448].rearrange("p (h d) -> p h d", d=64)
                act(lambda: nc.scalar.copy(out=qk_sb[j][:, 0:8, 16:64], in_=p0v[:, :, 16:64]),
                    r=[b_PB[0]], w=[b_qk[j]])
                act(lambda: nc.scalar.copy(out=qk_sb[j][:, 8:15, 16:64], in_=p1v[:, :, 16:64]),
                    r=[b_PB[1]], w=[b_qk[j]])
                ck(f"Ae1_{i}")
                act(lambda: nc.scalar.copy(out=rsrc[:, 0:8, :], in_=p0v[:, :, 0:16]), r=[b_PB[0]], w=[b_rsrc])
                act(lambda: nc.scalar.copy(out=rsrc[:, 8:15, :], in_=p1v[:, :, 0:16]), r=[b_PB[1]], w=[b_rsrc])
                nh = 15
                cb = cos_t[:, i:i + 1, :].to_broadcast([128, nh, 8])
                sb_ = sin_t[:, i:i + 1, :].to_broadcast([128, nh, 8])
                x1 = rsrc[:, :, 0:8]
                x2 = rsrc[:, :, 8:16]
                dve(lambda: nc.vector.tensor_tensor(out=rt[0][:], in0=x1, in1=cb, op=ALU.mult),
                    r=[b_rsrc, b_tab], w=[b_rt])
                dve(lambda: nc.vector.tensor_tensor(out=rt[1][:], in0=x2, in1=sb_, op=ALU.mult),
                    r=[b_rsrc, b_tab], w=[b_rt])
                dve(lambda: nc.vector.tensor_tensor(out=qk_sb[j][:, :, 0:8], in0=rt[0][:], in1=rt[1][:], op=ALU.subtract),
                    r=[b_rt], w=[b_qk[j]])
                dve(lambda: nc.vector.tensor_tensor(out=rt[2][:], in0=x2, in1=cb, op=ALU.mult),
                    r=[b_rsrc, b_tab], w=[b_rt])
                dve(lambda: nc.vector.tensor_tensor(out=rt[3][:], in0=x1, in1=sb_, op=ALU.mult),
                    r=[b_rsrc, b_tab], w=[b_rt])
                dve(lambda: nc.vector.tensor_tensor(out=qk_sb[j][:, :, 8:16], in0=rt[2][:], in1=rt[3][:], op=ALU.add),
                    r=[b_rt], w=[b_qk[j]])
                ck(f"Ae2_{i}")
                act(lambda: nc.scalar.copy(out=wv[:, i, :], in_=PB[1][:, 448:452]), r=[b_PB[1]], w=[b_wv[i]])
                act(lambda: nc.scalar.copy(out=Vaug[:, i, :, 0:64],
                                           in_=PB[2][:, 0:128].rearrange("p (g d) -> p g d", d=64)),
                    r=[b_PB[2]], w=[b_V[i]])
                ck(f"Ae3_{i}")
                act(lambda: nc.scalar.activation(out=sza[j][:, 0:384], in_=PB[2][:, 128:512], func=AF.Silu),
                    r=[b_PB[2]], w=[b_sza[j]])
                act(lambda: nc.scalar.activation(out=sza[j][:, 384:512], in_=PB[3][:, 0:128], func=AF.Silu),
                    r=[b_PB[3]], w=[b_sza[j]])
                ck(f"Aevac{i}")
                for h in range(15):
                    tv, sl, bk = (T0v, h, 4) if h < 8 else (T1v, h - 8, 5)
                    pe(lambda: nc.tensor.transpose(out=tv[0:64, sl, :], in_=qk_sb[j][:, h, :], identity=ident[:]),
                       r=[b_qk[j], b_const], w=[b_PB[bk]], inc=(h == 7 or h == 14))
                act(lambda: nc.scalar.copy(out=QT[j][0:64, 0:8, :], in_=T0v[0:64, :, :]), r=[b_PB[4]], w=[b_QT[j]])
                act(lambda: nc.scalar.copy(out=KT[0:64, :, tsl], in_=T1v[0:64, 0:2, :]), r=[b_PB[5]], w=[b_KT[i]])
                act(lambda: nc.scalar.copy(out=QT[j][0:64, 8:12, :], in_=T1v[0:64, 2:6, :]), r=[b_PB[5]], w=[b_QT[j]])
                act(lambda: nc.scalar.copy(out=KIT[0:64, tsl], in_=T1v[0:64, 6, :]), r=[b_PB[5]], w=[b_KT[i]])


            def stage2(i):
                j, tsl, L, masked = hdr_(i)
                if not masked:
                    return

                nch = (L + 511) // 512
                for h in range(4):
                    q = h % 2
                    for c in range(nch):
                        c0 = c * 512
                        n = min(512, L - c0)
                        bk = ctr["ix"] % 4
                        ctr["ix"] += 1
                        pe(lambda: nc.tensor.matmul(PB[bk][:, 0:n], lhsT=QT[j][:, 8 + h, :], rhs=KIT[:, c0:c0 + n],
                                                    start=True, stop=True),
                           r=[b_QT[j]] + b_KT[0:i + 1], w=[b_PB[bk]])
                        act(lambda: nc.scalar.activation(out=rl[q][:, c0:c0 + n], in_=PB[bk][:, 0:n], func=AF.Relu),
                            r=[b_PB[bk]], w=[b_rl[q]])
                    if h == 0:
                        dve(lambda: nc.vector.tensor_scalar(out=sc[j][:, 0:L], in0=rl[q][:, 0:L],
                                                            scalar1=wv[:, i, 0:1], scalar2=None, op0=ALU.mult),
                            r=[b_rl[q], b_wv[i]], w=[b_sc[j]])
                    else:
                        dve(lambda: nc.vector.scalar_tensor_tensor(out=sc[j][:, 0:L], in0=rl[q][:, 0:L],
                                                                   scalar=wv[:, i, h:h + 1], in1=sc[j][:, 0:L],
                                                                   op0=ALU.mult, op1=ALU.add),
                            r=[b_rl[q], b_wv[i]], w=[b_sc[j]])


            def bisect(i):
                j, tsl, L, masked = hdr_(i)
                if not masked:
                    return
                yield

                dve(lambda: nc.vector.tensor_reduce(out=Bv[:], in_=sc[j][:, 0:L], axis=AX.X, op=ALU.max,
                                                    apply_absolute_value=True),
                    r=[b_sc[j]], w=[b_bis])
                dve(lambda: nc.vector.tensor_tensor(out=sc[j][:, L - 128:L], in0=sc[j][:, L - 128:L],
                                                    in1=cbias[:], op=ALU.add),
                    r=[b_const], w=[b_sc[j]])
                dve(lambda: nc.vector.tensor_scalar(out=Bk[:], in0=pow2[:], scalar1=Bv[:, 0:1], scalar2=None,
                                                    op0=ALU.mult), r=[b_const], w=[b_bis])
                dve(lambda: nc.vector.memset(mid[0][:], 0.0), w=[b_bis])
                for it in range(NITER):
                    ma, mb = mid[it % 2], mid[(it + 1) % 2]
                    dve(lambda: nc.vector.tensor_scalar(out=junkb[:, 0:L], in0=sc[j][:, 0:L], scalar1=ma[:, 0:1],
                                                        scalar2=None, op0=ALU.is_ge, op1=ALU.add,
                                                        accum_out=cnt[:, 0:1]),
                        r=[b_sc[j]], w=[b_bis, b_junkb])
                    dve(lambda: nc.vector.tensor_scalar(out=dd[:], in0=cnt[:], scalar1=255.5,
                                                        scalar2=Bk[:, it:it + 1], op0=ALU.is_ge, op1=ALU.mult),
                        w=[b_bis])
                    dve(lambda: nc.vector.tensor_scalar(out=mb[:], in0=dd[:], scalar1=Bk[:, it + 1:it + 2],
                                                        scalar2=ma[:, 0:1], op0=ALU.subtract, op1=ALU.add),
                        w=[b_bis])
                    yield
                mfin = mid[NITER % 2]
                dve(lambda: nc.vector.tensor_tensor(out=thr[:, i:i + 1], in0=mfin[:], in1=Bk[:, NITER:NITER + 1],
                                                    op=ALU.subtract), w=[b_bis])
                dve(lambda: nc.vector.tensor_scalar(out=m01[:, 0:L], in0=sc[j][:, 0:L], scalar1=thr[:, i:i + 1],
                                                    scalar2=None, op0=ALU.is_ge),
                    r=[b_sc[j], b_bis], w=[b_m01])
                if ("sc%d" % i) in dbg:
                    k.dump("d_sc", sc[j][:, 0:L], b_sc[j])
                    k.dump("d_thr", thr[:, i:i + 1], b_bis)


            def masktr(i):
                j, tsl, L, masked = hdr_(i)
                if not masked:
                    return

                for jb in range(i + 1):
                    tv, sl, bk = (T0v, jb, 4) if jb < 8 else (T1v, jb - 8, 5)
                    last = (jb == i) or (jb == 7)
                    pe(lambda: nc.tensor.transpose(out=tv[:, sl, :], in_=m01[:, jb * 128:(jb + 1) * 128],
                                                   identity=ident[:]),
                       r=[b_m01, b_const], w=[b_PB[bk]], inc=last)
                n0 = min(i + 1, 8)
                act(lambda: nc.scalar.activation(out=maskT[j][:, 0:n0, :], in_=T0v[:, 0:n0, :], func=AF.Identity,
                                                 scale=30000.0, bias=-30000.0),
                    r=[b_PB[4]], w=[b_maskT[j]])
                if i + 1 > 8:
                    act(lambda: nc.scalar.activation(out=maskT[j][:, 8:i + 1, :], in_=T1v[:, 0:i + 1 - 8, :],
                                                     func=AF.Identity, scale=30000.0, bias=-30000.0),
                        r=[b_PB[5]], w=[b_maskT[j]])


            def attn_main(i):
                j, tsl, L, masked = hdr_(i)
                nkt = i + 1
                for g in range(2):
                    Ov = PB[6 + g][:, 0:260].rearrange("p (h d) -> p h d", d=65)

                    def st_mm(jb):
                        bk = (2, 3, 0, 1)[ctr["st"] % 4]
                        ctr["st"] += 1
                        has_mask = masked or jb == i
                        pe(lambda: nc.tensor.matmul(PB[bk][:].rearrange("p (h t) -> p h t", t=128),
                                                    lhsT=KT[:, g, jb * 128:(jb + 1) * 128],
                                                    rhs=QT[j][:, 4 * g:4 * g + 4, :], start=True, stop=(not has_mask)),
                           r=[b_QT[j], b_KT[jb]], w=[b_PB[bk]], inc=(not has_mask))
                        if has_mask:
                            if masked:
                                mb_ap = maskT[j][:, jb:jb + 1, :].to_broadcast([128, 4, 128])
                                rd = [b_maskT[j], b_const]
                            else:
                                mb_ap = negU[:].unsqueeze(1).to_broadcast([128, 4, 128])
                                rd = [b_const]
                            pe(lambda: nc.tensor.matmul(PB[bk][:].rearrange("p (h t) -> p h t", t=128),
                                                        lhsT=ident[:], rhs=mb_ap, start=False, stop=True),
                               r=rd, w=[b_PB[bk]])
                        return bk

                    def exp_pv(jb, bk):
                        e = ctr["e"] % 3
                        ctr["e"] += 1
                        act(lambda: nc.scalar.activation(out=Eb[e][:], in_=PB[bk][:], func=AF.Exp, scale=0.125),
                            r=[b_PB[bk]], w=[b_Eb[e]])
                        for hh in range(4):
                            pe(lambda: nc.tensor.matmul(Ov[:, hh, :], lhsT=Eb[e][:, hh * 128:(hh + 1) * 128],
                                                        rhs=Vaug[:, jb, g, :], start=(jb == 0 and hh == 0),
                                                        stop=(jb == i and hh == 3)),
                               r=[b_Eb[e], b_V[jb]], w=[b_PB[6 + g]], inc=(hh == 3))

                    LA = 2
                    bks = {}
                    for jb0 in range(min(LA, nkt)):
                        bks[jb0] = st_mm(jb0)
                    for jb in range(nkt):
                        if jb + LA < nkt:
                            bks[jb + LA] = st_mm(jb + LA)
                        exp_pv(jb, bks[jb])

            def attn_fin(i):
                j, tsl, L, masked = hdr_(i)
                for g in range(2):
                    Ov = PB[6 + g][:, 0:260].rearrange("p (h d) -> p h d", d=65)
                    dve(lambda: nc.vector.reciprocal(out=rinv8[:, 4 * g:4 * g + 4].unsqueeze(2), in_=Ov[:, :, 64:65]),
                        r=[b_PB[6 + g]], w=[b_otmp])
                    dve(lambda: nc.vector.tensor_tensor(out=otmp8[:, 4 * g:4 * g + 4, :], in0=Ov[:, :, 0:64],
                                                        in1=rinv8[:, 4 * g:4 * g + 4].unsqueeze(2)
                                                        .to_broadcast([128, 4, 64]), op=ALU.mult),
                        r=[b_PB[6 + g]], w=[b_otmp])
                dve(lambda: nc.vector.tensor_tensor(out=oa_sb[:], in0=otmp8[:].rearrange("p h d -> p (h d)"),
                                                    in1=sza[j][:], op=ALU.mult),
                    r=[b_otmp, b_sza[j]], w=[b_oa])


            def fin_tr(i):
                j, tsl, L, masked = hdr_(i)
                for c in range(4):
                    pe(lambda: nc.tensor.transpose(out=T0v[:, c, :], in_=oa_sb[:, c * 128:(c + 1) * 128],
                                                   identity=ident[:]),
                       r=[b_oa, b_const], w=[b_PB[4]], inc=(c == 3))
                act(lambda: nc.scalar.copy(out=oaT[:, :, tsl], in_=T0v[:, 0:4, :]), r=[b_PB[4]], w=[b_oaT[i]])


            nA = NT if "skipA" not in dbg else 0
            pend = None
            for i in range(nA + 2):
                if i < nA:
                    stage1(i)
                if pend is not None:
                    for _ in pend:
                        pass
                    pend = None
                if i < nA:
                    stage2(i)
                if 1 <= i <= nA:
                    masktr(i - 1)
                if 2 <= i <= nA + 1:
                    fin_tr(i - 2)
                if 1 <= i <= nA:
                    attn_main(i - 1)
                if i < nA:
                    g_ = bisect(i)
                    nsteps = (NITER - 2) if (i + 1 < nA) else 10 ** 9
                    done_ = False
                    for _s in range(nsteps):
                        try:
                            next(g_)
                        except StopIteration:
                            done_ = True
                            break
                    if not done_:
                        pend = g_
                if 1 <= i <= nA:
                    attn_fin(i - 1)
            i = nA - 1
            j = i % 2
            L = (i + 1) * 128

            if "qk" in dbg:
                k.dump("d_KT", KT[:, :, 0:L], b_KT)
                k.dump("d_KIT", KIT[:, 0:L], b_KT)
                k.dump("d_QT", QT[j][:], b_QT[j])
                k.dump("d_V", Vaug[:, 0:i + 1], b_V)
            if "oaT" in dbg:
                k.dump("d_oaT", oaT[:, :, 0:L], b_oaT)
            k.barrier()
        esWA.close()
        if stop_after.startswith("A"):
            return nc, k


        obT = k.sb("obT", [128, 8, S], BF16)
        b_obT = [Buf() for _ in range(NT)]
        with ExitStack() as esB:
            wdt = k.sb("wdt", [128, 8, 16], BF16, esB)
            b_wdt = Buf()
            dpool(out=wdt[:], in_=win_d[:, 4164:4180].rearrange("(k p) n -> p k n", p=128), w=[b_wdt])
            X_tm = k.sb("X_tm", [128, NT, 1024], BF16, esB)
            B_tm = k.sb("B_tm", [128, NT, 256], BF16, esB)
            BT = k.sb("BT", [128, 2, S], BF16, esB)
            CT = k.sb("CT", [128, 2, S], BF16, esB)
            b_X = Buf()
            dtb_b = k.sb("dtb_b", [128, 16], F32, esB)
            a_b = k.sb("a_b", [128, 16], F32, esB)
            dsk_b = k.sb("dsk_b", [128, 16], F32, esB)
            snw_b = k.sb("snw_b", [128, D], F32, esB)
            dt_all = k.sb("dt_all", [128, NT, 16], F32, esB)
            dA_all = k.sb("dA_all", [128, NT, 16], F32, esB)
            spt = [k.sb(f"spt{j}", [128, NT, 16], F32, esB) for j in range(3)]
            ones_f = k.sb("ones_f", [128, 128], F32, esB)
            NEGU4 = k.sb("NEGU4", [128, 4, 128], BF16, esB)
            Dg = k.sb("Dg", [128, 16, 128], BF16, esB)
            b_ptab = Buf()
            dsync(out=dtb_b[:], in_=dtb_d[0, :].partition_broadcast(128), w=[b_ptab])
            dsync(out=a_b[:], in_=alog_d[0, :].partition_broadcast(128), w=[b_ptab])
            dsync(out=dsk_b[:], in_=dsk_d[0, :].partition_broadcast(128), w=[b_ptab])
            dsync(out=snw_b[:], in_=snw_d[0, :].partition_broadcast(128), w=[b_ptab])
            pool(lambda: nc.gpsimd.memset(ones_f[:], 1.0), w=[b_ptab])
            pool(lambda: nc.gpsimd.memset(NEGU4[:], 0.0), w=[b_ptab])
            pool(lambda: nc.gpsimd.affine_select(out=NEGU4[:], in_=NEGU4[:], pattern=[[0, 4], [1, 128]],
                                                 compare_op=ALU.is_ge, fill=-1.0e4, base=0, channel_multiplier=-1),
                 w=[b_ptab])
            act(lambda: nc.scalar.activation(out=a_b[:], in_=a_b[:], func=AF.Exp), w=[b_ptab])
            dve(lambda: nc.vector.tensor_scalar(out=a_b[:], in0=a_b[:], scalar1=-1.0, scalar2=None, op0=ALU.mult),
                w=[b_ptab])
            dve(lambda: nc.vector.tensor_tensor(out=Dg[:], in0=ident[:].unsqueeze(1).to_broadcast([128, 16, 128]),
                                                in1=dsk_b[:].unsqueeze(2).to_broadcast([128, 16, 128]), op=ALU.mult),
                r=[b_const], w=[b_ptab])
            ck("Btab")

            for i in range(NT):
                for kc in range(8):
                    pe(lambda: nc.tensor.matmul(PB[0][:, i * 16:(i + 1) * 16], lhsT=hT[:, kc, i * 128:(i + 1) * 128],
                                                rhs=wdt[:, kc, :], start=(kc == 0), stop=(kc == 7)),
                       r=[b_hT[i], b_wdt], w=[b_PB[0]], inc=(kc == 7))
            dve(lambda: nc.vector.tensor_tensor(out=spt[0][:], in0=PB[0][:, 0:256].rearrange("p (i h) -> p i h", h=16),
                                                in1=dtb_b[:].unsqueeze(1).to_broadcast([128, NT, 16]), op=ALU.add),
                r=[b_PB[0]], w=[b_ptab])
            dve(lambda: nc.vector.tensor_scalar(out=spt[2][:], in0=spt[0][:], scalar1=-1.0, scalar2=None, op0=ALU.mult),
                w=[b_ptab])
            dve(lambda: nc.vector.tensor_tensor(out=spt[1][:], in0=spt[0][:], in1=spt[2][:], op=ALU.max), w=[b_ptab])
            act(lambda: nc.scalar.activation(out=spt[1][:], in_=spt[1][:], func=AF.Exp, scale=-1.0), w=[b_ptab])
            act(lambda: nc.scalar.activation(out=spt[1][:], in_=spt[1][:], func=AF.Ln, bias=1.0), w=[b_ptab])
            dve(lambda: nc.vector.tensor_scalar(out=spt[2][:], in0=spt[0][:], scalar1=0.0, scalar2=None, op0=ALU.max),
                w=[b_ptab])
            dve(lambda: nc.vector.tensor_tensor(out=dt_all[:], in0=spt[2][:], in1=spt[1][:], op=ALU.add), w=[b_ptab])
            dve(lambda: nc.vector.tensor_tensor(out=dA_all[:], in0=dt_all[:],
                                                in1=a_b[:].unsqueeze(1).to_broadcast([128, NT, 16]), op=ALU.mult),
                w=[b_ptab])
            if "dt" in dbg:
                k.dump("d_dt", dt_all[:], b_ptab)
            ck("Bdt")

            with ExitStack() as esC:
                cwT = k.sb("cwT_sb", [128, 12, 4], F32, esC)
                cbT = k.sb("cbT_sb", [128, 12], F32, esC)
                b_cw = Buf()
                dsync(out=cwT[:], in_=cwT_d[:, :, :], w=[b_cw])
                dsync(out=cbT[:], in_=cbT_d[:, :], w=[b_cw])
                pre = [k.sb(f"pre{j}", [128, S + 3], F32, esC) for j in range(2)]
                b_pre = [Buf(), Buf()]
                accs = [k.sb(f"acc{j}", [128, S], F32, esC) for j in range(2)]
                b_accs = [Buf(), Buf()]
                xs_fm = k.sb("xs_fm", [128, S], BF16, esC)
                b_xs = Buf()
                for q in range(2):
                    pool(lambda: nc.gpsimd.memset(pre[q][:, 0:3], 0.0), w=[b_pre[q]])
                b_xs2 = [Buf(), Buf()]
                b_cv = [Buf() for _ in range(12)]

                def cproj(m):
                    q = m % 2
                    slot = (m // 4) % 2
                    wc0 = slot * 512 + (m % 4) * 128
                    for tc in range(4):
                        for kc in range(8):
                            pe(lambda: nc.tensor.matmul(PB[tc][:], lhsT=wBC[:, kc, wc0:wc0 + 128],
                                                        rhs=hT[:, kc, tc * 512:(tc + 1) * 512],
                                                        start=(kc == 0), stop=(kc == 7)),
                               r=b_hT[tc * 4:(tc + 1) * 4] + [b_wslot[slot]], w=[b_PB[tc]], inc=(kc == 7))
                        act(lambda: nc.scalar.copy(out=pre[q][:, 3 + tc * 512:3 + (tc + 1) * 512], in_=PB[tc][:]),
                            r=[b_PB[tc]], w=[b_pre[q]])

                def cpost(m):
                    q = m % 2
                    acc = accs[q]
                    b_acc = b_accs[q]
                    dve(lambda: nc.vector.tensor_scalar(out=acc[:], in0=pre[q][:, 0:S], scalar1=cwT[:, m, 0:1],
                                                        scalar2=None, op0=ALU.mult),
                        r=[b_pre[q], b_cw], w=[b_acc])
                    for kk in range(1, 4):
                        dve(lambda: nc.vector.scalar_tensor_tensor(out=acc[:], in0=pre[q][:, kk:kk + S],
                                                                   scalar=cwT[:, m, kk:kk + 1], in1=acc[:],
                                                                   op0=ALU.mult, op1=ALU.add),
                            r=[b_pre[q], b_cw], w=[b_acc])
                    if m < 10:
                        dst = xs_fm[:] if m < 8 else BT[:, m - 8, :]
                        bdst = b_xs if m < 8 else b_cv[m]
                        act(lambda: nc.scalar.activation(out=dst, in_=acc[:], func=AF.Silu, bias=cbT[:, m:m + 1]),
                            r=[b_acc, b_cw], w=[bdst])
                        for half in range(2):
                            bk = 4 + half
                            tv = bfv(bk)
                            for s8 in range(8):
                                ti_ = half * 8 + s8
                                in_ap = (xs_fm[:, ti_ * 128:(ti_ + 1) * 128] if m < 8
                                         else BT[:, m - 8, ti_ * 128:(ti_ + 1) * 128])
                                pe(lambda: nc.tensor.transpose(out=tv[:, s8, :], in_=in_ap, identity=ident[:]),
                                   r=[bdst, b_const], w=[b_PB[bk]], inc=(s8 == 7))
                            if m < 8:
                                act(lambda: nc.scalar.copy(out=X_tm[:, half * 8:(half + 1) * 8, m * 128:(m + 1) * 128],
                                                           in_=tv[:, :, :]), r=[b_PB[bk]], w=[b_cv[m]])
                            else:
                                act(lambda: nc.scalar.copy(out=B_tm[:, half * 8:(half + 1) * 8,
                                                                    (m - 8) * 128:(m - 7) * 128],
                                                           in_=tv[:, :, :]), r=[b_PB[bk]], w=[b_cv[m]])
                    else:
                        act(lambda: nc.scalar.activation(out=CT[:, m - 10, :], in_=acc[:], func=AF.Silu,
                                                         bias=cbT[:, m:m + 1]),
                            r=[b_acc, b_cw], w=[b_cv[m]])

                def wreload(m_done):
                    if m_done == 3:
                        dpool(out=wBC[:, :, 0:512], in_=win_d[:, CONV0 + 1024:CONV0 + 1536]
                              .rearrange("(k p) n -> p k n", p=128), w=[b_wslot[0]])
                    elif m_done == 7:
                        dpool(out=wBC[:, :, 512:1024], in_=win_d[:, 1604:2116]
                              .rearrange("(k p) n -> p k n", p=128), w=[b_wslot[1]])
                    elif m_done == 11:
                        dpool(out=wBC[:, :, 0:512], in_=win_d[:, 2116:2628]
                              .rearrange("(k p) n -> p k n", p=128), w=[b_wslot[0]])

                cproj(0)
                wreload(0)
                for m in range(12):
                    if m + 1 < 12:
                        cproj(m + 1)
                        wreload(m + 1)
                    cpost(m)
                    ck(f"Bconv{m}")
                b_X.w = {}
                for bb in b_cv:
                    _merge(b_X.w, bb.w)
                if "conv" in dbg:
                    k.dump("d_Xtm", X_tm[:], b_X)
                    k.dump("d_Btm", B_tm[:], b_X)
                    k.dump("d_BT", BT[:], b_X)
                    k.dump("d_CT", CT[:], b_X)
                ck("Bconv")
                k.barrier()

            with ExitStack() as esS:
                ones_b = k.sb("ones_b", [128, 128], BF16, esS)
                dAhl = k.sb("dAhl", [128, NT, 2, 16], BF16, esS)
                dAres = spt[0]
                b_hl = Buf()
                pool(lambda: nc.gpsimd.memset(ones_b[:], 1.0), w=[b_hl])
                dve(lambda: nc.vector.tensor_copy(out=dAhl[:, :, 0, :], in_=dA_all[:]), r=[b_ptab], w=[b_hl])
                dve(lambda: nc.vector.tensor_tensor(out=dAres[:], in0=dA_all[:], in1=dAhl[:, :, 0, :], op=ALU.subtract),
                    r=[b_ptab], w=[b_hl])
                dve(lambda: nc.vector.tensor_copy(out=dAhl[:, :, 1, :], in_=dAres[:]), w=[b_hl])
                szb = [k.sb(f"szb{j}", [128, 1024], BF16, esS) for j in range(2)]
                b_szb = [Buf(), Buf()]
                smalls = [k.sb(f"small{j}", [128, 32], F32, esS) for j in range(2)]
                nacums = [k.sb(f"nacum{j}", [128, 16], F32, esS) for j in range(2)]
                eas = [k.sb(f"ea{j}", [128, 16], F32, esS) for j in range(2)]
                decs = [k.sb(f"dec{j}", [128, 16], F32, esS) for j in range(2)]
                dtds = [k.sb(f"dtd{j}", [128, 16], F32, esS) for j in range(2)]
                eASs = [k.sb(f"eAS{j}", [128, 2, 4], F32, esS) for j in range(2)]
                b_sms = [Buf(), Buf()]
                LTg = [k.sb(f"LTg{j}", [128, 4, 128], F32, esS) for j in range(2)]
                b_LT = [Buf(), Buf()]
                MTg = [k.sb(f"MTg{j}", [128, 4, 128], BF16, esS) for j in range(2)]
                b_MT = [Buf(), Buf()]
                CBs = [k.sb("CBs0", [128, 4, 128], F32, esS)] * 2
                b_CBs = [Buf()] * 2
                xds = [k.sb(f"xd{j}", [128, 16, 64], BF16, esS) for j in range(2)]
                xdds = [k.sb(f"xdd{j}", [128, 16, 64], BF16, esS) for j in range(2)]
                b_xds = [Buf(), Buf()]
                b_xdds = [Buf(), Buf()]
                ysb = k.sb("ysb", [128, 16, 64], F32, esS)
                b_y = Buf()
                ssq = k.sb("ssq", [128, 4], F32, esS)
                rs4 = k.sb("rs4", [128, 4], F32, esS)
                junkf = k.sb("junkf", [128, 256], F32, esS)
                ob_sb = k.sb("ob_sb", [128, 1024], BF16, esS)
                b_ob = Buf()
                S_sb = k.sb("S_sb", [128, 2, 256], F32, esS)
                S_bf = k.sb("S_bf", [128, 2, 256], BF16, esS)
                b_S = Buf()
                b_Sbf = Buf()
                gctr = {"g": 0, "d": 0}
                Yv = [PB[4][:].rearrange("p (h d) -> p h d", d=64), PB[5][:].rearrange("p (h d) -> p h d", d=64)]

                def head(c):
                    q = c % 2
                    csl = slice(c * 128, (c + 1) * 128)
                    small, nacum, ea, dec, dtd, eAS, b_sm = smalls[q], nacums[q], eas[q], decs[q], dtds[q], eASs[q], b_sms[q]
                    pe(lambda: nc.tensor.matmul(PB[2][:, 0:16], lhsT=Uf[:], rhs=dA_all[:, c, :], start=True, stop=False),
                       r=[b_const, b_ptab], w=[b_PB[2]], inc=False)
                    pe(lambda: nc.tensor.matmul(PB[2][:, 16:32], lhsT=ones_f[:], rhs=dA_all[:, c, :], start=False,
                                                stop=True), r=[b_ptab], w=[b_PB[2]])
                    CBv = PB[3][:].rearrange("p (g l) -> p g l", l=128)
                    tk = None
                    for gi, g in enumerate((0, 2, 1, 3)):
                        p0 = (g % 2) * 64
                        tk2 = pe(lambda: nc.tensor.matmul(CBv[:, g, :], lhsT=BT[p0:p0 + 64, g // 2, csl],
                                                          rhs=CT[p0:p0 + 64, g // 2, csl], start=(gi == 0),
                                                          stop=(gi == 3)),
                                 r=[b_X], w=[b_PB[3]], inc=(gi == 1 or gi == 3), selfwait=(tk if gi == 2 else None))
                        if gi == 1:
                            tk = tk2
                    for hb in range(2):
                        for kc in range(8):
                            pe(lambda: nc.tensor.matmul(PB[hb][:], lhsT=hT[:, kc, csl],
                                                        rhs=wBC[:, kc, (1 - hb) * 512:(2 - hb) * 512],
                                                        start=(kc == 0), stop=(kc == 7)),
                               r=[b_hT[c], b_wslot[1 - hb]], w=[b_PB[hb]], inc=(kc == 7))
                    yield
                    dve(lambda: nc.vector.tensor_copy(out=small[:], in_=PB[2][:, 0:32]), r=[b_PB[2]], w=[b_sm])
                    acum = small[:, 0:16]
                    atot = small[:, 16:32]
                    dve(lambda: nc.vector.tensor_tensor(out=dec[:], in0=atot, in1=acum, op=ALU.subtract), w=[b_sm])
                    act(lambda: nc.scalar.activation(out=ea[:], in_=acum, func=AF.Exp), w=[b_sm])
                    act(lambda: nc.scalar.activation(out=dec[:], in_=dec[:], func=AF.Exp), w=[b_sm])
                    atv = small[:, 16:32].rearrange("p (s f h) -> p s f h", s=2, f=2)
                    act(lambda: nc.scalar.activation(out=eAS[0:64], in_=atv[0:64, :, 0, :], func=AF.Exp), w=[b_sm])
                    act(lambda: nc.scalar.activation(out=eAS[64:128], in_=atv[64:128, :, 1, :], func=AF.Exp), w=[b_sm])
                    act(lambda: nc.scalar.copy(out=CBs[q][:], in_=CBv), r=[b_PB[3]], w=[b_CBs[q]])
                    for hb in range(2):
                        act(lambda: nc.scalar.activation(out=szb[q][:, hb * 512:(hb + 1) * 512], in_=PB[hb][:],
                                                         func=AF.Silu), r=[b_PB[hb]], w=[b_szb[q]])
                    dve(lambda: nc.vector.tensor_tensor(out=dtd[:], in0=dt_all[:, c, :], in1=dec[:], op=ALU.mult),
                        r=[b_ptab], w=[b_sm])
                    Xc = X_tm[:, c, :].rearrange("p (h d) -> p h d", d=64)
                    dve(lambda: nc.vector.tensor_tensor(out=xds[q][:], in0=Xc,
                                                        in1=dt_all[:, c, :].unsqueeze(2).to_broadcast([128, 16, 64]),
                                                        op=ALU.mult), r=[b_X, b_ptab], w=[b_xds[q]])
                    dve(lambda: nc.vector.tensor_tensor(out=xdds[q][:], in0=Xc,
                                                        in1=dtd[:].unsqueeze(2).to_broadcast([128, 16, 64]),
                                                        op=ALU.mult), r=[b_X, b_sm], w=[b_xdds[q]])

                    yield

                def groups(c):
                    q = c % 2
                    csl = slice(c * 128, (c + 1) * 128)
                    nacum, b_sm = nacums[q], b_sms[q]
                    xd = xds[q]
                    st = {}

                    def acumb(g):
                        gq = gctr["g"] % 2
                        gctr["g"] += 1
                        abk = 2 if gq == 0 else 6
                        first = True
                        for hh in range(4):
                            hd = 4 * g + hh
                            for part in range(2):
                                pe(lambda: nc.tensor.matmul(PB[abk][:, hh * 128:(hh + 1) * 128],
                                                            lhsT=dAhl[:, c, part, hd:hd + 1].to_broadcast([128, 128]),
                                                            rhs=Ub[:], start=first, stop=False),
                                   r=[b_hl, b_const], w=[b_PB[abk]], inc=False)
                                first = False
                        for part in range(2):
                            pe(lambda: nc.tensor.matmul(
                                PB[abk][:].rearrange("p (h l) -> p h l", l=128), lhsT=negUb[:],
                                rhs=dAhl[:, c, part, 4 * g:4 * g + 4].unsqueeze(2).to_broadcast([128, 4, 128]),
                                start=False, stop=False), r=[b_hl, b_const], w=[b_PB[abk]], inc=False)
                        pe(lambda: nc.tensor.matmul(PB[abk][:], lhsT=ident[:],
                                                    rhs=NEGU4[:].rearrange("p h l -> p (h l)"), start=False, stop=True),
                           r=[b_const, b_ptab], w=[b_PB[abk]])
                        st[g] = (gq, abk)

                    def ymm(g):
                        gq, abk = st[g]
                        act(lambda: nc.scalar.activation(out=LTg[gq][:].rearrange("p h l -> p (h l)"), in_=PB[abk][:],
                                                         func=AF.Exp), r=[b_PB[abk]], w=[b_LT[gq]])
                        dve(lambda: nc.vector.tensor_tensor(out=MTg[gq][:], in0=LTg[gq][:],
                                                            in1=CBs[q][:, g:g + 1, :].to_broadcast([128, 4, 128]),
                                                            op=ALU.mult),
                            r=[b_LT[gq], b_CBs[q]], w=[b_MT[gq]])
                        for hh in range(4):
                            hd = 4 * g + hh
                            yb = 4 + hd // 8
                            pe(lambda: nc.tensor.matmul(Yv[hd // 8][:, hd % 8, :], lhsT=MTg[gq][:, hh, :],
                                                        rhs=xd[:, hd, :], start=(hd % 8 == 0), stop=False),
                               r=[b_MT[gq], b_xds[q]], w=[b_PB[yb]], inc=False)
                            pe(lambda: nc.tensor.matmul(Yv[hd // 8][:, hd % 8, :], lhsT=Dg[:, hd, :],
                                                        rhs=X_tm[:, c, hd * 64:(hd + 1) * 64], start=False,
                                                        stop=(hd % 8 == 7)),
                               r=[b_ptab, b_X], w=[b_PB[yb]], inc=(hh == 3))

                    acumb(0)
                    acumb(1)
                    yield
                    ymm(0)
                    yield
                    acumb(2)
                    ymm(1)
                    yield
                    acumb(3)
                    ymm(2)
                    yield
                    ymm(3)
                    yield

                def tail(c):
                    q = c % 2
                    csl = slice(c * 128, (c + 1) * 128)
                    ea, eAS, b_sm = eas[q], eASs[q], b_sms[q]
                    xdd = xdds[q]
                    if c > 0:
                        tk = None
                        for gi, g in enumerate((0, 2, 1, 3)):
                            p0 = (g % 2) * 64
                            ob_ = 6 + g // 2
                            tk2 = pe(lambda: nc.tensor.matmul(PB[ob_][:, (g % 2) * 256:(g % 2 + 1) * 256],
                                                              lhsT=CT[p0:p0 + 64, g // 2, csl],
                                                              rhs=S_bf[p0:p0 + 64, g // 2, :], start=(g % 2 == 0),
                                                              stop=(g % 2 == 1)),
                                     r=[b_X, b_Sbf], w=[b_PB[ob_]], inc=(gi >= 1),
                                     selfwait=(tk if gi == 2 else None))
                            if gi == 1:
                                tk = tk2
                        for hb in range(2):
                            dve(lambda: nc.vector.tensor_tensor(
                                out=ysb[:, hb * 8:(hb + 1) * 8, :],
                                in0=PB[6 + hb][:].rearrange("p (h d) -> p h d", d=64),
                                in1=ea[:, hb * 8:(hb + 1) * 8].unsqueeze(2).to_broadcast([128, 8, 64]), op=ALU.mult),
                                r=[b_PB[6 + hb], b_sm], w=[b_y])
                            dve(lambda: nc.vector.tensor_tensor(out=ysb[:, hb * 8:(hb + 1) * 8, :], in0=Yv[hb],
                                                                in1=ysb[:, hb * 8:(hb + 1) * 8, :], op=ALU.add),
                                r=[b_PB[4 + hb]], w=[b_y])
                    else:
                        for hb in range(2):
                            dve(lambda: nc.vector.tensor_copy(out=ysb[:, hb * 8:(hb + 1) * 8, :], in_=Yv[hb]),
                                r=[b_PB[4 + hb]], w=[b_y])
                    yield
                    dve(lambda: nc.vector.tensor_tensor(out=ysb[:].rearrange("p h d -> p (h d)"),
                                                        in0=ysb[:].rearrange("p h d -> p (h d)"), in1=szb[q][:],
                                                        op=ALU.mult), r=[b_szb[q]], w=[b_y])
                    yf = ysb[:].rearrange("p h d -> p (h d)")
                    for g in range(4):
                        act(lambda: nc.scalar.activation(out=junkf[:], in_=yf[:, g * 256:(g + 1) * 256], func=AF.Square,
                                                         accum_out=ssq[:, g:g + 1]), r=[b_y], w=[b_ob])
                    pool(lambda: nc.gpsimd.tensor_scalar(out=rs4[:], in0=ssq[:], scalar1=1.0 / 256, scalar2=EPS,
                                                         op0=ALU.mult, op1=ALU.add), w=[b_ob])
                    pool(lambda: nc.gpsimd.tensor_tensor(out=rs4[:], in0=rs4[:], in1=mhalf[:, 0:4], op=ALU.pow),
                         r=[b_const], w=[b_ob])
                    yield
                    if c < NT - 1:
                        for g in range(4):
                            p0 = (g % 2) * 64
                            pe(lambda: nc.tensor.matmul(PB[7][p0:p0 + 64, (g // 2) * 256:(g // 2 + 1) * 256],
                                                        lhsT=B_tm[:, c, g * 64:(g + 1) * 64],
                                                        rhs=xdd[:, 4 * g:4 * g + 4, :].rearrange("p h d -> p (h d)"),
                                                        start=(g < 2), stop=(g >= 2)),
                               r=[b_X, b_xdds[q]], w=[b_PB[7]], inc=(g == 3))
                        Sv = S_sb[:].rearrange("p s (h d) -> p (s h) d", d=64)
                        if c == 0:
                            dve(lambda: nc.vector.tensor_copy(out=S_sb[:].rearrange("p s f -> p (s f)"), in_=PB[7][:]),
                                r=[b_PB[7]], w=[b_S])
                        else:
                            dve(lambda: nc.vector.tensor_tensor(
                                out=Sv, in0=Sv,
                                in1=eAS[:].rearrange("p s h -> p (s h)").unsqueeze(2).to_broadcast([128, 8, 64]),
                                op=ALU.mult), r=[b_sm, b_Sbf], w=[b_S])
                            dve(lambda: nc.vector.tensor_tensor(out=S_sb[:].rearrange("p s f -> p (s f)"),
                                                                in0=S_sb[:].rearrange("p s f -> p (s f)"),
                                                                in1=PB[7][:], op=ALU.add),
                                r=[b_PB[7]], w=[b_S])
                        act(lambda: nc.scalar.copy(out=S_bf[:], in_=S_sb[:]), r=[b_S], w=[b_Sbf])
                    yield
                    for g in range(4):
                        dve(lambda: nc.vector.scalar_tensor_tensor(out=ob_sb[:, g * 256:(g + 1) * 256],
                                                                   in0=yf[:, g * 256:(g + 1) * 256],
                                                                   scalar=rs4[:, g:g + 1],
                                                                   in1=snw_b[:, g * 256:(g + 1) * 256],
                                                                   op0=ALU.mult, op1=ALU.mult),
                            r=[b_y, b_ptab], w=[b_ob])
                    yield
                    tv = bfv(7)
                    for cc in range(8):
                        pe(lambda: nc.tensor.transpose(out=tv[:, cc, :], in_=ob_sb[:, cc * 128:(cc + 1) * 128],
                                                       identity=ident[:]),
                           r=[b_ob, b_const], w=[b_PB[7]], inc=(cc == 7))
                    act(lambda: nc.scalar.copy(out=obT[:, :, csl], in_=tv[:, :, :]), r=[b_PB[7]], w=[b_obT[c]])

                def front(c):
                    yield from head(c)
                    yield from groups(c)

                def interleave(gens):
                    gens = list(gens)
                    while gens:
                        for g_ in list(gens):
                            try:
                                next(g_)
                            except StopIteration:
                                gens.remove(g_)

                interleave([front(0)])
                for c in range(NT):
                    gl = [tail(c)]
                    if c + 1 < NT:
                        gl.append(front(c + 1))
                    interleave(gl)
                    if c == NT - 2:
                        dpool(out=wG0[:, :, 0, :], in_=win_d[:, 4180:4180 + 128].rearrange("(k p) n -> p k n", p=128),
                              w=[b_wslot[0]])
                        dpool(out=wG0[:, :, 1, :], in_=win_d[:, 4180 + 1024:4180 + 1152]
                              .rearrange("(k p) n -> p k n", p=128), w=[b_wslot[0]])
                        dpool(out=Wpa0, in_=wpa_d[:, 0:128].rearrange("(k p) n -> p k n", p=128), w=[b_wslot[0]])
                        dpool(out=Wpb0, in_=wpb_d[:, 0:128].rearrange("(k p) n -> p k n", p=128), w=[b_wslot[0]])
                if "obT" in dbg:
                    k.dump("d_obT", obT[:], b_obT)
                k.barrier()
        if stop_after.startswith("B"):
            return nc, k


        with ExitStack() as esM:
            Wout = k.sb("Wout", [128, 8, D], BF16, esM)
            b_W = Buf()
            gbT = k.sb("gbT_sb", [128, 16], F32, esM)
            fnw_b = k.sb("fnw_b", [128, D], F32, esM)
            b_ct = Buf()
            dsync(out=gbT[:], in_=gbT_d[:, :], w=[b_ct])
            dsync(out=fnw_b[:], in_=fnw_d[0, :].partition_broadcast(128), w=[b_ct])
            mT = k.sb("mT", [128, 8, S], BF16, esM)
            b_mT = [Buf() for _ in range(4)]
            with ExitStack() as esM1:
                wG = [k.sb(f"wG{j}", [128, 8, 2, 128], BF16, esM1) for j in range(2)]
                Wpa = [k.sb(f"Wpa{j}", [128, 4, 128], BF16, esM1) for j in range(2)]
                Wpb = [k.sb(f"Wpb{j}", [128, 8, 128], BF16, esM1) for j in range(2)]
                b_wG = [Buf(), Buf()]
                gA = [k.sb(f"gA{j}", [128, 512], F32, esM1) for j in range(2)]
                gB = [k.sb(f"gB{j}", [128, 512], F32, esM1) for j in range(2)]
                b_g = [Buf(), Buf()]
                t1 = [k.sb(f"t1{j}", [128, 512], F32, esM1) for j in range(2)]
                t2 = [k.sb(f"t2{j}", [128, 512], F32, esM1) for j in range(2)]
                b_t = [Buf(), Buf()]
                it = 0
                for m in range(8):
                    wq = m % 2
                    if m == 0:
                        gw_, pa_, pb_, bw_ = wG0, Wpa0, Wpb0, b_wslot[0]
                    else:
                        gw_, pa_, pb_, bw_ = wG[wq][:], Wpa[wq][:], Wpb[wq][:], b_wG[wq]
                        dpool(out=gw_[:, :, 0, :], in_=win_d[:, 4180 + m * 128:4180 + (m + 1) * 128]
                              .rearrange("(k p) n -> p k n", p=128), w=[bw_])
                        dpool(out=gw_[:, :, 1, :], in_=win_d[:, 4180 + (8 + m) * 128:4180 + (9 + m) * 128]
                              .rearrange("(k p) n -> p k n", p=128), w=[bw_])
                        dpool(out=pa_, in_=wpa_d[:, m * 128:(m + 1) * 128].rearrange("(k p) n -> p k n", p=128),
                              w=[bw_])
                        dpool(out=pb_, in_=wpb_d[:, m * 128:(m + 1) * 128].rearrange("(k p) n -> p k n", p=128),
                              w=[bw_])
                    if m == 0:
                        for hf in range(2):
                            cs_ = slice(hf * 512, (hf + 1) * 512)
                            dpool(out=Wout[:, :, cs_], in_=wout_d[:, cs_].rearrange("(k p) n -> p k n", p=128),
                                  w=[b_W])
                    for tc in range(4):
                        q = it % 2
                        it += 1
                        b0 = 4 * q
                        ts_ = slice(tc * 512, (tc + 1) * 512)
                        for kc in range(8):
                            pe(lambda: nc.tensor.matmul(PB[b0][:], lhsT=gw_[:, kc, 0, :], rhs=hT[:, kc, ts_],
                                                        start=(kc == 0), stop=(kc == 7)),
                               r=b_hT[tc * 4:(tc + 1) * 4] + [bw_], w=[b_PB[b0]], inc=(kc == 7))
                        for kc in range(8):
                            pe(lambda: nc.tensor.matmul(PB[b0 + 1][:], lhsT=gw_[:, kc, 1, :], rhs=hT[:, kc, ts_],
                                                        start=(kc == 0), stop=(kc == 7)),
                               r=b_hT[tc * 4:(tc + 1) * 4] + [bw_], w=[b_PB[b0 + 1]], inc=(kc == 7))
                        for kc in range(4):
                            pe(lambda: nc.tensor.matmul(PB[b0 + 2][:], lhsT=pa_[:, kc, :],
                                                        rhs=oaT[:, kc, ts_], start=(kc == 0), stop=(kc == 3)),
                               r=b_oaT[tc * 4:(tc + 1) * 4] + [bw_], w=[b_PB[b0 + 2]], inc=(kc == 3))
                        for kc in range(8):
                            pe(lambda: nc.tensor.matmul(PB[b0 + 3][:], lhsT=pb_[:, kc, :],
                                                        rhs=obT[:, kc, ts_], start=(kc == 0), stop=(kc == 7)),
                               r=b_obT[tc * 4:(tc + 1) * 4] + [bw_], w=[b_PB[b0 + 3]], inc=(kc == 7))
                        act(lambda: nc.scalar.activation(out=gA[q][:], in_=PB[b0][:], func=AF.Sigmoid,
                                                         bias=gbT[:, m:m + 1]), r=[b_PB[b0], b_ct], w=[b_g[q]])
                        act(lambda: nc.scalar.activation(out=gB[q][:], in_=PB[b0 + 1][:], func=AF.Sigmoid,
                                                         bias=gbT[:, 8 + m:9 + m]), r=[b_PB[b0 + 1], b_ct], w=[b_g[q]])
                        dve(lambda: nc.vector.tensor_tensor(out=t1[q][:], in0=PB[b0 + 2][:], in1=gA[q][:], op=ALU.mult),
                            r=[b_PB[b0 + 2], b_g[q]], w=[b_t[q]])
                        dve(lambda: nc.vector.tensor_tensor(out=t2[q][:], in0=PB[b0 + 3][:], in1=gB[q][:], op=ALU.mult),
                            r=[b_PB[b0 + 3], b_g[q]], w=[b_t[q]])
                        dve(lambda: nc.vector.tensor_tensor(out=mT[:, m, ts_], in0=t1[q][:], in1=t2[q][:], op=ALU.add),
                            r=[b_t[q]], w=[b_mT[tc]])
                k.barrier()
            if "mT" in dbg:
                k.dump("d_mT", mT[:], b_mT)
            ck("Cm")
            xr = [k.sb(f"xr{j}", [128, D], F32, esM) for j in range(2)]
            b_xr = [Buf(), Buf()]
            xo = [k.sb(f"xo{j}", [128, D], F32, esM) for j in range(2)]
            b_xo = [Buf(), Buf()]
            fo = [k.sb(f"fo{j}", [128, D], F32, esM) for j in range(2)]
            b_fo = [Buf(), Buf()]
            junkc = k.sb("junkc", [128, D], BF16, esM)
            fss = k.sb("fss", [128, NT], F32, esM)
            b_fs = Buf()
            dpool(out=xr[0][:], in_=x_d[0:128, :], w=[b_xr[0]])
            b_fsi = [Buf() for _ in range(NT)]

            def out1(i):
                q = i % 2
                tsl = slice(i * 128, (i + 1) * 128)
                if i + 1 < NT:
                    dpool(out=xr[1 - q][:], in_=x_d[(i + 1) * 128:(i + 2) * 128, :], w=[b_xr[1 - q]])
                for hf in range(2):
                    bk = 2 * q + hf
                    for kc in range(8):
                        pe(lambda: nc.tensor.matmul(PB[bk][:], lhsT=mT[:, kc, tsl], rhs=Wout[:, kc, hf * 512:(hf + 1) * 512],
                                                    start=(kc == 0), stop=(kc == 7)),
                           r=[b_mT[i // 4], b_W], w=[b_PB[bk]], inc=(kc == 7))
                    dve(lambda: nc.vector.tensor_tensor(out=xo[q][:, hf * 512:(hf + 1) * 512], in0=PB[bk][:],
                                                        in1=xr[q][:, hf * 512:(hf + 1) * 512], op=ALU.add),
                        r=[b_PB[bk], b_xr[q]], w=[b_xo[q]])
                act(lambda: nc.scalar.activation(out=junkc[:], in_=xo[q][:], func=AF.Square, accum_out=fss[:, i:i + 1]),
                    r=[b_xo[q]], w=[b_fs, b_fsi[i]])
                act(lambda: nc.scalar.activation(out=fss[:, i:i + 1], in_=fss[:, i:i + 1], func=AF.Sqrt, scale=1.0 / D,
                                                 bias=EPS), w=[b_fsi[i]])

            def out2(i):
                q = i % 2
                tsl = slice(i * 128, (i + 1) * 128)
                dve(lambda: nc.vector.reciprocal(out=fss[:, i:i + 1], in_=fss[:, i:i + 1]), w=[b_fsi[i]])
                dve(lambda: nc.vector.scalar_tensor_tensor(out=fo[q][:], in0=xo[q][:], scalar=fss[:, i:i + 1],
                                                           in1=fnw_b[:], op0=ALU.mult, op1=ALU.mult),
                    r=[b_xo[q], b_ct, b_fsi[i]], w=[b_fo[q]])
                dsync(out=out_d[tsl, :], in_=fo[q][:], r=[b_fo[q]])

            out1(0)
            for i in range(NT):
                if i + 1 < NT:
                    out1(i + 1)
                out2(i)
            k.barrier()

        k.barrier()
    return nc, k


_NC_CACHE = {}


def kernel(x, positions, norm_w, w_in, gate_bias, conv_w, conv_b, dt_bias, a_log, d_skip,
           ssm_norm_w, w_branch_a, w_branch_b, w_out, final_norm_w):
    f32 = np.float32
    x = np.asarray(x, dtype=f32)
    positions = np.asarray(positions).astype(np.int32)
    nb = x.shape[0]
    assert nb == 8 and x.shape[1] == S and x.shape[2] == D
    if "nc" not in _NC_CACHE:
        _NC_CACHE["nc"] = build()[0]
    nc = _NC_CACHE["nc"]
    invf = (500000.0 ** (-np.arange(0, 16, 2, dtype=f32) / 16)).astype(f32)
    shared = {
        "invf": np.ascontiguousarray(np.broadcast_to(invf, (128, 8))),
        "norm_w": np.ascontiguousarray(np.asarray(norm_w, f32).reshape(1, D)),
        "w_in": np.ascontiguousarray(np.asarray(w_in, f32)[0]),
        "cwT": np.ascontiguousarray(np.asarray(conv_w, f32)[0].reshape(4, 12, 128).transpose(2, 1, 0)),
        "cbT": np.ascontiguousarray(np.asarray(conv_b, f32)[0].reshape(12, 128).T),
        "dt_bias": np.ascontiguousarray(np.asarray(dt_bias, f32).reshape(1, 16)),
        "a_log": np.ascontiguousarray(np.asarray(a_log, f32).reshape(1, 16)),
        "d_skip": np.ascontiguousarray(np.asarray(d_skip, f32).reshape(1, 16)),
        "ssm_norm_w": np.ascontiguousarray(np.asarray(ssm_norm_w, f32).reshape(1, D)),
        "gbT": np.ascontiguousarray(np.asarray(gate_bias, f32)[0].reshape(16, 128).T),
        "w_branch_a": np.ascontiguousarray(np.asarray(w_branch_a, f32)[0]),
        "w_branch_b": np.ascontiguousarray(np.asarray(w_branch_b, f32)[0]),
        "w_out": np.ascontiguousarray(np.asarray(w_out, f32)[0]),
        "final_norm_w": np.ascontiguousarray(np.asarray(final_norm_w, f32).reshape(1, D)),
    }
    in_maps = []
    for b in range(nb):
        m = dict(shared)
        m["x"] = np.ascontiguousarray(x[b])
        m["posT"] = np.ascontiguousarray(positions[b].reshape(NT, 128).T)
        in_maps.append(m)
    res = run_bass_kernel_spmd(nc, in_maps, core_ids=list(range(nb)))
    out = np.stack([np.asarray(res.results[b]["out"], dtype=f32) for b in range(nb)], axis=0)
    return out
```
